# Optimizing a Trainium2 kernel written in Bass

```python
import jax, jax.numpy as jnp
from jax import lax
import numpy as np

D_MODEL = 2048
BATCH = 2
SEQ = 16384
DEPTH = 2

ALPHA = (2 * DEPTH) ** 0.25
BETA = (8 * DEPTH) ** -0.25
LN_EPS = 1e-5
RMS_EPS = 1e-6
ROPE_THETA = 10000.0

MLA_HEADS = 8
MLA_Q_RANK = 448
MLA_KV_RANK = 128
MLA_NOPE = 128
MLA_ROPE = 64
MLA_V = 128
Q_BLOCK = 128
RWKV_HEADS = 16
RWKV_HEAD = 64
RWKV_DIM = RWKV_HEADS * RWKV_HEAD
DECAY_LORA = 64
AAA_LORA = 64
GATE_LORA = 160
RWKV_GN_EPS = 64e-5
GDN_QK_HEADS = 4
GDN_V_HEADS = 8
GDN_DK = 128
GDN_DV = 128
GDN_CONV = 4
CHUNK = 64
RET_HEADS = 8
RET_DK = 64
RET_DV = 128
FFN_DIM = 5632
FFN_CONV = 3

MLA_COLS = MLA_Q_RANK + MLA_KV_RANK + MLA_ROPE
RWKV_COLS = 3 * RWKV_DIM + DECAY_LORA + AAA_LORA + GATE_LORA
EVEN_IN = MLA_COLS + RWKV_COLS
EVEN_OUT = MLA_HEADS * MLA_V + RWKV_DIM
GDN_COLS = 2 * GDN_QK_HEADS * GDN_DK + 2 * GDN_V_HEADS * GDN_DV + 2 * GDN_V_HEADS
RET_COLS = 2 * RET_HEADS * RET_DK + 2 * RET_HEADS * RET_DV
ODD_IN = GDN_COLS + RET_COLS
ODD_OUT = GDN_V_HEADS * GDN_DV + RET_HEADS * RET_DV

kernel_name = "hybrid_mla_rwkv7_gdn_retention_deepnorm"


def split_cols(p, sizes):
    return jnp.split(p, [int(s) for s in np.cumsum(sizes)[:-1]], axis=-1)


def layer_norm(x, g, b, eps=LN_EPS):
    xf = x.astype(jnp.float32)
    mu = xf.mean(-1, keepdims=True)
    var = jnp.square(xf - mu).mean(-1, keepdims=True)
    return ((xf - mu) * lax.rsqrt(var + eps) * g + b).astype(x.dtype)


def rms_norm(x, g, eps=RMS_EPS):
    xf = x.astype(jnp.float32)
    return (xf * lax.rsqrt(jnp.mean(xf * xf, -1, keepdims=True) + eps) * g).astype(x.dtype)


def head_norm(y, g, b, eps):
    B, S, H, d = y.shape
    yf = y.astype(jnp.float32)
    mu = yf.mean(-1, keepdims=True)
    var = jnp.square(yf - mu).mean(-1, keepdims=True)
    yn = (yf - mu) * lax.rsqrt(var + eps)
    return yn.reshape(B, S, H * d) * g + b


def l2_normalize(x, eps=1e-6):
    xf = x.astype(jnp.float32)
    return xf * lax.rsqrt(jnp.sum(xf * xf, -1, keepdims=True) + eps)


def rope(x, positions):
    d = x.shape[-1]
    inv = ROPE_THETA ** (-jnp.arange(0, d, 2, dtype=jnp.float32) / d)
    ang = positions.astype(jnp.float32)[..., None] * inv
    cos, sin = jnp.cos(ang)[:, :, None, :], jnp.sin(ang)[:, :, None, :]
    xf = x.astype(jnp.float32)
    x1, x2 = xf[..., : d // 2], xf[..., d // 2:]
    return jnp.concatenate([x1 * cos - x2 * sin, x1 * sin + x2 * cos], -1).astype(x.dtype)


def causal_dwconv(x, w):
    K, C = w.shape
    return lax.conv_general_dilated(x, w[:, None, :].astype(x.dtype), (1,), [(K - 1, 0)],
                                    dimension_numbers=("NWC", "WIO", "NWC"), feature_group_count=C)


def to_chunks(t):
    B, S = t.shape[:2]
    t = t.astype(jnp.float32).reshape((B, S // CHUNK, CHUNK) + t.shape[2:])
    return jnp.swapaxes(t, 2, 3)


def from_chunks(t):
    B, N, H, C, d = t.shape
    return jnp.swapaxes(t, 2, 3).reshape(B, N * C, H, d)


def causal_block_attention(q_nope, q_pe, k_nope, k_pe, v, scale):
    B, S, H, _ = q_nope.shape
    dv = v.shape[-1]
    kpos = jnp.arange(S)

    def one_block(i):
        start = i * Q_BLOCK
        qn = lax.dynamic_slice_in_dim(q_nope, start, Q_BLOCK, axis=1)
        qp = lax.dynamic_slice_in_dim(q_pe, start, Q_BLOCK, axis=1)
        s = (jnp.einsum('bqhd,bkhd->bhqk', qn, k_nope) +
             jnp.einsum('bqhr,bkr->bhqk', qp, k_pe)).astype(jnp.float32) * scale
        qpos = start + jnp.arange(Q_BLOCK)
        s = jnp.where(kpos[None, :] <= qpos[:, None], s, jnp.finfo(jnp.float32).min)
        p = jax.nn.softmax(s, axis=-1)
        return jnp.einsum('bhqk,bkhd->bqhd', p.astype(v.dtype), v)

    out = lax.map(one_block, jnp.arange(S // Q_BLOCK))
    return jnp.moveaxis(out, 0, 1).reshape(B, S, H, dv)


def rwkv7_recurrence(r, w, k, v, a_vec, b_vec):
    B, S, H, N = r.shape
    xs = tuple(jnp.moveaxis(t.astype(jnp.float32), 1, 0) for t in (r, w, k, v, a_vec, b_vec))

    def step(st, inp):
        r_t, w_t, k_t, v_t, a_t, b_t = inp
        sa = jnp.einsum('bhvk,bhk->bhv', st, a_t)
        st = st * w_t[:, :, None, :] + sa[..., None] * b_t[:, :, None, :] + v_t[..., None] * k_t[:, :, None, :]
        return st, jnp.einsum('bhvk,bhk->bhv', st, r_t)

    _, y = lax.scan(step, jnp.zeros((B, H, N, N), jnp.float32), xs)
    return jnp.moveaxis(y, 0, 1)


def gated_delta_rule_chunked(q, k, v, g, beta):
    B, S, H, dk = q.shape
    dv = v.shape[-1]
    q = to_chunks(q) * dk ** -0.5
    k, v, g, beta = to_chunks(k), to_chunks(v), to_chunks(g), to_chunks(beta)
    gc = jnp.cumsum(g, axis=-1)
    lower = jnp.tril(jnp.ones((CHUNK, CHUNK), bool))
    strict = jnp.tril(jnp.ones((CHUNK, CHUNK), bool), -1)
    diff = gc[..., :, None] - gc[..., None, :]
    decay = jnp.where(lower, jnp.exp(jnp.where(lower, diff, 0.0)), 0.0)
    k_beta = k * beta[..., None]
    a_mat = jnp.where(strict, jnp.einsum('bnhid,bnhjd->bnhij', k_beta, k) * decay, 0.0)
    t_mat = a_mat + jnp.eye(CHUNK, dtype=jnp.float32)
    solve = lambda rhs: lax.linalg.triangular_solve(t_mat, rhs, left_side=True, lower=True, unit_diagonal=True)
    value = solve(v * beta[..., None])
    k_cum = solve(k_beta * jnp.exp(gc)[..., None])
    attn = jnp.einsum('bnhid,bnhjd->bnhij', q, k) * decay
    q_dec = q * jnp.exp(gc)[..., None]
    k_dec = k * jnp.exp(gc[..., -1:] - gc)[..., None]
    g_last = jnp.exp(gc[..., -1])

    def step(st, inp):
        value_n, kcum_n, attn_n, qdec_n, kdec_n, glast_n = inp
        v_new = value_n - jnp.einsum('bhcd,bhde->bhce', kcum_n, st)
        o = jnp.einsum('bhcd,bhde->bhce', qdec_n, st) + jnp.einsum('bhij,bhje->bhie', attn_n, v_new)
        st = st * glast_n[..., None, None] + jnp.einsum('bhcd,bhce->bhde', kdec_n, v_new)
        return st, o

    xs = tuple(jnp.moveaxis(t, 1, 0) for t in (value, k_cum, attn, q_dec, k_dec, g_last))
    _, o = lax.scan(step, jnp.zeros((B, H, dk, dv), jnp.float32), xs)
    return from_chunks(jnp.moveaxis(o, 0, 1))


def retention_chunked(q, k, v):
    B, S, H, dk = q.shape
    dv = v.shape[-1]
    lg = jnp.log(1.0 - 2.0 ** (-5.0 - jnp.arange(H, dtype=jnp.float32)))
    q, k, v = to_chunks(q), to_chunks(k), to_chunks(v)
    idx = jnp.arange(CHUNK, dtype=jnp.float32)
    diff = idx[:, None] - idx[None, :]
    dmat = jnp.where(diff >= 0, jnp.exp(jnp.maximum(diff, 0.0)[None] * lg[:, None, None]), 0.0)
    inner = jnp.einsum('bnhij,bnhje->bnhie', jnp.einsum('bnhid,bnhjd->bnhij', q, k) * dmat, v)
    q_dec = q * jnp.exp((idx + 1.0)[None, :] * lg[:, None])[..., None]
    k_dec = k * jnp.exp((CHUNK - 1.0 - idx)[None, :] * lg[:, None])[..., None]
    u = jnp.einsum('bnhcd,bnhce->bnhde', k_dec, v)
    g_chunk = jnp.exp(CHUNK * lg)[:, None, None]

    def step(rst, u_n):
        return rst * g_chunk + u_n, rst

    _, r_prev = lax.scan(step, jnp.zeros((B, H, dk, dv), jnp.float32), jnp.moveaxis(u, 1, 0))
    cross = jnp.einsum('bnhcd,nbhde->bnhce', q_dec, r_prev)
    return from_chunks(inner + cross)


def mla_rwkv_mixer(x, positions, w_in, q_norm, w_uq, kv_norm, w_ukv, rwkv_mu, rwkv_w0, rwkv_w2,
                   rwkv_a0, rwkv_a2, rwkv_g2, rwkv_k_k, rwkv_k_a, rwkv_r_k, rwkv_gn_w, rwkv_gn_b, w_out):
    B, S, _ = x.shape
    p = x @ w_in
    mla_p, rwkv_p = p[..., :MLA_COLS], p[..., MLA_COLS:]
    c_q, c_kv, k_pe = split_cols(mla_p, (MLA_Q_RANK, MLA_KV_RANK, MLA_ROPE))
    q = (rms_norm(c_q, q_norm) @ w_uq).reshape(B, S, MLA_HEADS, MLA_NOPE + MLA_ROPE)
    q_nope, q_pe = q[..., :MLA_NOPE], rope(q[..., MLA_NOPE:], positions)
    kv = (rms_norm(c_kv, kv_norm) @ w_ukv).reshape(B, S, MLA_HEADS, MLA_NOPE + MLA_V)
    k_nope, v_mla = kv[..., :MLA_NOPE], kv[..., MLA_NOPE:]
    k_pe = rope(k_pe[:, :, None, :], positions)[:, :, 0, :]
    o_mla = causal_block_attention(q_nope, q_pe, k_nope, k_pe, v_mla, (MLA_NOPE + MLA_ROPE) ** -0.5)
    o_mla = o_mla.reshape(B, S, MLA_HEADS * MLA_V)
    prev = jnp.pad(rwkv_p, ((0, 0), (1, 0), (0, 0)))[:, :-1]
    rwkv_p = rwkv_p + rwkv_mu * (prev - rwkv_p)
    r, k, v, xw, xa, xg = split_cols(rwkv_p, (RWKV_DIM, RWKV_DIM, RWKV_DIM, DECAY_LORA, AAA_LORA, GATE_LORA))
    w_log = -jax.nn.softplus(-(rwkv_w0 + jnp.tanh(xw) @ rwkv_w2)) - 0.5
    decay = jnp.exp(-jnp.exp(w_log.astype(jnp.float32)))
    a = jax.nn.sigmoid(rwkv_a0 + xa @ rwkv_a2)
    g = jax.nn.sigmoid(xg) @ rwkv_g2
    hs = lambda t: t.reshape(B, S, RWKV_HEADS, RWKV_HEAD)
    kk = l2_normalize(hs(k * rwkv_k_k))
    k = k * (1.0 + (a - 1.0) * rwkv_k_a)
    r_h, k_h, v_h, a_h = hs(r), hs(k), hs(v), hs(a)
    y = rwkv7_recurrence(r_h, hs(decay), k_h, v_h, -kk, kk * a_h)
    y = head_norm(y, rwkv_gn_w, rwkv_gn_b, RWKV_GN_EPS)
    bonus = jnp.sum(r_h * k_h * rwkv_r_k, axis=-1, keepdims=True) * v_h
    o_rwkv = ((y + bonus.reshape(B, S, RWKV_DIM)) * g).astype(x.dtype)
    return jnp.concatenate([o_mla.astype(x.dtype), o_rwkv], -1) @ w_out


def gdn_retention_mixer(x, positions, w_in, gdn_conv_w, gdn_A_log, gdn_dt_bias, gdn_norm,
                        ret_gn_w, ret_gn_b, w_out):
    B, S, _ = x.shape
    p = x @ w_in
    gdn_p, ret_p = p[..., :GDN_COLS], p[..., GDN_COLS:]
    qk_w, v_w = GDN_QK_HEADS * GDN_DK, GDN_V_HEADS * GDN_DV
    qkv, z, b_logit, a_logit = split_cols(gdn_p, (2 * qk_w + v_w, v_w, GDN_V_HEADS, GDN_V_HEADS))
    qkv = jax.nn.silu(causal_dwconv(qkv, gdn_conv_w))
    q, k, v = split_cols(qkv, (qk_w, qk_w, v_w))
    rep = GDN_V_HEADS // GDN_QK_HEADS
    q = jnp.repeat(l2_normalize(q.reshape(B, S, GDN_QK_HEADS, GDN_DK)), rep, axis=2)
    k = jnp.repeat(l2_normalize(k.reshape(B, S, GDN_QK_HEADS, GDN_DK)), rep, axis=2)
    v = v.reshape(B, S, GDN_V_HEADS, GDN_DV)
    beta = jax.nn.sigmoid(b_logit.astype(jnp.float32))
    g = -jnp.exp(gdn_A_log.astype(jnp.float32)) * jax.nn.softplus((a_logit + gdn_dt_bias).astype(jnp.float32))
    o = gated_delta_rule_chunked(q, k, v, g, beta)
    o = rms_norm(o, gdn_norm) * jax.nn.silu(z.reshape(B, S, GDN_V_HEADS, GDN_DV).astype(jnp.float32))
    o_gdn = o.reshape(B, S, v_w).astype(x.dtype)
    rq, rk, rv, rg = split_cols(ret_p, (RET_HEADS * RET_DK, RET_HEADS * RET_DK, RET_HEADS * RET_DV, RET_HEADS * RET_DV))
    rq = rope(rq.reshape(B, S, RET_HEADS, RET_DK), positions) * RET_DK ** -0.5
    rk = rope(rk.reshape(B, S, RET_HEADS, RET_DK), positions)
    o_ret = retention_chunked(rq, rk, rv.reshape(B, S, RET_HEADS, RET_DV))
    o_ret = (head_norm(o_ret, ret_gn_w, ret_gn_b, LN_EPS) * jax.nn.silu(rg.astype(jnp.float32))).astype(x.dtype)
    return jnp.concatenate([o_gdn, o_ret], -1) @ w_out


def conv_ffn(x, w_gate, w_val, conv_w, conv_b, w_down):
    h = causal_dwconv(x @ w_gate, conv_w) + conv_b
    return (jax.nn.silu(h) * (x @ w_val)) @ w_down


def setup_inputs(seed: int = 0) -> dict:
    key = jax.random.key(seed)
    ks = iter(jax.random.split(key, 64))
    f32 = jnp.float32

    def nrm(shape, scale):
        return jax.random.normal(next(ks), shape, f32) * scale

    def gain(n):
        return 1.0 + nrm((n,), 0.02)

    def bias(n):
        return nrm((n,), 0.02)

    inp = {}
    inp["x"] = nrm((BATCH, SEQ, D_MODEL), 1.0)
    offs = jax.random.randint(next(ks), (BATCH, 1), 0, 1024, dtype=jnp.int32)
    inp["positions"] = offs + jnp.arange(SEQ, dtype=jnp.int32)[None, :]
    inp["l0_w_in"] = nrm((D_MODEL, EVEN_IN), D_MODEL ** -0.5)
    inp["l0_q_norm"] = gain(MLA_Q_RANK)
    inp["l0_w_uq"] = nrm((MLA_Q_RANK, MLA_HEADS * (MLA_NOPE + MLA_ROPE)), MLA_Q_RANK ** -0.5)
    inp["l0_kv_norm"] = gain(MLA_KV_RANK)
    inp["l0_w_ukv"] = nrm((MLA_KV_RANK, MLA_HEADS * (MLA_NOPE + MLA_V)), MLA_KV_RANK ** -0.5)
    inp["l0_rwkv_mu"] = jax.random.uniform(next(ks), (RWKV_COLS,), f32)
    w0_base = -6.0 + 5.0 * jnp.arange(RWKV_HEAD, dtype=f32) / (RWKV_HEAD - 1)
    inp["l0_rwkv_w0"] = jnp.tile(w0_base, RWKV_HEADS) + nrm((RWKV_DIM,), 0.1)
    inp["l0_rwkv_w2"] = nrm((DECAY_LORA, RWKV_DIM), 0.5 * DECAY_LORA ** -0.5)
    inp["l0_rwkv_a0"] = nrm((RWKV_DIM,), 0.1)
    inp["l0_rwkv_a2"] = nrm((AAA_LORA, RWKV_DIM), AAA_LORA ** -0.5)
    inp["l0_rwkv_g2"] = nrm((GATE_LORA, RWKV_DIM), GATE_LORA ** -0.5)
    inp["l0_rwkv_k_k"] = 0.85 + nrm((RWKV_DIM,), 0.02)
    inp["l0_rwkv_k_a"] = gain(RWKV_DIM)
    inp["l0_rwkv_r_k"] = nrm((RWKV_HEADS, RWKV_HEAD), 0.1)
    inp["l0_rwkv_gn_w"] = gain(RWKV_DIM)
    inp["l0_rwkv_gn_b"] = bias(RWKV_DIM)
    inp["l0_w_out"] = nrm((EVEN_OUT, D_MODEL), BETA * EVEN_OUT ** -0.5)
    inp["l0_ln1_g"] = gain(D_MODEL)
    inp["l0_ln1_b"] = bias(D_MODEL)
    inp["l0_ffn_w_gate"] = nrm((D_MODEL, FFN_DIM), D_MODEL ** -0.5)
    inp["l0_ffn_w_val"] = nrm((D_MODEL, FFN_DIM), D_MODEL ** -0.5)
    inp["l0_ffn_conv_w"] = nrm((FFN_CONV, FFN_DIM), FFN_CONV ** -0.5)
    inp["l0_ffn_conv_b"] = bias(FFN_DIM)
    inp["l0_ffn_w_down"] = nrm((FFN_DIM, D_MODEL), BETA * FFN_DIM ** -0.5)
    inp["l0_ln2_g"] = gain(D_MODEL)
    inp["l0_ln2_b"] = bias(D_MODEL)
    inp["l1_w_in"] = nrm((D_MODEL, ODD_IN), D_MODEL ** -0.5)
    inp["l1_gdn_conv_w"] = nrm((GDN_CONV, 2 * GDN_QK_HEADS * GDN_DK + GDN_V_HEADS * GDN_DV), GDN_CONV ** -0.5)
    inp["l1_gdn_A_log"] = jnp.log(jax.random.uniform(next(ks), (GDN_V_HEADS,), f32, 1.0, 16.0))
    dt = jnp.exp(jax.random.uniform(next(ks), (GDN_V_HEADS,), f32, float(np.log(1e-3)), float(np.log(1e-1))))
    inp["l1_gdn_dt_bias"] = dt + jnp.log(-jnp.expm1(-dt))
    inp["l1_gdn_norm"] = gain(GDN_DV)
    inp["l1_ret_gn_w"] = gain(RET_HEADS * RET_DV)
    inp["l1_ret_gn_b"] = bias(RET_HEADS * RET_DV)
    inp["l1_w_out"] = nrm((ODD_OUT, D_MODEL), BETA * ODD_OUT ** -0.5)
    inp["l1_ln1_g"] = gain(D_MODEL)
    inp["l1_ln1_b"] = bias(D_MODEL)
    inp["l1_ffn_w_gate"] = nrm((D_MODEL, FFN_DIM), D_MODEL ** -0.5)
    inp["l1_ffn_w_val"] = nrm((D_MODEL, FFN_DIM), D_MODEL ** -0.5)
    inp["l1_ffn_conv_w"] = nrm((FFN_CONV, FFN_DIM), FFN_CONV ** -0.5)
    inp["l1_ffn_conv_b"] = bias(FFN_DIM)
    inp["l1_ffn_w_down"] = nrm((FFN_DIM, D_MODEL), BETA * FFN_DIM ** -0.5)
    inp["l1_ln2_g"] = gain(D_MODEL)
    inp["l1_ln2_b"] = bias(D_MODEL)
    return inp


def reference(x, positions,
              l0_w_in, l0_q_norm, l0_w_uq, l0_kv_norm, l0_w_ukv, l0_rwkv_mu, l0_rwkv_w0, l0_rwkv_w2,
              l0_rwkv_a0, l0_rwkv_a2, l0_rwkv_g2, l0_rwkv_k_k, l0_rwkv_k_a, l0_rwkv_r_k, l0_rwkv_gn_w,
              l0_rwkv_gn_b, l0_w_out, l0_ln1_g, l0_ln1_b, l0_ffn_w_gate, l0_ffn_w_val, l0_ffn_conv_w,
              l0_ffn_conv_b, l0_ffn_w_down, l0_ln2_g, l0_ln2_b,
              l1_w_in, l1_gdn_conv_w, l1_gdn_A_log, l1_gdn_dt_bias, l1_gdn_norm, l1_ret_gn_w, l1_ret_gn_b,
              l1_w_out, l1_ln1_g, l1_ln1_b, l1_ffn_w_gate, l1_ffn_w_val, l1_ffn_conv_w, l1_ffn_conv_b,
              l1_ffn_w_down, l1_ln2_g, l1_ln2_b):
    mixers = (
        lambda h: mla_rwkv_mixer(h, positions, l0_w_in, l0_q_norm, l0_w_uq, l0_kv_norm, l0_w_ukv,
                                 l0_rwkv_mu, l0_rwkv_w0, l0_rwkv_w2, l0_rwkv_a0, l0_rwkv_a2, l0_rwkv_g2,
                                 l0_rwkv_k_k, l0_rwkv_k_a, l0_rwkv_r_k, l0_rwkv_gn_w, l0_rwkv_gn_b, l0_w_out),
        lambda h: gdn_retention_mixer(h, positions, l1_w_in, l1_gdn_conv_w, l1_gdn_A_log, l1_gdn_dt_bias,
                                      l1_gdn_norm, l1_ret_gn_w, l1_ret_gn_b, l1_w_out),
    )
    ffns = ((l0_ffn_w_gate, l0_ffn_w_val, l0_ffn_conv_w, l0_ffn_conv_b, l0_ffn_w_down),
            (l1_ffn_w_gate, l1_ffn_w_val, l1_ffn_conv_w, l1_ffn_conv_b, l1_ffn_w_down))
    ln1 = ((l0_ln1_g, l0_ln1_b), (l1_ln1_g, l1_ln1_b))
    ln2 = ((l0_ln2_g, l0_ln2_b), (l1_ln2_g, l1_ln2_b))
    for layer in range(DEPTH):
        x = layer_norm(ALPHA * x + mixers[layer](x), *ln1[layer])
        x = layer_norm(ALPHA * x + conv_ffn(x, *ffns[layer]), *ln2[layer])
    return x
```

```python
import contextlib
import numpy as np
import concourse.bass as bass
import concourse.mybir as mybir

F32 = mybir.dt.float32
BF16 = mybir.dt.bfloat16
I32 = mybir.dt.int32
AF = mybir.ActivationFunctionType
ALU = mybir.AluOpType
AX = mybir.AxisListType

EPOCH = 8000
NDSEM = 12


class Region:
    __slots__ = ("w", "r", "name", "excl")

    def __init__(self, name="", excl=False):
        self.w = None
        self.r = {}
        self.name = name
        self.excl = excl


class Prog:
    def __init__(self, nc, stack, self_sync=None):
        self.nc = nc
        self.stack = stack
        import os as _os
        self.self_sync = (not _os.environ.get("NOSELF")) if self_sync is None else self_sync
        self.engs = {"pe": nc.tensor, "dve": nc.vector, "act": nc.scalar,
                     "pool": nc.gpsimd, "sp": nc.sync}
        self.cnt = {e: 0 for e in self.engs}
        self.sems = {}
        self.seen = {e: {} for e in self.engs}
        self.dn = {}
        self.dsem = {}
        self.ninstr = 0

    def _sem(self, key):
        if key not in self.sems:
            nm = "s_" + "_".join(str(k) for k in key)
            self.sems[key] = self.stack.enter_context(self.nc.semaphore(nm))
        return self.sems[key]

    def region(self, name=""):
        return Region(name)

    def regions(self, n, name=""):
        return [Region(f"{name}{i}") for i in range(n)]

    def _wait(self, eng, toks):
        E = self.engs[eng]
        seen = self.seen[eng]
        best = {}
        for (key, val) in toks:
            if best.get(key, 0) < val:
                best[key] = val
        for key, val in best.items():
            if seen.get(key, 0) < val:
                E.wait_ge(self._sem(key), val)
                seen[key] = val
                self.ninstr += 1

    def _deps(self, eng, reads, writes):
        toks = []
        for r in reads:
            if r.w is not None:
                toks.append(r.w)
            if r.excl:
                toks.extend(t for k, t in r.r.items() if k != eng)
        for r in writes:
            if r.w is not None:
                toks.append(r.w)
            toks.extend(r.r.values())
        if eng == "pe" or not self.self_sync:
            toks = [t for t in toks if t[0][0] != eng]
        return toks

    def op(self, eng, fn, reads=(), writes=()):
        self._wait(eng, self._deps(eng, reads, writes))
        ins = fn(self.engs[eng])
        c = self.cnt[eng]
        ep, idx = divmod(c, EPOCH)
        key = (eng, ep)
        ins.then_inc(self._sem(key), 1)
        self.cnt[eng] = c + 1
        self.ninstr += 1
        tok = (key, idx + 1)
        for r in reads:
            r.r[eng] = tok
        for r in writes:
            r.w = tok
            r.r = {}
        return tok

    def dma(self, q, out, in_, reads=(), writes=(), **kw):
        n = self.dn.get(q, 0)
        j = n % NDSEM
        prev = 16 * (n // NDSEM)
        key = ("d" + q, j)
        toks = [t for t in self._deps("dma" + q, reads, writes)]
        if prev > 0:
            toks.append((key, prev))
        self._wait(q, toks)
        ins = self.engs[q].dma_start(out=out, in_=in_, **kw)
        ins.then_inc(self._sem(key), 16)
        self.dn[q] = n + 1
        self.ninstr += 1
        tok = (key, prev + 16)
        rk = "dma" + q + str(j)
        for r in reads:
            r.r[rk] = tok
        for r in writes:
            r.w = tok
            r.r = {}
        return tok

    def coll(self, kind, ins_ap, outs_ap, groups, reads=(), writes=()):
        q = "pool"
        n = self.dn.get(q, 0)
        j = n % NDSEM
        prev = 16 * (n // NDSEM)
        key = ("d" + q, j)
        toks = [t for t in self._deps("dma" + q, reads, writes)]
        if prev > 0:
            toks.append((key, prev))
        self._wait(q, toks)
        ins = self.engs[q].collective_compute(kind, ALU.bypass, replica_groups=groups, ins=[ins_ap], outs=[outs_ap])
        ins.then_inc(self._sem(key), 16)
        self.dn[q] = n + 1
        self.ninstr += 1
        tok = (key, prev + 16)
        rk = "dma" + q + str(j)
        for r in reads:
            r.r[rk] = tok
        for r in writes:
            r.w = tok
            r.r = {}
        return tok

    def finish(self, regions, eng="sp"):
        toks = [r.w for r in regions if r.w is not None]
        self._wait(eng, toks)

    def sbuf(self, name, shape, dt):
        return self.stack.enter_context(self.nc.sbuf_tensor(name, list(shape), dt))

    def psum(self, name, shape, dt):
        return self.stack.enter_context(self.nc.psum_tensor(name, list(shape), dt))


import contextlib
import math
import numpy as np

D = 2048
KC = 16
CH = 128
TWO_PI = 2 * math.pi
CW1 = 6.28125
CW2 = TWO_PI - CW1


class H:
    def __init__(self, P):
        self.P = P

    def mm(self, out, lhsT, rhs, rd, wr, start=True, stop=True):
        self.P.op("pe", lambda e: e.matmul(out, lhsT=lhsT, rhs=rhs, start=start, stop=stop), rd, wr)

    def tr(self, out, in_, ident, rd, wr):
        self.P.op("pe", lambda e: e.transpose(out, in_, ident), rd, wr)

    def tt(self, eng, out, a, b, op, rd, wr):
        self.P.op(eng, lambda e: e.tensor_tensor(out=out, in0=a, in1=b, op=op), rd, wr)

    def ts(self, eng, out, a, s1, op0, rd, wr, s2=None, op1=None):
        if op1 is None:
            self.P.op(eng, lambda e: e.tensor_scalar(out=out, in0=a, scalar1=s1, scalar2=None, op0=op0), rd, wr)
        else:
            self.P.op(eng, lambda e: e.tensor_scalar(out=out, in0=a, scalar1=s1, scalar2=s2, op0=op0, op1=op1), rd, wr)

    def stt(self, out, in0, scalar, in1, op0, op1, rd, wr):
        self.P.op("dve", lambda e: e.scalar_tensor_tensor(out=out, in0=in0, scalar=scalar, in1=in1, op0=op0, op1=op1), rd, wr)

    def act(self, out, in_, func, rd, wr, **kw):
        self.P.op("act", lambda e: e.activation(out=out, in_=in_, func=func, **kw), rd, wr)

    def cp(self, eng, out, in_, rd, wr):
        if eng == "act":
            self.act(out, in_, AF.Copy, rd, wr)
        else:
            self.P.op(eng, lambda e: e.tensor_copy(out=out, in_=in_), rd, wr)

    def rsqrt(self, out, in_, eps, rd, wr, scale=1.0):
        self.ts("dve", out, in_, scale, ALU.mult, rd, wr, s2=eps, op1=ALU.add)
        self.act(out, out, AF.Sqrt, wr, wr)
        self.P.op("dve", lambda e: e.reciprocal(out=out, in_=out), wr, wr)


def run_rr(gens):
    gens = list(gens)
    while gens:
        for g_ in list(gens):
            try:
                next(g_)
            except StopIteration:
                gens.remove(g_)


class TB:
    def __init__(self, P, name, shape, dt, n=2, psum=False):
        mk = P.psum if psum else P.sbuf
        self.a = [mk(f"{name}_{i}", shape, dt) for i in range(n)]
        self.r = [P.region(f"{name}_{i}") for i in range(n)]
        for r in self.r:
            r.excl = psum
        self.n = n

    def __call__(self, i):
        return self.a[i % self.n], self.r[i % self.n]


def rope_tables(P, h, ci, pos_f, r_pos, inv_s, r_inv, tb):
    ang, r_ang = tb["ang"](ci)
    u, r_u = tb["u"](ci)
    ki, r_ki = tb["ki"](ci)
    kf, r_kf = tb["kf"](ci)
    sn, r_sn = tb["sin"](ci)
    cs, r_cs = tb["cos"](ci)
    h.ts("dve", ang[:], pos_f, inv_s[:, 0:1], ALU.mult, [r_pos, r_inv], [r_ang])
    for (dst, r_dst, off, bias) in ((sn, r_sn, 0.0, 0.0), (cs, r_cs, 0.25, math.pi / 2)):
        h.ts("dve", u[:], ang[:], 1.0 / TWO_PI, ALU.mult, [r_ang], [r_u], s2=off, op1=ALU.add)
        h.cp("dve", ki[:], u[:], [r_u], [r_ki])
        h.cp("dve", kf[:], ki[:], [r_ki], [r_kf])
        h.stt(u[:], kf[:], -CW1, ang[:], ALU.mult, ALU.add, [r_kf, r_ang], [r_u])
        h.stt(u[:], kf[:], -CW2, u[:], ALU.mult, ALU.add, [r_kf, r_u], [r_u])
        sc = 1.0 - 2e-6
        if bias == 0.0:
            h.act(dst[:], u[:], AF.Sin, [r_u], [r_dst], scale=sc)
        else:
            h.ts("dve", u[:], u[:], bias, ALU.add, [r_u], [r_u])
            h.act(dst[:], u[:], AF.Sin, [r_u], [r_dst], scale=sc)
    return cs, sn, r_cs, r_sn


def build_c(S, stage=9):
    nc = bass.Bass("TRN2", target_bir_lowering=False)
    dt = nc.dram_tensor
    NCH = S // CH
    xT = dt("xT", [D, S], F32, kind="ExternalInput").ap()
    pos = dt("pos", [1, S], I32, kind="ExternalInput").ap()
    wF = dt("wF", [128, KC, 8 * 128], F32, kind="ExternalInput").ap()
    wT = dt("wT", [128, KC, 772], F32, kind="ExternalInput").ap()
    convw = dt("convw", [128, 16], F32, kind="ExternalInput").ap()
    rowt = dt("rowt", [128, 4 + 128 + 512], F32, kind="ExternalInput").ap()
    cst = dt("cst", [128, 5 * 128], F32, kind="ExternalInput").ap()
    rett = dt("rett", [128, 2 * 128 + 128 + 8], F32, kind="ExternalInput").ap()
    om = dt("om", [S, 512], BF16, kind="ExternalOutput").ap()

    with contextlib.ExitStack() as st:
        P = Prog(nc, st)
        h = H(P)
        wF_b = P.sbuf("wF_b", [128, KC, 8 * 128], BF16)
        wT_b = P.sbuf("wT_b", [128, KC, 772], BF16)
        r_wF, r_wT = P.region(), P.region()
        stg = TB(P, "stg", [128, 1024], F32)
        for kc in range(KC):
            a, r = stg(2 * kc)
            P.dma("sp", a[:, :1024], wF[:, kc, :], writes=[r])
            h.cp("dve", wF_b[:, kc, :], a[:, :1024], [r], [r_wF])
            a, r = stg(2 * kc + 1)
            P.dma("sp", a[:, :772], wT[:, kc, :], writes=[r])
            h.cp("act", wT_b[:, kc, :], a[:, :772], [r], [r_wT])
        convw_s = P.sbuf("convw_s", [128, 16], F32)
        rowt_s = P.sbuf("rowt_s", [128, 4 + 128 + 512], F32)
        cst_s = P.sbuf("cst_s", [128, 5 * 128], F32)
        rett_s = P.sbuf("rett_s", [128, 2 * 128 + 128 + 8], F32)
        r_c = P.region()
        for a, b in ((convw_s, convw), (rowt_s, rowt), (cst_s, cst), (rett_s, rett)):
            P.dma("sp", a[:], b, writes=[r_c])
        ident = cst_s[:, 0:128]
        maskL = cst_s[:, 128:256]
        maskU = cst_s[:, 256:384]
        ones_f = cst_s[:, 384:512]
        zeros_f = cst_s[:, 512:640]
        cb = P.sbuf("cb", [128, 3 * 128], BF16)
        h.cp("dve", cb[:, 0:128], ident, [r_c], [r_c])
        h.cp("dve", cb[:, 128:256], ones_f, [r_c], [r_c])
        h.cp("dve", cb[:, 256:384], maskU, [r_c], [r_c])
        ident_b, ones_b = cb[:, 0:128], cb[:, 128:256]
        DTret = [rett_s[:, 0:128], rett_s[:, 128:256]]
        QdRow = rett_s[:, 256:384]
        khcol = [rett_s[:, 384:385], rett_s[:, 385:386]]
        wcret = [rett_s[:, 386:387], rett_s[:, 387:388]]
        inv_s = rett_s[:, 388:389]
        sgn_s = rett_s[:, 389:390]
        Alog_row, dtb_row = rowt_s[:, 0:2], rowt_s[:, 2:4]
        gnorm_row = rowt_s[:, 4:132]
        retw_row = [rowt_s[:, 132:260], rowt_s[:, 260:388]]
        retb_row = [rowt_s[:, 388:516], rowt_s[:, 516:644]]
        eA = P.sbuf("eA", [128, 2], F32)
        h.act(eA[:], Alog_row, AF.Exp, [r_c], [r_c])
        h.ts("dve", eA[:], eA[:], -1.0, ALU.mult, [r_c], [r_c])

        pF = TB(P, "pF", [128, 512], F32, n=1, psum=True)
        pI = TB(P, "pI", [128, 512], F32, n=1, psum=True)
        pT = TB(P, "pT", [128, 512], F32, n=2, psum=True)
        pM = TB(P, "pM", [128, 512], F32, n=2, psum=True)
        pB = TB(P, "pB", [128, 1024], BF16, n=1, psum=True)
        pS = TB(P, "pS", [128, 512], F32, n=1, psum=True)
        pB2 = pS

        Sg = [P.sbuf(f"Sg{i}", [128, 128], F32) for i in range(2)]
        Sgb = [P.sbuf(f"Sgb{i}", [128, 128], BF16) for i in range(2)]
        Sr_ = P.sbuf("Sr", [128, 128], F32)
        Srb_ = P.sbuf("Srb", [128, 128], BF16)
        Sr = [Sr_[0:64, :], Sr_[64:128, :]]
        Srb = [Srb_[0:64, :], Srb_[64:128, :]]
        r_Sg, r_Sgb, r_Sr, r_Srb = P.regions(2), P.regions(2), P.regions(2), P.regions(2)
        for i in range(2):
            P.op("pool", lambda e: e.memset(Sg[i][:], 0.0), [], [r_Sg[i]])
            P.op("pool", lambda e: e.memset(Sgb[i][:], 0.0), [], [r_Sgb[i]])
            P.op("pool", lambda e: e.memset(Sr[i], 0.0), [], [r_Sr[i]])
            P.op("pool", lambda e: e.memset(Srb[i], 0.0), [], [r_Srb[i]])
        cvx = P.sbuf("cvx", [128, 4, 3 + CH], F32)
        r_cvx = P.regions(4)
        P.op("pool", lambda e: e.memset(cvx[:], 0.0), [], r_cvx)

        def T_(name, shape, dt_=F32, n=2):
            return TB(P, name, shape, dt_, n)

        xs_f = T_("xs_f", [128, KC, CH])
        xb = T_("xb", [128, KC, CH], BF16)
        posi = T_("posi", [128, CH], I32)
        posf = T_("posf", [128, CH])
        rt = {k: T_(k, [128, CH], I32 if k == "ki" else F32) for k in ("ang", "u", "ki", "kf", "sin", "cos")}
        cacc = T_("cacc", [128, 4, CH])
        qk_f = T_("qk_f", [128, 2, CH])
        vT_b = T_("vT_b", [128, 2, CH], BF16)
        sq_b = T_("sq_b", [128, 2, CH], BF16)
        rs = T_("rs", [128, 2, CH])
        qkn = T_("qkn", [128, 2, CH], BF16)
        kn_t = T_("kn_t", [128, 128], BF16)
        lg = T_("lg", [128, 4])
        beta = T_("beta", [128, 2])
        nbeta = T_("nbeta", [128, 2])
        gg = T_("gg", [128, 2])
        t2 = {k: T_("t2" + k, [128, 2]) for k in "abcd"}
        gcol = T_("gcol", [128, 2])
        gtot = T_("gtot", [128, 2])
        sc1 = T_("sc1", [128, 2])
        sc2 = T_("sc2", [128, 2])
        wc = T_("wc", [128, 2])
        gB = T_("gB", [128, 128], F32, 4)
        Gp = T_("Gp", [128, 128], F32, 4)
        Gm = T_("Gm", [128, 128], F32, 4)
        ER = T_("ER", [128, 128], F32, 4)
        A_ = T_("A_", [128, 128], F32, 4)
        N_ = T_("N_", [128, 128], F32, 4)
        A2 = T_("A2", [128, 128], F32, 4)
        N2 = T_("N2", [128, 128], F32, 4)
        Pm = T_("Pm", [128, 128], F32, 4)
        Pb = T_("Pb", [128, 128], BF16, 4)
        MrT = T_("MrT", [128, 128], BF16, 4)
        Vp = T_("Vp", [128, 128], BF16, 4)
        X1 = T_("X1", [128, 128], F32, 4)
        Ad = T_("Ad", [128, 128], BF16, 4)
        AdT = T_("AdT", [128, 128], BF16, 4)
        Kh = T_("Kh", [128, 128], BF16, 4)
        QdT = T_("QdT", [128, 128], BF16, 4)
        Ut = T_("Ut", [128, 128], BF16, 4)
        ssq = T_("ssq", [128, 1], F32, 4)
        junk = T_("junk", [128, 128], F32, 4)
        zs = T_("zs", [128, 128], F32, 4)
        ot = T_("ot", [128, 512], BF16, 2)
        KKQ = T_("KKQ", [128, 256], F32, 2)
        Vtok = T_("Vtok", [128, 256], BF16, 2)
        rfs = T_("rfs", [128, 512])
        rq = T_("rq", [128, CH])
        rk = T_("rk", [128, CH])
        rqb = T_("rqb", [128, CH], BF16)
        rkb = T_("rkb", [128, CH], BF16)
        rqd = T_("rqd", [128, CH], BF16)
        rk_t = T_("rk_t", [128, 128], BF16)
        rv_b = T_("rv_b", [128, 256], BF16)
        rMT = T_("rMT", [128, 128], BF16, 4)
        bst = T_("bst", [128, 6], F32, 4)
        mv = T_("mv", [128, 2], F32, 4)
        r_om = P.region()

        for ci in range(NCH):
            c0 = ci * CH
            xa, xr = xs_f(ci)
            xba, xbr = xb(ci)
            P.dma("sp", xa[:], xT[:, c0:c0 + CH].rearrange("(c p) n -> p c n", p=128), writes=[xr])
            h.cp("pool", xba[:, :KC // 2, :], xa[:, :KC // 2, :], [xr], [xbr])
            h.cp("dve", xba[:, KC // 2:, :], xa[:, KC // 2:, :], [xr], [xbr])
            pi_a, pi_r = posi(ci)
            pf_a, pf_r = posf(ci)
            P.dma("sp", pi_a[:], pos[:, c0:c0 + CH].partition_broadcast(128), writes=[pi_r])
            h.cp("dve", pf_a[:], pi_a[:], [pi_r], [pf_r])
            cs, sn, r_cs, r_sn = rope_tables(P, h, ci, pf_a[:], pf_r, inv_s, r_c, rt)
            gF, gFr = pF(0)

            def inproj_F(g0):
                for g in range(g0, g0 + 4):
                    for kc in range(KC):
                        h.mm(gF[:, (g % 4) * 128:(g % 4 + 1) * 128], wF_b[:, kc, g * 128:(g + 1) * 128], xba[:, kc, :],
                             [r_wF, xbr], [gFr], start=(kc == 0), stop=(kc == KC - 1))
            inproj_F(0)
            tA, tAr = pT(0)
            tB, tBr = pT(1)
            for kc in range(KC):
                h.mm(tA[:, :512], xba[:, kc, :], wT_b[:, kc, 0:512], [r_wT, xbr], [tAr], start=(kc == 0), stop=(kc == KC - 1))
            for kc in range(KC):
                h.mm(tB[:, :260], xba[:, kc, :], wT_b[:, kc, 512:772], [r_wT, xbr], [tBr], start=(kc == 0), stop=(kc == KC - 1))

            ca, car = cacc(ci)
            qka, qkr = qk_f(ci)
            vta, vtr = vT_b(ci)
            for g in range(4):
                h.cp("act", cvx[:, g, 3:3 + CH], gF[:, g * 128:(g + 1) * 128], [gFr], [r_cvx[g]])
                w = lambda j: convw_s[:, g * 4 + j:g * 4 + j + 1]
                h.ts("dve", ca[:, g, :], cvx[:, g, 3:3 + CH], w(3), ALU.mult, [r_cvx[g], r_c], [car])
                for j in range(3):
                    h.stt(ca[:, g, :], cvx[:, g, j:j + CH], w(j), ca[:, g, :], ALU.mult, ALU.add, [r_cvx[g], car, r_c], [car])
                h.cp("pool", cvx[:, g, 0:3], cvx[:, g, CH:CH + 3], [r_cvx[g]], [r_cvx[g]])
                if g < 2:
                    h.act(qka[:, g, :], ca[:, g, :], AF.Silu, [car], [qkr])
                else:
                    h.act(vta[:, g - 2, :], ca[:, g, :], AF.Silu, [car], [vtr])
            inproj_F(4)
            rF, rFr = rfs(ci)
            h.cp("act", rF[:], gF[:, :], [gFr], [rFr])
            sqa, sqr = sq_b(ci)
            h.act(sqa[:], qka[:], AF.Square, [qkr], [sqr])
            m0, m0r = pM(0)
            h.mm(m0[:, 0:256], ones_b, sqa[:].rearrange("p a b -> p (a b)"), [r_c, sqr], [m0r])
            rsa, rsr = rs(ci)
            h.rsqrt(rsa[:].rearrange("p a b -> p (a b)"), m0[:, 0:256], 1e-6, [m0r], [rsr])
            qna, qnr = qkn(ci)
            h.tt("dve", qna[:, 0, :], qka[:, 1, :], rsa[:, 1, :], ALU.mult, [qkr, rsr], [qnr])
            h.stt(qna[:, 1, :], qka[:, 0, :], 128.0 ** -0.5, rsa[:, 0, :], ALU.mult, ALU.mult, [qkr, rsr], [qnr])
            m1, m1r = pM(1)
            h.mm(m1[:, 0:256], qna[:, 0, :], qna[:].rearrange("p a b -> p (a b)"), [qnr], [m1r])
            pb, pbr = pB(0)
            h.tr(pb[:, 0:128], qna[:, 0, :], ident_b, [qnr, r_c], [pbr])
            h.tr(pb[:, 128:256], vta[:, 0, :], ident_b, [vtr, r_c], [pbr])
            h.tr(pb[:, 256:384], vta[:, 1, :], ident_b, [vtr, r_c], [pbr])
            kta, ktr = kn_t(ci)
            h.cp("act", kta[:], pb[:, 0:128], [pbr], [ktr])
            lga, lgr = lg(ci)
            h.cp("dve", lga[:], tB[:, 256:260], [tBr], [lgr])
            ba_, br_ = beta(ci)
            nb_, nbr_ = nbeta(ci)
            h.act(ba_[:], lga[:, 0:2], AF.Sigmoid, [lgr], [br_])
            h.ts("dve", nb_[:], ba_[:], -1.0, ALU.mult, [br_], [nbr_])
            xa2, xr2 = t2["a"](ci)
            ab2, abr2 = t2["b"](ci)
            e2, er2 = t2["c"](ci)
            mx2, mxr2 = t2["d"](ci)
            h.tt("dve", xa2[:], lga[:, 2:4], dtb_row, ALU.add, [lgr, r_c], [xr2])
            h.act(ab2[:], xa2[:], AF.Abs, [xr2], [abr2])
            h.act(e2[:], ab2[:], AF.Exp, [abr2], [er2], scale=-1.0)
            h.act(e2[:], e2[:], AF.Ln, [er2], [er2], bias=1.0)
            h.ts("dve", mx2[:], xa2[:], 0.0, ALU.max, [xr2], [mxr2])
            h.tt("dve", mx2[:], mx2[:], e2[:], ALU.add, [mxr2, er2], [mxr2])
            ga, gr = gg(ci)
            h.tt("dve", ga[:], mx2[:], eA[:], ALU.mult, [mxr2, r_c], [gr])
            m0b, m0br = pM(0)
            h.mm(m0b[:, 256:258], maskU, ga[:], [r_c, gr], [m0br])
            h.mm(m0b[:, 258:260], ones_f, ga[:], [r_c, gr], [m0br])
            gca, gcr = gcol(ci)
            gta, gtr = gtot(ci)
            h.cp("dve", gca[:], m0b[:, 256:258], [m0br], [gcr])
            h.cp("dve", gta[:], m0b[:, 258:260], [m0br], [gtr])
            s1a, s1r = sc1(ci)
            s2a, s2r = sc2(ci)
            wca, wcr = wc(ci)
            h.act(s1a[:], gca[:], AF.Exp, [gcr], [s1r])
            h.tt("dve", s1a[:], s1a[:], nb_[:], ALU.mult, [s1r, nbr_], [s1r])
            h.tt("dve", s2a[:], gta[:], gca[:], ALU.subtract, [gtr, gcr], [s2r])
            h.act(s2a[:], s2a[:], AF.Exp, [s2r], [s2r])
            h.act(wca[:], gta[:], AF.Exp, [gtr], [wcr])

            ota, otr = ot(ci)
            kkq, kkqr = KKQ(ci)
            h.cp("act", kkq[:], m1[:, 0:256], [m1r], [kkqr])
            vtk, vtkr = Vtok(ci)
            h.cp("act", vtk[:], pb[:, 128:384], [pbr], [vtkr])

            def gdn_head(hh):
                k2 = 2 * ci + hh
                mg_, mgr = pM(hh)
                mi, mir = (pI(0), pF(0))[hh]
                gBa, gBr = gB(k2)
                h.ts("dve", gBa[:], ones_f, ga[:, hh:hh + 1], ALU.mult, [gr, r_c], [gBr])
                GC = mg_[:, 0:128]
                h.mm(GC, gBa[:], maskU, [gBr, r_c], [mgr])
                yield
                Gpa, Gpr = Gp(k2)
                Gma, Gmr = Gm(k2)
                ERa, ERr = ER(k2)
                h.stt(Gpa[:], GC, gca[:, hh:hh + 1], zeros_f, ALU.subtract, ALU.max, [mgr, gcr, r_c], [Gpr])
                h.stt(Gma[:], GC, gca[:, hh:hh + 1], zeros_f, ALU.subtract, ALU.min, [mgr, gcr, r_c], [Gmr])
                yield
                h.act(ERa[:], GC, AF.Exp, [mgr], [ERr])
                h.act(Gpa[:], Gpa[:], AF.Exp, [Gpr], [Gpr], scale=-1.0)
                h.act(Gma[:], Gma[:], AF.Exp, [Gmr], [Gmr])
                yield
                Aa, Ar = A_(k2)
                h.tt("dve", Aa[:], kkq[:, 0:128], Gpa[:], ALU.mult, [kkqr, Gpr], [Ar])
                h.stt(Aa[:], Aa[:], nb_[:, hh:hh + 1], maskL, ALU.mult, ALU.mult, [Ar, nbr_, r_c], [Ar])
                yield
                Na, Nr = N_(k2)
                h.tr(mi[:, 0:128], Aa[:], ident, [Ar, r_c], [mir])
                Ma, Mr_ = MrT(k2)
                h.tt("dve", Gma[:], Gma[:], maskU, ALU.mult, [Gmr, r_c], [Gmr])
                h.tt("dve", Ma[:], kkq[:, 128:256], Gma[:], ALU.mult, [kkqr, Gmr], [Mr_])
                yield
                h.cp("act", Na[:], mi[:, 0:128], [mir], [Nr])
                yield
                Pa, Pr = Pm(k2)
                h.tt("dve", Pa[:], Na[:], ident, ALU.add, [Nr, r_c], [Pr])
                yield
                cur = (Na, Nr, Aa, Ar)
                nxt = (N2(k2)[0], N2(k2)[1], A2(k2)[0], A2(k2)[1])
                for lv in range(6):
                    curN, curNr, curA, curAr = cur
                    nxtN, nxtNr, nxtA, nxtAr = nxt
                    h.mm(mi[:, 128:256], curN[:], curA[:], [curNr, curAr], [mir])
                    if lv < 5:
                        h.mm(mi[:, 256:384], curA[:], curN[:], [curNr, curAr], [mir])
                    yield
                    h.cp("act", nxtA[:], mi[:, 128:256], [mir], [nxtAr])
                    if lv < 5:
                        h.cp("dve", nxtN[:], mi[:, 256:384], [mir], [nxtNr])
                    yield
                    h.mm(mi[:, 0:128], nxtA[:], Pa[:], [nxtAr, Pr], [mir])
                    yield
                    h.tt("dve", Pa[:], Pa[:], mi[:, 0:128], ALU.add, [Pr, mir], [Pr])
                    yield
                    cur, nxt = nxt, cur
                Pba, Pbr = Pb(k2)
                h.cp("act", Pba[:], Pa[:], [Pr], [Pbr])
                Vpa, Vpr = Vp(k2)
                h.ts("dve", Vpa[:], vtk[:, hh * 128:(hh + 1) * 128], ba_[:, hh:hh + 1], ALU.mult, [vtkr, br_], [Vpr])
                Ada, Adr = Ad(k2)
                h.ts("dve", Ada[:], kta[:], s1a[:, hh:hh + 1], ALU.mult, [ktr, s1r], [Adr])
                yield
                h.mm(mg_[:, 256:384], Pba[:], Vpa[:], [Pbr, Vpr], [mgr])
                h.mm(mg_[:, 384:512], Ada[:], Pba[:], [Adr, Pbr], [mgr])
                Kha, Khr = Kh(k2)
                h.ts("dve", Kha[:], kta[:], s2a[:, hh:hh + 1], ALU.mult, [ktr, s2r], [Khr])
                QdTa, QdTr = QdT(k2)
                h.tt("dve", QdTa[:], qna[:, 1, :], ERa[:], ALU.mult, [qnr, ERr], [QdTr])
                yield
                X1a, X1r = X1(k2)
                h.cp("act", X1a[:], mg_[:, 256:384], [mgr], [X1r])
                AdTa, AdTr = AdT(k2)
                h.cp("act", AdTa[:], mg_[:, 384:512], [mgr], [AdTr])
                yield
                h.mm(mg_[:, 0:128], AdTa[:], Sgb[hh][:], [AdTr, r_Sgb[hh]], [mgr])
                yield
                Uta, Utr = Ut(k2)
                h.tt("dve", Uta[:], mg_[:, 0:128], X1a[:], ALU.add, [mgr, X1r], [Utr])
                yield
                h.mm(mg_[:, 128:256], QdTa[:], Sgb[hh][:], [QdTr, r_Sgb[hh]], [mgr], start=True, stop=False)
                h.mm(mg_[:, 128:256], Ma[:], Uta[:], [Mr_, Utr], [mgr], start=False, stop=True)
                h.mm(mi[:, 384:512], Kha[:], Uta[:], [Khr, Utr], [mir])
                yield
                h.stt(Sg[hh][:], Sg[hh][:], wca[:, hh:hh + 1], mi[:, 384:512], ALU.mult, ALU.add, [r_Sg[hh], wcr, mir], [r_Sg[hh]])
                yield
                h.cp("act", Sgb[hh][:], Sg[hh][:], [r_Sg[hh]], [r_Sgb[hh]])
                ja, jr = junk(k2)
                ssa, ssr = ssq(k2)
                h.act(ja[:], mg_[:, 128:256], AF.Square, [mgr], [jr, ssr], accum_out=ssa[:])
                yield
                h.rsqrt(ssa[:], ssa[:], 1e-6, [ssr], [ssr], scale=1.0 / 128)
                za, zr = zs(k2)
                h.act(za[:], tA[:, hh * 128:(hh + 1) * 128], AF.Silu, [tAr], [zr])
                yield
                h.tt("dve", za[:], za[:], gnorm_row, ALU.mult, [zr, r_c], [zr])
                yield
                h.stt(ota[:, hh * 128:(hh + 1) * 128], mg_[:, 128:256], ssa[:, 0:1], za[:], ALU.mult, ALU.mult, [mgr, ssr, zr], [otr])

            run_rr([gdn_head(0), gdn_head(1)])

            rqa, rqr = rq(ci)
            rka, rkr = rk(ci)
            sns, snsr = rt["ang"](ci)
            h.ts("dve", sns[:], sn[:], sgn_s[:, 0:1], ALU.mult, [r_sn, r_c], [snsr])
            for (dst, dr, g0) in ((rqa, rqr, 0), (rka, rkr, 2)):
                h.tt("dve", dst[:], rF[:, g0 * 128:(g0 + 1) * 128], cs[:], ALU.mult, [rFr, r_cs], [dr])
                ja, jr = junk(2 * ci + g0 // 2)
                h.tt("dve", ja[:], rF[:, (g0 + 1) * 128:(g0 + 2) * 128], sns[:], ALU.mult, [rFr, snsr], [jr])
                h.tt("pool", dst[:], dst[:], ja[:], ALU.add, [dr, jr], [dr])
            rqba, rqbr = rqb(ci)
            rkba, rkbr = rkb(ci)
            rqda, rqdr = rqd(ci)
            h.cp("act", rqba[:], rqa[:], [rqr], [rqbr])
            h.cp("act", rkba[:], rka[:], [rkr], [rkbr])
            h.tt("dve", rqda[:], rqa[:], QdRow, ALU.mult, [rqr, r_c], [rqdr])
            h.tr(pb[:, 384:512], rkba[:], ident_b, [rkbr, r_c], [pbr])
            rvba, rvbr = rv_b(ci)
            h.cp("act", rvba[:], tA[:, 256:512], [tAr], [rvbr])
            rkta, rktr = rk_t(ci)
            for hh in range(2):
                h.ts("dve", rkta[:, hh * 64:(hh + 1) * 64], pb[:, 384 + hh * 64:384 + (hh + 1) * 64], khcol[hh], ALU.mult,
                     [pbr, r_c], [rktr])
            for hh in range(2):
                k2 = 2 * ci + hh
                hs = slice(hh * 64, (hh + 1) * 64)
                mi, mir = pI(0)
                h.mm(mi[:, 0:128], rkba[hs, :], rqba[hs, :], [rkbr, rqbr], [mir])
                Ma, Mr_ = rMT(k2)
                h.tt("dve", Ma[:], mi[:, 0:128], DTret[hh], ALU.mult, [mir, r_c], [Mr_])
                sa, sr = pS(0)
                h.mm(sa[:, 0:128], rqda[hs, :], Srb[hh], [rqdr, r_Srb[hh]], [sr], start=True, stop=False)
                h.mm(sa[:, 0:128], Ma[:], rvba[:, hh * 128:(hh + 1) * 128], [Mr_, rvbr], [sr], start=False, stop=True)
                h.mm(sa[hs, 128:256], rkta[:, hs], rvba[:, hh * 128:(hh + 1) * 128], [rktr, rvbr], [sr])
                h.stt(Sr[hh], Sr[hh], wcret[hh][hs, :], sa[hs, 128:256], ALU.mult, ALU.add, [r_Sr[hh], r_c, sr], [r_Sr[hh]])
                h.cp("act", Srb[hh], Sr[hh], [r_Sr[hh]], [r_Srb[hh]])
                ba2, br2 = bst(k2)
                mva, mvr = mv(k2)
                P.op("dve", lambda e: e.bn_stats(out=ba2[:], in_=sa[:, 0:128]), [sr], [br2])
                P.op("dve", lambda e: e.bn_aggr(out=mva[:], in_=ba2[:]), [br2], [mvr])
                h.rsqrt(mva[:, 1:2], mva[:, 1:2], 1e-5, [mvr], [mvr])
                ja, jr = junk(k2)
                h.ts("dve", ja[:], sa[:, 0:128], mva[:, 0:1], ALU.subtract, [sr, mvr], [jr], s2=mva[:, 1:2], op1=ALU.mult)
                h.tt("dve", ja[:], ja[:], retw_row[hh], ALU.mult, [jr, r_c], [jr])
                h.tt("pool", ja[:], ja[:], retb_row[hh], ALU.add, [jr, r_c], [jr])
                za, zr = zs(k2)
                h.act(za[:], tB[:, hh * 128:(hh + 1) * 128], AF.Silu, [tBr], [zr])
                h.tt("dve", ota[:, 256 + hh * 128:256 + (hh + 1) * 128], ja[:], za[:], ALU.mult, [jr, zr], [otr])
            P.dma("pool", om[c0:c0 + CH, :], ota[:], reads=[otr], writes=[r_om])
        P.finish([r_om], "sp")
        print("C ninstr", P.ninstr, P.cnt)
    return nc


GDN_QK_HEADS, GDN_V_HEADS, RET_HEADS = 4, 8, 8


def blk_in(w):
    return np.ascontiguousarray(w.reshape(KC, 128, -1).transpose(1, 0, 2))


def c_inputs(xT_b, pos_b, hg, w_in, gdn_conv_w, A_log, dt_bias, gdn_norm, ret_gn_w, ret_gn_b):
    f = np.float32
    qk_w, v_w = 512, 1024
    gq = w_in[:, hg * 128:(hg + 1) * 128]
    gk = w_in[:, qk_w + hg * 128: qk_w + (hg + 1) * 128]
    gv = [w_in[:, 2 * qk_w + (2 * hg + i) * 128: 2 * qk_w + (2 * hg + i + 1) * 128] for i in range(2)]
    zoff = 2 * qk_w + v_w
    gz = [w_in[:, zoff + (2 * hg + i) * 128: zoff + (2 * hg + i + 1) * 128] for i in range(2)]
    boff = zoff + v_w
    gb = w_in[:, boff + 2 * hg: boff + 2 * hg + 2]
    ga = w_in[:, boff + 8 + 2 * hg: boff + 8 + 2 * hg + 2]
    R0 = boff + 16
    rqw = w_in[:, R0 + 2 * hg * 64: R0 + (2 * hg + 2) * 64]
    rkw = w_in[:, R0 + 512 + 2 * hg * 64: R0 + 512 + (2 * hg + 2) * 64]
    rvw = w_in[:, R0 + 1024 + 2 * hg * 128: R0 + 1024 + (2 * hg + 2) * 128]
    rgw = w_in[:, R0 + 2048 + 2 * hg * 128: R0 + 2048 + (2 * hg + 2) * 128]

    def sw(w):
        a = w.reshape(w.shape[0], -1, 2, 32)
        return a[:, :, ::-1, :].reshape(w.shape)

    wF = np.concatenate([gq, gk, gv[0], gv[1], rqw, sw(rqw), rkw, sw(rkw)], 1)
    wT = np.concatenate([gz[0], gz[1], rvw, rgw, gb, ga], 1)
    cw = gdn_conv_w
    cols = [slice(hg * 128, (hg + 1) * 128), slice(qk_w + hg * 128, qk_w + (hg + 1) * 128),
            slice(2 * qk_w + 2 * hg * 128, 2 * qk_w + (2 * hg + 1) * 128),
            slice(2 * qk_w + (2 * hg + 1) * 128, 2 * qk_w + (2 * hg + 2) * 128)]
    convw = np.stack([cw[:, c].T for c in cols], 1).reshape(128, 16)
    rowt = np.concatenate([A_log[2 * hg:2 * hg + 2], dt_bias[2 * hg:2 * hg + 2], gdn_norm,
                           ret_gn_w[2 * hg * 128:(2 * hg + 2) * 128], ret_gn_b[2 * hg * 128:(2 * hg + 2) * 128]])
    rowt = np.broadcast_to(rowt[None], (128, rowt.size)).astype(f)
    i = np.arange(128)
    ident = np.eye(128, dtype=f)
    maskL = (i[:, None] > i[None, :]).astype(f)
    maskU = (i[:, None] <= i[None, :]).astype(f)
    cst = np.concatenate([ident, maskL, maskU, np.ones((128, 128), f), np.zeros((128, 128), f)], 1)
    lgam = [math.log(1.0 - 2.0 ** (-5.0 - (2 * hg + j))) for j in range(2)]
    diff = (i[None, :] - i[:, None]).astype(np.float64)
    DT = [np.where(diff >= 0, np.exp(np.maximum(diff, 0) * lgam[j]), 0.0) * 64 ** -0.5 for j in range(2)]
    QdRow = np.concatenate([np.broadcast_to(np.exp((i + 1.0) * lgam[j])[None] * 64 ** -0.5, (64, 128)) for j in range(2)], 0)
    khcol = np.stack([np.exp((127.0 - i) * lgam[j]) for j in range(2)], 1)
    wcr = np.broadcast_to(np.array([math.exp(128 * lgam[j]) for j in range(2)])[None], (128, 2))
    inv = (10000.0 ** (-np.arange(0, 64, 2) / 64.0))[i % 32][:, None]
    sgn = np.where((i % 64) < 32, -1.0, 1.0)[:, None]
    rett = np.concatenate([DT[0], DT[1], QdRow, khcol, wcr, inv, sgn, np.zeros((128, 2))], 1).astype(f)
    return dict(xT=xT_b, pos=pos_b, wF=blk_in(wF).astype(f), wT=blk_in(wT).astype(f), convw=convw.astype(f),
                rowt=rowt, cst=cst, rett=rett)


import contextlib
import math
import numpy as np

CDEC = math.exp(-0.5)
NCOL = 1056


def run_rr(gens):
    gens = list(gens)
    while gens:
        for g_ in list(gens):
            try:
                next(g_)
            except StopIteration:
                gens.remove(g_)


def dpl_inverse(h, mi, mir, Na, Nr, Aa, Ar, N2a, N2r, A2a, A2r, Pa, Pr, ident, r_c):
    h.tt("dve", Pa[:], Na[:], ident, ALU.add, [Nr, r_c], [Pr])
    yield
    cur = (Na, Nr, Aa, Ar)
    nxt = (N2a, N2r, A2a, A2r)
    for lv in range(6):
        curN, curNr, curA, curAr = cur
        nxtN, nxtNr, nxtA, nxtAr = nxt
        h.mm(mi[:, 128:256], curN[:], curA[:], [curNr, curAr], [mir])
        if lv < 5:
            h.mm(mi[:, 256:384], curA[:], curN[:], [curNr, curAr], [mir])
        yield
        h.cp("act", nxtA[:], mi[:, 128:256], [mir], [nxtAr])
        if lv < 5:
            h.cp("dve", nxtN[:], mi[:, 256:384], [mir], [nxtNr])
        yield
        h.mm(mi[:, 0:128], nxtA[:], Pa[:], [nxtAr, Pr], [mir])
        yield
        h.tt("dve", Pa[:], Pa[:], mi[:, 0:128], ALU.add, [Pr, mir], [Pr])
        yield
        cur, nxt = nxt, cur


def build_a2(S, stage=9):
    nc = bass.Bass("TRN2", target_bir_lowering=False)
    dt = nc.dram_tensor
    NCH = S // CH
    xT = dt("xT", [D, S], F32, kind="ExternalInput").ap()
    wF = dt("wF", [128, KC, NCOL], F32, kind="ExternalInput").ap()
    lora = dt("lora", [128, 768], F32, kind="ExternalInput").ap()
    pvec = dt("pvec", [128, 32], F32, kind="ExternalInput").ap()
    rowt = dt("rowt", [128, 512], F32, kind="ExternalInput").ap()
    cst = dt("cst", [128, 6 * 128 + 2], F32, kind="ExternalInput").ap()
    om = dt("om", [S, 256], BF16, kind="ExternalOutput").ap()

    with contextlib.ExitStack() as st:
        P = Prog(nc, st)
        h = H(P)
        wF_b = P.sbuf("wF_b", [128, KC, NCOL], BF16)
        r_wF = P.region()
        stg = TB(P, "stg", [128, NCOL], F32)
        for kc in range(KC):
            a, r = stg(kc)
            P.dma("sp", a[:], wF[:, kc, :], writes=[r])
            h.cp("dve" if kc % 2 == 0 else "act", wF_b[:, kc, :], a[:], [r], [r_wF])
        lora_s = P.sbuf("lora_s", [128, 768], F32)
        lora_b = P.sbuf("lora_b", [128, 768], BF16)
        pvec_s = P.sbuf("pvec_s", [128, 32], F32)
        rowt_s = P.sbuf("rowt_s", [128, 512], F32)
        cst_s = P.sbuf("cst_s", [128, 6 * 128 + 2], F32)
        r_c = P.region()
        for a, b in ((lora_s, lora), (pvec_s, pvec), (rowt_s, rowt), (cst_s, cst)):
            P.dma("sp", a[:], b, writes=[r_c])
        h.cp("dve", lora_b[:], lora_s[:], [r_c], [r_c])
        h.ts("dve", pvec_s[:, 17:19], pvec_s[:, 15:17], -1.0, ALU.mult, [r_c], [r_c], s2=1.0, op1=ALU.add)
        ident, maskL, maskU, maskUs = (cst_s[:, i * 128:(i + 1) * 128] for i in range(4))
        ones_f = cst_s[:, 512:640]
        cb = P.sbuf("cb", [128, 3 * 128 + 2], BF16)
        h.cp("dve", cb[:, 0:128], ident, [r_c], [r_c])
        h.cp("dve", cb[:, 128:256], cst_s[:, 640:768], [r_c], [r_c])
        h.cp("dve", cb[:, 256:384], ones_f, [r_c], [r_c])
        h.cp("dve", cb[:, 384:386], cst_s[:, 768:770], [r_c], [r_c])
        ident_b, bones_b, ones_b, bsel_b = cb[:, 0:128], cb[:, 128:256], cb[:, 256:384], cb[:, 384:386]
        pv = lambda i: pvec_s[:, i:i + 1]
        MU, W0, A0, KK_, KA_, OMKA, RK_ = 0, 9, 11, 13, 15, 17, 19

        pFa = TB(P, "pFa", [128, 512], F32, n=2, psum=True)
        pM = TB(P, "pM", [128, 512], F32, n=2, psum=True)
        pI = TB(P, "pI", [128, 512], F32, n=1, psum=True)
        pB = TB(P, "pB", [128, 1024], BF16, n=1, psum=True)
        pS = TB(P, "pS", [128, 512], F32, n=1, psum=True)
        pG = TB(P, "pG", [128, 512], F32, n=1, psum=True)

        Sst = [P.sbuf(f"Sst{i}", [128, 64], F32) for i in range(2)]
        Sb = [P.sbuf(f"Sb{i}", [128, 64], BF16) for i in range(2)]
        r_Sst = [[P.region(), P.region()] for _ in range(2)]
        r_Sb = [[P.region(), P.region()] for _ in range(2)]
        for i in range(2):
            P.op("pool", lambda e: e.memset(Sst[i][:], 0.0), [], r_Sst[i])
            P.op("pool", lambda e: e.memset(Sb[i][:], 0.0), [], r_Sb[i])
        pbuf = P.sbuf("pbuf", [128, 9, 1 + CH], F32)
        r_pbuf = P.region()
        P.op("pool", lambda e: e.memset(pbuf[:], 0.0), [], [r_pbuf])

        def T_(name, shape, dt_=F32, n=2):
            return TB(P, name, shape, dt_, n)

        xs_f = T_("xs_f", [128, KC, CH])
        xb = T_("xb", [128, KC, CH], BF16)
        dif = T_("dif", [128, 9, CH])
        mix = T_("mix", [128, 9, CH])
        wab = T_("wab", [128, CH], BF16)
        sgb = T_("sgb", [128, 2, CH], BF16)
        Gt = T_("Gt", [128, 256])
        sig = T_("sig", [128, 2, CH])
        cs_ = T_("cs_", [128, 2, CH])
        csm = T_("csm", [128, 2, CH])
        ncc = T_("ncc", [128, 2])
        wcc = T_("wcc", [128, 2])
        E1 = T_("E1", [128, 2, CH]); E2 = T_("E2", [128, 2, CH]); E3 = T_("E3", [128, 2, CH]); E4 = T_("E4", [128, 2, CH])
        aa = T_("aa", [128, 2, CH])
        kkt = T_("kkt", [128, 2, CH])
        sqb = T_("sqb", [128, 2, CH], BF16)
        rs = T_("rs", [128, 2, CH])
        ka = T_("ka", [128, 2, CH])
        kp = T_("kp", [128, 2, CH])
        AR = T_("AR", [128, 2, 2, CH], BF16)
        Bt = T_("Bt", [128, 2, CH], BF16)
        Kt = T_("Kt", [128, 2, CH], BF16)
        Bh_ = T_("Bh_", [128, 2, CH], BF16)
        Kh_ = T_("Kh_", [128, 2, CH], BF16)
        vb = T_("vb", [128, 2, CH], BF16)
        rkr = T_("rkr", [128, 2, CH], BF16)
        bon = T_("bon", [128, 4])
        tokm = T_("tokm", [128, 2, 4, 128], BF16)
        N_ = T_("N_", [128, 128], F32, 4); A_ = T_("A_", [128, 128], F32, 4)
        N2 = T_("N2", [128, 128], F32, 4); A2 = T_("A2", [128, 128], F32, 4)
        Pm = T_("Pm", [128, 128], F32, 4); Pb = T_("Pb", [128, 128], BF16, 4)
        MrbT = T_("MrbT", [128, 128], BF16, 4); MakT = T_("MakT", [128, 128], BF16, 4); MrkT = T_("MrkT", [128, 128], BF16, 4)
        Z1 = T_("Z1", [128, 64], BF16, 4)
        X1 = T_("X1", [128, 64], F32, 4)
        AdT = T_("AdT", [128, CH], BF16, 4)
        Ut = T_("Ut", [128, 64], BF16, 4)
        bst = T_("bst", [128, 6], F32, 4)
        mv = T_("mv", [128, 2], F32, 4)
        yn = T_("yn", [128, 64], F32, 4)
        ot = T_("ot", [128, 256], BF16, 2)
        r_om = P.region()

        for ci in range(NCH):
            c0 = ci * CH
            xa_, xr = xs_f(ci)
            xba, xbr = xb(ci)
            P.dma("sp", xa_[:], xT[:, c0:c0 + CH].rearrange("(c p) n -> p c n", p=128), writes=[xr])
            h.cp("pool", xba[:, :KC // 2, :], xa_[:, :KC // 2, :], [xr], [xbr])
            h.cp("dve", xba[:, KC // 2:, :], xa_[:, KC // 2:, :], [xr], [xbr])
            fa, far = pFa(0)
            fb, fbr = pFa(1)
            gcols = [(i * 128, 128) for i in range(8)] + [(1024, 32)]
            for g, (cc, m) in enumerate(gcols):
                if g < 4:
                    dst, dr, oc = fa, far, g * 128
                elif g < 8:
                    dst, dr, oc = fb, fbr, (g - 4) * 128
                else:
                    dst, dr, oc = pG(0)[0], pG(0)[1], 256
                for kc in range(KC):
                    h.mm(dst[0:m, oc:oc + 128], wF_b[:, kc, cc:cc + m], xba[:, kc, :], [r_wF, xbr], [dr],
                         start=(kc == 0), stop=(kc == KC - 1))
            h.cp("act", pbuf[:, 0:4, 1:1 + CH], fa[:, :].rearrange("p (g n) -> p g n", g=4), [far], [r_pbuf])
            h.cp("act", pbuf[:, 4:8, 1:1 + CH], fb[:, :].rearrange("p (g n) -> p g n", g=4), [fbr], [r_pbuf])
            h.cp("act", pbuf[0:32, 8, 1:1 + CH], pG(0)[0][0:32, 256:384], [pG(0)[1]], [r_pbuf])
            da, dr_ = dif(ci)
            ma, mr = mix(ci)
            h.tt("dve", da[:], pbuf[:, :, 0:CH], pbuf[:, :, 1:1 + CH], ALU.subtract, [r_pbuf], [dr_])
            for g in range(9):
                h.stt(ma[:, g, :], da[:, g, :], pv(MU + g), pbuf[:, g, 1:1 + CH], ALU.mult, ALU.add, [dr_, r_pbuf, r_c], [mr])
            h.cp("pool", pbuf[:, :, 0:1], pbuf[:, :, CH:CH + 1], [r_pbuf], [r_pbuf])
            if stage == 1:
                ota, otr = ot(ci)
                h.cp('dve', ota[:], xba[:, 0:2, :].rearrange('p a b -> p (a b)'), [xbr, mr, r_pbuf], [otr])
                P.dma('pool', om[c0:c0 + CH, :], ota[:], reads=[otr], writes=[r_om])
                continue
            waa, war = wab(ci)
            h.act(waa[0:64, :], ma[0:64, 6, :], AF.Tanh, [mr], [war])
            h.cp("dve", waa[64:128, :], ma[64:128, 6, :], [mr], [war])
            sga, sgr = sgb(ci)
            h.act(sga[:, 0, :], ma[:, 7, :], AF.Sigmoid, [mr], [sgr])
            h.act(sga[0:32, 1, :], ma[0:32, 8, :], AF.Sigmoid, [mr], [sgr])
            if stage == 21:
                ota, otr = ot(ci)
                h.cp('dve', ota[:], xba[:, 0:2, :].rearrange('p a b -> p (a b)'), [xbr, war, sgr], [otr])
                P.dma('pool', om[c0:c0 + CH, :], ota[:], reads=[otr], writes=[r_om])
                continue
            m0, m0r = pM(0)
            m1, m1r = pM(1)
            siga, sigr = sig(ci)
            aaa, aar = aa(ci)
            for cg in range(2):
                h.mm(m0[:, cg * 128:(cg + 1) * 128], lora_b[0:64, cg * 128:(cg + 1) * 128], waa[0:64, :], [r_c, war], [m0r])
                h.mm(m1[:, 256 + cg * 128:256 + (cg + 1) * 128], lora_b[64:128, cg * 128:(cg + 1) * 128], waa[64:128, :], [r_c, war], [m1r])
            if stage == 22:
                ota, otr = ot(ci)
                h.cp('dve', ota[:], xba[:, 0:2, :].rearrange('p a b -> p (a b)'), [xbr, war, sgr, m0r, m1r], [otr])
                P.dma('pool', om[c0:c0 + CH, :], ota[:], reads=[otr], writes=[r_om])
                continue
            for cg in range(2):
                h.act(siga[:, cg, :], m0[:, cg * 128:(cg + 1) * 128], AF.Sigmoid, [m0r, r_c], [sigr], bias=pv(W0 + cg))
                h.act(aaa[:, cg, :], m1[:, 256 + cg * 128:256 + (cg + 1) * 128], AF.Sigmoid, [m1r, r_c], [aar], bias=pv(A0 + cg))
            if stage == 23:
                ota, otr = ot(ci)
                h.cp('dve', ota[:], xba[:, 0:2, :].rearrange('p a b -> p (a b)'), [xbr, sigr, aar], [otr])
                P.dma('pool', om[c0:c0 + CH, :], ota[:], reads=[otr], writes=[r_om])
                continue
            gps, gpr = pG(0)
            h.mm(gps[:, 0:256], sga[:, 0, :], lora_b[:, 256:512], [sgr, r_c], [gpr], start=True, stop=False)
            h.mm(gps[:, 0:256], sga[0:32, 1, :], lora_b[0:32, 512:768], [sgr, r_c], [gpr], start=False, stop=True)
            Gta, Gtr = Gt(ci)
            h.cp("act", Gta[:], gps[:, 0:256], [gpr], [Gtr])
            if stage == 2:
                ota, otr = ot(ci)
                h.cp('dve', ota[:], xba[:, 0:2, :].rearrange('p a b -> p (a b)'), [xbr, mr, sigr, aar, Gtr], [otr])
                P.dma('pool', om[c0:c0 + CH, :], ota[:], reads=[otr], writes=[r_om])
                continue
            csa, csr = cs_(ci)
            cma, cmr = csm(ci)
            for cg in range(2):
                P.op("dve", lambda e: e.tensor_tensor_scan(out=csa[:, cg, :], data0=ones_f, data1=siga[:, cg, :], initial=0.0,
                                                           op0=ALU.mult, op1=ALU.add), [sigr, r_c], [csr])
            h.tt("dve", cma[:], csa[:], siga[:], ALU.subtract, [csr, sigr], [cmr])
            nca, ncr = ncc(ci)
            wca, wcr = wcc(ci)
            h.ts("dve", nca[:], csa[:, :, CH - 1], -CDEC, ALU.mult, [csr], [ncr])
            h.act(wca[:], nca[:], AF.Exp, [ncr], [wcr])
            e1, e1r = E1(ci); e2, e2r = E2(ci); e3, e3r = E3(ci); e4, e4r = E4(ci)
            h.act(e1[:], csa[:], AF.Exp, [csr], [e1r], scale=-CDEC)
            h.act(e2[:], cma[:], AF.Exp, [cmr], [e2r], scale=-CDEC)
            h.act(e3[:], csa[:], AF.Exp, [csr], [e3r], scale=CDEC)
            for cg in range(2):
                h.act(e4[:, cg, :], csa[:, cg, :], AF.Exp, [csr, ncr], [e4r], scale=CDEC, bias=nca[:, cg:cg + 1])
            if stage == 3:
                ota, otr = ot(ci)
                h.cp('dve', ota[:], xba[:, 0:2, :].rearrange('p a b -> p (a b)'), [xbr, e1r, e2r, e3r, e4r, wcr], [otr])
                P.dma('pool', om[c0:c0 + CH, :], ota[:], reads=[otr], writes=[r_om])
                continue
            kka, kkr = kkt(ci)
            sqa, sqr = sqb(ci)
            rsa, rsr = rs(ci)
            kaa, kar = ka(ci)
            kpa, kpr = kp(ci)
            for cg in range(2):
                h.ts("dve", kka[:, cg, :], ma[:, 2 + cg, :], pv(KK_ + cg), ALU.mult, [mr, r_c], [kkr])
            h.act(sqa[:], kka[:], AF.Square, [kkr], [sqr])
            m1, m1r = pM(1)
            h.mm(m1[:, 0:256], bones_b, sqa[:].rearrange("p a b -> p (a b)"), [r_c, sqr], [m1r])
            h.rsqrt(rsa[:].rearrange("p a b -> p (a b)"), m1[:, 0:256], 1e-6, [m1r], [rsr])
            h.tt("dve", kka[:], kka[:], rsa[:], ALU.mult, [kkr, rsr], [kkr])
            h.tt("dve", kaa[:], kka[:], aaa[:], ALU.mult, [kkr, aar], [kar])
            for cg in range(2):
                h.ts("dve", kpa[:, cg, :], aaa[:, cg, :], pv(KA_ + cg), ALU.mult, [aar, r_c], [kpr], s2=pv(OMKA + cg), op1=ALU.add)
            h.tt("dve", kpa[:], kpa[:], ma[:, 2:4, :], ALU.mult, [kpr, mr], [kpr])
            if stage == 4:
                ota, otr = ot(ci)
                h.cp('dve', ota[:], xba[:, 0:2, :].rearrange('p a b -> p (a b)'), [xbr, kkr, kar, kpr], [otr])
                P.dma('pool', om[c0:c0 + CH, :], ota[:], reads=[otr], writes=[r_om])
                continue
            ARa, ARr = AR(ci); Bta, Btr = Bt(ci); Kta, Ktr = Kt(ci); Bha, Bhr = Bh_(ci); Kha, Khr = Kh_(ci)
            vba, vbr = vb(ci); rka, rkr_ = rkr(ci)
            for cg in range(2):
                h.stt(ARa[:, cg, 0, :], kka[:, cg, :], -1.0, e2[:, cg, :], ALU.mult, ALU.mult, [kkr, e2r], [ARr])
                h.stt(rka[:, cg, :], ma[:, cg, :], pv(RK_ + cg), kpa[:, cg, :], ALU.mult, ALU.mult, [mr, kpr, r_c], [rkr_])
            h.tt("dve", ARa[:, :, 1, :], ma[:, 0:2, :], e1[:], ALU.mult, [mr, e1r], [ARr])
            h.tt("dve", Bta[:], kaa[:], e3[:], ALU.mult, [kar, e3r], [Btr])
            h.tt("dve", Kta[:], kpa[:], e3[:], ALU.mult, [kpr, e3r], [Ktr])
            h.tt("pool", Bha[:], kaa[:], e4[:], ALU.mult, [kar, e4r], [Bhr])
            h.tt("pool", Kha[:], kpa[:], e4[:], ALU.mult, [kpr, e4r], [Khr])
            h.cp("act", vba[:], ma[:, 4:6, :], [mr], [vbr])
            for cg in range(2):
                h.mm(m1[:, 256 + 2 * cg:256 + 2 * cg + 2], rka[:, cg, :], bsel_b, [rkr_, r_c], [m1r])
            bona, bonr = bon(ci)
            h.cp("dve", bona[:], m1[:, 256:260], [m1r], [bonr])
            pb, pbr = pB(0)
            tka, tkr = tokm(ci)
            for cg in range(2):
                for j, (src, sr_) in enumerate(((ARa[:, cg, 0, :], ARr), (Bha[:, cg, :], Bhr), (Kha[:, cg, :], Khr), (vba[:, cg, :], vbr))):
                    h.tr(pb[:, (cg * 4 + j) * 128:(cg * 4 + j + 1) * 128], src, ident_b, [sr_, r_c], [pbr])
            h.cp("act", tka[:].rearrange("p a b c -> p (a b c)"), pb[:, 0:1024], [pbr], [tkr])

            if stage == 5:
                ota, otr = ot(ci)
                h.cp('dve', ota[:], xba[:, 0:2, :].rearrange('p a b -> p (a b)'), [xbr, tkr, bonr], [otr])
                P.dma('pool', om[c0:c0 + CH, :], ota[:], reads=[otr], writes=[r_om])
                continue
            ota, otr = ot(ci)
            def head_gen(hd):
                cg, j = hd // 2, hd % 2
                hs = slice(j * 64, (j + 1) * 64)
                k4 = 4 * ci + hd
                mm_, mmr = pM(cg)
                h.mm(mm_[:, 0:256], Bta[hs, cg, :], ARa[hs, cg, :, :].rearrange("p a b -> p (a b)"), [Btr, ARr], [mmr])
                h.mm(mm_[:, 256:512], Kta[hs, cg, :], ARa[hs, cg, :, :].rearrange("p a b -> p (a b)"), [Ktr, ARr], [mmr])
                mi, mir = (pI(0), pG(0))[cg]
                h.mm(mi[:, 384:512], ARa[hs, cg, 0, :], Bta[hs, cg, :], [ARr, Btr], [mir])
                yield
                Na, Nr = N_(k4); Aa, Ar = A_(k4)
                h.tt("dve", Na[:], mm_[:, 0:128], maskUs, ALU.mult, [mmr, r_c], [Nr])
                h.tt("dve", Aa[:], mi[:, 384:512], maskL, ALU.mult, [mir, r_c], [Ar])
                Mrb, Mrbr = MrbT(k4); Mak, Makr = MakT(k4); Mrk, Mrkr = MrkT(k4)
                h.tt("dve", Mrb[:], mm_[:, 128:256], maskU, ALU.mult, [mmr, r_c], [Mrbr])
                h.tt("dve", Mak[:], mm_[:, 256:384], maskUs, ALU.mult, [mmr, r_c], [Makr])
                h.tt("dve", Mrk[:], mm_[:, 384:512], maskU, ALU.mult, [mmr, r_c], [Mrkr])
                yield
                Pa, Pr = Pm(k4)
                yield from dpl_inverse(h, mi, mir, Na, Nr, Aa, Ar, N2(k4)[0], N2(k4)[1], A2(k4)[0], A2(k4)[1], Pa, Pr, ident, r_c)
                Pba, Pbr = Pb(k4)
                h.cp("act", Pba[:], Pa[:], [Pr], [Pbr])
                yield
                Vh = tka[:, cg, 3, hs]
                h.mm(mi[:, 0:64], Mak[:], Vh, [Makr, tkr], [mir])
                h.mm(mi[hs, 128:256], tka[:, cg, 0, hs], Pba[:], [tkr, Pbr], [mir])
                yield
                Z1a, Z1r = Z1(k4)
                h.cp("act", Z1a[:], mi[:, 0:64], [mir], [Z1r])
                AdTa, AdTr = AdT(k4)
                h.cp("act", AdTa[hs, :], mi[hs, 128:256], [mir], [AdTr])
                yield
                h.mm(mi[:, 64:128], Pba[:], Z1a[:], [Pbr, Z1r], [mir])
                yield
                X1a, X1r = X1(k4)
                h.cp("act", X1a[:], mi[:, 64:128], [mir], [X1r])
                yield
                sa, sr = (pS(0), pFa(1))[cg]
                h.mm(sa[:, 0:64], AdTa[hs, :], Sb[cg][hs, :], [AdTr, r_Sb[cg][j]], [sr])
                yield
                Uta, Utr = Ut(k4)
                h.tt("dve", Uta[:], sa[:, 0:64], X1a[:], ALU.add, [sr, X1r], [Utr])
                yield
                h.mm(sa[:, 64:128], ARa[hs, cg, 1, :], Sb[cg][hs, :], [ARr, r_Sb[cg][j]], [sr], start=True, stop=False)
                h.mm(sa[:, 64:128], Mrb[:], Uta[:], [Mrbr, Utr], [sr], start=False, stop=False)
                h.mm(sa[:, 64:128], Mrk[:], Vh, [Mrkr, tkr], [sr], start=False, stop=True)
                h.mm(sa[hs, 128:192], tka[:, cg, 1, hs], Uta[:], [tkr, Utr], [sr], start=True, stop=False)
                h.mm(sa[hs, 128:192], tka[:, cg, 2, hs], Vh, [tkr], [sr], start=False, stop=True)
                yield
                h.stt(Sst[cg][hs, :], Sst[cg][hs, :], wca[hs, cg:cg + 1], sa[hs, 128:192], ALU.mult, ALU.add,
                      [r_Sst[cg][j], wcr, sr], [r_Sst[cg][j]])
                h.cp("act", Sb[cg][hs, :], Sst[cg][hs, :], [r_Sst[cg][j]], [r_Sb[cg][j]])
                yield
                ba2, br2 = bst(k4)
                mva, mvr = mv(k4)
                P.op("dve", lambda e: e.bn_stats(out=ba2[:], in_=sa[:, 64:128]), [sr], [br2])
                P.op("dve", lambda e: e.bn_aggr(out=mva[:], in_=ba2[:]), [br2], [mvr])
                yield
                h.rsqrt(mva[:, 1:2], mva[:, 1:2], 64e-5, [mvr], [mvr])
                yield
                yna, ynr = yn(k4)
                h.ts("dve", yna[:], sa[:, 64:128], mva[:, 0:1], ALU.subtract, [sr, mvr], [ynr], s2=mva[:, 1:2], op1=ALU.mult)
                yield
                h.tt("dve", yna[:], yna[:], rowt_s[:, hd * 64:(hd + 1) * 64], ALU.mult, [ynr, r_c], [ynr])
                yield
                h.tt("pool", yna[:], yna[:], rowt_s[:, 256 + hd * 64:256 + (hd + 1) * 64], ALU.add, [ynr, r_c], [ynr])
                yield
                h.stt(yna[:], Vh, bona[:, hd:hd + 1], yna[:], ALU.mult, ALU.add, [tkr, bonr, ynr], [ynr])
                yield
                h.tt("dve", ota[:, hd * 64:(hd + 1) * 64], yna[:], Gta[:, hd * 64:(hd + 1) * 64], ALU.mult, [ynr, Gtr], [otr])
            run_rr([head_gen(0), head_gen(2)])
            run_rr([head_gen(1), head_gen(3)])
            P.dma("pool", om[c0:c0 + CH, :], ota[:], reads=[otr], writes=[r_om])
        P.finish([r_om], "sp")
        print("A2 ninstr", P.ninstr, P.cnt)
    return nc


def a2_inputs(xT_b, hg, w_in, mu, w0, w2, a0, a2, g2, k_k, k_a, r_k, gn_w, gn_b):
    f = np.float32
    M0 = 640
    ch = slice(hg * 256, (hg + 1) * 256)
    colsel = np.concatenate([M0 + np.arange(hg * 256, (hg + 1) * 256), M0 + 1024 + np.arange(hg * 256, (hg + 1) * 256),
                             M0 + 2048 + np.arange(hg * 256, (hg + 1) * 256), M0 + 3072 + np.arange(288)])
    wF = w_in[:, colsel]
    wFb = np.ascontiguousarray(wF.reshape(KC, 128, -1).transpose(1, 0, 2)).astype(f)
    mu_c = mu[colsel - M0]
    pvec = np.zeros((128, 32), f)
    for g in range(8):
        pvec[:, g] = mu_c[g * 128:(g + 1) * 128]
    pvec[:32, 8] = mu_c[1024:1056]
    for cg in range(2):
        sl = slice(hg * 256 + cg * 128, hg * 256 + (cg + 1) * 128)
        pvec[:, 9 + cg] = w0[sl]
        pvec[:, 11 + cg] = a0[sl]
        pvec[:, 13 + cg] = k_k[sl]
        pvec[:, 15 + cg] = k_a[sl]
        pvec[:, 19 + cg] = r_k.reshape(-1)[sl]
    lora = np.zeros((128, 768), f)
    lora[0:64, 0:256] = w2[:, ch]
    lora[64:128, 0:256] = a2[:, ch]
    lora[:, 256:512] = g2[0:128, ch]
    lora[0:32, 512:768] = g2[128:160, ch]
    rowt = np.broadcast_to(np.concatenate([gn_w[ch], gn_b[ch]])[None], (128, 512)).astype(f)
    i = np.arange(128)
    ident = np.eye(128, dtype=f)
    maskL = (i[:, None] > i[None, :]).astype(f)
    maskU = (i[:, None] <= i[None, :]).astype(f)
    maskUs = (i[:, None] < i[None, :]).astype(f)
    bones = ((i[:, None] // 64) == (i[None, :] // 64)).astype(f)
    bsel = np.stack([(i // 64 == 0), (i // 64 == 1)], 1).astype(f)
    cst = np.concatenate([ident, maskL, maskU, maskUs, np.ones((128, 128), f), bones, bsel], 1)
    return dict(xT=xT_b, wF=wFb, lora=lora, pvec=pvec, rowt=rowt, cst=cst)


import contextlib
import math
import numpy as np

ST = 512
SCALE = 192.0 ** -0.5
NIN = 704


def build_a1(S):
    nc = bass.Bass("TRN2", target_bir_lowering=False)
    dt = nc.dram_tensor
    NST = S // ST
    NB = S // 128
    xT = dt("xT", [D, S], F32, kind="ExternalInput").ap()
    pos = dt("pos", [1, S], I32, kind="ExternalInput").ap()
    wF = dt("wF", [128, KC, NIN], F32, kind="ExternalInput").ap()
    wuq = dt("wuq", [128, 4, 512], F32, kind="ExternalInput").ap()
    wkv = dt("wkv", [128, 512], F32, kind="ExternalInput").ap()
    pvec = dt("pvec", [128, 8], F32, kind="ExternalInput").ap()
    cst = dt("cst", [128, 3 * 128], F32, kind="ExternalInput").ap()
    om = dt("om", [S, 256], BF16, kind="ExternalOutput").ap()

    with contextlib.ExitStack() as st:
        P = Prog(nc, st)
        h = H(P)
        wF_b = P.sbuf("wF_b", [128, KC, NIN], BF16)
        r_wF = P.region()
        stg = TB(P, "stg", [128, NIN], F32)
        for kc in range(KC):
            a, r = stg(kc)
            P.dma("sp", a[:], wF[:, kc, :], writes=[r])
            h.cp("dve" if kc % 2 == 0 else "act", wF_b[:, kc, :], a[:], [r], [r_wF])
        wuq_b = P.sbuf("wuq_b", [128, 4, 512], BF16)
        wkv_b = P.sbuf("wkv_b", [128, 512], BF16)
        wuq_fl = wuq.rearrange("p g c -> p (g c)")
        wuq_bfl = wuq_b[:].rearrange("p g c -> p (g c)")
        r_wq = P.region()
        for i, c0_ in enumerate(range(0, 2048, 512)):
            a, r = stg(i)
            P.dma("sp", a[:, 0:512], wuq_fl[:, c0_:c0_ + 512], writes=[r])
            h.cp("dve", wuq_bfl[:, c0_:c0_ + 512], a[:, 0:512], [r], [r_wq])
        a, r = stg(4)
        P.dma("sp", a[:, 0:512], wkv, writes=[r])
        h.cp("dve", wkv_b[:], a[:, 0:512], [r], [r_wq])
        pvec_s = P.sbuf("pvec_s", [128, 8], F32)
        cst_s = P.sbuf("cst_s", [128, 384], F32)
        cb = P.sbuf("cb", [128, 384], BF16)
        r_c = P.region()
        for a, b in ((pvec_s, pvec), (cst_s, cst)):
            P.dma("sp", a[:], b, writes=[r_c])
        P.op("dve", lambda e: e.tensor_copy(out=pvec_s[:, 7:8], in_=pvec_s[:, 7:8]), [r_c, r_wq], [r_c])
        h.cp("dve", cb[:], cst_s[:], [r_c], [r_c])
        ident_b, maskU_b, ones_b = cb[:, 0:128], cb[:, 128:256], cb[:, 256:384]
        pv = lambda i: pvec_s[:, i:i + 1]
        inv_s, sgn_s = pv(5), pv(6)

        Kc = P.sbuf("Kc", [128, S], BF16)
        Kpe = P.sbuf("Kpe", [128, S], BF16)
        Va = P.sbuf("Va", [128, NB, 130], BF16)
        r_Kc, r_Kpe, r_Va = P.regions(NST), P.regions(NST), P.regions(NST)
        rK_all = P.region()
        P.op("pool", lambda e: e.memset(Kpe[64:65, :], 1.0), [], [rK_all])
        P.op("pool", lambda e: e.memset(Va[:, :, 128:129], 1.0), [], [rK_all])
        kmax2 = P.sbuf("kmax2", [128, 1], F32)
        r_kmax2 = P.region()
        P.op("pool", lambda e: e.memset(kmax2[:], 0.0), [], [r_kmax2])

        pF = TB(P, "pF", [128, 512], F32, n=2, psum=True)
        pSs = TB(P, "pSs", [128, 512], F32, n=2, psum=True)
        pO = TB(P, "pO", [128, 512], F32, n=2, psum=True)
        pM = TB(P, "pM", [128, 512], F32, n=1, psum=True)
        pB = TB(P, "pB", [128, 1024], BF16, n=1, psum=True)

        def T_(name, shape, dt_=F32, n=2):
            return TB(P, name, shape, dt_, n)

        xstg = T_("xstg", [128, ST], F32, 2)
        xb = T_("xb", [128, KC, ST], BF16, 1)
        posi = T_("posi", [128, ST], I32, 1)
        posf = T_("posf", [128, ST], F32, 1)
        ang = T_("ang", [128, ST], F32, 1); uu = T_("uu", [128, ST], F32, 1); ki = posi; kf = posf
        cosT = T_("cosT", [128, ST], F32, 1); sinT = T_("sinT", [128, ST], F32, 1)
        cq_f = T_("cq_f", [128, 4, ST], BF16, 1)
        sq = T_("sq", [128, ST], BF16, 2)
        rstd = T_("rstd", [128, ST], F32, 1)
        cqn = T_("cqn", [128, 4, ST], BF16, 1)
        ckv_f = T_("ckv_f", [128, ST], F32, 1)
        kpf = T_("kpf", [64, 2, ST], F32, 1)
        kpr = T_("kpr", [64, ST], F32, 1)
        tmp64 = T_("tmp64", [64, ST], F32, 1)
        kmx = T_("kmx", [128, 1], F32, 1)
        kmaxn = T_("kmaxn", [128, 1], F32, 1)
        qn_b = T_("qn_b", [128, ST], BF16, 2)
        qpf = kpf
        qpr = kpr
        Qabs = T_("Qabs", [128, ST], BF16, 2)
        Qpe = T_("Qpe", [128, ST], BF16, 2)
        qnrm = T_("qnrm", [128, ST], F32, 1)
        PT = T_("PT", [128, ST], BF16, 3)
        rec = T_("rec", [128, 1], F32, 4)
        olat = T_("olat", [128, 128], BF16, 4)
        olT = T_("olT", [128, 128], BF16, 4)
        ot = T_("ot", [128, 4, 256], BF16, 1)
        r_om = P.region()

        def rope_tab(q0):
            pia, pir = posi(0); pfa, pfr = posf(0)
            P.dma("sp", pia[:], pos[:, q0:q0 + ST].partition_broadcast(128), writes=[pir])
            h.cp("dve", pfa[:], pia[:], [pir], [pfr])
            aa, ar = ang(0); ua, ur = uu(0); kia, kir = ki(0); kfa, kfr = kf(0)
            h.ts("dve", aa[:], pfa[:], inv_s, ALU.mult, [pfr, r_c], [ar])
            for (dst, off, bias) in ((sinT(0), 0.0, 0.0), (cosT(0), 0.25, math.pi / 2)):
                da, dr = dst
                h.ts("dve", ua[:], aa[:], 1.0 / TWO_PI, ALU.mult, [ar], [ur], s2=off, op1=ALU.add)
                h.cp("dve", kia[:], ua[:], [ur], [kir])
                h.cp("dve", kfa[:], kia[:], [kir], [kfr])
                h.stt(ua[:], kfa[:], -CW1, aa[:], ALU.mult, ALU.add, [kfr, ar], [ur])
                h.stt(ua[:], kfa[:], -CW2, ua[:], ALU.mult, ALU.add, [kfr, ur], [ur])
                if bias != 0.0:
                    h.ts("dve", ua[:], ua[:], bias, ALU.add, [ur], [ur])
                h.act(da[:], ua[:], AF.Sin, [ur], [dr], scale=1.0 - 2e-6)
            sa_, sr_ = sinT(0)
            h.ts("dve", sa_[:], sa_[:], sgn_s, ALU.mult, [sr_, r_c], [sr_])

        def rope_apply(dst, dr, src2, sr2):
            ca, cr = cosT(0); sa_, sr_ = sinT(0)
            ta, tr_ = tmp64(0)
            h.tt("dve", dst, src2[:, 0, :], ca[0:64, :], ALU.mult, [sr2, cr], [dr])
            h.tt("pool", ta[:], src2[:, 1, :], sa_[0:64, :], ALU.mult, [sr2, sr_], [tr_])
            h.tt("dve", dst, dst, ta[:], ALU.add, [dr, tr_], [dr])

        psi = [0]

        def inproj(c0, m, evac):
            pa, pr = pF(psi[0]); psi[0] += 1
            xba, xbr = xb(0)
            for kc in range(KC):
                h.mm(pa[0:m, :], wF_b[:, kc, c0:c0 + m], xba[:, kc, :], [r_wF, xbr], [pr], start=(kc == 0), stop=(kc == KC - 1))
            evac(pa, pr)

        for Q in range(NST):
            q0 = Q * ST
            xba, xbr = xb(0)
            for kc in range(KC):
                sa_, sr_ = xstg(kc)
                P.dma("sp", sa_[:], xT[kc * 128:(kc + 1) * 128, q0:q0 + ST], writes=[sr_])
                h.cp(("dve", "pool", "act")[kc % 3], xba[:, kc, :], sa_[:], [sr_], [xbr])
            rope_tab(q0)
            cqa, cqr = cq_f(0)
            m0, m0r = pM(0)
            gsz = (128, 128, 128, 64)
            for g in range(4):
                def ev(pa, pr, g=g):
                    m = gsz[g]
                    h.cp("act", cqa[0:m, g, :], pa[0:m, :], [pr], [cqr])
                    sqa, sqr = sq(g)
                    h.act(sqa[0:m, :], pa[0:m, :], AF.Square, [pr], [sqr])
                    h.mm(m0[:, :], ones_b[0:m, :], sqa[0:m, :], [r_c, sqr], [m0r], start=(g == 0), stop=(g == 3))
                inproj(g * 128, gsz[g], ev)
            rsa, rsr = rstd(0)
            h.rsqrt(rsa[:], m0[:, :], 1e-6, [m0r], [rsr], scale=1.0 / 448)
            cna, cnr = cqn(0)
            for g in range(4):
                m = gsz[g]
                h.stt(cna[0:m, g, :], cqa[0:m, g, :], pvec_s[0:m, g:g + 1], rsa[0:m, :], ALU.mult, ALU.mult, [cqr, rsr, r_c], [cnr])
            cka, ckr = ckv_f(0)

            def ev_kv(pa, pr):
                h.cp("act", cka[:], pa[:, :], [pr], [ckr])
                sqa, sqr = sq(0)
                h.act(sqa[:], pa[:, :], AF.Square, [pr], [sqr])
                h.mm(m0[:, :], ones_b, sqa[:], [r_c, sqr], [m0r])
            inproj(448, 128, ev_kv)
            h.rsqrt(rsa[:], m0[:, :], 1e-6, [m0r], [rsr], scale=1.0 / 128)
            h.stt(Kc[:, q0:q0 + ST], cka[:], pv(4), rsa[:], ALU.mult, ALU.mult, [ckr, rsr, r_c, rK_all], [r_Kc[Q]])
            pb, pbr = pB(0)
            for j in range(4):
                h.tr(pb[:, j * 128:(j + 1) * 128], Kc[:, q0 + j * 128:q0 + (j + 1) * 128], ident_b, [r_Kc[Q], r_c], [pbr])
            h.cp("act", Va[:, 4 * Q:4 * Q + 4, 0:128], pb[:, 0:512].rearrange("p (j n) -> p j n", j=4), [pbr, rK_all], [r_Va[Q]])
            kpa, kpr_ = kpf(0)
            inproj(576, 64, lambda pa, pr: h.cp("act", kpa[:, 0, :], pa[0:64, :], [pr], [kpr_]))
            inproj(640, 64, lambda pa, pr: h.cp("act", kpa[:, 1, :], pa[0:64, :], [pr], [kpr_]))
            kra, krr = kpr(0)
            rope_apply(kra[:], krr, kpa, kpr_)
            h.cp("act", Kpe[0:64, q0:q0 + ST], kra[:], [krr, rK_all], [r_Kpe[Q]])
            sqa, sqr = sq(0)
            h.act(sqa[:], Kc[:, q0:q0 + ST], AF.Square, [r_Kc[Q]], [sqr])
            sqb_, sqbr = sq(1)
            h.act(sqb_[0:64, :], Kpe[0:64, q0:q0 + ST], AF.Square, [r_Kpe[Q]], [sqbr])
            h.mm(m0[:, :], ones_b, sqa[:], [r_c, sqr], [m0r], start=True, stop=False)
            h.mm(m0[:, :], ones_b[0:64, :], sqb_[0:64, :], [r_c, sqbr], [m0r], start=False, stop=True)
            kma, kmr = kmx(0)
            P.op("dve", lambda e: e.reduce_max(out=kma[:], in_=m0[:, :], axis=AX.X), [m0r], [kmr])
            h.tt("dve", kmax2[:], kmax2[:], kma[:], ALU.max, [r_kmax2, kmr], [r_kmax2])
            kna, knr = kmaxn(0)
            h.act(kna[:], kmax2[:], AF.Sqrt, [r_kmax2], [knr])
            h.ts("dve", kna[:], kna[:], -1.0, ALU.mult, [knr], [knr])

            ota, otr = ot(Q)
            for hh in range(2):
                wc0 = hh * 256
                qna, qnr = qn_b(hh)
                pa, pr = pF(psi[0]); psi[0] += 1
                for g in range(4):
                    m = gsz[g]
                    h.mm(pa[:, :], wuq_b[0:m, g, wc0:wc0 + 128], cna[0:m, g, :], [r_c, cnr], [pr], start=(g == 0), stop=(g == 3))
                h.cp("act", qna[:], pa[:, :], [pr], [qnr])
                qpa, qpr_ = qpf(0)
                for w in range(2):
                    pa2, pr2 = pF(psi[0]); psi[0] += 1
                    for g in range(4):
                        m = gsz[g]
                        h.mm(pa2[0:64, :], wuq_b[0:m, g, wc0 + 128 + w * 64:wc0 + 192 + w * 64], cna[0:m, g, :], [r_c, cnr], [pr2],
                             start=(g == 0), stop=(g == 3))
                    h.cp("act", qpa[:, w, :], pa2[0:64, :], [pr2], [qpr_])
                qra, qrr = qpr(0)
                rope_apply(qra[:], qrr, qpa, qpr_)
                Qpa, Qpr = Qpe(hh)
                h.cp("act", Qpa[0:64, :], qra[:], [qrr], [Qpr])
                pa3, pr3 = pF(psi[0]); psi[0] += 1
                h.mm(pa3[:, :], wkv_b[:, hh * 128:(hh + 1) * 128], qna[:], [r_c, qnr], [pr3])
                Qaa, Qar = Qabs(hh)
                h.cp("act", Qaa[:], pa3[:, :], [pr3], [Qar])
                sqa, sqr = sq(0)
                h.act(sqa[:], Qaa[:], AF.Square, [Qar], [sqr])
                sqb_, sqbr = sq(1)
                h.act(sqb_[0:64, :], Qpa[0:64, :], AF.Square, [Qpr], [sqbr])
                h.mm(m0[:, :], ones_b, sqa[:], [r_c, sqr], [m0r], start=True, stop=False)
                h.mm(m0[:, :], ones_b[0:64, :], sqb_[0:64, :], [r_c, sqbr], [m0r], start=False, stop=True)
                qma, qmr = qnrm(0)
                h.act(qma[:], m0[:, :], AF.Sqrt, [m0r], [qmr])
                h.ts("dve", Qpa[64:65, :], qma[64:65, :], kna[64:65, 0:1], ALU.mult, [qmr, knr], [Qpr])

                nfull = 4 * Q
                o0, o0r = pO(0)
                o1, o1r = pO(1)
                Oacc = [(o0, o0r, 0), (o0, o0r, 129), (o1, o1r, 0), (o1, o1r, 129)]
                started = [False, False]
                def qk(jb):
                    jj = max(0, jb - nfull)
                    qc0 = jj * 128
                    s_, sr_ = pSs(jb)
                    kb = slice(jb * 128, (jb + 1) * 128)
                    Qi = jb // 4
                    h.mm(s_[:, qc0:ST], Kc[:, kb], Qaa[:, qc0:ST], [r_Kc[Qi], Qar], [sr_], start=True, stop=False)
                    h.mm(s_[:, qc0:ST], Kpe[0:65, kb], Qpa[0:65, qc0:ST], [r_Kpe[Qi], rK_all, Qpr], [sr_], start=False, stop=True)

                def rest(jb):
                    jj = max(0, jb - nfull)
                    qc0 = jj * 128
                    s_, sr_ = pSs(jb)
                    Qi = jb // 4
                    pta, ptr = PT(jb)
                    h.act(pta[:, qc0:ST], s_[:, qc0:ST], AF.Exp, [sr_], [ptr], scale=SCALE)
                    if jb >= nfull:
                        h.tt("dve", pta[:, qc0:qc0 + 128], pta[:, qc0:qc0 + 128], maskU_b, ALU.mult, [ptr, r_c], [ptr])
                    for qs in range(jj, 4):
                        oa, orr, oc = Oacc[qs]
                        bank = qs // 2
                        st_ = not started[bank]
                        started[bank] = True
                        P.op("pe", lambda e: e.matmul(oa[:, oc:oc + 129], lhsT=pta[:, qs * 128:(qs + 1) * 128], rhs=Va[:, jb, 0:129],
                                                      start=st_, stop=(jb == nfull + qs), skip_group_check=True),
                             [ptr, r_Va[Qi], rK_all], [orr])

                nblk = nfull + 4
                qk(0)
                for jb in range(nblk):
                    if jb + 1 < nblk:
                        qk(jb + 1)
                    rest(jb)
                for qs in range(4):
                    oa, orr, oc = Oacc[qs]
                    k4 = 4 * hh + qs
                    ra, rr = rec(k4)
                    P.op("dve", lambda e: e.reciprocal(out=ra[:], in_=oa[:, oc + 128:oc + 129]), [orr], [rr])
                    ola, olr = olat(k4)
                    h.ts("dve", ola[:], oa[:, oc:oc + 128], ra[:, 0:1], ALU.mult, [orr, rr], [olr])
                    h.tr(pb[:, 512 + (qs % 2) * 128:512 + (qs % 2 + 1) * 128], ola[:], ident_b, [olr, r_c], [pbr])
                    olTa, olTr = olT(k4)
                    h.cp("act", olTa[:], pb[:, 512 + (qs % 2) * 128:512 + (qs % 2 + 1) * 128], [pbr], [olTr])
                    h.mm(m0[:, (qs % 2) * 128:(qs % 2 + 1) * 128], olTa[:], wkv_b[:, 256 + hh * 128:256 + (hh + 1) * 128], [olTr, r_c], [m0r])
                    h.cp("act", ota[:, qs, hh * 128:(hh + 1) * 128], m0[:, (qs % 2) * 128:(qs % 2 + 1) * 128], [m0r], [otr])
            P.dma("pool", om[q0:q0 + ST, :].rearrange("(j p) n -> p j n", p=128), ota[:], reads=[otr], writes=[r_om])
        P.finish([r_om], "sp")
        print("A1 ninstr", P.ninstr, P.cnt)
    return nc


def a1_inputs(xT_b, pos_b, hg, w_in, q_norm, w_uq, kv_norm, w_ukv):
    f = np.float32
    kpe = w_in[:, 576:640]
    kpe_sw = np.concatenate([kpe[:, 32:], kpe[:, :32]], 1)
    wF = np.concatenate([w_in[:, 0:576], kpe, kpe_sw], 1)
    wFb = np.ascontiguousarray(wF.reshape(KC, 128, -1).transpose(1, 0, 2)).astype(f)
    wq = np.zeros((512, 512), f)
    for j in range(2):
        hd = 2 * hg + j
        nope = w_uq[:, hd * 192:hd * 192 + 128]
        pe = w_uq[:, hd * 192 + 128:hd * 192 + 192]
        pesw = np.concatenate([pe[:, 32:], pe[:, :32]], 1)
        wq[:448, j * 256:(j + 1) * 256] = np.concatenate([nope, pe, pesw], 1)
    wuq = np.ascontiguousarray(wq.reshape(4, 128, 512).transpose(1, 0, 2))
    wkv = np.concatenate([w_ukv[:, (2 * hg + j) * 256:(2 * hg + j) * 256 + 128].T for j in range(2)] +
                         [w_ukv[:, (2 * hg + j) * 256 + 128:(2 * hg + j) * 256 + 256] for j in range(2)], 1).astype(f)
    i = np.arange(128)
    pvec = np.zeros((128, 8), f)
    qn = np.zeros(512, f); qn[:448] = q_norm
    pvec[:, 0:4] = qn.reshape(4, 128).T
    pvec[:, 4] = kv_norm
    pvec[:, 5] = (10000.0 ** (-np.arange(0, 64, 2) / 64.0))[i % 32]
    pvec[:, 6] = np.where((i % 64) < 32, -1.0, 1.0)
    cst = np.concatenate([np.eye(128, dtype=f), (i[:, None] <= i[None, :]).astype(f), np.ones((128, 128), f)], 1)
    return dict(xT=xT_b, pos=pos_b, wF=wFb, wuq=wuq, wkv=np.ascontiguousarray(wkv), pvec=pvec, cst=cst)


import contextlib
import numpy as np

D = 2048
FF = 5632
KC = D // 128
HC = FF // 128
ALPHA_ = (2 * 2) ** 0.25
LN_EPS_ = 1e-5


def cast_weight(P, src, dst, dst_reg, stage, stage_bf, nrows, ncols_total, qi=[0]):
    nblk = src.shape[0]
    for b in range(nblk):
        i = qi[0] % len(stage)
        qi[0] += 1
        sa, sr = stage[i]
        ba, br = stage_bf[i]
        cols = src.shape[2]
        P.dma("sp", sa[:, :cols], src[b], writes=[sr])
        eng = ("dve", "act", "pool")[b % 3]
        if eng == "act":
            P.op("act", lambda e: e.activation(out=ba[:, :cols], in_=sa[:, :cols], func=AF.Copy), reads=[sr], writes=[br])
        else:
            P.op(eng, lambda e: e.tensor_copy(out=ba[:, :cols], in_=sa[:, :cols]), reads=[sr], writes=[br])
        P.dma("pool", dst[b], ba[:, :cols], reads=[br], writes=[dst_reg])


def build_tail(T, ntile=512):
    nc = bass.Bass("TRN2", target_bir_lowering=False)
    dt = nc.dram_tensor
    omT = dt("omT", [D, 2 + T], BF16, kind="ExternalInput").ap()
    xT = dt("xT", [D, 2 + T], F32, kind="ExternalInput").ap()
    hmask = dt("hmask", [128, 1], F32, kind="ExternalInput").ap()
    w_out = dt("w_out", [KC, 128, KC * 128], F32, kind="ExternalInput").ap()
    w_gate = dt("w_gate", [HC, 128, KC * 128], F32, kind="ExternalInput").ap()
    w_val = dt("w_val", [HC, 128, KC * 128], F32, kind="ExternalInput").ap()
    w_down = dt("w_down", [KC, 128, HC * 128], F32, kind="ExternalInput").ap()
    vecs = dt("vecs", [128, 4 * KC], F32, kind="ExternalInput").ap()
    cvec = dt("cvec", [128, 4 * HC], F32, kind="ExternalInput").ap()
    yT = dt("yT", [D, T], F32, kind="ExternalOutput").ap()
    wo_b = dt("wo_b", [KC, 128, KC * 128], BF16, kind="Internal").ap()
    wg_b = dt("wg_b", [HC, 128, KC * 128], BF16, kind="Internal").ap()
    wv_b = dt("wv_b", [HC, 128, KC * 128], BF16, kind="Internal").ap()
    wd_b = dt("wd_b", [KC, 128, HC * 128], BF16, kind="Internal").ap()

    with contextlib.ExitStack() as st:
        P = Prog(nc, st)
        NT = ntile
        om_b = P.sbuf("om_b", [128, KC, NT], BF16)
        xs = P.sbuf("xs", [128, KC, NT], F32)
        x1b = P.sbuf("x1b", [128, KC, NT], BF16)
        actb = P.sbuf("actb", [128, HC, NT], BF16)
        NWB = 3
        wgb = [P.sbuf(f"wgb{i}", [128, KC * 128], BF16) for i in range(NWB)]
        wvb = [P.sbuf(f"wvb{i}", [128, KC * 128], BF16) for i in range(NWB)]
        wdb = [P.sbuf(f"wdb{i}", [128, HC * 128], BF16) for i in range(2)]
        r_wgb, r_wvb, r_wdb = P.regions(NWB), P.regions(NWB), P.regions(2)
        hbuf = [P.sbuf(f"hbuf{i}", [128, NT + 2], F32) for i in range(2)]
        r_hbuf = P.regions(2)
        cbuf = [P.sbuf(f"cbuf{i}", [128, NT], F32) for i in range(2)]
        r_cbuf = P.regions(2)
        sbuf_ = [P.sbuf(f"sbuf{i}", [128, NT], F32) for i in range(2)]
        r_sbuf = P.regions(2)
        carry = P.sbuf("carry", [128, HC, 2], F32)
        r_carry = P.regions(HC)
        sq = [P.sbuf(f"sq{i}", [128, NT], BF16) for i in range(2)]
        r_sq = P.regions(2)
        rb = [P.sbuf(f"rb{i}", [128, NT], BF16) for i in range(2)]
        r_rb = P.regions(2)
        mean = P.sbuf("mean", [128, NT], F32)
        rstd = P.sbuf("rstd", [128, NT], F32)
        tmp = P.sbuf("tmp", [128, NT], F32)
        r_mean, r_rstd, r_tmp = P.regions(3)
        lt = [P.sbuf(f"lt{i}", [128, NT], F32) for i in range(2)]
        r_lt = P.regions(2)
        ones = P.sbuf("ones", [128, 128], BF16)
        r_ones = P.region()
        vec_s = P.sbuf("vec_s", [128, 4 * KC], F32)
        cvec_s = P.sbuf("cvec_s", [128, 4 * HC], F32)
        hm_s = P.sbuf("hm_s", [128, 1], F32)
        r_vec, r_cvec, r_hm = P.regions(3)
        stage = [(P.sbuf(f"stg{i}", [128, 2048], F32), P.region()) for i in range(2)]
        stage_bf = [(P.sbuf(f"stgb{i}", [128, 2048], BF16), P.region()) for i in range(2)]
        r_om, r_xs, r_x1b = P.region(), P.regions(KC), P.regions(KC)
        r_act = P.regions(HC)
        ps = [P.psum(f"ps{i}", [128, 512], F32) for i in range(8)]
        r_ps = P.regions(8)

        P.op("pool", lambda e: e.memset(ones[:], 1.0), writes=[r_ones])
        P.dma("sp", vec_s[:], vecs, writes=[r_vec])
        P.dma("sp", cvec_s[:], cvec, writes=[r_cvec])
        P.dma("sp", hm_s[:], hmask, writes=[r_hm])

        r_wo, r_wg, r_wv, r_wd = P.regions(4)

        def cast_w(src, dst, reg):
            nblk, _, cols = src.shape
            step = 2048
            k = 0
            for b in range(nblk):
                for c0 in range(0, cols, step):
                    cw = min(step, cols - c0)
                    i = k % 2
                    k += 1
                    (sa, sr), (ba, br) = stage[i], stage_bf[i]
                    P.dma("sp", sa[:, :cw], src[b, :, c0:c0 + cw], writes=[sr])
                    if k % 2 == 0:
                        P.op("act", lambda e: e.activation(out=ba[:, :cw], in_=sa[:, :cw], func=AF.Copy),
                             reads=[sr], writes=[br])
                    else:
                        P.op("dve", lambda e: e.tensor_copy(out=ba[:, :cw], in_=sa[:, :cw]), reads=[sr], writes=[br])
                    P.dma("pool", dst[b, :, c0:c0 + cw], ba[:, :cw], reads=[br], writes=[reg])

        cast_w(w_out, wo_b, r_wo)
        cast_w(w_gate, wg_b, r_wg)
        cast_w(w_val, wv_b, r_wv)
        cast_w(w_down, wd_b, r_wd)

        psi = [0]

        def next_ps():
            i = psi[0] % 4
            psi[0] += 1
            return ps[i], r_ps[i]

        wq = [0]

        def layer_norm(N, gcol, bcol, out_f32, r_out_f32, out_bf, r_out_bf, src, r_src):
            s1, rs1 = ps[4], r_ps[4]
            s2, rs2 = ps[5], r_ps[5]
            for c in range(KC):
                i = c % 2
                P.op("pool", lambda e: e.tensor_copy(out=rb[i][:, :N], in_=src(c)), reads=[r_src[c]], writes=[r_rb[i]])
                P.op("act", lambda e: e.activation(out=sq[i][:, :N], in_=src(c), func=AF.Square),
                     reads=[r_src[c]], writes=[r_sq[i]])
                P.op("pe", lambda e: e.matmul(s1[:, :N], lhsT=ones[:], rhs=rb[i][:, :N], start=(c == 0), stop=(c == KC - 1)),
                     reads=[r_ones, r_rb[i]], writes=[rs1])
                P.op("pe", lambda e: e.matmul(s2[:, :N], lhsT=ones[:], rhs=sq[i][:, :N], start=(c == 0), stop=(c == KC - 1)),
                     reads=[r_ones, r_sq[i]], writes=[rs2])
            P.op("act", lambda e: e.activation(out=mean[:, :N], in_=s1[:, :N], func=AF.Copy, scale=1.0 / D),
                 reads=[rs1], writes=[r_mean])
            P.op("dve", lambda e: e.tensor_tensor(out=tmp[:, :N], in0=mean[:, :N], in1=mean[:, :N], op=ALU.mult),
                 reads=[r_mean], writes=[r_tmp])
            P.op("dve", lambda e: e.scalar_tensor_tensor(out=tmp[:, :N], in0=s2[:, :N], scalar=1.0 / D, in1=tmp[:, :N],
                                                         op0=ALU.mult, op1=ALU.subtract), reads=[rs2, r_tmp], writes=[r_tmp])
            P.op("dve", lambda e: e.tensor_scalar(out=tmp[:, :N], in0=tmp[:, :N], scalar1=LN_EPS_, scalar2=None, op0=ALU.add),
                 reads=[r_tmp], writes=[r_tmp])
            P.op("act", lambda e: e.activation(out=tmp[:, :N], in_=tmp[:, :N], func=AF.Sqrt), reads=[r_tmp], writes=[r_tmp])
            P.op("dve", lambda e: e.reciprocal(out=rstd[:, :N], in_=tmp[:, :N]), reads=[r_tmp], writes=[r_rstd])
            for c in range(KC):
                i = c % 2
                eng = "dve" if c % 2 == 0 else "pool"
                P.op(eng, lambda e: e.tensor_tensor(out=lt[i][:, :N], in0=src(c), in1=mean[:, :N], op=ALU.subtract),
                     reads=[r_src[c], r_mean], writes=[r_lt[i]])
                P.op(eng, lambda e: e.tensor_tensor(out=lt[i][:, :N], in0=lt[i][:, :N], in1=rstd[:, :N], op=ALU.mult),
                     reads=[r_lt[i], r_rstd], writes=[r_lt[i]])
                P.op(eng, lambda e: e.tensor_scalar(out=out_f32(c), in0=lt[i][:, :N],
                                                    scalar1=vec_s[:, gcol * KC + c:gcol * KC + c + 1],
                                                    scalar2=vec_s[:, bcol * KC + c:bcol * KC + c + 1],
                                                    op0=ALU.mult, op1=ALU.add),
                     reads=[r_lt[i], r_vec], writes=[r_out_f32[c]])
                if out_bf is not None:
                    P.op("act", lambda e: e.activation(out=out_bf(c), in_=out_f32(c), func=AF.Copy),
                         reads=[r_out_f32[c]], writes=[r_out_bf[c]])

        def process(col0, N, halo_only, out_col0):
            P.dma("sp", om_b[:, :, :N], omT[:, col0:col0 + N].rearrange("(c p) n -> p c n", p=128), writes=[r_om])
            for c in range(KC):
                P.dma("sp", xs[:, c, :N], xT[c * 128:(c + 1) * 128, col0:col0 + N], writes=[r_xs[c]])
            for mo in range(KC):
                i = wq[0] % NWB
                wq[0] += 1
                P.dma("sp", wgb[i][:], wo_b[mo], reads=[r_wo], writes=[r_wgb[i]])
                pa, pr = next_ps()
                for kc in range(KC):
                    P.op("pe", lambda e: e.matmul(pa[:, :N], lhsT=wgb[i][:, kc * 128:(kc + 1) * 128], rhs=om_b[:, kc, :N],
                                                  start=(kc == 0), stop=(kc == KC - 1)),
                         reads=[r_wgb[i], r_om], writes=[pr])
                P.op("dve", lambda e: e.scalar_tensor_tensor(out=xs[:, mo, :N], in0=xs[:, mo, :N], scalar=ALPHA_, in1=pa[:, :N],
                                                             op0=ALU.mult, op1=ALU.add), reads=[pr, r_xs[mo]], writes=[r_xs[mo]])
            layer_norm(N, 0, 1, lambda c: xs[:, c, :N], r_xs, lambda c: x1b[:, c, :N], r_x1b, lambda c: xs[:, c, :N], r_xs)
            for hc in range(HC):
                i = wq[0] % NWB
                wq[0] += 1
                P.dma("sp", wgb[i][:], wg_b[hc], reads=[r_wg], writes=[r_wgb[i]])
                pg, prg = next_ps()
                for kc in range(KC):
                    P.op("pe", lambda e: e.matmul(pg[:, :N], lhsT=wgb[i][:, kc * 128:(kc + 1) * 128], rhs=x1b[:, kc, :N],
                                                  start=(kc == 0), stop=(kc == KC - 1)),
                         reads=[r_wgb[i], r_x1b[kc]], writes=[prg])
                if halo_only:
                    P.op("dve", lambda e: e.tensor_scalar(out=carry[:, hc, :], in0=pg[:, :2], scalar1=hm_s[:, 0:1], scalar2=None,
                                                          op0=ALU.mult), reads=[prg, r_hm], writes=[r_carry[hc]])
                    continue
                P.dma("sp", wvb[i][:], wv_b[hc], reads=[r_wv], writes=[r_wvb[i]])
                pv, prv = next_ps()
                for kc in range(KC):
                    P.op("pe", lambda e: e.matmul(pv[:, :N], lhsT=wvb[i][:, kc * 128:(kc + 1) * 128], rhs=x1b[:, kc, :N],
                                                  start=(kc == 0), stop=(kc == KC - 1)),
                         reads=[r_wvb[i], r_x1b[kc]], writes=[prv])
                j = hc % 2
                hb, rh = hbuf[j], r_hbuf[j]
                cb, rc = cbuf[j], r_cbuf[j]
                sb, rs = sbuf_[j], r_sbuf[j]
                P.op("act", lambda e: e.activation(out=hb[:, 2:2 + N], in_=pg[:, :N], func=AF.Copy), reads=[prg], writes=[rh])
                P.op("pool", lambda e: e.tensor_copy(out=hb[:, 0:2], in_=carry[:, hc, :]), reads=[r_carry[hc]], writes=[rh])
                P.op("pool", lambda e: e.tensor_copy(out=carry[:, hc, :], in_=hb[:, N:N + 2]), reads=[rh], writes=[r_carry[hc]])
                cw = lambda k: cvec_s[:, k * HC + hc:k * HC + hc + 1]
                P.op("dve", lambda e: e.tensor_scalar(out=cb[:, :N], in0=hb[:, 2:2 + N], scalar1=cw(2), scalar2=cw(3),
                                                      op0=ALU.mult, op1=ALU.add), reads=[rh, r_cvec], writes=[rc])
                P.op("dve", lambda e: e.scalar_tensor_tensor(out=cb[:, :N], in0=hb[:, 1:1 + N], scalar=cw(1), in1=cb[:, :N],
                                                             op0=ALU.mult, op1=ALU.add), reads=[rh, rc, r_cvec], writes=[rc])
                P.op("dve", lambda e: e.scalar_tensor_tensor(out=cb[:, :N], in0=hb[:, 0:N], scalar=cw(0), in1=cb[:, :N],
                                                             op0=ALU.mult, op1=ALU.add), reads=[rh, rc, r_cvec], writes=[rc])
                P.op("act", lambda e: e.activation(out=sb[:, :N], in_=cb[:, :N], func=AF.Silu), reads=[rc], writes=[rs])
                P.op("dve", lambda e: e.tensor_tensor(out=actb[:, hc, :N], in0=sb[:, :N], in1=pv[:, :N], op=ALU.mult),
                     reads=[rs, prv], writes=[r_act[hc]])
            if halo_only:
                return
            for mo in range(KC):
                i = mo % 2
                P.dma("sp", wdb[i][:], wd_b[mo], reads=[r_wd], writes=[r_wdb[i]])
                pa, pr = next_ps()
                for hc in range(HC):
                    P.op("pe", lambda e: e.matmul(pa[:, :N], lhsT=wdb[i][:, hc * 128:(hc + 1) * 128], rhs=actb[:, hc, :N],
                                                  start=(hc == 0), stop=(hc == HC - 1)),
                         reads=[r_wdb[i], r_act[hc]], writes=[pr])
                P.op("dve", lambda e: e.scalar_tensor_tensor(out=xs[:, mo, :N], in0=xs[:, mo, :N], scalar=ALPHA_, in1=pa[:, :N],
                                                             op0=ALU.mult, op1=ALU.add), reads=[pr, r_xs[mo]], writes=[r_xs[mo]])
            layer_norm(N, 2, 3, lambda c: xs[:, c, :N], r_xs, None, None, lambda c: xs[:, c, :N], r_xs)
            for c in range(KC):
                P.dma("pool", yT[c * 128:(c + 1) * 128, out_col0:out_col0 + N], xs[:, c, :N], reads=[r_xs[c]], writes=[r_y])

        r_y = P.region()
        process(0, 2, True, 0)
        for t0 in range(0, T, NT):
            n = min(NT, T - t0)
            process(2 + t0, n, False, t0)
        P.finish([r_y], "sp")
        print("tail ninstr", P.ninstr, P.cnt)
    return nc


def blk_w(w, kc_rows=True):
    K, M = w.shape
    a = w.reshape(K // 128, 128, M // 128, 128)
    return np.ascontiguousarray(a.transpose(2, 1, 0, 3)).reshape(M // 128, 128, (K // 128) * 128)


def vec_pc(v):
    return np.ascontiguousarray(v.reshape(-1, 128).T)


import ml_dtypes
from concourse.bass_utils import run_bass_kernel_spmd

_B, _S, _NC = 2, 16384, 8
_TT = _S // 4
_PROGS = {}


def _prog(name, fn):
    if name not in _PROGS:
        _PROGS[name] = fn()
    return _PROGS[name]


def _run(nc, in_maps):
    res = run_bass_kernel_spmd(nc, in_maps, core_ids=list(range(_NC)))
    return res.results


def _tail(omix, xres, w_out, g1, b1, w_gate, w_val, conv_w, conv_b, w_down, g2, b2):
    f = np.float32
    wo, wg, wv, wd = blk_w(w_out), blk_w(w_gate), blk_w(w_val), blk_w(w_down)
    vecs = np.concatenate([vec_pc(v) for v in (g1, b1, g2, b2)], 1).astype(f)
    cvec = np.concatenate([vec_pc(v) for v in (conv_w[0], conv_w[1], conv_w[2], conv_b)], 1).astype(f)
    maps = []
    for c in range(_NC):
        b, q = divmod(c, 4)
        t0 = q * _TT
        omT = np.zeros((D, 2 + _TT), ml_dtypes.bfloat16)
        xT = np.zeros((D, 2 + _TT), f)
        lo = max(t0 - 2, 0)
        omT[:, 2 - (t0 - lo):] = omix[b, lo:t0 + _TT, :].T
        xT[:, 2 - (t0 - lo):] = xres[b][:, lo:t0 + _TT]
        maps.append(dict(omT=omT, xT=xT, hmask=np.full((128, 1), 0.0 if q == 0 else 1.0, f),
                         w_out=wo, w_gate=wg, w_val=wv, w_down=wd, vecs=vecs, cvec=cvec))
    res = _run(_prog("tail", lambda: build_tail(_TT)), maps)
    out = [np.empty((D, _S), f) for _ in range(_B)]
    for c in range(_NC):
        b, q = divmod(c, 4)
        out[b][:, q * _TT:(q + 1) * _TT] = res[c]["yT"]
    return out


def kernel(**inp):
    f = np.float32
    inp = {k: np.asarray(v) for k, v in inp.items()}
    x = inp["x"].astype(f, copy=False)
    pos = inp["positions"].astype(np.int32, copy=False)
    xT = [np.ascontiguousarray(x[b].T) for b in range(_B)]
    posb = [np.ascontiguousarray(pos[b][None, :]) for b in range(_B)]
    g = lambda k: inp[k].astype(f, copy=False)
    maps = [a1_inputs(xT[c // 4], posb[c // 4], c % 4, g("l0_w_in"), g("l0_q_norm"), g("l0_w_uq"), g("l0_kv_norm"), g("l0_w_ukv"))
            for c in range(_NC)]
    r1 = _run(_prog("a1", lambda: build_a1(_S)), maps)
    maps = [a2_inputs(xT[c // 4], c % 4, g("l0_w_in"), g("l0_rwkv_mu"), g("l0_rwkv_w0"), g("l0_rwkv_w2"), g("l0_rwkv_a0"),
                      g("l0_rwkv_a2"), g("l0_rwkv_g2"), g("l0_rwkv_k_k"), g("l0_rwkv_k_a"), g("l0_rwkv_r_k"),
                      g("l0_rwkv_gn_w"), g("l0_rwkv_gn_b")) for c in range(_NC)]
    r2 = _run(_prog("a2", lambda: build_a2(_S)), maps)
    omix = np.empty((_B, _S, D), ml_dtypes.bfloat16)
    for c in range(_NC):
        b, hg = divmod(c, 4)
        omix[b, :, hg * 256:(hg + 1) * 256] = r1[c]["om"]
        omix[b, :, 1024 + hg * 256:1024 + (hg + 1) * 256] = r2[c]["om"]
    x1T = _tail(omix, xT, g("l0_w_out"), g("l0_ln1_g"), g("l0_ln1_b"), g("l0_ffn_w_gate"), g("l0_ffn_w_val"),
                g("l0_ffn_conv_w"), g("l0_ffn_conv_b"), g("l0_ffn_w_down"), g("l0_ln2_g"), g("l0_ln2_b"))
    maps = [c_inputs(x1T[c // 4], posb[c // 4], c % 4, g("l1_w_in"), g("l1_gdn_conv_w"), g("l1_gdn_A_log"), g("l1_gdn_dt_bias"),
                     g("l1_gdn_norm"), g("l1_ret_gn_w"), g("l1_ret_gn_b")) for c in range(_NC)]
    r3 = _run(_prog("c", lambda: build_c(_S)), maps)
    for c in range(_NC):
        b, hg = divmod(c, 4)
        omix[b, :, 2 * hg * 128:(2 * hg + 2) * 128] = r3[c]["om"][:, 0:256]
        omix[b, :, 1024 + 2 * hg * 128:1024 + (2 * hg + 2) * 128] = r3[c]["om"][:, 256:512]
    x2T = _tail(omix, x1T, g("l1_w_out"), g("l1_ln1_g"), g("l1_ln1_b"), g("l1_ffn_w_gate"), g("l1_ffn_w_val"),
                g("l1_ffn_conv_w"), g("l1_ffn_conv_b"), g("l1_ffn_w_down"), g("l1_ln2_g"), g("l1_ln2_b"))
    out = np.empty((_B, _S, D), f)
    for b in range(_B):
        out[b] = x2T[b].T
    return out
```

```python
import contextlib
import numpy as np
import concourse.bass as bass
import concourse.mybir as mybir

F32 = mybir.dt.float32
BF16 = mybir.dt.bfloat16
I32 = mybir.dt.int32
AF = mybir.ActivationFunctionType
ALU = mybir.AluOpType
AX = mybir.AxisListType

EPOCH = 8000
NDSEM = 12


class Region:
    __slots__ = ("w", "r", "name", "excl")

    def __init__(self, name="", excl=False):
        self.w = None
        self.r = {}
        self.name = name
        self.excl = excl


class Prog:
    def __init__(self, nc, stack, self_sync=None):
        self.nc = nc
        self.stack = stack
        import os as _os
        self.self_sync = (not _os.environ.get("NOSELF")) if self_sync is None else self_sync
        self.engs = {"pe": nc.tensor, "dve": nc.vector, "act": nc.scalar,
                     "pool": nc.gpsimd, "sp": nc.sync}
        self.cnt = {e: 0 for e in self.engs}
        self.sems = {}
        self.seen = {e: {} for e in self.engs}
        self.dn = {}
        self.dsem = {}
        self.ninstr = 0

    def _sem(self, key):
        if key not in self.sems:
            nm = "s_" + "_".join(str(k) for k in key)
            self.sems[key] = self.stack.enter_context(self.nc.semaphore(nm))
        return self.sems[key]

    def region(self, name=""):
        return Region(name)

    def regions(self, n, name=""):
        return [Region(f"{name}{i}") for i in range(n)]

    def _wait(self, eng, toks):
        E = self.engs[eng]
        seen = self.seen[eng]
        best = {}
        for (key, val) in toks:
            if best.get(key, 0) < val:
                best[key] = val
        for key, val in best.items():
            if seen.get(key, 0) < val:
                E.wait_ge(self._sem(key), val)
                seen[key] = val
                self.ninstr += 1

    def _deps(self, eng, reads, writes):
        toks = []
        for r in reads:
            if r.w is not None:
                toks.append(r.w)
            if r.excl:
                toks.extend(t for k, t in r.r.items() if k != eng)
        for r in writes:
            if r.w is not None:
                toks.append(r.w)
            toks.extend(r.r.values())
        if eng == "pe" or not self.self_sync:
            toks = [t for t in toks if t[0][0] != eng]
        return toks

    def op(self, eng, fn, reads=(), writes=()):
        self._wait(eng, self._deps(eng, reads, writes))
        ins = fn(self.engs[eng])
        c = self.cnt[eng]
        ep, idx = divmod(c, EPOCH)
        key = (eng, ep)
        ins.then_inc(self._sem(key), 1)
        self.cnt[eng] = c + 1
        self.ninstr += 1
        tok = (key, idx + 1)
        for r in reads:
            r.r[eng] = tok
        for r in writes:
            r.w = tok
            r.r = {}
        return tok

    def dma(self, q, out, in_, reads=(), writes=(), **kw):
        n = self.dn.get(q, 0)
        j = n % NDSEM
        prev = 16 * (n // NDSEM)
        key = ("d" + q, j)
        toks = [t for t in self._deps("dma" + q, reads, writes)]
        if prev > 0:
            toks.append((key, prev))
        self._wait(q, toks)
        ins = self.engs[q].dma_start(out=out, in_=in_, **kw)
        ins.then_inc(self._sem(key), 16)
        self.dn[q] = n + 1
        self.ninstr += 1
        tok = (key, prev + 16)
        rk = "dma" + q + str(j)
        for r in reads:
            r.r[rk] = tok
        for r in writes:
            r.w = tok
            r.r = {}
        return tok

    def coll(self, kind, ins_ap, outs_ap, groups, reads=(), writes=()):
        q = "pool"
        n = self.dn.get(q, 0)
        j = n % NDSEM
        prev = 16 * (n // NDSEM)
        key = ("d" + q, j)
        toks = [t for t in self._deps("dma" + q, reads, writes)]
        if prev > 0:
            toks.append((key, prev))
        self._wait(q, toks)
        ins = self.engs[q].collective_compute(kind, ALU.bypass, replica_groups=groups, ins=[ins_ap], outs=[outs_ap])
        ins.then_inc(self._sem(key), 16)
        self.dn[q] = n + 1
        self.ninstr += 1
        tok = (key, prev + 16)
        rk = "dma" + q + str(j)
        for r in reads:
            r.r[rk] = tok
        for r in writes:
            r.w = tok
            r.r = {}
        return tok

    def finish(self, regions, eng="sp"):
        toks = [r.w for r in regions if r.w is not None]
        self._wait(eng, toks)

    def sbuf(self, name, shape, dt):
        return self.stack.enter_context(self.nc.sbuf_tensor(name, list(shape), dt))

    def psum(self, name, shape, dt):
        return self.stack.enter_context(self.nc.psum_tensor(name, list(shape), dt))


import contextlib
import math
import numpy as np

D = 2048
KC = 16
CH = 128
TWO_PI = 2 * math.pi
CW1 = 6.28125
CW2 = TWO_PI - CW1


class H:
    def __init__(self, P):
        self.P = P

    def mm(self, out, lhsT, rhs, rd, wr, start=True, stop=True):
        self.P.op("pe", lambda e: e.matmul(out, lhsT=lhsT, rhs=rhs, start=start, stop=stop), rd, wr)

    def tr(self, out, in_, ident, rd, wr):
        self.P.op("pe", lambda e: e.transpose(out, in_, ident), rd, wr)

    def tt(self, eng, out, a, b, op, rd, wr):
        self.P.op(eng, lambda e: e.tensor_tensor(out=out, in0=a, in1=b, op=op), rd, wr)

    def ts(self, eng, out, a, s1, op0, rd, wr, s2=None, op1=None):
        if op1 is None:
            self.P.op(eng, lambda e: e.tensor_scalar(out=out, in0=a, scalar1=s1, scalar2=None, op0=op0), rd, wr)
        else:
            self.P.op(eng, lambda e: e.tensor_scalar(out=out, in0=a, scalar1=s1, scalar2=s2, op0=op0, op1=op1), rd, wr)

    def stt(self, out, in0, scalar, in1, op0, op1, rd, wr):
        self.P.op("dve", lambda e: e.scalar_tensor_tensor(out=out, in0=in0, scalar=scalar, in1=in1, op0=op0, op1=op1), rd, wr)

    def act(self, out, in_, func, rd, wr, **kw):
        self.P.op("act", lambda e: e.activation(out=out, in_=in_, func=func, **kw), rd, wr)

    def cp(self, eng, out, in_, rd, wr):
        if eng == "act":
            self.act(out, in_, AF.Copy, rd, wr)
        else:
            self.P.op(eng, lambda e: e.tensor_copy(out=out, in_=in_), rd, wr)

    def rsqrt(self, out, in_, eps, rd, wr, scale=1.0):
        self.ts("dve", out, in_, scale, ALU.mult, rd, wr, s2=eps, op1=ALU.add)
        self.act(out, out, AF.Sqrt, wr, wr)
        self.P.op("dve", lambda e: e.reciprocal(out=out, in_=out), wr, wr)


def run_rr(gens):
    gens = list(gens)
    while gens:
        for g_ in list(gens):
            try:
                next(g_)
            except StopIteration:
                gens.remove(g_)


class TB:
    def __init__(self, P, name, shape, dt, n=2, psum=False):
        mk = P.psum if psum else P.sbuf
        self.a = [mk(f"{name}_{i}", shape, dt) for i in range(n)]
        self.r = [P.region(f"{name}_{i}") for i in range(n)]
        for r in self.r:
            r.excl = psum
        self.n = n

    def __call__(self, i):
        return self.a[i % self.n], self.r[i % self.n]


def rope_tables(P, h, ci, pos_f, r_pos, inv_s, r_inv, tb):
    ang, r_ang = tb["ang"](ci)
    u, r_u = tb["u"](ci)
    ki, r_ki = tb["ki"](ci)
    kf, r_kf = tb["kf"](ci)
    sn, r_sn = tb["sin"](ci)
    cs, r_cs = tb["cos"](ci)
    h.ts("dve", ang[:], pos_f, inv_s[:, 0:1], ALU.mult, [r_pos, r_inv], [r_ang])
    for (dst, r_dst, off, bias) in ((sn, r_sn, 0.0, 0.0), (cs, r_cs, 0.25, math.pi / 2)):
        h.ts("dve", u[:], ang[:], 1.0 / TWO_PI, ALU.mult, [r_ang], [r_u], s2=off, op1=ALU.add)
        h.cp("dve", ki[:], u[:], [r_u], [r_ki])
        h.cp("dve", kf[:], ki[:], [r_ki], [r_kf])
        h.stt(u[:], kf[:], -CW1, ang[:], ALU.mult, ALU.add, [r_kf, r_ang], [r_u])
        h.stt(u[:], kf[:], -CW2, u[:], ALU.mult, ALU.add, [r_kf, r_u], [r_u])
        sc = 1.0 - 2e-6
        if bias == 0.0:
            h.act(dst[:], u[:], AF.Sin, [r_u], [r_dst], scale=sc)
        else:
            h.ts("dve", u[:], u[:], bias, ALU.add, [r_u], [r_u])
            h.act(dst[:], u[:], AF.Sin, [r_u], [r_dst], scale=sc)
    return cs, sn, r_cs, r_sn


def build_c(S, stage=9):
    nc = bass.Bass("TRN2", target_bir_lowering=False)
    dt = nc.dram_tensor
    NCH = S // CH
    xT = dt("xT", [D, S], F32, kind="ExternalInput").ap()
    pos = dt("pos", [1, S], I32, kind="ExternalInput").ap()
    wF = dt("wF", [128, KC, 8 * 128], F32, kind="ExternalInput").ap()
    wT = dt("wT", [128, KC, 772], F32, kind="ExternalInput").ap()
    convw = dt("convw", [128, 16], F32, kind="ExternalInput").ap()
    rowt = dt("rowt", [128, 4 + 128 + 512], F32, kind="ExternalInput").ap()
    cst = dt("cst", [128, 5 * 128], F32, kind="ExternalInput").ap()
    rett = dt("rett", [128, 2 * 128 + 128 + 8], F32, kind="ExternalInput").ap()
    om = dt("om", [S, 512], BF16, kind="ExternalOutput").ap()

    with contextlib.ExitStack() as st:
        P = Prog(nc, st)
        h = H(P)
        wF_b = P.sbuf("wF_b", [128, KC, 8 * 128], BF16)
        wT_b = P.sbuf("wT_b", [128, KC, 772], BF16)
        r_wF, r_wT = P.region(), P.region()
        stg = TB(P, "stg", [128, 1024], F32)
        for kc in range(KC):
            a, r = stg(2 * kc)
            P.dma("sp", a[:, :1024], wF[:, kc, :], writes=[r])
            h.cp("dve", wF_b[:, kc, :], a[:, :1024], [r], [r_wF])
            a, r = stg(2 * kc + 1)
            P.dma("sp", a[:, :772], wT[:, kc, :], writes=[r])
            h.cp("act", wT_b[:, kc, :], a[:, :772], [r], [r_wT])
        convw_s = P.sbuf("convw_s", [128, 16], F32)
        rowt_s = P.sbuf("rowt_s", [128, 4 + 128 + 512], F32)
        cst_s = P.sbuf("cst_s", [128, 5 * 128], F32)
        rett_s = P.sbuf("rett_s", [128, 2 * 128 + 128 + 8], F32)
        r_c = P.region()
        for a, b in ((convw_s, convw), (rowt_s, rowt), (cst_s, cst), (rett_s, rett)):
            P.dma("sp", a[:], b, writes=[r_c])
        ident = cst_s[:, 0:128]
        maskL = cst_s[:, 128:256]
        maskU = cst_s[:, 256:384]
        ones_f = cst_s[:, 384:512]
        zeros_f = cst_s[:, 512:640]
        cb = P.sbuf("cb", [128, 3 * 128], BF16)
        h.cp("dve", cb[:, 0:128], ident, [r_c], [r_c])
        h.cp("dve", cb[:, 128:256], ones_f, [r_c], [r_c])
        h.cp("dve", cb[:, 256:384], maskU, [r_c], [r_c])
        ident_b, ones_b = cb[:, 0:128], cb[:, 128:256]
        DTret = [rett_s[:, 0:128], rett_s[:, 128:256]]
        QdRow = rett_s[:, 256:384]
        khcol = [rett_s[:, 384:385], rett_s[:, 385:386]]
        wcret = [rett_s[:, 386:387], rett_s[:, 387:388]]
        inv_s = rett_s[:, 388:389]
        sgn_s = rett_s[:, 389:390]
        Alog_row, dtb_row = rowt_s[:, 0:2], rowt_s[:, 2:4]
        gnorm_row = rowt_s[:, 4:132]
        retw_row = [rowt_s[:, 132:260], rowt_s[:, 260:388]]
        retb_row = [rowt_s[:, 388:516], rowt_s[:, 516:644]]
        eA = P.sbuf("eA", [128, 2], F32)
        h.act(eA[:], Alog_row, AF.Exp, [r_c], [r_c])
        h.ts("dve", eA[:], eA[:], -1.0, ALU.mult, [r_c], [r_c])

        pF = TB(P, "pF", [128, 512], F32, n=1, psum=True)
        pI = TB(P, "pI", [128, 512], F32, n=1, psum=True)
        pT = TB(P, "pT", [128, 512], F32, n=2, psum=True)
        pM = TB(P, "pM", [128, 512], F32, n=2, psum=True)
        pB = TB(P, "pB", [128, 1024], BF16, n=1, psum=True)
        pS = TB(P, "pS", [128, 512], F32, n=1, psum=True)
        pB2 = pS

        Sg = [P.sbuf(f"Sg{i}", [128, 128], F32) for i in range(2)]
        Sgb = [P.sbuf(f"Sgb{i}", [128, 128], BF16) for i in range(2)]
        Sr_ = P.sbuf("Sr", [128, 128], F32)
        Srb_ = P.sbuf("Srb", [128, 128], BF16)
        Sr = [Sr_[0:64, :], Sr_[64:128, :]]
        Srb = [Srb_[0:64, :], Srb_[64:128, :]]
        r_Sg, r_Sgb, r_Sr, r_Srb = P.regions(2), P.regions(2), P.regions(2), P.regions(2)
        for i in range(2):
            P.op("pool", lambda e: e.memset(Sg[i][:], 0.0), [], [r_Sg[i]])
            P.op("pool", lambda e: e.memset(Sgb[i][:], 0.0), [], [r_Sgb[i]])
            P.op("pool", lambda e: e.memset(Sr[i], 0.0), [], [r_Sr[i]])
            P.op("pool", lambda e: e.memset(Srb[i], 0.0), [], [r_Srb[i]])
        cvx = P.sbuf("cvx", [128, 4, 3 + CH], F32)
        r_cvx = P.regions(4)
        P.op("pool", lambda e: e.memset(cvx[:], 0.0), [], r_cvx)

        def T_(name, shape, dt_=F32, n=2):
            return TB(P, name, shape, dt_, n)

        xs_f = T_("xs_f", [128, KC, CH])
        xb = T_("xb", [128, KC, CH], BF16)
        posi = T_("posi", [128, CH], I32)
        posf = T_("posf", [128, CH])
        rt = {k: T_(k, [128, CH], I32 if k == "ki" else F32) for k in ("ang", "u", "ki", "kf", "sin", "cos")}
        cacc = T_("cacc", [128, 4, CH])
        qk_f = T_("qk_f", [128, 2, CH])
        vT_b = T_("vT_b", [128, 2, CH], BF16)
        sq_b = T_("sq_b", [128, 2, CH], BF16)
        rs = T_("rs", [128, 2, CH])
        qkn = T_("qkn", [128, 2, CH], BF16)
        kn_t = T_("kn_t", [128, 128], BF16)
        lg = T_("lg", [128, 4])
        beta = T_("beta", [128, 2])
        nbeta = T_("nbeta", [128, 2])
        gg = T_("gg", [128, 2])
        t2 = {k: T_("t2" + k, [128, 2]) for k in "abcd"}
        gcol = T_("gcol", [128, 2])
        gtot = T_("gtot", [128, 2])
        sc1 = T_("sc1", [128, 2])
        sc2 = T_("sc2", [128, 2])
        wc = T_("wc", [128, 2])
        gB = T_("gB", [128, 128], F32, 4)
        Gp = T_("Gp", [128, 128], F32, 4)
        Gm = T_("Gm", [128, 128], F32, 4)
        ER = T_("ER", [128, 128], F32, 4)
        A_ = T_("A_", [128, 128], F32, 4)
        N_ = T_("N_", [128, 128], F32, 4)
        A2 = T_("A2", [128, 128], F32, 4)
        N2 = T_("N2", [128, 128], F32, 4)
        Pm = T_("Pm", [128, 128], F32, 4)
        Pb = T_("Pb", [128, 128], BF16, 4)
        MrT = T_("MrT", [128, 128], BF16, 4)
        Vp = T_("Vp", [128, 128], BF16, 4)
        X1 = T_("X1", [128, 128], F32, 4)
        Ad = T_("Ad", [128, 128], BF16, 4)
        AdT = T_("AdT", [128, 128], BF16, 4)
        Kh = T_("Kh", [128, 128], BF16, 4)
        QdT = T_("QdT", [128, 128], BF16, 4)
        Ut = T_("Ut", [128, 128], BF16, 4)
        ssq = T_("ssq", [128, 1], F32, 4)
        junk = T_("junk", [128, 128], F32, 4)
        zs = T_("zs", [128, 128], F32, 4)
        ot = T_("ot", [128, 512], BF16, 2)
        KKQ = T_("KKQ", [128, 256], F32, 2)
        Vtok = T_("Vtok", [128, 256], BF16, 2)
        rfs = T_("rfs", [128, 512])
        rq = T_("rq", [128, CH])
        rk = T_("rk", [128, CH])
        rqb = T_("rqb", [128, CH], BF16)
        rkb = T_("rkb", [128, CH], BF16)
        rqd = T_("rqd", [128, CH], BF16)
        rk_t = T_("rk_t", [128, 128], BF16)
        rv_b = T_("rv_b", [128, 256], BF16)
        rMT = T_("rMT", [128, 128], BF16, 4)
        bst = T_("bst", [128, 6], F32, 4)
        mv = T_("mv", [128, 2], F32, 4)
        r_om = P.region()

        for ci in range(NCH):
            c0 = ci * CH
            xa, xr = xs_f(ci)
            xba, xbr = xb(ci)
            P.dma("sp", xa[:], xT[:, c0:c0 + CH].rearrange("(c p) n -> p c n", p=128), writes=[xr])
            h.cp("pool", xba[:, :KC // 2, :], xa[:, :KC // 2, :], [xr], [xbr])
            h.cp("dve", xba[:, KC // 2:, :], xa[:, KC // 2:, :], [xr], [xbr])
            pi_a, pi_r = posi(ci)
            pf_a, pf_r = posf(ci)
            P.dma("sp", pi_a[:], pos[:, c0:c0 + CH].partition_broadcast(128), writes=[pi_r])
            h.cp("dve", pf_a[:], pi_a[:], [pi_r], [pf_r])
            cs, sn, r_cs, r_sn = rope_tables(P, h, ci, pf_a[:], pf_r, inv_s, r_c, rt)
            gF, gFr = pF(0)

            def inproj_F(g0):
                for g in range(g0, g0 + 4):
                    for kc in range(KC):
                        h.mm(gF[:, (g % 4) * 128:(g % 4 + 1) * 128], wF_b[:, kc, g * 128:(g + 1) * 128], xba[:, kc, :],
                             [r_wF, xbr], [gFr], start=(kc == 0), stop=(kc == KC - 1))
            inproj_F(0)
            tA, tAr = pT(0)
            tB, tBr = pT(1)
            for kc in range(KC):
                h.mm(tA[:, :512], xba[:, kc, :], wT_b[:, kc, 0:512], [r_wT, xbr], [tAr], start=(kc == 0), stop=(kc == KC - 1))
            for kc in range(KC):
                h.mm(tB[:, :260], xba[:, kc, :], wT_b[:, kc, 512:772], [r_wT, xbr], [tBr], start=(kc == 0), stop=(kc == KC - 1))

            ca, car = cacc(ci)
            qka, qkr = qk_f(ci)
            vta, vtr = vT_b(ci)
            for g in range(4):
                h.cp("act", cvx[:, g, 3:3 + CH], gF[:, g * 128:(g + 1) * 128], [gFr], [r_cvx[g]])
                w = lambda j: convw_s[:, g * 4 + j:g * 4 + j + 1]
                h.ts("dve", ca[:, g, :], cvx[:, g, 3:3 + CH], w(3), ALU.mult, [r_cvx[g], r_c], [car])
                for j in range(3):
                    h.stt(ca[:, g, :], cvx[:, g, j:j + CH], w(j), ca[:, g, :], ALU.mult, ALU.add, [r_cvx[g], car, r_c], [car])
                h.cp("pool", cvx[:, g, 0:3], cvx[:, g, CH:CH + 3], [r_cvx[g]], [r_cvx[g]])
                if g < 2:
                    h.act(qka[:, g, :], ca[:, g, :], AF.Silu, [car], [qkr])
                else:
                    h.act(vta[:, g - 2, :], ca[:, g, :], AF.Silu, [car], [vtr])
            inproj_F(4)
            rF, rFr = rfs(ci)
            h.cp("act", rF[:], gF[:, :], [gFr], [rFr])
            sqa, sqr = sq_b(ci)
            h.act(sqa[:], qka[:], AF.Square, [qkr], [sqr])
            m0, m0r = pM(0)
            h.mm(m0[:, 0:256], ones_b, sqa[:].rearrange("p a b -> p (a b)"), [r_c, sqr], [m0r])
            rsa, rsr = rs(ci)
            h.rsqrt(rsa[:].rearrange("p a b -> p (a b)"), m0[:, 0:256], 1e-6, [m0r], [rsr])
            qna, qnr = qkn(ci)
            h.tt("dve", qna[:, 0, :], qka[:, 1, :], rsa[:, 1, :], ALU.mult, [qkr, rsr], [qnr])
            h.stt(qna[:, 1, :], qka[:, 0, :], 128.0 ** -0.5, rsa[:, 0, :], ALU.mult, ALU.mult, [qkr, rsr], [qnr])
            m1, m1r = pM(1)
            h.mm(m1[:, 0:256], qna[:, 0, :], qna[:].rearrange("p a b -> p (a b)"), [qnr], [m1r])
            pb, pbr = pB(0)
            h.tr(pb[:, 0:128], qna[:, 0, :], ident_b, [qnr, r_c], [pbr])
            h.tr(pb[:, 128:256], vta[:, 0, :], ident_b, [vtr, r_c], [pbr])
            h.tr(pb[:, 256:384], vta[:, 1, :], ident_b, [vtr, r_c], [pbr])
            kta, ktr = kn_t(ci)
            h.cp("act", kta[:], pb[:, 0:128], [pbr], [ktr])
            lga, lgr = lg(ci)
            h.cp("dve", lga[:], tB[:, 256:260], [tBr], [lgr])
            ba_, br_ = beta(ci)
            nb_, nbr_ = nbeta(ci)
            h.act(ba_[:], lga[:, 0:2], AF.Sigmoid, [lgr], [br_])
            h.ts("dve", nb_[:], ba_[:], -1.0, ALU.mult, [br_], [nbr_])
            xa2, xr2 = t2["a"](ci)
            ab2, abr2 = t2["b"](ci)
            e2, er2 = t2["c"](ci)
            mx2, mxr2 = t2["d"](ci)
            h.tt("dve", xa2[:], lga[:, 2:4], dtb_row, ALU.add, [lgr, r_c], [xr2])
            h.act(ab2[:], xa2[:], AF.Abs, [xr2], [abr2])
            h.act(e2[:], ab2[:], AF.Exp, [abr2], [er2], scale=-1.0)
            h.act(e2[:], e2[:], AF.Ln, [er2], [er2], bias=1.0)
            h.ts("dve", mx2[:], xa2[:], 0.0, ALU.max, [xr2], [mxr2])
            h.tt("dve", mx2[:], mx2[:], e2[:], ALU.add, [mxr2, er2], [mxr2])
            ga, gr = gg(ci)
            h.tt("dve", ga[:], mx2[:], eA[:], ALU.mult, [mxr2, r_c], [gr])
            m0b, m0br = pM(0)
            h.mm(m0b[:, 256:258], maskU, ga[:], [r_c, gr], [m0br])
            h.mm(m0b[:, 258:260], ones_f, ga[:], [r_c, gr], [m0br])
            gca, gcr = gcol(ci)
            gta, gtr = gtot(ci)
            h.cp("dve", gca[:], m0b[:, 256:258], [m0br], [gcr])
            h.cp("dve", gta[:], m0b[:, 258:260], [m0br], [gtr])
            s1a, s1r = sc1(ci)
            s2a, s2r = sc2(ci)
            wca, wcr = wc(ci)
            h.act(s1a[:], gca[:], AF.Exp, [gcr], [s1r])
            h.tt("dve", s1a[:], s1a[:], nb_[:], ALU.mult, [s1r, nbr_], [s1r])
            h.tt("dve", s2a[:], gta[:], gca[:], ALU.subtract, [gtr, gcr], [s2r])
            h.act(s2a[:], s2a[:], AF.Exp, [s2r], [s2r])
            h.act(wca[:], gta[:], AF.Exp, [gtr], [wcr])

            ota, otr = ot(ci)
            kkq, kkqr = KKQ(ci)
            h.cp("act", kkq[:], m1[:, 0:256], [m1r], [kkqr])
            vtk, vtkr = Vtok(ci)
            h.cp("act", vtk[:], pb[:, 128:384], [pbr], [vtkr])

            def gdn_head(hh):
                k2 = 2 * ci + hh
                mg_, mgr = pM(hh)
                mi, mir = (pI(0), pF(0))[hh]
                gBa, gBr = gB(k2)
                h.ts("dve", gBa[:], ones_f, ga[:, hh:hh + 1], ALU.mult, [gr, r_c], [gBr])
                GC = mg_[:, 0:128]
                h.mm(GC, gBa[:], maskU, [gBr, r_c], [mgr])
                yield
                Gpa, Gpr = Gp(k2)
                Gma, Gmr = Gm(k2)
                ERa, ERr = ER(k2)
                h.stt(Gpa[:], GC, gca[:, hh:hh + 1], zeros_f, ALU.subtract, ALU.max, [mgr, gcr, r_c], [Gpr])
                h.stt(Gma[:], GC, gca[:, hh:hh + 1], zeros_f, ALU.subtract, ALU.min, [mgr, gcr, r_c], [Gmr])
                yield
                h.act(ERa[:], GC, AF.Exp, [mgr], [ERr])
                h.act(Gpa[:], Gpa[:], AF.Exp, [Gpr], [Gpr], scale=-1.0)
                h.act(Gma[:], Gma[:], AF.Exp, [Gmr], [Gmr])
                yield
                Aa, Ar = A_(k2)
                h.tt("dve", Aa[:], kkq[:, 0:128], Gpa[:], ALU.mult, [kkqr, Gpr], [Ar])
                h.stt(Aa[:], Aa[:], nb_[:, hh:hh + 1], maskL, ALU.mult, ALU.mult, [Ar, nbr_, r_c], [Ar])
                yield
                Na, Nr = N_(k2)
                h.tr(mi[:, 0:128], Aa[:], ident, [Ar, r_c], [mir])
                Ma, Mr_ = MrT(k2)
                h.tt("dve", Gma[:], Gma[:], maskU, ALU.mult, [Gmr, r_c], [Gmr])
                h.tt("dve", Ma[:], kkq[:, 128:256], Gma[:], ALU.mult, [kkqr, Gmr], [Mr_])
                yield
                h.cp("act", Na[:], mi[:, 0:128], [mir], [Nr])
                yield
                Pa, Pr = Pm(k2)
                h.tt("dve", Pa[:], Na[:], ident, ALU.add, [Nr, r_c], [Pr])
                yield
                cur = (Na, Nr, Aa, Ar)
                nxt = (N2(k2)[0], N2(k2)[1], A2(k2)[0], A2(k2)[1])
                for lv in range(6):
                    curN, curNr, curA, curAr = cur
                    nxtN, nxtNr, nxtA, nxtAr = nxt
                    h.mm(mi[:, 128:256], curN[:], curA[:], [curNr, curAr], [mir])
                    if lv < 5:
                        h.mm(mi[:, 256:384], curA[:], curN[:], [curNr, curAr], [mir])
                    yield
                    h.cp("act", nxtA[:], mi[:, 128:256], [mir], [nxtAr])
                    if lv < 5:
                        h.cp("dve", nxtN[:], mi[:, 256:384], [mir], [nxtNr])
                    yield
                    h.mm(mi[:, 0:128], nxtA[:], Pa[:], [nxtAr, Pr], [mir])
                    yield
                    h.tt("dve", Pa[:], Pa[:], mi[:, 0:128], ALU.add, [Pr, mir], [Pr])
                    yield
                    cur, nxt = nxt, cur
                Pba, Pbr = Pb(k2)
                h.cp("act", Pba[:], Pa[:], [Pr], [Pbr])
                Vpa, Vpr = Vp(k2)
                h.ts("dve", Vpa[:], vtk[:, hh * 128:(hh + 1) * 128], ba_[:, hh:hh + 1], ALU.mult, [vtkr, br_], [Vpr])
                Ada, Adr = Ad(k2)
                h.ts("dve", Ada[:], kta[:], s1a[:, hh:hh + 1], ALU.mult, [ktr, s1r], [Adr])
                yield
                h.mm(mg_[:, 256:384], Pba[:], Vpa[:], [Pbr, Vpr], [mgr])
                h.mm(mg_[:, 384:512], Ada[:], Pba[:], [Adr, Pbr], [mgr])
                Kha, Khr = Kh(k2)
                h.ts("dve", Kha[:], kta[:], s2a[:, hh:hh + 1], ALU.mult, [ktr, s2r], [Khr])
                QdTa, QdTr = QdT(k2)
                h.tt("dve", QdTa[:], qna[:, 1, :], ERa[:], ALU.mult, [qnr, ERr], [QdTr])
                yield
                X1a, X1r = X1(k2)
                h.cp("act", X1a[:], mg_[:, 256:384], [mgr], [X1r])
                AdTa, AdTr = AdT(k2)
                h.cp("act", AdTa[:], mg_[:, 384:512], [mgr], [AdTr])
                yield
                h.mm(mg_[:, 0:128], AdTa[:], Sgb[hh][:], [AdTr, r_Sgb[hh]], [mgr])
                yield
                Uta, Utr = Ut(k2)
                h.tt("dve", Uta[:], mg_[:, 0:128], X1a[:], ALU.add, [mgr, X1r], [Utr])
                yield
                h.mm(mg_[:, 128:256], QdTa[:], Sgb[hh][:], [QdTr, r_Sgb[hh]], [mgr], start=True, stop=False)
                h.mm(mg_[:, 128:256], Ma[:], Uta[:], [Mr_, Utr], [mgr], start=False, stop=True)
                h.mm(mi[:, 384:512], Kha[:], Uta[:], [Khr, Utr], [mir])
                yield
                h.stt(Sg[hh][:], Sg[hh][:], wca[:, hh:hh + 1], mi[:, 384:512], ALU.mult, ALU.add, [r_Sg[hh], wcr, mir], [r_Sg[hh]])
                yield
                h.cp("act", Sgb[hh][:], Sg[hh][:], [r_Sg[hh]], [r_Sgb[hh]])
                ja, jr = junk(k2)
                ssa, ssr = ssq(k2)
                h.act(ja[:], mg_[:, 128:256], AF.Square, [mgr], [jr, ssr], accum_out=ssa[:])
                yield
                h.rsqrt(ssa[:], ssa[:], 1e-6, [ssr], [ssr], scale=1.0 / 128)
                za, zr = zs(k2)
                h.act(za[:], tA[:, hh * 128:(hh + 1) * 128], AF.Silu, [tAr], [zr])
                yield
                h.tt("dve", za[:], za[:], gnorm_row, ALU.mult, [zr, r_c], [zr])
                yield
                h.stt(ota[:, hh * 128:(hh + 1) * 128], mg_[:, 128:256], ssa[:, 0:1], za[:], ALU.mult, ALU.mult, [mgr, ssr, zr], [otr])

            def ret_gen():
                rqa, rqr = rq(ci)
                rka, rkr = rk(ci)
                sns, snsr = rt["ang"](ci)
                h.ts("dve", sns[:], sn[:], sgn_s[:, 0:1], ALU.mult, [r_sn, r_c], [snsr])
                yield
                for (dst, dr, g0) in ((rqa, rqr, 0), (rka, rkr, 2)):
                    h.tt("dve", dst[:], rF[:, g0 * 128:(g0 + 1) * 128], cs[:], ALU.mult, [rFr, r_cs], [dr])
                    ja, jr = junk(2 * ci + g0 // 2 + 2)
                    h.tt("dve", ja[:], rF[:, (g0 + 1) * 128:(g0 + 2) * 128], sns[:], ALU.mult, [rFr, snsr], [jr])
                    h.tt("pool", dst[:], dst[:], ja[:], ALU.add, [dr, jr], [dr])
                    yield
                rqba, rqbr = rqb(ci)
                rkba, rkbr = rkb(ci)
                rqda, rqdr = rqd(ci)
                h.cp("act", rqba[:], rqa[:], [rqr], [rqbr])
                h.cp("act", rkba[:], rka[:], [rkr], [rkbr])
                h.tt("dve", rqda[:], rqa[:], QdRow, ALU.mult, [rqr, r_c], [rqdr])
                yield
                h.tr(pb[:, 384:512], rkba[:], ident_b, [rkbr, r_c], [pbr])
                yield
                rvba, rvbr = rv_b(ci)
                h.cp("act", rvba[:], tA[:, 256:512], [tAr], [rvbr])
                yield
                rkta, rktr = rk_t(ci)
                for hh in range(2):
                    h.ts("dve", rkta[:, hh * 64:(hh + 1) * 64], pb[:, 384 + hh * 64:384 + (hh + 1) * 64], khcol[hh], ALU.mult,
                         [pbr, r_c], [rktr])
                for hh in range(2):
                    k2 = 2 * ci + hh
                    hs = slice(hh * 64, (hh + 1) * 64)
                    sa, sr = pS(0); mi, mir = sa[:, 256:512], sr
                    h.mm(mi[:, 0:128], rkba[hs, :], rqba[hs, :], [rkbr, rqbr], [mir])
                    yield
                    Ma, Mr_ = rMT(k2)
                    h.tt("dve", Ma[:], mi[:, 0:128], DTret[hh], ALU.mult, [mir, r_c], [Mr_])
                    yield
                    h.mm(sa[:, 0:128], rqda[hs, :], Srb[hh], [rqdr, r_Srb[hh]], [sr], start=True, stop=False)
                    h.mm(sa[:, 0:128], Ma[:], rvba[:, hh * 128:(hh + 1) * 128], [Mr_, rvbr], [sr], start=False, stop=True)
                    h.mm(sa[hs, 128:256], rkta[:, hs], rvba[:, hh * 128:(hh + 1) * 128], [rktr, rvbr], [sr])
                    yield
                    h.stt(Sr[hh], Sr[hh], wcret[hh][hs, :], sa[hs, 128:256], ALU.mult, ALU.add, [r_Sr[hh], r_c, sr], [r_Sr[hh]])
                    yield
                    h.cp("act", Srb[hh], Sr[hh], [r_Sr[hh]], [r_Srb[hh]])
                    ba2, br2 = bst(k2)
                    mva, mvr = mv(k2)
                    P.op("dve", lambda e: e.bn_stats(out=ba2[:], in_=sa[:, 0:128]), [sr], [br2])
                    P.op("dve", lambda e: e.bn_aggr(out=mva[:], in_=ba2[:]), [br2], [mvr])
                    yield
                    h.rsqrt(mva[:, 1:2], mva[:, 1:2], 1e-5, [mvr], [mvr])
                    yield
                    ja, jr = junk(k2 + 2)
                    h.ts("dve", ja[:], sa[:, 0:128], mva[:, 0:1], ALU.subtract, [sr, mvr], [jr], s2=mva[:, 1:2], op1=ALU.mult)
                    h.tt("dve", ja[:], ja[:], retw_row[hh], ALU.mult, [jr, r_c], [jr])
                    h.tt("pool", ja[:], ja[:], retb_row[hh], ALU.add, [jr, r_c], [jr])
                    yield
                    za, zr = zs(k2 + 2)
                    h.act(za[:], tB[:, hh * 128:(hh + 1) * 128], AF.Silu, [tBr], [zr])
                    yield
                    h.tt("dve", ota[:, 256 + hh * 128:256 + (hh + 1) * 128], ja[:], za[:], ALU.mult, [jr, zr], [otr])

            run_rr([gdn_head(0), gdn_head(1), ret_gen()])
            P.dma("pool", om[c0:c0 + CH, :], ota[:], reads=[otr], writes=[r_om])
        P.finish([r_om], "sp")
        print("C ninstr", P.ninstr, P.cnt)
    return nc


GDN_QK_HEADS, GDN_V_HEADS, RET_HEADS = 4, 8, 8


def blk_in(w):
    return np.ascontiguousarray(w.reshape(KC, 128, -1).transpose(1, 0, 2))


def c_inputs(xT_b, pos_b, hg, w_in, gdn_conv_w, A_log, dt_bias, gdn_norm, ret_gn_w, ret_gn_b):
    f = np.float32
    qk_w, v_w = 512, 1024
    gq = w_in[:, hg * 128:(hg + 1) * 128]
    gk = w_in[:, qk_w + hg * 128: qk_w + (hg + 1) * 128]
    gv = [w_in[:, 2 * qk_w + (2 * hg + i) * 128: 2 * qk_w + (2 * hg + i + 1) * 128] for i in range(2)]
    zoff = 2 * qk_w + v_w
    gz = [w_in[:, zoff + (2 * hg + i) * 128: zoff + (2 * hg + i + 1) * 128] for i in range(2)]
    boff = zoff + v_w
    gb = w_in[:, boff + 2 * hg: boff + 2 * hg + 2]
    ga = w_in[:, boff + 8 + 2 * hg: boff + 8 + 2 * hg + 2]
    R0 = boff + 16
    rqw = w_in[:, R0 + 2 * hg * 64: R0 + (2 * hg + 2) * 64]
    rkw = w_in[:, R0 + 512 + 2 * hg * 64: R0 + 512 + (2 * hg + 2) * 64]
    rvw = w_in[:, R0 + 1024 + 2 * hg * 128: R0 + 1024 + (2 * hg + 2) * 128]
    rgw = w_in[:, R0 + 2048 + 2 * hg * 128: R0 + 2048 + (2 * hg + 2) * 128]

    def sw(w):
        a = w.reshape(w.shape[0], -1, 2, 32)
        return a[:, :, ::-1, :].reshape(w.shape)

    wF = np.concatenate([gq, gk, gv[0], gv[1], rqw, sw(rqw), rkw, sw(rkw)], 1)
    wT = np.concatenate([gz[0], gz[1], rvw, rgw, gb, ga], 1)
    cw = gdn_conv_w
    cols = [slice(hg * 128, (hg + 1) * 128), slice(qk_w + hg * 128, qk_w + (hg + 1) * 128),
            slice(2 * qk_w + 2 * hg * 128, 2 * qk_w + (2 * hg + 1) * 128),
            slice(2 * qk_w + (2 * hg + 1) * 128, 2 * qk_w + (2 * hg + 2) * 128)]
    convw = np.stack([cw[:, c].T for c in cols], 1).reshape(128, 16)
    rowt = np.concatenate([A_log[2 * hg:2 * hg + 2], dt_bias[2 * hg:2 * hg + 2], gdn_norm,
                           ret_gn_w[2 * hg * 128:(2 * hg + 2) * 128], ret_gn_b[2 * hg * 128:(2 * hg + 2) * 128]])
    rowt = np.broadcast_to(rowt[None], (128, rowt.size)).astype(f)
    i = np.arange(128)
    ident = np.eye(128, dtype=f)
    maskL = (i[:, None] > i[None, :]).astype(f)
    maskU = (i[:, None] <= i[None, :]).astype(f)
    cst = np.concatenate([ident, maskL, maskU, np.ones((128, 128), f), np.zeros((128, 128), f)], 1)
    lgam = [math.log(1.0 - 2.0 ** (-5.0 - (2 * hg + j))) for j in range(2)]
    diff = (i[None, :] - i[:, None]).astype(np.float64)
    DT = [np.where(diff >= 0, np.exp(np.maximum(diff, 0) * lgam[j]), 0.0) * 64 ** -0.5 for j in range(2)]
    QdRow = np.concatenate([np.broadcast_to(np.exp((i + 1.0) * lgam[j])[None] * 64 ** -0.5, (64, 128)) for j in range(2)], 0)
    khcol = np.stack([np.exp((127.0 - i) * lgam[j]) for j in range(2)], 1)
    wcr = np.broadcast_to(np.array([math.exp(128 * lgam[j]) for j in range(2)])[None], (128, 2))
    inv = (10000.0 ** (-np.arange(0, 64, 2) / 64.0))[i % 32][:, None]
    sgn = np.where((i % 64) < 32, -1.0, 1.0)[:, None]
    rett = np.concatenate([DT[0], DT[1], QdRow, khcol, wcr, inv, sgn, np.zeros((128, 2))], 1).astype(f)
    return dict(xT=xT_b, pos=pos_b, wF=blk_in(wF).astype(f), wT=blk_in(wT).astype(f), convw=convw.astype(f),
                rowt=rowt, cst=cst, rett=rett)


import contextlib
import math
import numpy as np

CDEC = math.exp(-0.5)
NCOL = 1056


def run_rr(gens):
    gens = list(gens)
    while gens:
        for g_ in list(gens):
            try:
                next(g_)
            except StopIteration:
                gens.remove(g_)


def dpl_inverse(h, mi, mir, Na, Nr, Aa, Ar, N2a, N2r, A2a, A2r, Pa, Pr, ident, r_c):
    h.tt("dve", Pa[:], Na[:], ident, ALU.add, [Nr, r_c], [Pr])
    yield
    cur = (Na, Nr, Aa, Ar)
    nxt = (N2a, N2r, A2a, A2r)
    for lv in range(6):
        curN, curNr, curA, curAr = cur
        nxtN, nxtNr, nxtA, nxtAr = nxt
        h.mm(mi[:, 128:256], curN[:], curA[:], [curNr, curAr], [mir])
        if lv < 5:
            h.mm(mi[:, 256:384], curA[:], curN[:], [curNr, curAr], [mir])
        yield
        h.cp("act", nxtA[:], mi[:, 128:256], [mir], [nxtAr])
        if lv < 5:
            h.cp("dve", nxtN[:], mi[:, 256:384], [mir], [nxtNr])
        yield
        h.mm(mi[:, 0:128], nxtA[:], Pa[:], [nxtAr, Pr], [mir])
        yield
        h.tt("dve", Pa[:], Pa[:], mi[:, 0:128], ALU.add, [Pr, mir], [Pr])
        yield
        cur, nxt = nxt, cur


def build_a2(S, stage=9):
    nc = bass.Bass("TRN2", target_bir_lowering=False)
    dt = nc.dram_tensor
    NCH = S // CH
    xT = dt("xT", [D, S], F32, kind="ExternalInput").ap()
    wF = dt("wF", [128, KC, NCOL], F32, kind="ExternalInput").ap()
    lora = dt("lora", [128, 768], F32, kind="ExternalInput").ap()
    pvec = dt("pvec", [128, 32], F32, kind="ExternalInput").ap()
    rowt = dt("rowt", [128, 512], F32, kind="ExternalInput").ap()
    cst = dt("cst", [128, 6 * 128 + 2], F32, kind="ExternalInput").ap()
    om = dt("om", [S, 256], BF16, kind="ExternalOutput").ap()

    with contextlib.ExitStack() as st:
        P = Prog(nc, st)
        h = H(P)
        wF_b = P.sbuf("wF_b", [128, KC, NCOL], BF16)
        r_wF = P.region()
        stg = TB(P, "stg", [128, NCOL], F32)
        for kc in range(KC):
            a, r = stg(kc)
            P.dma("sp", a[:], wF[:, kc, :], writes=[r])
            h.cp("dve" if kc % 2 == 0 else "act", wF_b[:, kc, :], a[:], [r], [r_wF])
        lora_s = P.sbuf("lora_s", [128, 768], F32)
        lora_b = P.sbuf("lora_b", [128, 768], BF16)
        pvec_s = P.sbuf("pvec_s", [128, 32], F32)
        rowt_s = P.sbuf("rowt_s", [128, 512], F32)
        cst_s = P.sbuf("cst_s", [128, 6 * 128 + 2], F32)
        r_c = P.region()
        for a, b in ((lora_s, lora), (pvec_s, pvec), (rowt_s, rowt), (cst_s, cst)):
            P.dma("sp", a[:], b, writes=[r_c])
        h.cp("dve", lora_b[:], lora_s[:], [r_c], [r_c])
        h.ts("dve", pvec_s[:, 17:19], pvec_s[:, 15:17], -1.0, ALU.mult, [r_c], [r_c], s2=1.0, op1=ALU.add)
        ident, maskL, maskU, maskUs = (cst_s[:, i * 128:(i + 1) * 128] for i in range(4))
        ones_f = cst_s[:, 512:640]
        cb = P.sbuf("cb", [128, 3 * 128 + 2], BF16)
        h.cp("dve", cb[:, 0:128], ident, [r_c], [r_c])
        h.cp("dve", cb[:, 128:256], cst_s[:, 640:768], [r_c], [r_c])
        h.cp("dve", cb[:, 256:384], ones_f, [r_c], [r_c])
        h.cp("dve", cb[:, 384:386], cst_s[:, 768:770], [r_c], [r_c])
        ident_b, bones_b, ones_b, bsel_b = cb[:, 0:128], cb[:, 128:256], cb[:, 256:384], cb[:, 384:386]
        pv = lambda i: pvec_s[:, i:i + 1]
        MU, W0, A0, KK_, KA_, OMKA, RK_ = 0, 9, 11, 13, 15, 17, 19

        pFa = TB(P, "pFa", [128, 512], F32, n=2, psum=True)
        pM = TB(P, "pM", [128, 512], F32, n=2, psum=True)
        pI = TB(P, "pI", [128, 512], F32, n=1, psum=True)
        pB = TB(P, "pB", [128, 1024], BF16, n=1, psum=True)
        pS = TB(P, "pS", [128, 512], F32, n=1, psum=True)
        pG = TB(P, "pG", [128, 512], F32, n=1, psum=True)

        Sst = [P.sbuf(f"Sst{i}", [128, 64], F32) for i in range(2)]
        Sb = [P.sbuf(f"Sb{i}", [128, 64], BF16) for i in range(2)]
        r_Sst = [[P.region(), P.region()] for _ in range(2)]
        r_Sb = [[P.region(), P.region()] for _ in range(2)]
        for i in range(2):
            P.op("pool", lambda e: e.memset(Sst[i][:], 0.0), [], r_Sst[i])
            P.op("pool", lambda e: e.memset(Sb[i][:], 0.0), [], r_Sb[i])
        pbuf = P.sbuf("pbuf", [128, 9, 1 + CH], F32)
        r_pbuf = P.region()
        P.op("pool", lambda e: e.memset(pbuf[:], 0.0), [], [r_pbuf])

        def T_(name, shape, dt_=F32, n=2):
            return TB(P, name, shape, dt_, n)

        xs_f = T_("xs_f", [128, KC, CH])
        xb = T_("xb", [128, KC, CH], BF16)
        dif = T_("dif", [128, 9, CH])
        mix = T_("mix", [128, 9, CH])
        wab = T_("wab", [128, CH], BF16)
        sgb = T_("sgb", [128, 2, CH], BF16)
        Gt = T_("Gt", [128, 256])
        sig = T_("sig", [128, 2, CH])
        cs_ = T_("cs_", [128, 2, CH])
        csm = T_("csm", [128, 2, CH])
        ncc = T_("ncc", [128, 2])
        wcc = T_("wcc", [128, 2])
        E1 = T_("E1", [128, 2, CH]); E2 = T_("E2", [128, 2, CH]); E3 = T_("E3", [128, 2, CH]); E4 = T_("E4", [128, 2, CH])
        aa = T_("aa", [128, 2, CH])
        kkt = T_("kkt", [128, 2, CH])
        sqb = T_("sqb", [128, 2, CH], BF16)
        rs = T_("rs", [128, 2, CH])
        ka = T_("ka", [128, 2, CH])
        kp = T_("kp", [128, 2, CH])
        AR = T_("AR", [128, 2, 2, CH], BF16)
        Bt = T_("Bt", [128, 2, CH], BF16)
        Kt = T_("Kt", [128, 2, CH], BF16)
        Bh_ = T_("Bh_", [128, 2, CH], BF16)
        Kh_ = T_("Kh_", [128, 2, CH], BF16)
        vb = T_("vb", [128, 2, CH], BF16)
        rkr = T_("rkr", [128, 2, CH], BF16)
        bon = T_("bon", [128, 4])
        tokm = T_("tokm", [128, 2, 4, 128], BF16)
        N_ = T_("N_", [128, 128], F32, 4); A_ = T_("A_", [128, 128], F32, 4)
        N2 = T_("N2", [128, 128], F32, 4); A2 = T_("A2", [128, 128], F32, 4)
        Pm = T_("Pm", [128, 128], F32, 4); Pb = T_("Pb", [128, 128], BF16, 4)
        MrbT = T_("MrbT", [128, 128], BF16, 4); MakT = T_("MakT", [128, 128], BF16, 4); MrkT = T_("MrkT", [128, 128], BF16, 4)
        Z1 = T_("Z1", [128, 64], BF16, 4)
        X1 = T_("X1", [128, 64], F32, 4)
        AdT = T_("AdT", [128, CH], BF16, 4)
        Ut = T_("Ut", [128, 64], BF16, 4)
        bst = T_("bst", [128, 6], F32, 4)
        mv = T_("mv", [128, 2], F32, 4)
        yn = T_("yn", [128, 64], F32, 4)
        ot = T_("ot", [128, 256], BF16, 2)
        r_om = P.region()

        for ci in range(NCH):
            c0 = ci * CH
            xa_, xr = xs_f(ci)
            xba, xbr = xb(ci)
            P.dma("sp", xa_[:], xT[:, c0:c0 + CH].rearrange("(c p) n -> p c n", p=128), writes=[xr])
            h.cp("pool", xba[:, :KC // 2, :], xa_[:, :KC // 2, :], [xr], [xbr])
            h.cp("dve", xba[:, KC // 2:, :], xa_[:, KC // 2:, :], [xr], [xbr])
            fa, far = pFa(0)
            fb, fbr = pFa(1)
            gcols = [(i * 128, 128) for i in range(8)] + [(1024, 32)]
            for g, (cc, m) in enumerate(gcols):
                if g < 4:
                    dst, dr, oc = fa, far, g * 128
                elif g < 8:
                    dst, dr, oc = fb, fbr, (g - 4) * 128
                else:
                    dst, dr, oc = pG(0)[0], pG(0)[1], 256
                for kc in range(KC):
                    h.mm(dst[0:m, oc:oc + 128], wF_b[:, kc, cc:cc + m], xba[:, kc, :], [r_wF, xbr], [dr],
                         start=(kc == 0), stop=(kc == KC - 1))
            h.cp("act", pbuf[:, 0:4, 1:1 + CH], fa[:, :].rearrange("p (g n) -> p g n", g=4), [far], [r_pbuf])
            h.cp("act", pbuf[:, 4:8, 1:1 + CH], fb[:, :].rearrange("p (g n) -> p g n", g=4), [fbr], [r_pbuf])
            h.cp("act", pbuf[0:32, 8, 1:1 + CH], pG(0)[0][0:32, 256:384], [pG(0)[1]], [r_pbuf])
            da, dr_ = dif(ci)
            ma, mr = mix(ci)
            h.tt("dve", da[:], pbuf[:, :, 0:CH], pbuf[:, :, 1:1 + CH], ALU.subtract, [r_pbuf], [dr_])
            for g in range(9):
                h.stt(ma[:, g, :], da[:, g, :], pv(MU + g), pbuf[:, g, 1:1 + CH], ALU.mult, ALU.add, [dr_, r_pbuf, r_c], [mr])
            h.cp("pool", pbuf[:, :, 0:1], pbuf[:, :, CH:CH + 1], [r_pbuf], [r_pbuf])
            if stage == 1:
                ota, otr = ot(ci)
                h.cp('dve', ota[:], xba[:, 0:2, :].rearrange('p a b -> p (a b)'), [xbr, mr, r_pbuf], [otr])
                P.dma('pool', om[c0:c0 + CH, :], ota[:], reads=[otr], writes=[r_om])
                continue
            waa, war = wab(ci)
            h.act(waa[0:64, :], ma[0:64, 6, :], AF.Tanh, [mr], [war])
            h.cp("dve", waa[64:128, :], ma[64:128, 6, :], [mr], [war])
            sga, sgr = sgb(ci)
            h.act(sga[:, 0, :], ma[:, 7, :], AF.Sigmoid, [mr], [sgr])
            h.act(sga[0:32, 1, :], ma[0:32, 8, :], AF.Sigmoid, [mr], [sgr])
            if stage == 21:
                ota, otr = ot(ci)
                h.cp('dve', ota[:], xba[:, 0:2, :].rearrange('p a b -> p (a b)'), [xbr, war, sgr], [otr])
                P.dma('pool', om[c0:c0 + CH, :], ota[:], reads=[otr], writes=[r_om])
                continue
            m0, m0r = pM(0)
            m1, m1r = pM(1)
            siga, sigr = sig(ci)
            aaa, aar = aa(ci)
            for cg in range(2):
                h.mm(m0[:, cg * 128:(cg + 1) * 128], lora_b[0:64, cg * 128:(cg + 1) * 128], waa[0:64, :], [r_c, war], [m0r])
                h.mm(m1[:, 256 + cg * 128:256 + (cg + 1) * 128], lora_b[64:128, cg * 128:(cg + 1) * 128], waa[64:128, :], [r_c, war], [m1r])
            if stage == 22:
                ota, otr = ot(ci)
                h.cp('dve', ota[:], xba[:, 0:2, :].rearrange('p a b -> p (a b)'), [xbr, war, sgr, m0r, m1r], [otr])
                P.dma('pool', om[c0:c0 + CH, :], ota[:], reads=[otr], writes=[r_om])
                continue
            for cg in range(2):
                h.act(siga[:, cg, :], m0[:, cg * 128:(cg + 1) * 128], AF.Sigmoid, [m0r, r_c], [sigr], bias=pv(W0 + cg))
                h.act(aaa[:, cg, :], m1[:, 256 + cg * 128:256 + (cg + 1) * 128], AF.Sigmoid, [m1r, r_c], [aar], bias=pv(A0 + cg))
            if stage == 23:
                ota, otr = ot(ci)
                h.cp('dve', ota[:], xba[:, 0:2, :].rearrange('p a b -> p (a b)'), [xbr, sigr, aar], [otr])
                P.dma('pool', om[c0:c0 + CH, :], ota[:], reads=[otr], writes=[r_om])
                continue
            gps, gpr = pG(0)
            h.mm(gps[:, 0:256], sga[:, 0, :], lora_b[:, 256:512], [sgr, r_c], [gpr], start=True, stop=False)
            h.mm(gps[:, 0:256], sga[0:32, 1, :], lora_b[0:32, 512:768], [sgr, r_c], [gpr], start=False, stop=True)
            Gta, Gtr = Gt(ci)
            h.cp("act", Gta[:], gps[:, 0:256], [gpr], [Gtr])
            if stage == 2:
                ota, otr = ot(ci)
                h.cp('dve', ota[:], xba[:, 0:2, :].rearrange('p a b -> p (a b)'), [xbr, mr, sigr, aar, Gtr], [otr])
                P.dma('pool', om[c0:c0 + CH, :], ota[:], reads=[otr], writes=[r_om])
                continue
            csa, csr = cs_(ci)
            cma, cmr = csm(ci)
            for cg in range(2):
                P.op("dve", lambda e: e.tensor_tensor_scan(out=csa[:, cg, :], data0=ones_f, data1=siga[:, cg, :], initial=0.0,
                                                           op0=ALU.mult, op1=ALU.add), [sigr, r_c], [csr])
            h.tt("dve", cma[:], csa[:], siga[:], ALU.subtract, [csr, sigr], [cmr])
            nca, ncr = ncc(ci)
            wca, wcr = wcc(ci)
            h.ts("dve", nca[:], csa[:, :, CH - 1], -CDEC, ALU.mult, [csr], [ncr])
            h.act(wca[:], nca[:], AF.Exp, [ncr], [wcr])
            e1, e1r = E1(ci); e2, e2r = E2(ci); e3, e3r = E3(ci); e4, e4r = E4(ci)
            h.act(e1[:], csa[:], AF.Exp, [csr], [e1r], scale=-CDEC)
            h.act(e2[:], cma[:], AF.Exp, [cmr], [e2r], scale=-CDEC)
            h.act(e3[:], csa[:], AF.Exp, [csr], [e3r], scale=CDEC)
            for cg in range(2):
                h.act(e4[:, cg, :], csa[:, cg, :], AF.Exp, [csr, ncr], [e4r], scale=CDEC, bias=nca[:, cg:cg + 1])
            if stage == 3:
                ota, otr = ot(ci)
                h.cp('dve', ota[:], xba[:, 0:2, :].rearrange('p a b -> p (a b)'), [xbr, e1r, e2r, e3r, e4r, wcr], [otr])
                P.dma('pool', om[c0:c0 + CH, :], ota[:], reads=[otr], writes=[r_om])
                continue
            kka, kkr = kkt(ci)
            sqa, sqr = sqb(ci)
            rsa, rsr = rs(ci)
            kaa, kar = ka(ci)
            kpa, kpr = kp(ci)
            for cg in range(2):
                h.ts("dve", kka[:, cg, :], ma[:, 2 + cg, :], pv(KK_ + cg), ALU.mult, [mr, r_c], [kkr])
            h.act(sqa[:], kka[:], AF.Square, [kkr], [sqr])
            m1, m1r = pM(1)
            h.mm(m1[:, 0:256], bones_b, sqa[:].rearrange("p a b -> p (a b)"), [r_c, sqr], [m1r])
            h.rsqrt(rsa[:].rearrange("p a b -> p (a b)"), m1[:, 0:256], 1e-6, [m1r], [rsr])
            h.tt("dve", kka[:], kka[:], rsa[:], ALU.mult, [kkr, rsr], [kkr])
            h.tt("dve", kaa[:], kka[:], aaa[:], ALU.mult, [kkr, aar], [kar])
            for cg in range(2):
                h.ts("dve", kpa[:, cg, :], aaa[:, cg, :], pv(KA_ + cg), ALU.mult, [aar, r_c], [kpr], s2=pv(OMKA + cg), op1=ALU.add)
            h.tt("dve", kpa[:], kpa[:], ma[:, 2:4, :], ALU.mult, [kpr, mr], [kpr])
            if stage == 4:
                ota, otr = ot(ci)
                h.cp('dve', ota[:], xba[:, 0:2, :].rearrange('p a b -> p (a b)'), [xbr, kkr, kar, kpr], [otr])
                P.dma('pool', om[c0:c0 + CH, :], ota[:], reads=[otr], writes=[r_om])
                continue
            ARa, ARr = AR(ci); Bta, Btr = Bt(ci); Kta, Ktr = Kt(ci); Bha, Bhr = Bh_(ci); Kha, Khr = Kh_(ci)
            vba, vbr = vb(ci); rka, rkr_ = rkr(ci)
            for cg in range(2):
                h.stt(ARa[:, cg, 0, :], kka[:, cg, :], -1.0, e2[:, cg, :], ALU.mult, ALU.mult, [kkr, e2r], [ARr])
                h.stt(rka[:, cg, :], ma[:, cg, :], pv(RK_ + cg), kpa[:, cg, :], ALU.mult, ALU.mult, [mr, kpr, r_c], [rkr_])
            h.tt("dve", ARa[:, :, 1, :], ma[:, 0:2, :], e1[:], ALU.mult, [mr, e1r], [ARr])
            h.tt("dve", Bta[:], kaa[:], e3[:], ALU.mult, [kar, e3r], [Btr])
            h.tt("dve", Kta[:], kpa[:], e3[:], ALU.mult, [kpr, e3r], [Ktr])
            h.tt("pool", Bha[:], kaa[:], e4[:], ALU.mult, [kar, e4r], [Bhr])
            h.tt("pool", Kha[:], kpa[:], e4[:], ALU.mult, [kpr, e4r], [Khr])
            h.cp("act", vba[:], ma[:, 4:6, :], [mr], [vbr])
            for cg in range(2):
                h.mm(m1[:, 256 + 2 * cg:256 + 2 * cg + 2], rka[:, cg, :], bsel_b, [rkr_, r_c], [m1r])
            bona, bonr = bon(ci)
            h.cp("dve", bona[:], m1[:, 256:260], [m1r], [bonr])
            pb, pbr = pB(0)
            tka, tkr = tokm(ci)
            for cg in range(2):
                for j, (src, sr_) in enumerate(((ARa[:, cg, 0, :], ARr), (Bha[:, cg, :], Bhr), (Kha[:, cg, :], Khr), (vba[:, cg, :], vbr))):
                    h.tr(pb[:, (cg * 4 + j) * 128:(cg * 4 + j + 1) * 128], src, ident_b, [sr_, r_c], [pbr])
            h.cp("act", tka[:].rearrange("p a b c -> p (a b c)"), pb[:, 0:1024], [pbr], [tkr])

            if stage == 5:
                ota, otr = ot(ci)
                h.cp('dve', ota[:], xba[:, 0:2, :].rearrange('p a b -> p (a b)'), [xbr, tkr, bonr], [otr])
                P.dma('pool', om[c0:c0 + CH, :], ota[:], reads=[otr], writes=[r_om])
                continue
            ota, otr = ot(ci)
            Mbank = [pM(0), pM(1), pFa(0), pFa(1)]
            Ibank = [pI(0), pG(0), pS(0), (pB(0)[0][:, :].bitcast(F32), pB(0)[1])]

            def head_gen(hd):
                cg, j = hd // 2, hd % 2
                hs = slice(j * 64, (j + 1) * 64)
                k4 = 4 * ci + hd
                mm_, mmr = Mbank[hd]
                h.mm(mm_[:, 0:256], Bta[hs, cg, :], ARa[hs, cg, :, :].rearrange("p a b -> p (a b)"), [Btr, ARr], [mmr])
                h.mm(mm_[:, 256:512], Kta[hs, cg, :], ARa[hs, cg, :, :].rearrange("p a b -> p (a b)"), [Ktr, ARr], [mmr])
                mi, mir = Ibank[hd]
                h.mm(mi[:, 384:512], ARa[hs, cg, 0, :], Bta[hs, cg, :], [ARr, Btr], [mir])
                yield
                Na, Nr = N_(k4); Aa, Ar = A_(k4)
                h.tt("dve", Na[:], mm_[:, 0:128], maskUs, ALU.mult, [mmr, r_c], [Nr])
                h.tt("dve", Aa[:], mi[:, 384:512], maskL, ALU.mult, [mir, r_c], [Ar])
                Mrb, Mrbr = MrbT(k4); Mak, Makr = MakT(k4); Mrk, Mrkr = MrkT(k4)
                h.tt("dve", Mrb[:], mm_[:, 128:256], maskU, ALU.mult, [mmr, r_c], [Mrbr])
                h.tt("dve", Mak[:], mm_[:, 256:384], maskUs, ALU.mult, [mmr, r_c], [Makr])
                h.tt("dve", Mrk[:], mm_[:, 384:512], maskU, ALU.mult, [mmr, r_c], [Mrkr])
                yield
                Pa, Pr = Pm(k4)
                yield from dpl_inverse(h, mi, mir, Na, Nr, Aa, Ar, N2(k4)[0], N2(k4)[1], A2(k4)[0], A2(k4)[1], Pa, Pr, ident, r_c)
                Pba, Pbr = Pb(k4)
                h.cp("act", Pba[:], Pa[:], [Pr], [Pbr])
                yield
                Vh = tka[:, cg, 3, hs]
                h.mm(mi[:, 0:64], Mak[:], Vh, [Makr, tkr], [mir])
                h.mm(mi[hs, 128:256], tka[:, cg, 0, hs], Pba[:], [tkr, Pbr], [mir])
                yield
                Z1a, Z1r = Z1(k4)
                h.cp("act", Z1a[:], mi[:, 0:64], [mir], [Z1r])
                AdTa, AdTr = AdT(k4)
                h.cp("act", AdTa[hs, :], mi[hs, 128:256], [mir], [AdTr])
                yield
                h.mm(mi[:, 64:128], Pba[:], Z1a[:], [Pbr, Z1r], [mir])
                yield
                X1a, X1r = X1(k4)
                h.cp("act", X1a[:], mi[:, 64:128], [mir], [X1r])
                yield
                sa, sr = mm_, mmr
                h.mm(sa[:, 0:64], AdTa[hs, :], Sb[cg][hs, :], [AdTr, r_Sb[cg][j]], [sr])
                yield
                Uta, Utr = Ut(k4)
                h.tt("dve", Uta[:], sa[:, 0:64], X1a[:], ALU.add, [sr, X1r], [Utr])
                yield
                h.mm(sa[:, 64:128], ARa[hs, cg, 1, :], Sb[cg][hs, :], [ARr, r_Sb[cg][j]], [sr], start=True, stop=False)
                h.mm(sa[:, 64:128], Mrb[:], Uta[:], [Mrbr, Utr], [sr], start=False, stop=False)
                h.mm(sa[:, 64:128], Mrk[:], Vh, [Mrkr, tkr], [sr], start=False, stop=True)
                h.mm(sa[hs, 128:192], tka[:, cg, 1, hs], Uta[:], [tkr, Utr], [sr], start=True, stop=False)
                h.mm(sa[hs, 128:192], tka[:, cg, 2, hs], Vh, [tkr], [sr], start=False, stop=True)
                yield
                h.stt(Sst[cg][hs, :], Sst[cg][hs, :], wca[hs, cg:cg + 1], sa[hs, 128:192], ALU.mult, ALU.add,
                      [r_Sst[cg][j], wcr, sr], [r_Sst[cg][j]])
                h.cp("act", Sb[cg][hs, :], Sst[cg][hs, :], [r_Sst[cg][j]], [r_Sb[cg][j]])
                yield
                ba2, br2 = bst(k4)
                mva, mvr = mv(k4)
                P.op("dve", lambda e: e.bn_stats(out=ba2[:], in_=sa[:, 64:128]), [sr], [br2])
                P.op("dve", lambda e: e.bn_aggr(out=mva[:], in_=ba2[:]), [br2], [mvr])
                yield
                h.rsqrt(mva[:, 1:2], mva[:, 1:2], 64e-5, [mvr], [mvr])
                yield
                yna, ynr = yn(k4)
                h.ts("dve", yna[:], sa[:, 64:128], mva[:, 0:1], ALU.subtract, [sr, mvr], [ynr], s2=mva[:, 1:2], op1=ALU.mult)
                yield
                h.tt("dve", yna[:], yna[:], rowt_s[:, hd * 64:(hd + 1) * 64], ALU.mult, [ynr, r_c], [ynr])
                yield
                h.tt("pool", yna[:], yna[:], rowt_s[:, 256 + hd * 64:256 + (hd + 1) * 64], ALU.add, [ynr, r_c], [ynr])
                yield
                h.stt(yna[:], Vh, bona[:, hd:hd + 1], yna[:], ALU.mult, ALU.add, [tkr, bonr, ynr], [ynr])
                yield
                h.tt("dve", ota[:, hd * 64:(hd + 1) * 64], yna[:], Gta[:, hd * 64:(hd + 1) * 64], ALU.mult, [ynr, Gtr], [otr])
            run_rr([head_gen(0), head_gen(1), head_gen(2), head_gen(3)])
            P.dma("pool", om[c0:c0 + CH, :], ota[:], reads=[otr], writes=[r_om])
        P.finish([r_om], "sp")
        print("A2 ninstr", P.ninstr, P.cnt)
    return nc


def a2_inputs(xT_b, hg, w_in, mu, w0, w2, a0, a2, g2, k_k, k_a, r_k, gn_w, gn_b):
    f = np.float32
    M0 = 640
    ch = slice(hg * 256, (hg + 1) * 256)
    colsel = np.concatenate([M0 + np.arange(hg * 256, (hg + 1) * 256), M0 + 1024 + np.arange(hg * 256, (hg + 1) * 256),
                             M0 + 2048 + np.arange(hg * 256, (hg + 1) * 256), M0 + 3072 + np.arange(288)])
    wF = w_in[:, colsel]
    wFb = np.ascontiguousarray(wF.reshape(KC, 128, -1).transpose(1, 0, 2)).astype(f)
    mu_c = mu[colsel - M0]
    pvec = np.zeros((128, 32), f)
    for g in range(8):
        pvec[:, g] = mu_c[g * 128:(g + 1) * 128]
    pvec[:32, 8] = mu_c[1024:1056]
    for cg in range(2):
        sl = slice(hg * 256 + cg * 128, hg * 256 + (cg + 1) * 128)
        pvec[:, 9 + cg] = w0[sl]
        pvec[:, 11 + cg] = a0[sl]
        pvec[:, 13 + cg] = k_k[sl]
        pvec[:, 15 + cg] = k_a[sl]
        pvec[:, 19 + cg] = r_k.reshape(-1)[sl]
    lora = np.zeros((128, 768), f)
    lora[0:64, 0:256] = w2[:, ch]
    lora[64:128, 0:256] = a2[:, ch]
    lora[:, 256:512] = g2[0:128, ch]
    lora[0:32, 512:768] = g2[128:160, ch]
    rowt = np.broadcast_to(np.concatenate([gn_w[ch], gn_b[ch]])[None], (128, 512)).astype(f)
    i = np.arange(128)
    ident = np.eye(128, dtype=f)
    maskL = (i[:, None] > i[None, :]).astype(f)
    maskU = (i[:, None] <= i[None, :]).astype(f)
    maskUs = (i[:, None] < i[None, :]).astype(f)
    bones = ((i[:, None] // 64) == (i[None, :] // 64)).astype(f)
    bsel = np.stack([(i // 64 == 0), (i // 64 == 1)], 1).astype(f)
    cst = np.concatenate([ident, maskL, maskU, maskUs, np.ones((128, 128), f), bones, bsel], 1)
    return dict(xT=xT_b, wF=wFb, lora=lora, pvec=pvec, rowt=rowt, cst=cst)


import contextlib
import math
import numpy as np

ST = 512
SCALE = 192.0 ** -0.5
NIN = 704


def build_a1(S):
    nc = bass.Bass("TRN2", target_bir_lowering=False)
    dt = nc.dram_tensor
    NST = S // ST
    NB = S // 128
    xT = dt("xT", [D, S], F32, kind="ExternalInput").ap()
    pos = dt("pos", [1, S], I32, kind="ExternalInput").ap()
    wF = dt("wF", [128, KC, NIN], F32, kind="ExternalInput").ap()
    wuq = dt("wuq", [128, 4, 512], F32, kind="ExternalInput").ap()
    wkv = dt("wkv", [128, 512], F32, kind="ExternalInput").ap()
    pvec = dt("pvec", [128, 8], F32, kind="ExternalInput").ap()
    cst = dt("cst", [128, 3 * 128], F32, kind="ExternalInput").ap()
    om = dt("om", [S, 256], BF16, kind="ExternalOutput").ap()

    with contextlib.ExitStack() as st:
        P = Prog(nc, st)
        h = H(P)
        wF_b = P.sbuf("wF_b", [128, KC, NIN], BF16)
        r_wF = P.region()
        stg = TB(P, "stg", [128, NIN], F32)
        for kc in range(KC):
            a, r = stg(kc)
            P.dma("sp", a[:], wF[:, kc, :], writes=[r])
            h.cp("dve" if kc % 2 == 0 else "act", wF_b[:, kc, :], a[:], [r], [r_wF])
        wuq_b = P.sbuf("wuq_b", [128, 4, 512], BF16)
        wkv_b = P.sbuf("wkv_b", [128, 512], BF16)
        wuq_fl = wuq.rearrange("p g c -> p (g c)")
        wuq_bfl = wuq_b[:].rearrange("p g c -> p (g c)")
        r_wq = P.region()
        for i, c0_ in enumerate(range(0, 2048, 512)):
            a, r = stg(i)
            P.dma("sp", a[:, 0:512], wuq_fl[:, c0_:c0_ + 512], writes=[r])
            h.cp("dve", wuq_bfl[:, c0_:c0_ + 512], a[:, 0:512], [r], [r_wq])
        a, r = stg(4)
        P.dma("sp", a[:, 0:512], wkv, writes=[r])
        h.cp("dve", wkv_b[:], a[:, 0:512], [r], [r_wq])
        pvec_s = P.sbuf("pvec_s", [128, 8], F32)
        cst_s = P.sbuf("cst_s", [128, 384], F32)
        cb = P.sbuf("cb", [128, 384], BF16)
        r_c = P.region()
        for a, b in ((pvec_s, pvec), (cst_s, cst)):
            P.dma("sp", a[:], b, writes=[r_c])
        P.op("dve", lambda e: e.tensor_copy(out=pvec_s[:, 7:8], in_=pvec_s[:, 7:8]), [r_c, r_wq], [r_c])
        h.cp("dve", cb[:], cst_s[:], [r_c], [r_c])
        ident_b, maskU_b, ones_b = cb[:, 0:128], cb[:, 128:256], cb[:, 256:384]
        pv = lambda i: pvec_s[:, i:i + 1]
        inv_s, sgn_s = pv(5), pv(6)

        Kc = P.sbuf("Kc", [128, S], BF16)
        Kpe = P.sbuf("Kpe", [128, S], BF16)
        Va = P.sbuf("Va", [128, NB, 130], BF16)
        r_Kc, r_Kpe, r_Va = P.regions(NST), P.regions(NST), P.regions(NST)
        rK_all = P.region()
        P.op("pool", lambda e: e.memset(Kpe[64:65, :], 1.0), [], [rK_all])
        P.op("pool", lambda e: e.memset(Va[:, :, 128:129], 1.0), [], [rK_all])
        kmax2 = P.sbuf("kmax2", [128, 1], F32)
        r_kmax2 = P.region()
        P.op("pool", lambda e: e.memset(kmax2[:], 0.0), [], [r_kmax2])

        pF = TB(P, "pF", [128, 512], F32, n=2, psum=True)
        pSs = TB(P, "pSs", [128, 512], F32, n=2, psum=True)
        pO = TB(P, "pO", [128, 512], F32, n=2, psum=True)
        pM = TB(P, "pM", [128, 512], F32, n=1, psum=True)
        pB = TB(P, "pB", [128, 1024], BF16, n=1, psum=True)

        def T_(name, shape, dt_=F32, n=2):
            return TB(P, name, shape, dt_, n)

        xstg = T_("xstg", [128, ST], F32, 2)
        xb = T_("xb", [128, KC, ST], BF16, 1)
        posi = T_("posi", [128, ST], I32, 1)
        posf = T_("posf", [128, ST], F32, 1)
        ang = T_("ang", [128, ST], F32, 1); uu = T_("uu", [128, ST], F32, 1); ki = posi; kf = posf
        cosT = T_("cosT", [128, ST], F32, 1); sinT = T_("sinT", [128, ST], F32, 1)
        cq_f = T_("cq_f", [128, 4, ST], BF16, 1)
        sq = T_("sq", [128, ST], BF16, 2)
        rstd = T_("rstd", [128, ST], F32, 1)
        cqn = T_("cqn", [128, 4, ST], BF16, 1)
        ckv_f = T_("ckv_f", [128, ST], F32, 1)
        kpf = T_("kpf", [64, 2, ST], F32, 1)
        kpr = T_("kpr", [64, ST], F32, 1)
        tmp64 = T_("tmp64", [64, ST], F32, 1)
        kmx = T_("kmx", [128, 1], F32, 1)
        kmaxn = T_("kmaxn", [128, 1], F32, 1)
        qn_b = T_("qn_b", [128, ST], BF16, 2)
        qpf = kpf
        qpr = kpr
        Qabs = T_("Qabs", [128, ST], BF16, 2)
        Qpe = T_("Qpe", [128, ST], BF16, 2)
        qnrm = T_("qnrm", [128, ST], F32, 1)
        PT = T_("PT", [128, ST], BF16, 3)
        rec = T_("rec", [128, 1], F32, 4)
        olat = T_("olat", [128, 128], BF16, 4)
        olT = T_("olT", [128, 128], BF16, 4)
        ot = T_("ot", [128, 4, 256], BF16, 1)
        r_om = P.region()

        def rope_tab(q0):
            pia, pir = posi(0); pfa, pfr = posf(0)
            P.dma("sp", pia[:], pos[:, q0:q0 + ST].partition_broadcast(128), writes=[pir])
            h.cp("dve", pfa[:], pia[:], [pir], [pfr])
            aa, ar = ang(0); ua, ur = uu(0); kia, kir = ki(0); kfa, kfr = kf(0)
            h.ts("dve", aa[:], pfa[:], inv_s, ALU.mult, [pfr, r_c], [ar])
            for (dst, off, bias) in ((sinT(0), 0.0, 0.0), (cosT(0), 0.25, math.pi / 2)):
                da, dr = dst
                h.ts("dve", ua[:], aa[:], 1.0 / TWO_PI, ALU.mult, [ar], [ur], s2=off, op1=ALU.add)
                h.cp("dve", kia[:], ua[:], [ur], [kir])
                h.cp("dve", kfa[:], kia[:], [kir], [kfr])
                h.stt(ua[:], kfa[:], -CW1, aa[:], ALU.mult, ALU.add, [kfr, ar], [ur])
                h.stt(ua[:], kfa[:], -CW2, ua[:], ALU.mult, ALU.add, [kfr, ur], [ur])
                if bias != 0.0:
                    h.ts("dve", ua[:], ua[:], bias, ALU.add, [ur], [ur])
                h.act(da[:], ua[:], AF.Sin, [ur], [dr], scale=1.0 - 2e-6)
            sa_, sr_ = sinT(0)
            h.ts("dve", sa_[:], sa_[:], sgn_s, ALU.mult, [sr_, r_c], [sr_])

        def rope_apply(dst, dr, src2, sr2):
            ca, cr = cosT(0); sa_, sr_ = sinT(0)
            ta, tr_ = tmp64(0)
            h.tt("dve", dst, src2[:, 0, :], ca[0:64, :], ALU.mult, [sr2, cr], [dr])
            h.tt("pool", ta[:], src2[:, 1, :], sa_[0:64, :], ALU.mult, [sr2, sr_], [tr_])
            h.tt("dve", dst, dst, ta[:], ALU.add, [dr, tr_], [dr])

        psi = [0]

        def inproj(c0, m, evac):
            pa, pr = pF(psi[0]); psi[0] += 1
            xba, xbr = xb(0)
            for kc in range(KC):
                h.mm(pa[0:m, :], wF_b[:, kc, c0:c0 + m], xba[:, kc, :], [r_wF, xbr], [pr], start=(kc == 0), stop=(kc == KC - 1))
            evac(pa, pr)

        for Q in range(NST):
            q0 = Q * ST
            xba, xbr = xb(0)
            for kc in range(KC):
                sa_, sr_ = xstg(kc)
                P.dma("sp", sa_[:], xT[kc * 128:(kc + 1) * 128, q0:q0 + ST], writes=[sr_])
                h.cp(("dve", "pool", "act")[kc % 3], xba[:, kc, :], sa_[:], [sr_], [xbr])
            rope_tab(q0)
            cqa, cqr = cq_f(0)
            m0, m0r = pM(0)
            gsz = (128, 128, 128, 64)
            for g in range(4):
                def ev(pa, pr, g=g):
                    m = gsz[g]
                    h.cp("act", cqa[0:m, g, :], pa[0:m, :], [pr], [cqr])
                    sqa, sqr = sq(g)
                    h.act(sqa[0:m, :], pa[0:m, :], AF.Square, [pr], [sqr])
                    h.mm(m0[:, :], ones_b[0:m, :], sqa[0:m, :], [r_c, sqr], [m0r], start=(g == 0), stop=(g == 3))
                inproj(g * 128, gsz[g], ev)
            rsa, rsr = rstd(0)
            h.rsqrt(rsa[:], m0[:, :], 1e-6, [m0r], [rsr], scale=1.0 / 448)
            cna, cnr = cqn(0)
            for g in range(4):
                m = gsz[g]
                h.stt(cna[0:m, g, :], cqa[0:m, g, :], pvec_s[0:m, g:g + 1], rsa[0:m, :], ALU.mult, ALU.mult, [cqr, rsr, r_c], [cnr])
            cka, ckr = ckv_f(0)

            def ev_kv(pa, pr):
                h.cp("act", cka[:], pa[:, :], [pr], [ckr])
                sqa, sqr = sq(0)
                h.act(sqa[:], pa[:, :], AF.Square, [pr], [sqr])
                h.mm(m0[:, :], ones_b, sqa[:], [r_c, sqr], [m0r])
            inproj(448, 128, ev_kv)
            h.rsqrt(rsa[:], m0[:, :], 1e-6, [m0r], [rsr], scale=1.0 / 128)
            h.stt(Kc[:, q0:q0 + ST], cka[:], pv(4), rsa[:], ALU.mult, ALU.mult, [ckr, rsr, r_c, rK_all], [r_Kc[Q]])
            pb, pbr = pB(0)
            for j in range(4):
                h.tr(pb[:, j * 128:(j + 1) * 128], Kc[:, q0 + j * 128:q0 + (j + 1) * 128], ident_b, [r_Kc[Q], r_c], [pbr])
            h.cp("act", Va[:, 4 * Q:4 * Q + 4, 0:128], pb[:, 0:512].rearrange("p (j n) -> p j n", j=4), [pbr, rK_all], [r_Va[Q]])
            kpa, kpr_ = kpf(0)
            inproj(576, 64, lambda pa, pr: h.cp("act", kpa[:, 0, :], pa[0:64, :], [pr], [kpr_]))
            inproj(640, 64, lambda pa, pr: h.cp("act", kpa[:, 1, :], pa[0:64, :], [pr], [kpr_]))
            kra, krr = kpr(0)
            rope_apply(kra[:], krr, kpa, kpr_)
            h.cp("act", Kpe[0:64, q0:q0 + ST], kra[:], [krr, rK_all], [r_Kpe[Q]])
            sqa, sqr = sq(0)
            h.act(sqa[:], Kc[:, q0:q0 + ST], AF.Square, [r_Kc[Q]], [sqr])
            sqb_, sqbr = sq(1)
            h.act(sqb_[0:64, :], Kpe[0:64, q0:q0 + ST], AF.Square, [r_Kpe[Q]], [sqbr])
            h.mm(m0[:, :], ones_b, sqa[:], [r_c, sqr], [m0r], start=True, stop=False)
            h.mm(m0[:, :], ones_b[0:64, :], sqb_[0:64, :], [r_c, sqbr], [m0r], start=False, stop=True)
            kma, kmr = kmx(0)
            P.op("dve", lambda e: e.reduce_max(out=kma[:], in_=m0[:, :], axis=AX.X), [m0r], [kmr])
            h.tt("dve", kmax2[:], kmax2[:], kma[:], ALU.max, [r_kmax2, kmr], [r_kmax2])
            kna, knr = kmaxn(0)
            h.act(kna[:], kmax2[:], AF.Sqrt, [r_kmax2], [knr])
            h.ts("dve", kna[:], kna[:], -1.0, ALU.mult, [knr], [knr])

            ota, otr = ot(Q)
            for hh in range(2):
                wc0 = hh * 256
                qna, qnr = qn_b(hh)
                pa, pr = pF(psi[0]); psi[0] += 1
                for g in range(4):
                    m = gsz[g]
                    h.mm(pa[:, :], wuq_b[0:m, g, wc0:wc0 + 128], cna[0:m, g, :], [r_c, cnr], [pr], start=(g == 0), stop=(g == 3))
                h.cp("act", qna[:], pa[:, :], [pr], [qnr])
                qpa, qpr_ = qpf(0)
                for w in range(2):
                    pa2, pr2 = pF(psi[0]); psi[0] += 1
                    for g in range(4):
                        m = gsz[g]
                        h.mm(pa2[0:64, :], wuq_b[0:m, g, wc0 + 128 + w * 64:wc0 + 192 + w * 64], cna[0:m, g, :], [r_c, cnr], [pr2],
                             start=(g == 0), stop=(g == 3))
                    h.cp("act", qpa[:, w, :], pa2[0:64, :], [pr2], [qpr_])
                qra, qrr = qpr(0)
                rope_apply(qra[:], qrr, qpa, qpr_)
                Qpa, Qpr = Qpe(hh)
                h.cp("act", Qpa[0:64, :], qra[:], [qrr], [Qpr])
                pa3, pr3 = pF(psi[0]); psi[0] += 1
                h.mm(pa3[:, :], wkv_b[:, hh * 128:(hh + 1) * 128], qna[:], [r_c, qnr], [pr3])
                Qaa, Qar = Qabs(hh)
                h.cp("act", Qaa[:], pa3[:, :], [pr3], [Qar])
                sqa, sqr = sq(0)
                h.act(sqa[:], Qaa[:], AF.Square, [Qar], [sqr])
                sqb_, sqbr = sq(1)
                h.act(sqb_[0:64, :], Qpa[0:64, :], AF.Square, [Qpr], [sqbr])
                h.mm(m0[:, :], ones_b, sqa[:], [r_c, sqr], [m0r], start=True, stop=False)
                h.mm(m0[:, :], ones_b[0:64, :], sqb_[0:64, :], [r_c, sqbr], [m0r], start=False, stop=True)
                qma, qmr = qnrm(0)
                h.act(qma[:], m0[:, :], AF.Sqrt, [m0r], [qmr])
                h.ts("dve", Qpa[64:65, :], qma[64:65, :], kna[64:65, 0:1], ALU.mult, [qmr, knr], [Qpr])

                nfull = 4 * Q
                o0, o0r = pO(0)
                o1, o1r = pO(1)
                Oacc = [(o0, o0r, 0), (o0, o0r, 129), (o1, o1r, 0), (o1, o1r, 129)]
                started = [False, False]
                def qk(jb):
                    jj = max(0, jb - nfull)
                    qc0 = jj * 128
                    s_, sr_ = pSs(jb)
                    kb = slice(jb * 128, (jb + 1) * 128)
                    Qi = jb // 4
                    h.mm(s_[:, qc0:ST], Kc[:, kb], Qaa[:, qc0:ST], [r_Kc[Qi], Qar], [sr_], start=True, stop=False)
                    h.mm(s_[:, qc0:ST], Kpe[0:65, kb], Qpa[0:65, qc0:ST], [r_Kpe[Qi], rK_all, Qpr], [sr_], start=False, stop=True)

                def rest(jb):
                    jj = max(0, jb - nfull)
                    qc0 = jj * 128
                    s_, sr_ = pSs(jb)
                    Qi = jb // 4
                    pta, ptr = PT(jb)
                    h.act(pta[:, qc0:ST], s_[:, qc0:ST], AF.Exp, [sr_], [ptr], scale=SCALE)
                    if jb >= nfull:
                        h.tt("dve", pta[:, qc0:qc0 + 128], pta[:, qc0:qc0 + 128], maskU_b, ALU.mult, [ptr, r_c], [ptr])
                    for qs in range(jj, 4):
                        oa, orr, oc = Oacc[qs]
                        bank = qs // 2
                        st_ = not started[bank]
                        started[bank] = True
                        P.op("pe", lambda e: e.matmul(oa[:, oc:oc + 129], lhsT=pta[:, qs * 128:(qs + 1) * 128], rhs=Va[:, jb, 0:129],
                                                      start=st_, stop=(jb == nfull + qs), skip_group_check=True),
                             [ptr, r_Va[Qi], rK_all], [orr])

                nblk = nfull + 4
                qk(0)
                for jb in range(nblk):
                    if jb + 1 < nblk:
                        qk(jb + 1)
                    rest(jb)
                for qs in range(4):
                    oa, orr, oc = Oacc[qs]
                    k4 = 4 * hh + qs
                    ra, rr = rec(k4)
                    P.op("dve", lambda e: e.reciprocal(out=ra[:], in_=oa[:, oc + 128:oc + 129]), [orr], [rr])
                    ola, olr = olat(k4)
                    h.ts("dve", ola[:], oa[:, oc:oc + 128], ra[:, 0:1], ALU.mult, [orr, rr], [olr])
                    h.tr(pb[:, 512 + (qs % 2) * 128:512 + (qs % 2 + 1) * 128], ola[:], ident_b, [olr, r_c], [pbr])
                    olTa, olTr = olT(k4)
                    h.cp("act", olTa[:], pb[:, 512 + (qs % 2) * 128:512 + (qs % 2 + 1) * 128], [pbr], [olTr])
                    h.mm(m0[:, (qs % 2) * 128:(qs % 2 + 1) * 128], olTa[:], wkv_b[:, 256 + hh * 128:256 + (hh + 1) * 128], [olTr, r_c], [m0r])
                    h.cp("act", ota[:, qs, hh * 128:(hh + 1) * 128], m0[:, (qs % 2) * 128:(qs % 2 + 1) * 128], [m0r], [otr])
            P.dma("pool", om[q0:q0 + ST, :].rearrange("(j p) n -> p j n", p=128), ota[:], reads=[otr], writes=[r_om])
        P.finish([r_om], "sp")
        print("A1 ninstr", P.ninstr, P.cnt)
    return nc


def a1_inputs(xT_b, pos_b, hg, w_in, q_norm, w_uq, kv_norm, w_ukv):
    f = np.float32
    kpe = w_in[:, 576:640]
    kpe_sw = np.concatenate([kpe[:, 32:], kpe[:, :32]], 1)
    wF = np.concatenate([w_in[:, 0:576], kpe, kpe_sw], 1)
    wFb = np.ascontiguousarray(wF.reshape(KC, 128, -1).transpose(1, 0, 2)).astype(f)
    wq = np.zeros((512, 512), f)
    for j in range(2):
        hd = 2 * hg + j
        nope = w_uq[:, hd * 192:hd * 192 + 128]
        pe = w_uq[:, hd * 192 + 128:hd * 192 + 192]
        pesw = np.concatenate([pe[:, 32:], pe[:, :32]], 1)
        wq[:448, j * 256:(j + 1) * 256] = np.concatenate([nope, pe, pesw], 1)
    wuq = np.ascontiguousarray(wq.reshape(4, 128, 512).transpose(1, 0, 2))
    wkv = np.concatenate([w_ukv[:, (2 * hg + j) * 256:(2 * hg + j) * 256 + 128].T for j in range(2)] +
                         [w_ukv[:, (2 * hg + j) * 256 + 128:(2 * hg + j) * 256 + 256] for j in range(2)], 1).astype(f)
    i = np.arange(128)
    pvec = np.zeros((128, 8), f)
    qn = np.zeros(512, f); qn[:448] = q_norm
    pvec[:, 0:4] = qn.reshape(4, 128).T
    pvec[:, 4] = kv_norm
    pvec[:, 5] = (10000.0 ** (-np.arange(0, 64, 2) / 64.0))[i % 32]
    pvec[:, 6] = np.where((i % 64) < 32, -1.0, 1.0)
    cst = np.concatenate([np.eye(128, dtype=f), (i[:, None] <= i[None, :]).astype(f), np.ones((128, 128), f)], 1)
    return dict(xT=xT_b, pos=pos_b, wF=wFb, wuq=wuq, wkv=np.ascontiguousarray(wkv), pvec=pvec, cst=cst)


import contextlib
import numpy as np

D = 2048
FF = 5632
KC = D // 128
HC = FF // 128
ALPHA_ = (2 * 2) ** 0.25
LN_EPS_ = 1e-5


def cast_weight(P, src, dst, dst_reg, stage, stage_bf, nrows, ncols_total, qi=[0]):
    nblk = src.shape[0]
    for b in range(nblk):
        i = qi[0] % len(stage)
        qi[0] += 1
        sa, sr = stage[i]
        ba, br = stage_bf[i]
        cols = src.shape[2]
        P.dma("sp", sa[:, :cols], src[b], writes=[sr])
        eng = ("dve", "act", "pool")[b % 3]
        if eng == "act":
            P.op("act", lambda e: e.activation(out=ba[:, :cols], in_=sa[:, :cols], func=AF.Copy), reads=[sr], writes=[br])
        else:
            P.op(eng, lambda e: e.tensor_copy(out=ba[:, :cols], in_=sa[:, :cols]), reads=[sr], writes=[br])
        P.dma("pool", dst[b], ba[:, :cols], reads=[br], writes=[dst_reg])


def build_tail(T, ntile=512):
    nc = bass.Bass("TRN2", target_bir_lowering=False)
    dt = nc.dram_tensor
    omT = dt("omT", [D, 2 + T], BF16, kind="ExternalInput").ap()
    xT = dt("xT", [D, 2 + T], F32, kind="ExternalInput").ap()
    hmask = dt("hmask", [128, 1], F32, kind="ExternalInput").ap()
    w_out = dt("w_out", [KC, 128, KC * 128], F32, kind="ExternalInput").ap()
    w_gate = dt("w_gate", [HC, 128, KC * 128], F32, kind="ExternalInput").ap()
    w_val = dt("w_val", [HC, 128, KC * 128], F32, kind="ExternalInput").ap()
    w_down = dt("w_down", [KC, 128, HC * 128], F32, kind="ExternalInput").ap()
    vecs = dt("vecs", [128, 4 * KC], F32, kind="ExternalInput").ap()
    cvec = dt("cvec", [128, 4 * HC], F32, kind="ExternalInput").ap()
    yT = dt("yT", [D, T], F32, kind="ExternalOutput").ap()
    wo_b = dt("wo_b", [KC, 128, KC * 128], BF16, kind="Internal").ap()
    wg_b = dt("wg_b", [HC, 128, KC * 128], BF16, kind="Internal").ap()
    wv_b = dt("wv_b", [HC, 128, KC * 128], BF16, kind="Internal").ap()
    wd_b = dt("wd_b", [KC, 128, HC * 128], BF16, kind="Internal").ap()

    with contextlib.ExitStack() as st:
        P = Prog(nc, st)
        NT = ntile
        om_b = P.sbuf("om_b", [128, KC, NT], BF16)
        xs = P.sbuf("xs", [128, KC, NT], F32)
        x1b = P.sbuf("x1b", [128, KC, NT], BF16)
        actb = P.sbuf("actb", [128, HC, NT], BF16)
        NWB = 3
        wgb = [P.sbuf(f"wgb{i}", [128, KC * 128], BF16) for i in range(NWB)]
        wvb = [P.sbuf(f"wvb{i}", [128, KC * 128], BF16) for i in range(NWB)]
        wdb = [P.sbuf(f"wdb{i}", [128, HC * 128], BF16) for i in range(2)]
        r_wgb, r_wvb, r_wdb = P.regions(NWB), P.regions(NWB), P.regions(2)
        hbuf = [P.sbuf(f"hbuf{i}", [128, NT + 2], F32) for i in range(2)]
        r_hbuf = P.regions(2)
        cbuf = [P.sbuf(f"cbuf{i}", [128, NT], F32) for i in range(2)]
        r_cbuf = P.regions(2)
        sbuf_ = [P.sbuf(f"sbuf{i}", [128, NT], F32) for i in range(2)]
        r_sbuf = P.regions(2)
        carry = P.sbuf("carry", [128, HC, 2], F32)
        r_carry = P.regions(HC)
        sq = [P.sbuf(f"sq{i}", [128, NT], BF16) for i in range(2)]
        r_sq = P.regions(2)
        rb = [P.sbuf(f"rb{i}", [128, NT], BF16) for i in range(2)]
        r_rb = P.regions(2)
        mean = P.sbuf("mean", [128, NT], F32)
        rstd = P.sbuf("rstd", [128, NT], F32)
        tmp = P.sbuf("tmp", [128, NT], F32)
        r_mean, r_rstd, r_tmp = P.regions(3)
        lt = [P.sbuf(f"lt{i}", [128, NT], F32) for i in range(2)]
        r_lt = P.regions(2)
        ones = P.sbuf("ones", [128, 128], BF16)
        r_ones = P.region()
        vec_s = P.sbuf("vec_s", [128, 4 * KC], F32)
        cvec_s = P.sbuf("cvec_s", [128, 4 * HC], F32)
        hm_s = P.sbuf("hm_s", [128, 1], F32)
        r_vec, r_cvec, r_hm = P.regions(3)
        stage = [(P.sbuf(f"stg{i}", [128, 2048], F32), P.region()) for i in range(2)]
        r_om, r_xs, r_x1b = P.region(), P.regions(KC), P.regions(KC)
        r_act = P.regions(HC)
        ps = [P.psum(f"ps{i}", [128, 512], F32) for i in range(8)]
        r_ps = P.regions(8)

        P.op("pool", lambda e: e.memset(ones[:], 1.0), writes=[r_ones])
        P.dma("sp", vec_s[:], vecs, writes=[r_vec])
        P.dma("sp", cvec_s[:], cvec, writes=[r_cvec])
        P.dma("sp", hm_s[:], hmask, writes=[r_hm])

        r_wo, r_wg, r_wv, r_wd = P.regions(4)

        def cast_w(src, dst, reg):
            nblk, _, cols = src.shape
            step = 2048
            k = 0
            for b in range(nblk):
                for c0 in range(0, cols, step):
                    cw = min(step, cols - c0)
                    i = k % 2
                    k += 1
                    (sa, sr), (ba, br) = stage[i], stage_bf[i]
                    P.dma("sp", sa[:, :cw], src[b, :, c0:c0 + cw], writes=[sr])
                    if k % 2 == 0:
                        P.op("act", lambda e: e.activation(out=ba[:, :cw], in_=sa[:, :cw], func=AF.Copy),
                             reads=[sr], writes=[br])
                    else:
                        P.op("dve", lambda e: e.tensor_copy(out=ba[:, :cw], in_=sa[:, :cw]), reads=[sr], writes=[br])
                    P.dma("pool", dst[b, :, c0:c0 + cw], ba[:, :cw], reads=[br], writes=[reg])

        seen_blk = set()
        stq = [0]

        def fetch(kind, idx, dst, dst_reg):
            src32, scr, reg = {"o": (w_out, wo_b, r_wo), "g": (w_gate, wg_b, r_wg), "v": (w_val, wv_b, r_wv), "d": (w_down, wd_b, r_wd)}[kind]
            if (kind, idx) in seen_blk:
                P.dma("sp", dst[:], scr[idx], reads=[reg], writes=[dst_reg])
                return
            seen_blk.add((kind, idx))
            cols = src32.shape[2]
            for c0 in range(0, cols, 2048):
                cw = min(2048, cols - c0)
                sa, sr = stage[stq[0] % 2]
                stq[0] += 1
                P.dma("sp", sa[:, :cw], src32[idx, :, c0:c0 + cw], writes=[sr])
                if stq[0] % 2 == 0:
                    P.op("act", lambda e: e.activation(out=dst[:, c0:c0 + cw], in_=sa[:, :cw], func=AF.Copy), reads=[sr], writes=[dst_reg])
                else:
                    P.op("pool", lambda e: e.tensor_copy(out=dst[:, c0:c0 + cw], in_=sa[:, :cw]), reads=[sr], writes=[dst_reg])
            P.dma("pool", scr[idx], dst[:], reads=[dst_reg], writes=[reg])

        psi = [0]

        def next_ps():
            i = psi[0] % 4
            psi[0] += 1
            return ps[i], r_ps[i]

        wq = [0]

        def layer_norm(N, gcol, bcol, out_f32, r_out_f32, out_bf, r_out_bf, src, r_src):
            s1, rs1 = ps[4], r_ps[4]
            s2, rs2 = ps[5], r_ps[5]
            for c in range(KC):
                i = c % 2
                P.op("pool", lambda e: e.tensor_copy(out=rb[i][:, :N], in_=src(c)), reads=[r_src[c]], writes=[r_rb[i]])
                P.op("act", lambda e: e.activation(out=sq[i][:, :N], in_=src(c), func=AF.Square),
                     reads=[r_src[c]], writes=[r_sq[i]])
                P.op("pe", lambda e: e.matmul(s1[:, :N], lhsT=ones[:], rhs=rb[i][:, :N], start=(c == 0), stop=(c == KC - 1)),
                     reads=[r_ones, r_rb[i]], writes=[rs1])
                P.op("pe", lambda e: e.matmul(s2[:, :N], lhsT=ones[:], rhs=sq[i][:, :N], start=(c == 0), stop=(c == KC - 1)),
                     reads=[r_ones, r_sq[i]], writes=[rs2])
            P.op("act", lambda e: e.activation(out=mean[:, :N], in_=s1[:, :N], func=AF.Copy, scale=1.0 / D),
                 reads=[rs1], writes=[r_mean])
            P.op("dve", lambda e: e.tensor_tensor(out=tmp[:, :N], in0=mean[:, :N], in1=mean[:, :N], op=ALU.mult),
                 reads=[r_mean], writes=[r_tmp])
            P.op("dve", lambda e: e.scalar_tensor_tensor(out=tmp[:, :N], in0=s2[:, :N], scalar=1.0 / D, in1=tmp[:, :N],
                                                         op0=ALU.mult, op1=ALU.subtract), reads=[rs2, r_tmp], writes=[r_tmp])
            P.op("dve", lambda e: e.tensor_scalar(out=tmp[:, :N], in0=tmp[:, :N], scalar1=LN_EPS_, scalar2=None, op0=ALU.add),
                 reads=[r_tmp], writes=[r_tmp])
            P.op("act", lambda e: e.activation(out=tmp[:, :N], in_=tmp[:, :N], func=AF.Sqrt), reads=[r_tmp], writes=[r_tmp])
            P.op("dve", lambda e: e.reciprocal(out=rstd[:, :N], in_=tmp[:, :N]), reads=[r_tmp], writes=[r_rstd])
            for c in range(KC):
                i = c % 2
                eng = "dve" if c % 2 == 0 else "pool"
                P.op(eng, lambda e: e.tensor_tensor(out=lt[i][:, :N], in0=src(c), in1=mean[:, :N], op=ALU.subtract),
                     reads=[r_src[c], r_mean], writes=[r_lt[i]])
                P.op(eng, lambda e: e.tensor_tensor(out=lt[i][:, :N], in0=lt[i][:, :N], in1=rstd[:, :N], op=ALU.mult),
                     reads=[r_lt[i], r_rstd], writes=[r_lt[i]])
                P.op(eng, lambda e: e.tensor_scalar(out=out_f32(c), in0=lt[i][:, :N],
                                                    scalar1=vec_s[:, gcol * KC + c:gcol * KC + c + 1],
                                                    scalar2=vec_s[:, bcol * KC + c:bcol * KC + c + 1],
                                                    op0=ALU.mult, op1=ALU.add),
                     reads=[r_lt[i], r_vec], writes=[r_out_f32[c]])
                if out_bf is not None:
                    P.op("act", lambda e: e.activation(out=out_bf(c), in_=out_f32(c), func=AF.Copy),
                         reads=[r_out_f32[c]], writes=[r_out_bf[c]])

        def process(col0, N, halo_only, out_col0):
            P.dma("sp", om_b[:, :, :N], omT[:, col0:col0 + N].rearrange("(c p) n -> p c n", p=128), writes=[r_om])
            for c in range(KC):
                P.dma("sp", xs[:, c, :N], xT[c * 128:(c + 1) * 128, col0:col0 + N], writes=[r_xs[c]])
            for mo in range(KC):
                i = wq[0] % NWB
                wq[0] += 1
                fetch("o", mo, wgb[i], r_wgb[i])
                pa, pr = next_ps()
                for kc in range(KC):
                    P.op("pe", lambda e: e.matmul(pa[:, :N], lhsT=wgb[i][:, kc * 128:(kc + 1) * 128], rhs=om_b[:, kc, :N],
                                                  start=(kc == 0), stop=(kc == KC - 1)),
                         reads=[r_wgb[i], r_om], writes=[pr])
                P.op("dve", lambda e: e.scalar_tensor_tensor(out=xs[:, mo, :N], in0=xs[:, mo, :N], scalar=ALPHA_, in1=pa[:, :N],
                                                             op0=ALU.mult, op1=ALU.add), reads=[pr, r_xs[mo]], writes=[r_xs[mo]])
            layer_norm(N, 0, 1, lambda c: xs[:, c, :N], r_xs, lambda c: x1b[:, c, :N], r_x1b, lambda c: xs[:, c, :N], r_xs)
            for hc in range(HC):
                i = wq[0] % NWB
                wq[0] += 1
                fetch("g", hc, wgb[i], r_wgb[i])
                pg, prg = next_ps()
                for kc in range(KC):
                    P.op("pe", lambda e: e.matmul(pg[:, :N], lhsT=wgb[i][:, kc * 128:(kc + 1) * 128], rhs=x1b[:, kc, :N],
                                                  start=(kc == 0), stop=(kc == KC - 1)),
                         reads=[r_wgb[i], r_x1b[kc]], writes=[prg])
                if halo_only:
                    P.op("dve", lambda e: e.tensor_scalar(out=carry[:, hc, :], in0=pg[:, :2], scalar1=hm_s[:, 0:1], scalar2=None,
                                                          op0=ALU.mult), reads=[prg, r_hm], writes=[r_carry[hc]])
                    continue
                fetch("v", hc, wvb[i], r_wvb[i])
                pv, prv = next_ps()
                for kc in range(KC):
                    P.op("pe", lambda e: e.matmul(pv[:, :N], lhsT=wvb[i][:, kc * 128:(kc + 1) * 128], rhs=x1b[:, kc, :N],
                                                  start=(kc == 0), stop=(kc == KC - 1)),
                         reads=[r_wvb[i], r_x1b[kc]], writes=[prv])
                j = hc % 2
                hb, rh = hbuf[j], r_hbuf[j]
                cb, rc = cbuf[j], r_cbuf[j]
                sb, rs = sbuf_[j], r_sbuf[j]
                P.op("act", lambda e: e.activation(out=hb[:, 2:2 + N], in_=pg[:, :N], func=AF.Copy), reads=[prg], writes=[rh])
                P.op("pool", lambda e: e.tensor_copy(out=hb[:, 0:2], in_=carry[:, hc, :]), reads=[r_carry[hc]], writes=[rh])
                P.op("pool", lambda e: e.tensor_copy(out=carry[:, hc, :], in_=hb[:, N:N + 2]), reads=[rh], writes=[r_carry[hc]])
                cw = lambda k: cvec_s[:, k * HC + hc:k * HC + hc + 1]
                P.op("dve", lambda e: e.tensor_scalar(out=cb[:, :N], in0=hb[:, 2:2 + N], scalar1=cw(2), scalar2=cw(3),
                                                      op0=ALU.mult, op1=ALU.add), reads=[rh, r_cvec], writes=[rc])
                P.op("dve", lambda e: e.scalar_tensor_tensor(out=cb[:, :N], in0=hb[:, 1:1 + N], scalar=cw(1), in1=cb[:, :N],
                                                             op0=ALU.mult, op1=ALU.add), reads=[rh, rc, r_cvec], writes=[rc])
                P.op("dve", lambda e: e.scalar_tensor_tensor(out=cb[:, :N], in0=hb[:, 0:N], scalar=cw(0), in1=cb[:, :N],
                                                             op0=ALU.mult, op1=ALU.add), reads=[rh, rc, r_cvec], writes=[rc])
                P.op("act", lambda e: e.activation(out=sb[:, :N], in_=cb[:, :N], func=AF.Silu), reads=[rc], writes=[rs])
                P.op("dve", lambda e: e.tensor_tensor(out=actb[:, hc, :N], in0=sb[:, :N], in1=pv[:, :N], op=ALU.mult),
                     reads=[rs, prv], writes=[r_act[hc]])
            if halo_only:
                return
            for mo in range(KC):
                i = mo % 2
                fetch("d", mo, wdb[i], r_wdb[i])
                pa, pr = next_ps()
                for hc in range(HC):
                    P.op("pe", lambda e: e.matmul(pa[:, :N], lhsT=wdb[i][:, hc * 128:(hc + 1) * 128], rhs=actb[:, hc, :N],
                                                  start=(hc == 0), stop=(hc == HC - 1)),
                         reads=[r_wdb[i], r_act[hc]], writes=[pr])
                P.op("dve", lambda e: e.scalar_tensor_tensor(out=xs[:, mo, :N], in0=xs[:, mo, :N], scalar=ALPHA_, in1=pa[:, :N],
                                                             op0=ALU.mult, op1=ALU.add), reads=[pr, r_xs[mo]], writes=[r_xs[mo]])
            layer_norm(N, 2, 3, lambda c: xs[:, c, :N], r_xs, None, None, lambda c: xs[:, c, :N], r_xs)
            for c in range(KC):
                P.dma("pool", yT[c * 128:(c + 1) * 128, out_col0:out_col0 + N], xs[:, c, :N], reads=[r_xs[c]], writes=[r_y])

        r_y = P.region()
        process(0, 2, True, 0)
        for t0 in range(0, T, NT):
            n = min(NT, T - t0)
            process(2 + t0, n, False, t0)
        P.finish([r_y], "sp")
        print("tail ninstr", P.ninstr, P.cnt)
    return nc


def blk_w(w, kc_rows=True):
    K, M = w.shape
    a = w.reshape(K // 128, 128, M // 128, 128)
    return np.ascontiguousarray(a.transpose(2, 1, 0, 3)).reshape(M // 128, 128, (K // 128) * 128)


def vec_pc(v):
    return np.ascontiguousarray(v.reshape(-1, 128).T)


import ml_dtypes
from concourse.bass_utils import run_bass_kernel_spmd

_B, _S, _NC = 2, 16384, 8
_TT = _S // 4
_PROGS = {}


def _prog(name, fn):
    if name not in _PROGS:
        _PROGS[name] = fn()
    return _PROGS[name]


def _run(nc, in_maps):
    res = run_bass_kernel_spmd(nc, in_maps, core_ids=list(range(_NC)))
    return res.results


def _tail(omix, xres, w_out, g1, b1, w_gate, w_val, conv_w, conv_b, w_down, g2, b2):
    f = np.float32
    wo, wg, wv, wd = blk_w(w_out), blk_w(w_gate), blk_w(w_val), blk_w(w_down)
    vecs = np.concatenate([vec_pc(v) for v in (g1, b1, g2, b2)], 1).astype(f)
    cvec = np.concatenate([vec_pc(v) for v in (conv_w[0], conv_w[1], conv_w[2], conv_b)], 1).astype(f)
    maps = []
    for c in range(_NC):
        b, q = divmod(c, 4)
        t0 = q * _TT
        omT = np.zeros((D, 2 + _TT), ml_dtypes.bfloat16)
        xT = np.zeros((D, 2 + _TT), f)
        lo = max(t0 - 2, 0)
        omT[:, 2 - (t0 - lo):] = omix[b, lo:t0 + _TT, :].T
        xT[:, 2 - (t0 - lo):] = xres[b][:, lo:t0 + _TT]
        maps.append(dict(omT=omT, xT=xT, hmask=np.full((128, 1), 0.0 if q == 0 else 1.0, f),
                         w_out=wo, w_gate=wg, w_val=wv, w_down=wd, vecs=vecs, cvec=cvec))
    res = _run(_prog("tail", lambda: build_tail(_TT)), maps)
    out = [np.empty((D, _S), f) for _ in range(_B)]
    for c in range(_NC):
        b, q = divmod(c, 4)
        out[b][:, q * _TT:(q + 1) * _TT] = res[c]["yT"]
    return out


def kernel(**inp):
    f = np.float32
    inp = {k: np.asarray(v) for k, v in inp.items()}
    x = inp["x"].astype(f, copy=False)
    pos = inp["positions"].astype(np.int32, copy=False)
    xT = [np.ascontiguousarray(x[b].T) for b in range(_B)]
    posb = [np.ascontiguousarray(pos[b][None, :]) for b in range(_B)]
    g = lambda k: inp[k].astype(f, copy=False)
    maps = [a1_inputs(xT[c // 4], posb[c // 4], c % 4, g("l0_w_in"), g("l0_q_norm"), g("l0_w_uq"), g("l0_kv_norm"), g("l0_w_ukv"))
            for c in range(_NC)]
    r1 = _run(_prog("a1", lambda: build_a1(_S)), maps)
    maps = [a2_inputs(xT[c // 4], c % 4, g("l0_w_in"), g("l0_rwkv_mu"), g("l0_rwkv_w0"), g("l0_rwkv_w2"), g("l0_rwkv_a0"),
                      g("l0_rwkv_a2"), g("l0_rwkv_g2"), g("l0_rwkv_k_k"), g("l0_rwkv_k_a"), g("l0_rwkv_r_k"),
                      g("l0_rwkv_gn_w"), g("l0_rwkv_gn_b")) for c in range(_NC)]
    r2 = _run(_prog("a2", lambda: build_a2(_S)), maps)
    omix = np.empty((_B, _S, D), ml_dtypes.bfloat16)
    for c in range(_NC):
        b, hg = divmod(c, 4)
        omix[b, :, hg * 256:(hg + 1) * 256] = r1[c]["om"]
        omix[b, :, 1024 + hg * 256:1024 + (hg + 1) * 256] = r2[c]["om"]
    x1T = _tail(omix, xT, g("l0_w_out"), g("l0_ln1_g"), g("l0_ln1_b"), g("l0_ffn_w_gate"), g("l0_ffn_w_val"),
                g("l0_ffn_conv_w"), g("l0_ffn_conv_b"), g("l0_ffn_w_down"), g("l0_ln2_g"), g("l0_ln2_b"))
    maps = [c_inputs(x1T[c // 4], posb[c // 4], c % 4, g("l1_w_in"), g("l1_gdn_conv_w"), g("l1_gdn_A_log"), g("l1_gdn_dt_bias"),
                     g("l1_gdn_norm"), g("l1_ret_gn_w"), g("l1_ret_gn_b")) for c in range(_NC)]
    r3 = _run(_prog("c", lambda: build_c(_S)), maps)
    for c in range(_NC):
        b, hg = divmod(c, 4)
        omix[b, :, 2 * hg * 128:(2 * hg + 2) * 128] = r3[c]["om"][:, 0:256]
        omix[b, :, 1024 + 2 * hg * 128:1024 + (2 * hg + 2) * 128] = r3[c]["om"][:, 256:512]
    x2T = _tail(omix, x1T, g("l1_w_out"), g("l1_ln1_g"), g("l1_ln1_b"), g("l1_ffn_w_gate"), g("l1_ffn_w_val"),
                g("l1_ffn_conv_w"), g("l1_ffn_conv_b"), g("l1_ffn_w_down"), g("l1_ln2_g"), g("l1_ln2_b"))
    out = np.empty((_B, _S, D), f)
    for b in range(_B):
        out[b] = x2T[b].T
    return out
```

```python
import contextlib
import numpy as np
import concourse.bass as bass
import concourse.mybir as mybir

F32 = mybir.dt.float32
BF16 = mybir.dt.bfloat16
I32 = mybir.dt.int32
AF = mybir.ActivationFunctionType
ALU = mybir.AluOpType
AX = mybir.AxisListType

EPOCH = 8000
NDSEM = 12


class Region:
    __slots__ = ("w", "r", "name", "excl")

    def __init__(self, name="", excl=False):
        self.w = None
        self.r = {}
        self.name = name
        self.excl = excl


class Prog:
    def __init__(self, nc, stack, self_sync=None):
        self.nc = nc
        self.stack = stack
        import os as _os
        self.self_sync = (not _os.environ.get("NOSELF")) if self_sync is None else self_sync
        self.engs = {"pe": nc.tensor, "dve": nc.vector, "act": nc.scalar,
                     "pool": nc.gpsimd, "sp": nc.sync}
        self.cnt = {e: 0 for e in self.engs}
        self.sems = {}
        self.seen = {e: {} for e in self.engs}
        self.dn = {}
        self.dsem = {}
        self.ninstr = 0

    def _sem(self, key):
        if key not in self.sems:
            nm = "s_" + "_".join(str(k) for k in key)
            self.sems[key] = self.stack.enter_context(self.nc.semaphore(nm))
        return self.sems[key]

    def region(self, name=""):
        return Region(name)

    def regions(self, n, name=""):
        return [Region(f"{name}{i}") for i in range(n)]

    def _wait(self, eng, toks):
        E = self.engs[eng]
        seen = self.seen[eng]
        best = {}
        for (key, val) in toks:
            if best.get(key, 0) < val:
                best[key] = val
        for key, val in best.items():
            if seen.get(key, 0) < val:
                E.wait_ge(self._sem(key), val)
                seen[key] = val
                self.ninstr += 1

    def _deps(self, eng, reads, writes):
        toks = []
        for r in reads:
            if r.w is not None:
                toks.append(r.w)
            if r.excl:
                toks.extend(t for k, t in r.r.items() if k != eng)
        for r in writes:
            if r.w is not None:
                toks.append(r.w)
            toks.extend(r.r.values())
        if eng == "pe" or not self.self_sync:
            toks = [t for t in toks if t[0][0] != eng]
        return toks

    def op(self, eng, fn, reads=(), writes=()):
        self._wait(eng, self._deps(eng, reads, writes))
        ins = fn(self.engs[eng])
        c = self.cnt[eng]
        ep, idx = divmod(c, EPOCH)
        key = (eng, ep)
        ins.then_inc(self._sem(key), 1)
        self.cnt[eng] = c + 1
        self.ninstr += 1
        tok = (key, idx + 1)
        for r in reads:
            r.r[eng] = tok
        for r in writes:
            r.w = tok
            r.r = {}
        return tok

    def dma(self, q, out, in_, reads=(), writes=(), **kw):
        n = self.dn.get(q, 0)
        j = n % NDSEM
        prev = 16 * (n // NDSEM)
        key = ("d" + q, j)
        toks = [t for t in self._deps("dma" + q, reads, writes)]
        if prev > 0:
            toks.append((key, prev))
        self._wait(q, toks)
        ins = self.engs[q].dma_start(out=out, in_=in_, **kw)
        ins.then_inc(self._sem(key), 16)
        self.dn[q] = n + 1
        self.ninstr += 1
        tok = (key, prev + 16)
        rk = "dma" + q + str(j)
        for r in reads:
            r.r[rk] = tok
        for r in writes:
            r.w = tok
            r.r = {}
        return tok

    def coll(self, kind, ins_ap, outs_ap, groups, reads=(), writes=()):
        q = "pool"
        n = self.dn.get(q, 0)
        j = n % NDSEM
        prev = 16 * (n // NDSEM)
        key = ("d" + q, j)
        toks = [t for t in self._deps("dma" + q, reads, writes)]
        if prev > 0:
            toks.append((key, prev))
        self._wait(q, toks)
        ins = self.engs[q].collective_compute(kind, ALU.bypass, replica_groups=groups, ins=[ins_ap], outs=[outs_ap])
        ins.then_inc(self._sem(key), 16)
        self.dn[q] = n + 1
        self.ninstr += 1
        tok = (key, prev + 16)
        rk = "dma" + q + str(j)
        for r in reads:
            r.r[rk] = tok
        for r in writes:
            r.w = tok
            r.r = {}
        return tok

    def finish(self, regions, eng="sp"):
        toks = [r.w for r in regions if r.w is not None]
        self._wait(eng, toks)

    def sbuf(self, name, shape, dt):
        return self.stack.enter_context(self.nc.sbuf_tensor(name, list(shape), dt))

    def psum(self, name, shape, dt):
        return self.stack.enter_context(self.nc.psum_tensor(name, list(shape), dt))


import contextlib
import math
import numpy as np

D = 2048
KC = 16
CH = 128
TWO_PI = 2 * math.pi
CW1 = 6.28125
CW2 = TWO_PI - CW1


class H:
    def __init__(self, P):
        self.P = P

    def mm(self, out, lhsT, rhs, rd, wr, start=True, stop=True):
        self.P.op("pe", lambda e: e.matmul(out, lhsT=lhsT, rhs=rhs, start=start, stop=stop), rd, wr)

    def tr(self, out, in_, ident, rd, wr):
        self.P.op("pe", lambda e: e.transpose(out, in_, ident), rd, wr)

    def tt(self, eng, out, a, b, op, rd, wr):
        self.P.op(eng, lambda e: e.tensor_tensor(out=out, in0=a, in1=b, op=op), rd, wr)

    def ts(self, eng, out, a, s1, op0, rd, wr, s2=None, op1=None):
        if op1 is None:
            self.P.op(eng, lambda e: e.tensor_scalar(out=out, in0=a, scalar1=s1, scalar2=None, op0=op0), rd, wr)
        else:
            self.P.op(eng, lambda e: e.tensor_scalar(out=out, in0=a, scalar1=s1, scalar2=s2, op0=op0, op1=op1), rd, wr)

    def stt(self, out, in0, scalar, in1, op0, op1, rd, wr):
        self.P.op("dve", lambda e: e.scalar_tensor_tensor(out=out, in0=in0, scalar=scalar, in1=in1, op0=op0, op1=op1), rd, wr)

    def act(self, out, in_, func, rd, wr, **kw):
        self.P.op("act", lambda e: e.activation(out=out, in_=in_, func=func, **kw), rd, wr)

    def cp(self, eng, out, in_, rd, wr):
        if eng == "act":
            self.act(out, in_, AF.Copy, rd, wr)
        else:
            self.P.op(eng, lambda e: e.tensor_copy(out=out, in_=in_), rd, wr)

    def rsqrt(self, out, in_, eps, rd, wr, scale=1.0):
        self.ts("dve", out, in_, scale, ALU.mult, rd, wr, s2=eps, op1=ALU.add)
        self.act(out, out, AF.Sqrt, wr, wr)
        self.P.op("dve", lambda e: e.reciprocal(out=out, in_=out), wr, wr)


def run_rr(gens):
    gens = list(gens)
    while gens:
        for g_ in list(gens):
            try:
                next(g_)
            except StopIteration:
                gens.remove(g_)


class TB:
    def __init__(self, P, name, shape, dt, n=2, psum=False):
        mk = P.psum if psum else P.sbuf
        self.a = [mk(f"{name}_{i}", shape, dt) for i in range(n)]
        self.r = [P.region(f"{name}_{i}") for i in range(n)]
        for r in self.r:
            r.excl = psum
        self.n = n

    def __call__(self, i):
        return self.a[i % self.n], self.r[i % self.n]


def rope_tables(P, h, ci, pos_f, r_pos, inv_s, r_inv, tb):
    ang, r_ang = tb["ang"](ci)
    u, r_u = tb["u"](ci)
    ki, r_ki = tb["ki"](ci)
    kf, r_kf = tb["kf"](ci)
    sn, r_sn = tb["sin"](ci)
    cs, r_cs = tb["cos"](ci)
    h.ts("dve", ang[:], pos_f, inv_s[:, 0:1], ALU.mult, [r_pos, r_inv], [r_ang])
    for (dst, r_dst, off, bias) in ((sn, r_sn, 0.0, 0.0), (cs, r_cs, 0.25, math.pi / 2)):
        h.ts("dve", u[:], ang[:], 1.0 / TWO_PI, ALU.mult, [r_ang], [r_u], s2=off, op1=ALU.add)
        h.cp("dve", ki[:], u[:], [r_u], [r_ki])
        h.cp("dve", kf[:], ki[:], [r_ki], [r_kf])
        h.stt(u[:], kf[:], -CW1, ang[:], ALU.mult, ALU.add, [r_kf, r_ang], [r_u])
        h.stt(u[:], kf[:], -CW2, u[:], ALU.mult, ALU.add, [r_kf, r_u], [r_u])
        sc = 1.0 - 2e-6
        if bias == 0.0:
            h.act(dst[:], u[:], AF.Sin, [r_u], [r_dst], scale=sc)
        else:
            h.ts("dve", u[:], u[:], bias, ALU.add, [r_u], [r_u])
            h.act(dst[:], u[:], AF.Sin, [r_u], [r_dst], scale=sc)
    return cs, sn, r_cs, r_sn


def build_c(S, stage=9):
    nc = bass.Bass("TRN2", target_bir_lowering=False)
    dt = nc.dram_tensor
    NCH = S // CH
    xT = dt("xT", [D, S], F32, kind="ExternalInput").ap()
    pos = dt("pos", [1, S], I32, kind="ExternalInput").ap()
    wF = dt("wF", [128, KC, 8 * 128], F32, kind="ExternalInput").ap()
    wT = dt("wT", [128, KC, 772], F32, kind="ExternalInput").ap()
    convw = dt("convw", [128, 16], F32, kind="ExternalInput").ap()
    rowt = dt("rowt", [128, 4 + 128 + 512], F32, kind="ExternalInput").ap()
    cst = dt("cst", [128, 5 * 128], F32, kind="ExternalInput").ap()
    rett = dt("rett", [128, 2 * 128 + 128 + 8], F32, kind="ExternalInput").ap()
    om = dt("om", [S, 512], BF16, kind="ExternalOutput").ap()

    with contextlib.ExitStack() as st:
        P = Prog(nc, st)
        h = H(P)
        wF_b = P.sbuf("wF_b", [128, KC, 8 * 128], BF16)
        wT_b = P.sbuf("wT_b", [128, KC, 772], BF16)
        r_wF, r_wT = P.region(), P.region()
        stg = TB(P, "stg", [128, 1024], F32)
        for kc in range(KC):
            a, r = stg(2 * kc)
            P.dma("sp", a[:, :1024], wF[:, kc, :], writes=[r])
            h.cp("dve", wF_b[:, kc, :], a[:, :1024], [r], [r_wF])
            a, r = stg(2 * kc + 1)
            P.dma("sp", a[:, :772], wT[:, kc, :], writes=[r])
            h.cp("act", wT_b[:, kc, :], a[:, :772], [r], [r_wT])
        convw_s = P.sbuf("convw_s", [128, 16], F32)
        rowt_s = P.sbuf("rowt_s", [128, 4 + 128 + 512], F32)
        cst_s = P.sbuf("cst_s", [128, 5 * 128], F32)
        rett_s = P.sbuf("rett_s", [128, 2 * 128 + 128 + 8], F32)
        r_c = P.region()
        for a, b in ((convw_s, convw), (rowt_s, rowt), (cst_s, cst), (rett_s, rett)):
            P.dma("sp", a[:], b, writes=[r_c])
        ident = cst_s[:, 0:128]
        maskL = cst_s[:, 128:256]
        maskU = cst_s[:, 256:384]
        ones_f = cst_s[:, 384:512]
        zeros_f = cst_s[:, 512:640]
        cb = P.sbuf("cb", [128, 3 * 128], BF16)
        h.cp("dve", cb[:, 0:128], ident, [r_c], [r_c])
        h.cp("dve", cb[:, 128:256], ones_f, [r_c], [r_c])
        h.cp("dve", cb[:, 256:384], maskU, [r_c], [r_c])
        ident_b, ones_b = cb[:, 0:128], cb[:, 128:256]
        DTret = [rett_s[:, 0:128], rett_s[:, 128:256]]
        QdRow = rett_s[:, 256:384]
        khcol = [rett_s[:, 384:385], rett_s[:, 385:386]]
        wcret = [rett_s[:, 386:387], rett_s[:, 387:388]]
        inv_s = rett_s[:, 388:389]
        sgn_s = rett_s[:, 389:390]
        Alog_row, dtb_row = rowt_s[:, 0:2], rowt_s[:, 2:4]
        gnorm_row = rowt_s[:, 4:132]
        retw_row = [rowt_s[:, 132:260], rowt_s[:, 260:388]]
        retb_row = [rowt_s[:, 388:516], rowt_s[:, 516:644]]
        eA = P.sbuf("eA", [128, 2], F32)
        h.act(eA[:], Alog_row, AF.Exp, [r_c], [r_c])
        h.ts("dve", eA[:], eA[:], -1.0, ALU.mult, [r_c], [r_c])

        pF = TB(P, "pF", [128, 512], F32, n=1, psum=True)
        pI = TB(P, "pI", [128, 512], F32, n=1, psum=True)
        pT = TB(P, "pT", [128, 512], F32, n=2, psum=True)
        pM = TB(P, "pM", [128, 512], F32, n=2, psum=True)
        pB = TB(P, "pB", [128, 1024], BF16, n=1, psum=True)
        pS = TB(P, "pS", [128, 512], F32, n=1, psum=True)
        pB2 = pS

        Sg = [P.sbuf(f"Sg{i}", [128, 128], F32) for i in range(2)]
        Sgb = [P.sbuf(f"Sgb{i}", [128, 128], BF16) for i in range(2)]
        Sr_ = P.sbuf("Sr", [128, 128], F32)
        Srb_ = P.sbuf("Srb", [128, 128], BF16)
        Sr = [Sr_[0:64, :], Sr_[64:128, :]]
        Srb = [Srb_[0:64, :], Srb_[64:128, :]]
        r_Sg, r_Sgb, r_Sr, r_Srb = P.regions(2), P.regions(2), P.regions(2), P.regions(2)
        for i in range(2):
            P.op("pool", lambda e: e.memset(Sg[i][:], 0.0), [], [r_Sg[i]])
            P.op("pool", lambda e: e.memset(Sgb[i][:], 0.0), [], [r_Sgb[i]])
            P.op("pool", lambda e: e.memset(Sr[i], 0.0), [], [r_Sr[i]])
            P.op("pool", lambda e: e.memset(Srb[i], 0.0), [], [r_Srb[i]])
        cvx = P.sbuf("cvx", [128, 4, 3 + CH], F32)
        r_cvx = P.regions(4)
        P.op("pool", lambda e: e.memset(cvx[:], 0.0), [], r_cvx)

        def T_(name, shape, dt_=F32, n=2):
            return TB(P, name, shape, dt_, n)

        xs_f = T_("xs_f", [128, KC, CH])
        xb = T_("xb", [128, KC, CH], BF16)
        posi = T_("posi", [128, CH], I32)
        posf = T_("posf", [128, CH])
        rt = {k: T_(k, [128, CH], I32 if k == "ki" else F32) for k in ("ang", "u", "ki", "kf", "sin", "cos")}
        cacc = T_("cacc", [128, 4, CH])
        qk_f = T_("qk_f", [128, 2, CH])
        vT_b = T_("vT_b", [128, 2, CH], BF16)
        sq_b = T_("sq_b", [128, 2, CH], BF16)
        rs = T_("rs", [128, 2, CH])
        qkn = T_("qkn", [128, 2, CH], BF16)
        kn_t = T_("kn_t", [128, 128], BF16)
        lg = T_("lg", [128, 4])
        beta = T_("beta", [128, 2])
        nbeta = T_("nbeta", [128, 2])
        gg = T_("gg", [128, 2])
        t2 = {k: T_("t2" + k, [128, 2]) for k in "abcd"}
        gcol = T_("gcol", [128, 2])
        gtot = T_("gtot", [128, 2])
        sc1 = T_("sc1", [128, 2])
        sc2 = T_("sc2", [128, 2])
        wc = T_("wc", [128, 2])
        gB = T_("gB", [128, 128], F32, 4)
        Gp = T_("Gp", [128, 128], F32, 4)
        Gm = T_("Gm", [128, 128], F32, 4)
        ER = T_("ER", [128, 128], F32, 4)
        A_ = T_("A_", [128, 128], F32, 4)
        N_ = T_("N_", [128, 128], F32, 4)
        A2 = T_("A2", [128, 128], F32, 4)
        N2 = T_("N2", [128, 128], F32, 4)
        Pm = T_("Pm", [128, 128], F32, 4)
        Pb = T_("Pb", [128, 128], BF16, 4)
        MrT = T_("MrT", [128, 128], BF16, 4)
        Vp = T_("Vp", [128, 128], BF16, 4)
        X1 = T_("X1", [128, 128], F32, 4)
        Ad = T_("Ad", [128, 128], BF16, 4)
        AdT = T_("AdT", [128, 128], BF16, 4)
        Kh = T_("Kh", [128, 128], BF16, 4)
        QdT = T_("QdT", [128, 128], BF16, 4)
        Ut = T_("Ut", [128, 128], BF16, 4)
        ssq = T_("ssq", [128, 1], F32, 4)
        junk = T_("junk", [128, 128], F32, 4)
        zs = T_("zs", [128, 128], F32, 4)
        ot = T_("ot", [128, 512], BF16, 2)
        KKQ = T_("KKQ", [128, 256], F32, 2)
        Vtok = T_("Vtok", [128, 256], BF16, 2)
        rfs = T_("rfs", [128, 512])
        rq = T_("rq", [128, CH])
        rk = T_("rk", [128, CH])
        rqb = T_("rqb", [128, CH], BF16)
        rkb = T_("rkb", [128, CH], BF16)
        rqd = T_("rqd", [128, CH], BF16)
        rk_t = T_("rk_t", [128, 128], BF16)
        rv_b = T_("rv_b", [128, 256], BF16)
        rMT = T_("rMT", [128, 128], BF16, 4)
        bst = T_("bst", [128, 6], F32, 4)
        mv = T_("mv", [128, 2], F32, 4)
        r_om = P.region()

        for ci in range(NCH):
            c0 = ci * CH
            xa, xr = xs_f(ci)
            xba, xbr = xb(ci)
            P.dma("sp", xa[:], xT[:, c0:c0 + CH].rearrange("(c p) n -> p c n", p=128), writes=[xr])
            h.cp("pool", xba[:, :KC // 2, :], xa[:, :KC // 2, :], [xr], [xbr])
            h.cp("dve", xba[:, KC // 2:, :], xa[:, KC // 2:, :], [xr], [xbr])
            pi_a, pi_r = posi(ci)
            pf_a, pf_r = posf(ci)
            P.dma("sp", pi_a[:], pos[:, c0:c0 + CH].partition_broadcast(128), writes=[pi_r])
            h.cp("dve", pf_a[:], pi_a[:], [pi_r], [pf_r])
            cs, sn, r_cs, r_sn = rope_tables(P, h, ci, pf_a[:], pf_r, inv_s, r_c, rt)
            gF, gFr = pF(0)

            def inproj_F(g0):
                for g in range(g0, g0 + 4):
                    for kc in range(KC):
                        h.mm(gF[:, (g % 4) * 128:(g % 4 + 1) * 128], wF_b[:, kc, g * 128:(g + 1) * 128], xba[:, kc, :],
                             [r_wF, xbr], [gFr], start=(kc == 0), stop=(kc == KC - 1))
            inproj_F(0)
            tA, tAr = pT(0)
            tB, tBr = pT(1)
            for kc in range(KC):
                h.mm(tA[:, :512], xba[:, kc, :], wT_b[:, kc, 0:512], [r_wT, xbr], [tAr], start=(kc == 0), stop=(kc == KC - 1))
            for kc in range(KC):
                h.mm(tB[:, :260], xba[:, kc, :], wT_b[:, kc, 512:772], [r_wT, xbr], [tBr], start=(kc == 0), stop=(kc == KC - 1))

            ca, car = cacc(ci)
            qka, qkr = qk_f(ci)
            vta, vtr = vT_b(ci)
            for g in range(4):
                h.cp("act", cvx[:, g, 3:3 + CH], gF[:, g * 128:(g + 1) * 128], [gFr], [r_cvx[g]])
                w = lambda j: convw_s[:, g * 4 + j:g * 4 + j + 1]
                h.ts("dve", ca[:, g, :], cvx[:, g, 3:3 + CH], w(3), ALU.mult, [r_cvx[g], r_c], [car])
                for j in range(3):
                    h.stt(ca[:, g, :], cvx[:, g, j:j + CH], w(j), ca[:, g, :], ALU.mult, ALU.add, [r_cvx[g], car, r_c], [car])
                h.cp("pool", cvx[:, g, 0:3], cvx[:, g, CH:CH + 3], [r_cvx[g]], [r_cvx[g]])
                if g < 2:
                    h.act(qka[:, g, :], ca[:, g, :], AF.Silu, [car], [qkr])
                else:
                    h.act(vta[:, g - 2, :], ca[:, g, :], AF.Silu, [car], [vtr])
            inproj_F(4)
            rF, rFr = rfs(ci)
            h.cp("act", rF[:], gF[:, :], [gFr], [rFr])
            sqa, sqr = sq_b(ci)
            h.act(sqa[:], qka[:], AF.Square, [qkr], [sqr])
            m0, m0r = pM(0)
            h.mm(m0[:, 0:256], ones_b, sqa[:].rearrange("p a b -> p (a b)"), [r_c, sqr], [m0r])
            rsa, rsr = rs(ci)
            h.rsqrt(rsa[:].rearrange("p a b -> p (a b)"), m0[:, 0:256], 1e-6, [m0r], [rsr])
            qna, qnr = qkn(ci)
            h.tt("dve", qna[:, 0, :], qka[:, 1, :], rsa[:, 1, :], ALU.mult, [qkr, rsr], [qnr])
            h.stt(qna[:, 1, :], qka[:, 0, :], 128.0 ** -0.5, rsa[:, 0, :], ALU.mult, ALU.mult, [qkr, rsr], [qnr])
            m1, m1r = pM(1)
            h.mm(m1[:, 0:256], qna[:, 0, :], qna[:].rearrange("p a b -> p (a b)"), [qnr], [m1r])
            pb, pbr = pB(0)
            h.tr(pb[:, 0:128], qna[:, 0, :], ident_b, [qnr, r_c], [pbr])
            h.tr(pb[:, 128:256], vta[:, 0, :], ident_b, [vtr, r_c], [pbr])
            h.tr(pb[:, 256:384], vta[:, 1, :], ident_b, [vtr, r_c], [pbr])
            kta, ktr = kn_t(ci)
            h.cp("act", kta[:], pb[:, 0:128], [pbr], [ktr])
            lga, lgr = lg(ci)
            h.cp("dve", lga[:], tB[:, 256:260], [tBr], [lgr])
            ba_, br_ = beta(ci)
            nb_, nbr_ = nbeta(ci)
            h.act(ba_[:], lga[:, 0:2], AF.Sigmoid, [lgr], [br_])
            h.ts("dve", nb_[:], ba_[:], -1.0, ALU.mult, [br_], [nbr_])
            xa2, xr2 = t2["a"](ci)
            ab2, abr2 = t2["b"](ci)
            e2, er2 = t2["c"](ci)
            mx2, mxr2 = t2["d"](ci)
            h.tt("dve", xa2[:], lga[:, 2:4], dtb_row, ALU.add, [lgr, r_c], [xr2])
            h.act(ab2[:], xa2[:], AF.Abs, [xr2], [abr2])
            h.act(e2[:], ab2[:], AF.Exp, [abr2], [er2], scale=-1.0)
            h.act(e2[:], e2[:], AF.Ln, [er2], [er2], bias=1.0)
            h.ts("dve", mx2[:], xa2[:], 0.0, ALU.max, [xr2], [mxr2])
            h.tt("dve", mx2[:], mx2[:], e2[:], ALU.add, [mxr2, er2], [mxr2])
            ga, gr = gg(ci)
            h.tt("dve", ga[:], mx2[:], eA[:], ALU.mult, [mxr2, r_c], [gr])
            m0b, m0br = pM(0)
            h.mm(m0b[:, 256:258], maskU, ga[:], [r_c, gr], [m0br])
            h.mm(m0b[:, 258:260], ones_f, ga[:], [r_c, gr], [m0br])
            gca, gcr = gcol(ci)
            gta, gtr = gtot(ci)
            h.cp("dve", gca[:], m0b[:, 256:258], [m0br], [gcr])
            h.cp("dve", gta[:], m0b[:, 258:260], [m0br], [gtr])
            s1a, s1r = sc1(ci)
            s2a, s2r = sc2(ci)
            wca, wcr = wc(ci)
            h.act(s1a[:], gca[:], AF.Exp, [gcr], [s1r])
            h.tt("dve", s1a[:], s1a[:], nb_[:], ALU.mult, [s1r, nbr_], [s1r])
            h.tt("dve", s2a[:], gta[:], gca[:], ALU.subtract, [gtr, gcr], [s2r])
            h.act(s2a[:], s2a[:], AF.Exp, [s2r], [s2r])
            h.act(wca[:], gta[:], AF.Exp, [gtr], [wcr])

            ota, otr = ot(ci)
            kkq, kkqr = KKQ(ci)
            h.cp("act", kkq[:], m1[:, 0:256], [m1r], [kkqr])
            vtk, vtkr = Vtok(ci)
            h.cp("act", vtk[:], pb[:, 128:384], [pbr], [vtkr])

            def gdn_head(hh):
                k2 = 2 * ci + hh
                mg_, mgr = pM(hh)
                mi, mir = (pI(0), pF(0))[hh]
                gBa, gBr = gB(k2)
                h.ts("dve", gBa[:], ones_f, ga[:, hh:hh + 1], ALU.mult, [gr, r_c], [gBr])
                GC = mg_[:, 0:128]
                h.mm(GC, gBa[:], maskU, [gBr, r_c], [mgr])
                yield
                Gpa, Gpr = Gp(k2)
                Gma, Gmr = Gm(k2)
                ERa, ERr = ER(k2)
                h.stt(Gpa[:], GC, gca[:, hh:hh + 1], zeros_f, ALU.subtract, ALU.max, [mgr, gcr, r_c], [Gpr])
                h.stt(Gma[:], GC, gca[:, hh:hh + 1], zeros_f, ALU.subtract, ALU.min, [mgr, gcr, r_c], [Gmr])
                yield
                h.act(ERa[:], GC, AF.Exp, [mgr], [ERr])
                h.act(Gpa[:], Gpa[:], AF.Exp, [Gpr], [Gpr], scale=-1.0)
                h.act(Gma[:], Gma[:], AF.Exp, [Gmr], [Gmr])
                yield
                Aa, Ar = A_(k2)
                h.tt("dve", Aa[:], kkq[:, 0:128], Gpa[:], ALU.mult, [kkqr, Gpr], [Ar])
                h.stt(Aa[:], Aa[:], nb_[:, hh:hh + 1], maskL, ALU.mult, ALU.mult, [Ar, nbr_, r_c], [Ar])
                yield
                Na, Nr = N_(k2)
                h.tr(mi[:, 0:128], Aa[:], ident, [Ar, r_c], [mir])
                Ma, Mr_ = MrT(k2)
                h.tt("dve", Gma[:], Gma[:], maskU, ALU.mult, [Gmr, r_c], [Gmr])
                h.tt("dve", Ma[:], kkq[:, 128:256], Gma[:], ALU.mult, [kkqr, Gmr], [Mr_])
                yield
                h.cp("act", Na[:], mi[:, 0:128], [mir], [Nr])
                yield
                Pa, Pr = Pm(k2)
                h.tt("dve", Pa[:], Na[:], ident, ALU.add, [Nr, r_c], [Pr])
                yield
                cur = (Na, Nr, Aa, Ar)
                nxt = (N2(k2)[0], N2(k2)[1], A2(k2)[0], A2(k2)[1])
                for lv in range(6):
                    curN, curNr, curA, curAr = cur
                    nxtN, nxtNr, nxtA, nxtAr = nxt
                    h.mm(mi[:, 128:256], curN[:], curA[:], [curNr, curAr], [mir])
                    if lv < 5:
                        h.mm(mi[:, 256:384], curA[:], curN[:], [curNr, curAr], [mir])
                    yield
                    h.cp("act", nxtA[:], mi[:, 128:256], [mir], [nxtAr])
                    if lv < 5:
                        h.cp("dve", nxtN[:], mi[:, 256:384], [mir], [nxtNr])
                    yield
                    h.mm(mi[:, 0:128], nxtA[:], Pa[:], [nxtAr, Pr], [mir])
                    yield
                    h.tt("dve", Pa[:], Pa[:], mi[:, 0:128], ALU.add, [Pr, mir], [Pr])
                    yield
                    cur, nxt = nxt, cur
                Pba, Pbr = Pb(k2)
                h.cp("act", Pba[:], Pa[:], [Pr], [Pbr])
                Vpa, Vpr = Vp(k2)
                h.ts("dve", Vpa[:], vtk[:, hh * 128:(hh + 1) * 128], ba_[:, hh:hh + 1], ALU.mult, [vtkr, br_], [Vpr])
                Ada, Adr = Ad(k2)
                h.ts("dve", Ada[:], kta[:], s1a[:, hh:hh + 1], ALU.mult, [ktr, s1r], [Adr])
                yield
                h.mm(mg_[:, 256:384], Pba[:], Vpa[:], [Pbr, Vpr], [mgr])
                h.mm(mg_[:, 384:512], Ada[:], Pba[:], [Adr, Pbr], [mgr])
                Kha, Khr = Kh(k2)
                h.ts("dve", Kha[:], kta[:], s2a[:, hh:hh + 1], ALU.mult, [ktr, s2r], [Khr])
                QdTa, QdTr = QdT(k2)
                h.tt("dve", QdTa[:], qna[:, 1, :], ERa[:], ALU.mult, [qnr, ERr], [QdTr])
                yield
                X1a, X1r = X1(k2)
                h.cp("act", X1a[:], mg_[:, 256:384], [mgr], [X1r])
                AdTa, AdTr = AdT(k2)
                h.cp("act", AdTa[:], mg_[:, 384:512], [mgr], [AdTr])
                yield
                h.mm(mg_[:, 0:128], AdTa[:], Sgb[hh][:], [AdTr, r_Sgb[hh]], [mgr])
                yield
                Uta, Utr = Ut(k2)
                h.tt("dve", Uta[:], mg_[:, 0:128], X1a[:], ALU.add, [mgr, X1r], [Utr])
                yield
                h.mm(mg_[:, 128:256], QdTa[:], Sgb[hh][:], [QdTr, r_Sgb[hh]], [mgr], start=True, stop=False)
                h.mm(mg_[:, 128:256], Ma[:], Uta[:], [Mr_, Utr], [mgr], start=False, stop=True)
                h.mm(mi[:, 384:512], Kha[:], Uta[:], [Khr, Utr], [mir])
                yield
                h.stt(Sg[hh][:], Sg[hh][:], wca[:, hh:hh + 1], mi[:, 384:512], ALU.mult, ALU.add, [r_Sg[hh], wcr, mir], [r_Sg[hh]])
                yield
                h.cp("act", Sgb[hh][:], Sg[hh][:], [r_Sg[hh]], [r_Sgb[hh]])
                ja, jr = junk(k2)
                ssa, ssr = ssq(k2)
                h.act(ja[:], mg_[:, 128:256], AF.Square, [mgr], [jr, ssr], accum_out=ssa[:])
                yield
                h.rsqrt(ssa[:], ssa[:], 1e-6, [ssr], [ssr], scale=1.0 / 128)
                za, zr = zs(k2)
                h.act(za[:], tA[:, hh * 128:(hh + 1) * 128], AF.Silu, [tAr], [zr])
                yield
                h.tt("dve", za[:], za[:], gnorm_row, ALU.mult, [zr, r_c], [zr])
                yield
                h.stt(ota[:, hh * 128:(hh + 1) * 128], mg_[:, 128:256], ssa[:, 0:1], za[:], ALU.mult, ALU.mult, [mgr, ssr, zr], [otr])

            def ret_gen():
                rqa, rqr = rq(ci)
                rka, rkr = rk(ci)
                sns, snsr = rt["ang"](ci)
                h.ts("dve", sns[:], sn[:], sgn_s[:, 0:1], ALU.mult, [r_sn, r_c], [snsr])
                yield
                for (dst, dr, g0) in ((rqa, rqr, 0), (rka, rkr, 2)):
                    h.tt("dve", dst[:], rF[:, g0 * 128:(g0 + 1) * 128], cs[:], ALU.mult, [rFr, r_cs], [dr])
                    ja, jr = junk(2 * ci + g0 // 2 + 2)
                    h.tt("dve", ja[:], rF[:, (g0 + 1) * 128:(g0 + 2) * 128], sns[:], ALU.mult, [rFr, snsr], [jr])
                    h.tt("pool", dst[:], dst[:], ja[:], ALU.add, [dr, jr], [dr])
                    yield
                rqba, rqbr = rqb(ci)
                rkba, rkbr = rkb(ci)
                rqda, rqdr = rqd(ci)
                h.cp("act", rqba[:], rqa[:], [rqr], [rqbr])
                h.cp("act", rkba[:], rka[:], [rkr], [rkbr])
                h.tt("dve", rqda[:], rqa[:], QdRow, ALU.mult, [rqr, r_c], [rqdr])
                yield
                h.tr(pb[:, 384:512], rkba[:], ident_b, [rkbr, r_c], [pbr])
                yield
                rvba, rvbr = rv_b(ci)
                h.cp("act", rvba[:], tA[:, 256:512], [tAr], [rvbr])
                yield
                rkta, rktr = rk_t(ci)
                for hh in range(2):
                    h.ts("dve", rkta[:, hh * 64:(hh + 1) * 64], pb[:, 384 + hh * 64:384 + (hh + 1) * 64], khcol[hh], ALU.mult,
                         [pbr, r_c], [rktr])
                for hh in range(2):
                    k2 = 2 * ci + hh
                    hs = slice(hh * 64, (hh + 1) * 64)
                    sa, sr = pS(0); mi, mir = sa[:, 256:512], sr
                    h.mm(mi[:, 0:128], rkba[hs, :], rqba[hs, :], [rkbr, rqbr], [mir])
                    yield
                    Ma, Mr_ = rMT(k2)
                    h.tt("dve", Ma[:], mi[:, 0:128], DTret[hh], ALU.mult, [mir, r_c], [Mr_])
                    yield
                    h.mm(sa[:, 0:128], rqda[hs, :], Srb[hh], [rqdr, r_Srb[hh]], [sr], start=True, stop=False)
                    h.mm(sa[:, 0:128], Ma[:], rvba[:, hh * 128:(hh + 1) * 128], [Mr_, rvbr], [sr], start=False, stop=True)
                    h.mm(sa[hs, 128:256], rkta[:, hs], rvba[:, hh * 128:(hh + 1) * 128], [rktr, rvbr], [sr])
                    yield
                    h.stt(Sr[hh], Sr[hh], wcret[hh][hs, :], sa[hs, 128:256], ALU.mult, ALU.add, [r_Sr[hh], r_c, sr], [r_Sr[hh]])
                    yield
                    h.cp("act", Srb[hh], Sr[hh], [r_Sr[hh]], [r_Srb[hh]])
                    ba2, br2 = bst(k2)
                    mva, mvr = mv(k2)
                    P.op("dve", lambda e: e.bn_stats(out=ba2[:], in_=sa[:, 0:128]), [sr], [br2])
                    P.op("dve", lambda e: e.bn_aggr(out=mva[:], in_=ba2[:]), [br2], [mvr])
                    yield
                    h.rsqrt(mva[:, 1:2], mva[:, 1:2], 1e-5, [mvr], [mvr])
                    yield
                    ja, jr = junk(k2 + 2)
                    h.ts("dve", ja[:], sa[:, 0:128], mva[:, 0:1], ALU.subtract, [sr, mvr], [jr], s2=mva[:, 1:2], op1=ALU.mult)
                    h.tt("dve", ja[:], ja[:], retw_row[hh], ALU.mult, [jr, r_c], [jr])
                    h.tt("pool", ja[:], ja[:], retb_row[hh], ALU.add, [jr, r_c], [jr])
                    yield
                    za, zr = zs(k2 + 2)
                    h.act(za[:], tB[:, hh * 128:(hh + 1) * 128], AF.Silu, [tBr], [zr])
                    yield
                    h.tt("dve", ota[:, 256 + hh * 128:256 + (hh + 1) * 128], ja[:], za[:], ALU.mult, [jr, zr], [otr])

            run_rr([gdn_head(0), gdn_head(1), ret_gen()])
            P.dma("pool", om[c0:c0 + CH, :], ota[:], reads=[otr], writes=[r_om])
        P.finish([r_om], "sp")
        print("C ninstr", P.ninstr, P.cnt)
    return nc


GDN_QK_HEADS, GDN_V_HEADS, RET_HEADS = 4, 8, 8


def blk_in(w):
    return np.ascontiguousarray(w.reshape(KC, 128, -1).transpose(1, 0, 2))


def c_inputs(xT_b, pos_b, hg, w_in, gdn_conv_w, A_log, dt_bias, gdn_norm, ret_gn_w, ret_gn_b):
    f = np.float32
    qk_w, v_w = 512, 1024
    gq = w_in[:, hg * 128:(hg + 1) * 128]
    gk = w_in[:, qk_w + hg * 128: qk_w + (hg + 1) * 128]
    gv = [w_in[:, 2 * qk_w + (2 * hg + i) * 128: 2 * qk_w + (2 * hg + i + 1) * 128] for i in range(2)]
    zoff = 2 * qk_w + v_w
    gz = [w_in[:, zoff + (2 * hg + i) * 128: zoff + (2 * hg + i + 1) * 128] for i in range(2)]
    boff = zoff + v_w
    gb = w_in[:, boff + 2 * hg: boff + 2 * hg + 2]
    ga = w_in[:, boff + 8 + 2 * hg: boff + 8 + 2 * hg + 2]
    R0 = boff + 16
    rqw = w_in[:, R0 + 2 * hg * 64: R0 + (2 * hg + 2) * 64]
    rkw = w_in[:, R0 + 512 + 2 * hg * 64: R0 + 512 + (2 * hg + 2) * 64]
    rvw = w_in[:, R0 + 1024 + 2 * hg * 128: R0 + 1024 + (2 * hg + 2) * 128]
    rgw = w_in[:, R0 + 2048 + 2 * hg * 128: R0 + 2048 + (2 * hg + 2) * 128]

    def sw(w):
        a = w.reshape(w.shape[0], -1, 2, 32)
        return a[:, :, ::-1, :].reshape(w.shape)

    wF = np.concatenate([gq, gk, gv[0], gv[1], rqw, sw(rqw), rkw, sw(rkw)], 1)
    wT = np.concatenate([gz[0], gz[1], rvw, rgw, gb, ga], 1)
    cw = gdn_conv_w
    cols = [slice(hg * 128, (hg + 1) * 128), slice(qk_w + hg * 128, qk_w + (hg + 1) * 128),
            slice(2 * qk_w + 2 * hg * 128, 2 * qk_w + (2 * hg + 1) * 128),
            slice(2 * qk_w + (2 * hg + 1) * 128, 2 * qk_w + (2 * hg + 2) * 128)]
    convw = np.stack([cw[:, c].T for c in cols], 1).reshape(128, 16)
    rowt = np.concatenate([A_log[2 * hg:2 * hg + 2], dt_bias[2 * hg:2 * hg + 2], gdn_norm,
                           ret_gn_w[2 * hg * 128:(2 * hg + 2) * 128], ret_gn_b[2 * hg * 128:(2 * hg + 2) * 128]])
    rowt = np.broadcast_to(rowt[None], (128, rowt.size)).astype(f)
    i = np.arange(128)
    ident = np.eye(128, dtype=f)
    maskL = (i[:, None] > i[None, :]).astype(f)
    maskU = (i[:, None] <= i[None, :]).astype(f)
    cst = np.concatenate([ident, maskL, maskU, np.ones((128, 128), f), np.zeros((128, 128), f)], 1)
    lgam = [math.log(1.0 - 2.0 ** (-5.0 - (2 * hg + j))) for j in range(2)]
    diff = (i[None, :] - i[:, None]).astype(np.float64)
    DT = [np.where(diff >= 0, np.exp(np.maximum(diff, 0) * lgam[j]), 0.0) * 64 ** -0.5 for j in range(2)]
    QdRow = np.concatenate([np.broadcast_to(np.exp((i + 1.0) * lgam[j])[None] * 64 ** -0.5, (64, 128)) for j in range(2)], 0)
    khcol = np.stack([np.exp((127.0 - i) * lgam[j]) for j in range(2)], 1)
    wcr = np.broadcast_to(np.array([math.exp(128 * lgam[j]) for j in range(2)])[None], (128, 2))
    inv = (10000.0 ** (-np.arange(0, 64, 2) / 64.0))[i % 32][:, None]
    sgn = np.where((i % 64) < 32, -1.0, 1.0)[:, None]
    rett = np.concatenate([DT[0], DT[1], QdRow, khcol, wcr, inv, sgn, np.zeros((128, 2))], 1).astype(f)
    return dict(xT=xT_b, pos=pos_b, wF=blk_in(wF).astype(f), wT=blk_in(wT).astype(f), convw=convw.astype(f),
                rowt=rowt, cst=cst, rett=rett)


import contextlib
import math
import numpy as np

CDEC = math.exp(-0.5)
NCOL = 1056


def run_rr(gens):
    gens = list(gens)
    while gens:
        for g_ in list(gens):
            try:
                next(g_)
            except StopIteration:
                gens.remove(g_)


def dpl_inverse(h, mi, mir, Na, Nr, Aa, Ar, N2a, N2r, A2a, A2r, Pa, Pr, ident, r_c):
    h.tt("dve", Pa[:], Na[:], ident, ALU.add, [Nr, r_c], [Pr])
    yield
    cur = (Na, Nr, Aa, Ar)
    nxt = (N2a, N2r, A2a, A2r)
    for lv in range(6):
        curN, curNr, curA, curAr = cur
        nxtN, nxtNr, nxtA, nxtAr = nxt
        h.mm(mi[:, 128:256], curN[:], curA[:], [curNr, curAr], [mir])
        if lv < 5:
            h.mm(mi[:, 256:384], curA[:], curN[:], [curNr, curAr], [mir])
        yield
        h.cp("act", nxtA[:], mi[:, 128:256], [mir], [nxtAr])
        if lv < 5:
            h.cp("dve", nxtN[:], mi[:, 256:384], [mir], [nxtNr])
        yield
        h.mm(mi[:, 0:128], nxtA[:], Pa[:], [nxtAr, Pr], [mir])
        yield
        h.tt("dve", Pa[:], Pa[:], mi[:, 0:128], ALU.add, [Pr, mir], [Pr])
        yield
        cur, nxt = nxt, cur


def build_a2(S, stage=9):
    nc = bass.Bass("TRN2", target_bir_lowering=False)
    dt = nc.dram_tensor
    NCH = S // CH
    xT = dt("xT", [D, S], F32, kind="ExternalInput").ap()
    wF = dt("wF", [128, KC, NCOL], F32, kind="ExternalInput").ap()
    lora = dt("lora", [128, 768], F32, kind="ExternalInput").ap()
    pvec = dt("pvec", [128, 32], F32, kind="ExternalInput").ap()
    rowt = dt("rowt", [128, 512], F32, kind="ExternalInput").ap()
    cst = dt("cst", [128, 6 * 128 + 2], F32, kind="ExternalInput").ap()
    om = dt("om", [S, 256], BF16, kind="ExternalOutput").ap()

    with contextlib.ExitStack() as st:
        P = Prog(nc, st)
        h = H(P)
        wF_b = P.sbuf("wF_b", [128, KC, NCOL], BF16)
        r_wF = P.region()
        stg = TB(P, "stg", [128, NCOL], F32)
        for kc in range(KC):
            a, r = stg(kc)
            P.dma("sp", a[:], wF[:, kc, :], writes=[r])
            h.cp("dve" if kc % 2 == 0 else "act", wF_b[:, kc, :], a[:], [r], [r_wF])
        lora_s = P.sbuf("lora_s", [128, 768], F32)
        lora_b = P.sbuf("lora_b", [128, 768], BF16)
        pvec_s = P.sbuf("pvec_s", [128, 32], F32)
        rowt_s = P.sbuf("rowt_s", [128, 512], F32)
        cst_s = P.sbuf("cst_s", [128, 6 * 128 + 2], F32)
        r_c = P.region()
        for a, b in ((lora_s, lora), (pvec_s, pvec), (rowt_s, rowt), (cst_s, cst)):
            P.dma("sp", a[:], b, writes=[r_c])
        h.cp("dve", lora_b[:], lora_s[:], [r_c], [r_c])
        h.ts("dve", pvec_s[:, 17:19], pvec_s[:, 15:17], -1.0, ALU.mult, [r_c], [r_c], s2=1.0, op1=ALU.add)
        ident, maskL, maskU, maskUs = (cst_s[:, i * 128:(i + 1) * 128] for i in range(4))
        ones_f = cst_s[:, 512:640]
        cb = P.sbuf("cb", [128, 3 * 128 + 2], BF16)
        h.cp("dve", cb[:, 0:128], ident, [r_c], [r_c])
        h.cp("dve", cb[:, 128:256], cst_s[:, 640:768], [r_c], [r_c])
        h.cp("dve", cb[:, 256:384], ones_f, [r_c], [r_c])
        h.cp("dve", cb[:, 384:386], cst_s[:, 768:770], [r_c], [r_c])
        ident_b, bones_b, ones_b, bsel_b = cb[:, 0:128], cb[:, 128:256], cb[:, 256:384], cb[:, 384:386]
        pv = lambda i: pvec_s[:, i:i + 1]
        MU, W0, A0, KK_, KA_, OMKA, RK_ = 0, 9, 11, 13, 15, 17, 19

        pFa = TB(P, "pFa", [128, 512], F32, n=2, psum=True)
        pM = TB(P, "pM", [128, 512], F32, n=2, psum=True)
        pI = TB(P, "pI", [128, 512], F32, n=1, psum=True)
        pB = TB(P, "pB", [128, 1024], BF16, n=1, psum=True)
        pS = TB(P, "pS", [128, 512], F32, n=1, psum=True)
        pG = TB(P, "pG", [128, 512], F32, n=1, psum=True)

        Sst = [P.sbuf(f"Sst{i}", [128, 64], F32) for i in range(2)]
        Sb = [P.sbuf(f"Sb{i}", [128, 64], BF16) for i in range(2)]
        r_Sst = [[P.region(), P.region()] for _ in range(2)]
        r_Sb = [[P.region(), P.region()] for _ in range(2)]
        for i in range(2):
            P.op("pool", lambda e: e.memset(Sst[i][:], 0.0), [], r_Sst[i])
            P.op("pool", lambda e: e.memset(Sb[i][:], 0.0), [], r_Sb[i])
        pbuf = P.sbuf("pbuf", [128, 9, 1 + CH], F32)
        r_pbuf = P.region()
        P.op("pool", lambda e: e.memset(pbuf[:], 0.0), [], [r_pbuf])

        def T_(name, shape, dt_=F32, n=2):
            return TB(P, name, shape, dt_, n)

        xs_f = T_("xs_f", [128, KC, CH])
        xb = T_("xb", [128, KC, CH], BF16)
        dif = T_("dif", [128, 9, CH])
        mix = T_("mix", [128, 9, CH])
        wab = T_("wab", [128, CH], BF16)
        sgb = T_("sgb", [128, 2, CH], BF16)
        Gt = T_("Gt", [128, 256])
        sig = T_("sig", [128, 2, CH])
        cs_ = T_("cs_", [128, 2, CH])
        csm = T_("csm", [128, 2, CH])
        ncc = T_("ncc", [128, 2])
        wcc = T_("wcc", [128, 2])
        E1 = T_("E1", [128, 2, CH]); E2 = T_("E2", [128, 2, CH]); E3 = T_("E3", [128, 2, CH]); E4 = T_("E4", [128, 2, CH])
        aa = T_("aa", [128, 2, CH])
        kkt = T_("kkt", [128, 2, CH])
        sqb = T_("sqb", [128, 2, CH], BF16)
        rs = T_("rs", [128, 2, CH])
        ka = T_("ka", [128, 2, CH])
        kp = T_("kp", [128, 2, CH])
        AR = T_("AR", [128, 2, 2, CH], BF16)
        Bt = T_("Bt", [128, 2, CH], BF16)
        Kt = T_("Kt", [128, 2, CH], BF16)
        Bh_ = T_("Bh_", [128, 2, CH], BF16)
        Kh_ = T_("Kh_", [128, 2, CH], BF16)
        vb = T_("vb", [128, 2, CH], BF16)
        rkr = T_("rkr", [128, 2, CH], BF16)
        bon = T_("bon", [128, 4])
        tokm = T_("tokm", [128, 2, 4, 128], BF16)
        N_ = T_("N_", [128, 128], F32, 4); A_ = T_("A_", [128, 128], F32, 4)
        N2 = T_("N2", [128, 128], F32, 4); A2 = T_("A2", [128, 128], F32, 4)
        Pm = T_("Pm", [128, 128], F32, 4); Pb = T_("Pb", [128, 128], BF16, 4)
        MrbT = T_("MrbT", [128, 128], BF16, 4); MakT = T_("MakT", [128, 128], BF16, 4); MrkT = T_("MrkT", [128, 128], BF16, 4)
        Z1 = T_("Z1", [128, 64], BF16, 4)
        X1 = T_("X1", [128, 64], F32, 4)
        AdT = T_("AdT", [128, CH], BF16, 4)
        Ut = T_("Ut", [128, 64], BF16, 4)
        bst = T_("bst", [128, 6], F32, 4)
        mv = T_("mv", [128, 2], F32, 4)
        yn = T_("yn", [128, 64], F32, 4)
        ot = T_("ot", [128, 256], BF16, 2)
        r_om = P.region()

        for ci in range(NCH):
            c0 = ci * CH
            xa_, xr = xs_f(ci)
            xba, xbr = xb(ci)
            P.dma("sp", xa_[:], xT[:, c0:c0 + CH].rearrange("(c p) n -> p c n", p=128), writes=[xr])
            h.cp("pool", xba[:, :KC // 2, :], xa_[:, :KC // 2, :], [xr], [xbr])
            h.cp("dve", xba[:, KC // 2:, :], xa_[:, KC // 2:, :], [xr], [xbr])
            fa, far = pFa(0)
            fb, fbr = pFa(1)
            gcols = [(i * 128, 128) for i in range(8)] + [(1024, 32)]
            for g, (cc, m) in enumerate(gcols):
                if g < 4:
                    dst, dr, oc = fa, far, g * 128
                elif g < 8:
                    dst, dr, oc = fb, fbr, (g - 4) * 128
                else:
                    dst, dr, oc = pG(0)[0], pG(0)[1], 256
                for kc in range(KC):
                    h.mm(dst[0:m, oc:oc + 128], wF_b[:, kc, cc:cc + m], xba[:, kc, :], [r_wF, xbr], [dr],
                         start=(kc == 0), stop=(kc == KC - 1))
            h.cp("act", pbuf[:, 0:4, 1:1 + CH], fa[:, :].rearrange("p (g n) -> p g n", g=4), [far], [r_pbuf])
            h.cp("act", pbuf[:, 4:8, 1:1 + CH], fb[:, :].rearrange("p (g n) -> p g n", g=4), [fbr], [r_pbuf])
            h.cp("act", pbuf[0:32, 8, 1:1 + CH], pG(0)[0][0:32, 256:384], [pG(0)[1]], [r_pbuf])
            da, dr_ = dif(ci)
            ma, mr = mix(ci)
            h.tt("dve", da[:], pbuf[:, :, 0:CH], pbuf[:, :, 1:1 + CH], ALU.subtract, [r_pbuf], [dr_])
            for g in range(9):
                h.stt(ma[:, g, :], da[:, g, :], pv(MU + g), pbuf[:, g, 1:1 + CH], ALU.mult, ALU.add, [dr_, r_pbuf, r_c], [mr])
            h.cp("pool", pbuf[:, :, 0:1], pbuf[:, :, CH:CH + 1], [r_pbuf], [r_pbuf])
            if stage == 1:
                ota, otr = ot(ci)
                h.cp('dve', ota[:], xba[:, 0:2, :].rearrange('p a b -> p (a b)'), [xbr, mr, r_pbuf], [otr])
                P.dma('pool', om[c0:c0 + CH, :], ota[:], reads=[otr], writes=[r_om])
                continue
            waa, war = wab(ci)
            h.act(waa[0:64, :], ma[0:64, 6, :], AF.Tanh, [mr], [war])
            h.cp("dve", waa[64:128, :], ma[64:128, 6, :], [mr], [war])
            sga, sgr = sgb(ci)
            h.act(sga[:, 0, :], ma[:, 7, :], AF.Sigmoid, [mr], [sgr])
            h.act(sga[0:32, 1, :], ma[0:32, 8, :], AF.Sigmoid, [mr], [sgr])
            if stage == 21:
                ota, otr = ot(ci)
                h.cp('dve', ota[:], xba[:, 0:2, :].rearrange('p a b -> p (a b)'), [xbr, war, sgr], [otr])
                P.dma('pool', om[c0:c0 + CH, :], ota[:], reads=[otr], writes=[r_om])
                continue
            m0, m0r = pM(0)
            m1, m1r = pM(1)
            siga, sigr = sig(ci)
            aaa, aar = aa(ci)
            for cg in range(2):
                h.mm(m0[:, cg * 128:(cg + 1) * 128], lora_b[0:64, cg * 128:(cg + 1) * 128], waa[0:64, :], [r_c, war], [m0r])
                h.mm(m1[:, 256 + cg * 128:256 + (cg + 1) * 128], lora_b[64:128, cg * 128:(cg + 1) * 128], waa[64:128, :], [r_c, war], [m1r])
            if stage == 22:
                ota, otr = ot(ci)
                h.cp('dve', ota[:], xba[:, 0:2, :].rearrange('p a b -> p (a b)'), [xbr, war, sgr, m0r, m1r], [otr])
                P.dma('pool', om[c0:c0 + CH, :], ota[:], reads=[otr], writes=[r_om])
                continue
            for cg in range(2):
                h.act(siga[:, cg, :], m0[:, cg * 128:(cg + 1) * 128], AF.Sigmoid, [m0r, r_c], [sigr], bias=pv(W0 + cg))
                h.act(aaa[:, cg, :], m1[:, 256 + cg * 128:256 + (cg + 1) * 128], AF.Sigmoid, [m1r, r_c], [aar], bias=pv(A0 + cg))
            if stage == 23:
                ota, otr = ot(ci)
                h.cp('dve', ota[:], xba[:, 0:2, :].rearrange('p a b -> p (a b)'), [xbr, sigr, aar], [otr])
                P.dma('pool', om[c0:c0 + CH, :], ota[:], reads=[otr], writes=[r_om])
                continue
            gps, gpr = pG(0)
            h.mm(gps[:, 0:256], sga[:, 0, :], lora_b[:, 256:512], [sgr, r_c], [gpr], start=True, stop=False)
            h.mm(gps[:, 0:256], sga[0:32, 1, :], lora_b[0:32, 512:768], [sgr, r_c], [gpr], start=False, stop=True)
            Gta, Gtr = Gt(ci)
            h.cp("act", Gta[:], gps[:, 0:256], [gpr], [Gtr])
            if stage == 2:
                ota, otr = ot(ci)
                h.cp('dve', ota[:], xba[:, 0:2, :].rearrange('p a b -> p (a b)'), [xbr, mr, sigr, aar, Gtr], [otr])
                P.dma('pool', om[c0:c0 + CH, :], ota[:], reads=[otr], writes=[r_om])
                continue
            csa, csr = cs_(ci)
            cma, cmr = csm(ci)
            for cg in range(2):
                P.op("dve", lambda e: e.tensor_tensor_scan(out=csa[:, cg, :], data0=ones_f, data1=siga[:, cg, :], initial=0.0,
                                                           op0=ALU.mult, op1=ALU.add), [sigr, r_c], [csr])
            h.tt("dve", cma[:], csa[:], siga[:], ALU.subtract, [csr, sigr], [cmr])
            nca, ncr = ncc(ci)
            wca, wcr = wcc(ci)
            h.ts("dve", nca[:], csa[:, :, CH - 1], -CDEC, ALU.mult, [csr], [ncr])
            h.act(wca[:], nca[:], AF.Exp, [ncr], [wcr])
            e1, e1r = E1(ci); e2, e2r = E2(ci); e3, e3r = E3(ci); e4, e4r = E4(ci)
            h.act(e1[:], csa[:], AF.Exp, [csr], [e1r], scale=-CDEC)
            h.act(e2[:], cma[:], AF.Exp, [cmr], [e2r], scale=-CDEC)
            h.act(e3[:], csa[:], AF.Exp, [csr], [e3r], scale=CDEC)
            for cg in range(2):
                h.act(e4[:, cg, :], csa[:, cg, :], AF.Exp, [csr, ncr], [e4r], scale=CDEC, bias=nca[:, cg:cg + 1])
            if stage == 3:
                ota, otr = ot(ci)
                h.cp('dve', ota[:], xba[:, 0:2, :].rearrange('p a b -> p (a b)'), [xbr, e1r, e2r, e3r, e4r, wcr], [otr])
                P.dma('pool', om[c0:c0 + CH, :], ota[:], reads=[otr], writes=[r_om])
                continue
            kka, kkr = kkt(ci)
            sqa, sqr = sqb(ci)
            rsa, rsr = rs(ci)
            kaa, kar = ka(ci)
            kpa, kpr = kp(ci)
            for cg in range(2):
                h.ts("dve", kka[:, cg, :], ma[:, 2 + cg, :], pv(KK_ + cg), ALU.mult, [mr, r_c], [kkr])
            h.act(sqa[:], kka[:], AF.Square, [kkr], [sqr])
            m1, m1r = pM(1)
            h.mm(m1[:, 0:256], bones_b, sqa[:].rearrange("p a b -> p (a b)"), [r_c, sqr], [m1r])
            h.rsqrt(rsa[:].rearrange("p a b -> p (a b)"), m1[:, 0:256], 1e-6, [m1r], [rsr])
            h.tt("dve", kka[:], kka[:], rsa[:], ALU.mult, [kkr, rsr], [kkr])
            h.tt("dve", kaa[:], kka[:], aaa[:], ALU.mult, [kkr, aar], [kar])
            for cg in range(2):
                h.ts("dve", kpa[:, cg, :], aaa[:, cg, :], pv(KA_ + cg), ALU.mult, [aar, r_c], [kpr], s2=pv(OMKA + cg), op1=ALU.add)
            h.tt("dve", kpa[:], kpa[:], ma[:, 2:4, :], ALU.mult, [kpr, mr], [kpr])
            if stage == 4:
                ota, otr = ot(ci)
                h.cp('dve', ota[:], xba[:, 0:2, :].rearrange('p a b -> p (a b)'), [xbr, kkr, kar, kpr], [otr])
                P.dma('pool', om[c0:c0 + CH, :], ota[:], reads=[otr], writes=[r_om])
                continue
            ARa, ARr = AR(ci); Bta, Btr = Bt(ci); Kta, Ktr = Kt(ci); Bha, Bhr = Bh_(ci); Kha, Khr = Kh_(ci)
            vba, vbr = vb(ci); rka, rkr_ = rkr(ci)
            for cg in range(2):
                h.stt(ARa[:, cg, 0, :], kka[:, cg, :], -1.0, e2[:, cg, :], ALU.mult, ALU.mult, [kkr, e2r], [ARr])
                h.stt(rka[:, cg, :], ma[:, cg, :], pv(RK_ + cg), kpa[:, cg, :], ALU.mult, ALU.mult, [mr, kpr, r_c], [rkr_])
            h.tt("dve", ARa[:, :, 1, :], ma[:, 0:2, :], e1[:], ALU.mult, [mr, e1r], [ARr])
            h.tt("dve", Bta[:], kaa[:], e3[:], ALU.mult, [kar, e3r], [Btr])
            h.tt("dve", Kta[:], kpa[:], e3[:], ALU.mult, [kpr, e3r], [Ktr])
            h.tt("pool", Bha[:], kaa[:], e4[:], ALU.mult, [kar, e4r], [Bhr])
            h.tt("pool", Kha[:], kpa[:], e4[:], ALU.mult, [kpr, e4r], [Khr])
            h.cp("act", vba[:], ma[:, 4:6, :], [mr], [vbr])
            for cg in range(2):
                h.mm(m1[:, 256 + 2 * cg:256 + 2 * cg + 2], rka[:, cg, :], bsel_b, [rkr_, r_c], [m1r])
            bona, bonr = bon(ci)
            h.cp("dve", bona[:], m1[:, 256:260], [m1r], [bonr])
            pb, pbr = pB(0)
            tka, tkr = tokm(ci)
            for cg in range(2):
                for j, (src, sr_) in enumerate(((ARa[:, cg, 0, :], ARr), (Bha[:, cg, :], Bhr), (Kha[:, cg, :], Khr), (vba[:, cg, :], vbr))):
                    h.tr(pb[:, (cg * 4 + j) * 128:(cg * 4 + j + 1) * 128], src, ident_b, [sr_, r_c], [pbr])
            h.cp("act", tka[:].rearrange("p a b c -> p (a b c)"), pb[:, 0:1024], [pbr], [tkr])

            if stage == 5:
                ota, otr = ot(ci)
                h.cp('dve', ota[:], xba[:, 0:2, :].rearrange('p a b -> p (a b)'), [xbr, tkr, bonr], [otr])
                P.dma('pool', om[c0:c0 + CH, :], ota[:], reads=[otr], writes=[r_om])
                continue
            ota, otr = ot(ci)
            Mbank = [pM(0), pM(1), pFa(0), pFa(1)]
            Ibank = [pI(0), pG(0), pS(0), (pB(0)[0][:, :].bitcast(F32), pB(0)[1])]

            def head_gen(hd):
                cg, j = hd // 2, hd % 2
                hs = slice(j * 64, (j + 1) * 64)
                k4 = 4 * ci + hd
                mm_, mmr = Mbank[hd]
                h.mm(mm_[:, 0:256], Bta[hs, cg, :], ARa[hs, cg, :, :].rearrange("p a b -> p (a b)"), [Btr, ARr], [mmr])
                h.mm(mm_[:, 256:512], Kta[hs, cg, :], ARa[hs, cg, :, :].rearrange("p a b -> p (a b)"), [Ktr, ARr], [mmr])
                mi, mir = Ibank[hd]
                h.mm(mi[:, 384:512], ARa[hs, cg, 0, :], Bta[hs, cg, :], [ARr, Btr], [mir])
                yield
                Na, Nr = N_(k4); Aa, Ar = A_(k4)
                h.tt("dve", Na[:], mm_[:, 0:128], maskUs, ALU.mult, [mmr, r_c], [Nr])
                h.tt("dve", Aa[:], mi[:, 384:512], maskL, ALU.mult, [mir, r_c], [Ar])
                Mrb, Mrbr = MrbT(k4); Mak, Makr = MakT(k4); Mrk, Mrkr = MrkT(k4)
                h.tt("dve", Mrb[:], mm_[:, 128:256], maskU, ALU.mult, [mmr, r_c], [Mrbr])
                h.tt("dve", Mak[:], mm_[:, 256:384], maskUs, ALU.mult, [mmr, r_c], [Makr])
                h.tt("dve", Mrk[:], mm_[:, 384:512], maskU, ALU.mult, [mmr, r_c], [Mrkr])
                yield
                Pa, Pr = Pm(k4)
                yield from dpl_inverse(h, mi, mir, Na, Nr, Aa, Ar, N2(k4)[0], N2(k4)[1], A2(k4)[0], A2(k4)[1], Pa, Pr, ident, r_c)
                Pba, Pbr = Pb(k4)
                h.cp("act", Pba[:], Pa[:], [Pr], [Pbr])
                yield
                Vh = tka[:, cg, 3, hs]
                h.mm(mi[:, 0:64], Mak[:], Vh, [Makr, tkr], [mir])
                h.mm(mi[hs, 128:256], tka[:, cg, 0, hs], Pba[:], [tkr, Pbr], [mir])
                yield
                Z1a, Z1r = Z1(k4)
                h.cp("act", Z1a[:], mi[:, 0:64], [mir], [Z1r])
                AdTa, AdTr = AdT(k4)
                h.cp("act", AdTa[hs, :], mi[hs, 128:256], [mir], [AdTr])
                yield
                h.mm(mi[:, 64:128], Pba[:], Z1a[:], [Pbr, Z1r], [mir])
                yield
                X1a, X1r = X1(k4)
                h.cp("act", X1a[:], mi[:, 64:128], [mir], [X1r])
                yield
                sa, sr = mm_, mmr
                h.mm(sa[:, 0:64], AdTa[hs, :], Sb[cg][hs, :], [AdTr, r_Sb[cg][j]], [sr])
                yield
                Uta, Utr = Ut(k4)
                h.tt("dve", Uta[:], sa[:, 0:64], X1a[:], ALU.add, [sr, X1r], [Utr])
                yield
                h.mm(sa[:, 64:128], ARa[hs, cg, 1, :], Sb[cg][hs, :], [ARr, r_Sb[cg][j]], [sr], start=True, stop=False)
                h.mm(sa[:, 64:128], Mrb[:], Uta[:], [Mrbr, Utr], [sr], start=False, stop=False)
                h.mm(sa[:, 64:128], Mrk[:], Vh, [Mrkr, tkr], [sr], start=False, stop=True)
                h.mm(sa[hs, 128:192], tka[:, cg, 1, hs], Uta[:], [tkr, Utr], [sr], start=True, stop=False)
                h.mm(sa[hs, 128:192], tka[:, cg, 2, hs], Vh, [tkr], [sr], start=False, stop=True)
                yield
                h.stt(Sst[cg][hs, :], Sst[cg][hs, :], wca[hs, cg:cg + 1], sa[hs, 128:192], ALU.mult, ALU.add,
                      [r_Sst[cg][j], wcr, sr], [r_Sst[cg][j]])
                h.cp("act", Sb[cg][hs, :], Sst[cg][hs, :], [r_Sst[cg][j]], [r_Sb[cg][j]])
                yield
                ba2, br2 = bst(k4)
                mva, mvr = mv(k4)
                P.op("dve", lambda e: e.bn_stats(out=ba2[:], in_=sa[:, 64:128]), [sr], [br2])
                P.op("dve", lambda e: e.bn_aggr(out=mva[:], in_=ba2[:]), [br2], [mvr])
                yield
                h.rsqrt(mva[:, 1:2], mva[:, 1:2], 64e-5, [mvr], [mvr])
                yield
                yna, ynr = yn(k4)
                h.ts("dve", yna[:], sa[:, 64:128], mva[:, 0:1], ALU.subtract, [sr, mvr], [ynr], s2=mva[:, 1:2], op1=ALU.mult)
                yield
                h.tt("dve", yna[:], yna[:], rowt_s[:, hd * 64:(hd + 1) * 64], ALU.mult, [ynr, r_c], [ynr])
                yield
                h.tt("pool", yna[:], yna[:], rowt_s[:, 256 + hd * 64:256 + (hd + 1) * 64], ALU.add, [ynr, r_c], [ynr])
                yield
                h.stt(yna[:], Vh, bona[:, hd:hd + 1], yna[:], ALU.mult, ALU.add, [tkr, bonr, ynr], [ynr])
                yield
                h.tt("dve", ota[:, hd * 64:(hd + 1) * 64], yna[:], Gta[:, hd * 64:(hd + 1) * 64], ALU.mult, [ynr, Gtr], [otr])
            run_rr([head_gen(0), head_gen(1), head_gen(2), head_gen(3)])
            P.dma("pool", om[c0:c0 + CH, :], ota[:], reads=[otr], writes=[r_om])
        P.finish([r_om], "sp")
        print("A2 ninstr", P.ninstr, P.cnt)
    return nc


def a2_inputs(xT_b, hg, w_in, mu, w0, w2, a0, a2, g2, k_k, k_a, r_k, gn_w, gn_b):
    f = np.float32
    M0 = 640
    ch = slice(hg * 256, (hg + 1) * 256)
    colsel = np.concatenate([M0 + np.arange(hg * 256, (hg + 1) * 256), M0 + 1024 + np.arange(hg * 256, (hg + 1) * 256),
                             M0 + 2048 + np.arange(hg * 256, (hg + 1) * 256), M0 + 3072 + np.arange(288)])
    wF = w_in[:, colsel]
    wFb = np.ascontiguousarray(wF.reshape(KC, 128, -1).transpose(1, 0, 2)).astype(f)
    mu_c = mu[colsel - M0]
    pvec = np.zeros((128, 32), f)
    for g in range(8):
        pvec[:, g] = mu_c[g * 128:(g + 1) * 128]
    pvec[:32, 8] = mu_c[1024:1056]
    for cg in range(2):
        sl = slice(hg * 256 + cg * 128, hg * 256 + (cg + 1) * 128)
        pvec[:, 9 + cg] = w0[sl]
        pvec[:, 11 + cg] = a0[sl]
        pvec[:, 13 + cg] = k_k[sl]
        pvec[:, 15 + cg] = k_a[sl]
        pvec[:, 19 + cg] = r_k.reshape(-1)[sl]
    lora = np.zeros((128, 768), f)
    lora[0:64, 0:256] = w2[:, ch]
    lora[64:128, 0:256] = a2[:, ch]
    lora[:, 256:512] = g2[0:128, ch]
    lora[0:32, 512:768] = g2[128:160, ch]
    rowt = np.broadcast_to(np.concatenate([gn_w[ch], gn_b[ch]])[None], (128, 512)).astype(f)
    i = np.arange(128)
    ident = np.eye(128, dtype=f)
    maskL = (i[:, None] > i[None, :]).astype(f)
    maskU = (i[:, None] <= i[None, :]).astype(f)
    maskUs = (i[:, None] < i[None, :]).astype(f)
    bones = ((i[:, None] // 64) == (i[None, :] // 64)).astype(f)
    bsel = np.stack([(i // 64 == 0), (i // 64 == 1)], 1).astype(f)
    cst = np.concatenate([ident, maskL, maskU, maskUs, np.ones((128, 128), f), bones, bsel], 1)
    return dict(xT=xT_b, wF=wFb, lora=lora, pvec=pvec, rowt=rowt, cst=cst)


import contextlib
import math
import numpy as np

ST = 512
SCALE = 192.0 ** -0.5
NIN = 704


def build_a1(S):
    nc = bass.Bass("TRN2", target_bir_lowering=False)
    dt = nc.dram_tensor
    NST = S // ST
    NB = S // 128
    xT = dt("xT", [D, S], F32, kind="ExternalInput").ap()
    pos = dt("pos", [1, S], I32, kind="ExternalInput").ap()
    wF = dt("wF", [128, KC, NIN], F32, kind="ExternalInput").ap()
    wuq = dt("wuq", [128, 4, 512], F32, kind="ExternalInput").ap()
    wkv = dt("wkv", [128, 512], F32, kind="ExternalInput").ap()
    pvec = dt("pvec", [128, 8], F32, kind="ExternalInput").ap()
    cst = dt("cst", [128, 3 * 128], F32, kind="ExternalInput").ap()
    om = dt("om", [S, 256], BF16, kind="ExternalOutput").ap()

    with contextlib.ExitStack() as st:
        P = Prog(nc, st)
        h = H(P)
        wF_b = P.sbuf("wF_b", [128, KC, NIN], BF16)
        r_wF = P.region()
        stg = TB(P, "stg", [128, NIN], F32)
        for kc in range(KC):
            a, r = stg(kc)
            P.dma("sp", a[:], wF[:, kc, :], writes=[r])
            h.cp("dve" if kc % 2 == 0 else "act", wF_b[:, kc, :], a[:], [r], [r_wF])
        wuq_b = P.sbuf("wuq_b", [128, 4, 512], BF16)
        wkv_b = P.sbuf("wkv_b", [128, 512], BF16)
        wuq_fl = wuq.rearrange("p g c -> p (g c)")
        wuq_bfl = wuq_b[:].rearrange("p g c -> p (g c)")
        r_wq = P.region()
        for i, c0_ in enumerate(range(0, 2048, 512)):
            a, r = stg(i)
            P.dma("sp", a[:, 0:512], wuq_fl[:, c0_:c0_ + 512], writes=[r])
            h.cp("dve", wuq_bfl[:, c0_:c0_ + 512], a[:, 0:512], [r], [r_wq])
        a, r = stg(4)
        P.dma("sp", a[:, 0:512], wkv, writes=[r])
        h.cp("dve", wkv_b[:], a[:, 0:512], [r], [r_wq])
        pvec_s = P.sbuf("pvec_s", [128, 8], F32)
        cst_s = P.sbuf("cst_s", [128, 384], F32)
        cb = P.sbuf("cb", [128, 384], BF16)
        r_c = P.region()
        for a, b in ((pvec_s, pvec), (cst_s, cst)):
            P.dma("sp", a[:], b, writes=[r_c])
        P.op("dve", lambda e: e.tensor_copy(out=pvec_s[:, 7:8], in_=pvec_s[:, 7:8]), [r_c, r_wq], [r_c])
        h.cp("dve", cb[:], cst_s[:], [r_c], [r_c])
        ident_b, maskU_b, ones_b = cb[:, 0:128], cb[:, 128:256], cb[:, 256:384]
        pv = lambda i: pvec_s[:, i:i + 1]
        inv_s, sgn_s = pv(5), pv(6)

        Kc = P.sbuf("Kc", [128, S], BF16)
        Kpe = P.sbuf("Kpe", [128, S], BF16)
        Va = P.sbuf("Va", [128, NB, 130], BF16)
        r_Kc, r_Kpe, r_Va = P.regions(NST), P.regions(NST), P.regions(NST)
        rK_all = P.region()
        P.op("pool", lambda e: e.memset(Kpe[64:65, :], 1.0), [], [rK_all])
        P.op("pool", lambda e: e.memset(Va[:, :, 128:129], 1.0), [], [rK_all])
        kmax2 = P.sbuf("kmax2", [128, 1], F32)
        r_kmax2 = P.region()
        P.op("pool", lambda e: e.memset(kmax2[:], 0.0), [], [r_kmax2])

        pF = TB(P, "pF", [128, 512], F32, n=2, psum=True)
        pSs = TB(P, "pSs", [128, 512], F32, n=2, psum=True)
        pO = TB(P, "pO", [128, 512], F32, n=2, psum=True)
        pM = TB(P, "pM", [128, 512], F32, n=1, psum=True)
        pB = TB(P, "pB", [128, 1024], BF16, n=1, psum=True)

        def T_(name, shape, dt_=F32, n=2):
            return TB(P, name, shape, dt_, n)

        xstg = T_("xstg", [128, ST], F32, 2)
        xb = T_("xb", [128, KC, ST], BF16, 1)
        posi = T_("posi", [128, ST], I32, 1)
        posf = T_("posf", [128, ST], F32, 1)
        ang = T_("ang", [128, ST], F32, 1); uu = T_("uu", [128, ST], F32, 1); ki = posi; kf = posf
        cosT = T_("cosT", [128, ST], F32, 1); sinT = T_("sinT", [128, ST], F32, 1)
        cq_f = T_("cq_f", [128, 4, ST], BF16, 1)
        sq = T_("sq", [128, ST], BF16, 2)
        rstd = T_("rstd", [128, ST], F32, 1)
        cqn = T_("cqn", [128, 4, ST], BF16, 1)
        ckv_f = T_("ckv_f", [128, ST], F32, 1)
        kpf = T_("kpf", [64, 2, ST], F32, 1)
        kpr = T_("kpr", [64, ST], F32, 1)
        tmp64 = T_("tmp64", [64, ST], F32, 1)
        kmx = T_("kmx", [128, 1], F32, 1)
        kmaxn = T_("kmaxn", [128, 1], F32, 1)
        qn_b = T_("qn_b", [128, ST], BF16, 2)
        qpf = kpf
        qpr = kpr
        Qabs = T_("Qabs", [128, ST], BF16, 2)
        Qpe = T_("Qpe", [128, ST], BF16, 2)
        qnrm = T_("qnrm", [128, ST], F32, 1)
        PTh = [T_("PT0", [128, ST], BF16, 2), T_("PT1", [128, ST], BF16, 2)]
        rec = T_("rec", [128, 1], F32, 8)
        olat = T_("olat", [128, 128], BF16, 8)
        olT = T_("olT", [128, 128], BF16, 8)
        ot = T_("ot", [128, 4, 256], BF16, 1)
        r_om = P.region()

        def rope_tab(q0):
            pia, pir = posi(0); pfa, pfr = posf(0)
            P.dma("sp", pia[:], pos[:, q0:q0 + ST].partition_broadcast(128), writes=[pir])
            h.cp("dve", pfa[:], pia[:], [pir], [pfr])
            aa, ar = ang(0); ua, ur = uu(0); kia, kir = ki(0); kfa, kfr = kf(0)
            h.ts("dve", aa[:], pfa[:], inv_s, ALU.mult, [pfr, r_c], [ar])
            for (dst, off, bias) in ((sinT(0), 0.0, 0.0), (cosT(0), 0.25, math.pi / 2)):
                da, dr = dst
                h.ts("dve", ua[:], aa[:], 1.0 / TWO_PI, ALU.mult, [ar], [ur], s2=off, op1=ALU.add)
                h.cp("dve", kia[:], ua[:], [ur], [kir])
                h.cp("dve", kfa[:], kia[:], [kir], [kfr])
                h.stt(ua[:], kfa[:], -CW1, aa[:], ALU.mult, ALU.add, [kfr, ar], [ur])
                h.stt(ua[:], kfa[:], -CW2, ua[:], ALU.mult, ALU.add, [kfr, ur], [ur])
                if bias != 0.0:
                    h.ts("dve", ua[:], ua[:], bias, ALU.add, [ur], [ur])
                h.act(da[:], ua[:], AF.Sin, [ur], [dr], scale=1.0 - 2e-6)
            sa_, sr_ = sinT(0)
            h.ts("dve", sa_[:], sa_[:], sgn_s, ALU.mult, [sr_, r_c], [sr_])

        def rope_apply(dst, dr, src2, sr2):
            ca, cr = cosT(0); sa_, sr_ = sinT(0)
            ta, tr_ = tmp64(0)
            h.tt("dve", dst, src2[:, 0, :], ca[0:64, :], ALU.mult, [sr2, cr], [dr])
            h.tt("pool", ta[:], src2[:, 1, :], sa_[0:64, :], ALU.mult, [sr2, sr_], [tr_])
            h.tt("dve", dst, dst, ta[:], ALU.add, [dr, tr_], [dr])

        psi = [0]

        def inproj(c0, m, evac):
            pa, pr = pF(psi[0]); psi[0] += 1
            xba, xbr = xb(0)
            for kc in range(KC):
                h.mm(pa[0:m, :], wF_b[:, kc, c0:c0 + m], xba[:, kc, :], [r_wF, xbr], [pr], start=(kc == 0), stop=(kc == KC - 1))
            evac(pa, pr)

        for Q in range(NST):
            q0 = Q * ST
            xba, xbr = xb(0)
            for kc in range(KC):
                sa_, sr_ = xstg(kc)
                P.dma("sp", sa_[:], xT[kc * 128:(kc + 1) * 128, q0:q0 + ST], writes=[sr_])
                h.cp(("dve", "pool", "act")[kc % 3], xba[:, kc, :], sa_[:], [sr_], [xbr])
            rope_tab(q0)
            cqa, cqr = cq_f(0)
            m0, m0r = pM(0)
            gsz = (128, 128, 128, 64)
            for g in range(4):
                def ev(pa, pr, g=g):
                    m = gsz[g]
                    h.cp("act", cqa[0:m, g, :], pa[0:m, :], [pr], [cqr])
                    sqa, sqr = sq(g)
                    h.act(sqa[0:m, :], pa[0:m, :], AF.Square, [pr], [sqr])
                    h.mm(m0[:, :], ones_b[0:m, :], sqa[0:m, :], [r_c, sqr], [m0r], start=(g == 0), stop=(g == 3))
                inproj(g * 128, gsz[g], ev)
            rsa, rsr = rstd(0)
            h.rsqrt(rsa[:], m0[:, :], 1e-6, [m0r], [rsr], scale=1.0 / 448)
            cna, cnr = cqn(0)
            for g in range(4):
                m = gsz[g]
                h.stt(cna[0:m, g, :], cqa[0:m, g, :], pvec_s[0:m, g:g + 1], rsa[0:m, :], ALU.mult, ALU.mult, [cqr, rsr, r_c], [cnr])
            cka, ckr = ckv_f(0)

            def ev_kv(pa, pr):
                h.cp("act", cka[:], pa[:, :], [pr], [ckr])
                sqa, sqr = sq(0)
                h.act(sqa[:], pa[:, :], AF.Square, [pr], [sqr])
                h.mm(m0[:, :], ones_b, sqa[:], [r_c, sqr], [m0r])
            inproj(448, 128, ev_kv)
            h.rsqrt(rsa[:], m0[:, :], 1e-6, [m0r], [rsr], scale=1.0 / 128)
            h.stt(Kc[:, q0:q0 + ST], cka[:], pv(4), rsa[:], ALU.mult, ALU.mult, [ckr, rsr, r_c, rK_all], [r_Kc[Q]])
            pb, pbr = pB(0)
            for j in range(4):
                h.tr(pb[:, j * 128:(j + 1) * 128], Kc[:, q0 + j * 128:q0 + (j + 1) * 128], ident_b, [r_Kc[Q], r_c], [pbr])
            h.cp("act", Va[:, 4 * Q:4 * Q + 4, 0:128], pb[:, 0:512].rearrange("p (j n) -> p j n", j=4), [pbr, rK_all], [r_Va[Q]])
            kpa, kpr_ = kpf(0)
            inproj(576, 64, lambda pa, pr: h.cp("act", kpa[:, 0, :], pa[0:64, :], [pr], [kpr_]))
            inproj(640, 64, lambda pa, pr: h.cp("act", kpa[:, 1, :], pa[0:64, :], [pr], [kpr_]))
            kra, krr = kpr(0)
            rope_apply(kra[:], krr, kpa, kpr_)
            h.cp("act", Kpe[0:64, q0:q0 + ST], kra[:], [krr, rK_all], [r_Kpe[Q]])
            sqa, sqr = sq(0)
            h.act(sqa[:], Kc[:, q0:q0 + ST], AF.Square, [r_Kc[Q]], [sqr])
            sqb_, sqbr = sq(1)
            h.act(sqb_[0:64, :], Kpe[0:64, q0:q0 + ST], AF.Square, [r_Kpe[Q]], [sqbr])
            h.mm(m0[:, :], ones_b, sqa[:], [r_c, sqr], [m0r], start=True, stop=False)
            h.mm(m0[:, :], ones_b[0:64, :], sqb_[0:64, :], [r_c, sqbr], [m0r], start=False, stop=True)
            kma, kmr = kmx(0)
            P.op("dve", lambda e: e.reduce_max(out=kma[:], in_=m0[:, :], axis=AX.X), [m0r], [kmr])
            h.tt("dve", kmax2[:], kmax2[:], kma[:], ALU.max, [r_kmax2, kmr], [r_kmax2])
            kna, knr = kmaxn(0)
            h.act(kna[:], kmax2[:], AF.Sqrt, [r_kmax2], [knr])
            h.ts("dve", kna[:], kna[:], -1.0, ALU.mult, [knr], [knr])

            ota, otr = ot(Q)
            for hh in range(2):
                wc0 = hh * 256
                qna, qnr = qn_b(hh)
                pa, pr = pF(psi[0]); psi[0] += 1
                for g in range(4):
                    m = gsz[g]
                    h.mm(pa[:, :], wuq_b[0:m, g, wc0:wc0 + 128], cna[0:m, g, :], [r_c, cnr], [pr], start=(g == 0), stop=(g == 3))
                h.cp("act", qna[:], pa[:, :], [pr], [qnr])
                qpa, qpr_ = qpf(0)
                for w in range(2):
                    pa2, pr2 = pF(psi[0]); psi[0] += 1
                    for g in range(4):
                        m = gsz[g]
                        h.mm(pa2[0:64, :], wuq_b[0:m, g, wc0 + 128 + w * 64:wc0 + 192 + w * 64], cna[0:m, g, :], [r_c, cnr], [pr2],
                             start=(g == 0), stop=(g == 3))
                    h.cp("act", qpa[:, w, :], pa2[0:64, :], [pr2], [qpr_])
                qra, qrr = qpr(0)
                rope_apply(qra[:], qrr, qpa, qpr_)
                Qpa, Qpr = Qpe(hh)
                h.cp("act", Qpa[0:64, :], qra[:], [qrr], [Qpr])
                pa3, pr3 = pF(psi[0]); psi[0] += 1
                h.mm(pa3[:, :], wkv_b[:, hh * 128:(hh + 1) * 128], qna[:], [r_c, qnr], [pr3])
                Qaa, Qar = Qabs(hh)
                h.cp("act", Qaa[:], pa3[:, :], [pr3], [Qar])
                sqa, sqr = sq(0)
                h.act(sqa[:], Qaa[:], AF.Square, [Qar], [sqr])
                sqb_, sqbr = sq(1)
                h.act(sqb_[0:64, :], Qpa[0:64, :], AF.Square, [Qpr], [sqbr])
                h.mm(m0[:, :], ones_b, sqa[:], [r_c, sqr], [m0r], start=True, stop=False)
                h.mm(m0[:, :], ones_b[0:64, :], sqb_[0:64, :], [r_c, sqbr], [m0r], start=False, stop=True)
                qma, qmr = qnrm(0)
                h.act(qma[:], m0[:, :], AF.Sqrt, [m0r], [qmr])
                h.ts("dve", Qpa[64:65, :], qma[64:65, :], kna[64:65, 0:1], ALU.mult, [qmr, knr], [Qpr])

            nfull = 4 * Q
            nblk = nfull + 4
            pbf = (pB(0)[0][:, :].bitcast(F32), pB(0)[1])
            Sbanks = [[pSs(0), pSs(1)], [pF(0), pF(1)]]
            Obanks = [[pO(0), pO(1)], [pM(0), pbf]]
            Oaccs = []
            for hh in range(2):
                (o0, o0r), (o1, o1r) = Obanks[hh]
                Oaccs.append([(o0, o0r, 0), (o0, o0r, 129), (o1, o1r, 0), (o1, o1r, 129)])

            def attn_gen(hh):
                Qaa, Qar = Qabs(hh)
                Qpa, Qpr = Qpe(hh)
                Oacc = Oaccs[hh]
                started = [False, False]

                def qk(jb):
                    jj = max(0, jb - nfull)
                    qc0 = jj * 128
                    s_, sr_ = Sbanks[hh][jb % 2]
                    kb = slice(jb * 128, (jb + 1) * 128)
                    Qi = jb // 4
                    h.mm(s_[:, qc0:ST], Kc[:, kb], Qaa[:, qc0:ST], [r_Kc[Qi], Qar], [sr_], start=True, stop=False)
                    h.mm(s_[:, qc0:ST], Kpe[0:65, kb], Qpa[0:65, qc0:ST], [r_Kpe[Qi], rK_all, Qpr], [sr_], start=False, stop=True)

                qk(0)
                yield
                for jb in range(nblk):
                    if jb + 1 < nblk:
                        qk(jb + 1)
                        yield
                    jj = max(0, jb - nfull)
                    qc0 = jj * 128
                    s_, sr_ = Sbanks[hh][jb % 2]
                    Qi = jb // 4
                    pta, ptr = PTh[hh](jb)
                    h.act(pta[:, qc0:ST], s_[:, qc0:ST], AF.Exp, [sr_], [ptr], scale=SCALE)
                    if jb >= nfull:
                        h.tt("dve", pta[:, qc0:qc0 + 128], pta[:, qc0:qc0 + 128], maskU_b, ALU.mult, [ptr, r_c], [ptr])
                    yield
                    for qs in range(jj, 4):
                        oa, orr, oc = Oacc[qs]
                        bank = qs // 2
                        st_ = not started[bank]
                        started[bank] = True
                        P.op("pe", lambda e: e.matmul(oa[:, oc:oc + 129], lhsT=pta[:, qs * 128:(qs + 1) * 128], rhs=Va[:, jb, 0:129],
                                                      start=st_, stop=(jb == nfull + qs), skip_group_check=True),
                             [ptr, r_Va[Qi], rK_all], [orr])
                    yield

            run_rr([attn_gen(0), attn_gen(1)])
            for hh in range(2):
                Oacc = Oaccs[hh]
                for qs in range(4):
                    oa, orr, oc = Oacc[qs]
                    k4 = 4 * hh + qs
                    ra, rr = rec(k4)
                    P.op("dve", lambda e: e.reciprocal(out=ra[:], in_=oa[:, oc + 128:oc + 129]), [orr], [rr])
                    ola, olr = olat(k4)
                    h.ts("dve", ola[:], oa[:, oc:oc + 128], ra[:, 0:1], ALU.mult, [orr, rr], [olr])
            for hh in range(2):
                for qs in range(4):
                    k4 = 4 * hh + qs
                    ola, olr = olat(k4)
                    s_, sr_ = pSs(qs)
                    sb16 = s_[:, :].bitcast(BF16)
                    h.tr(sb16[:, 0:128], ola[:], ident_b, [olr, r_c], [sr_])
                    olTa, olTr = olT(k4)
                    h.cp("act", olTa[:], sb16[:, 0:128], [sr_], [olTr])
                    h.mm(s_[:, 256:384], olTa[:], wkv_b[:, 256 + hh * 128:256 + (hh + 1) * 128], [olTr, r_c], [sr_])
                    h.cp("act", ota[:, qs, hh * 128:(hh + 1) * 128], s_[:, 256:384], [sr_], [otr])
            P.dma("pool", om[q0:q0 + ST, :].rearrange("(j p) n -> p j n", p=128), ota[:], reads=[otr], writes=[r_om])
        P.finish([r_om], "sp")
        print("A1 ninstr", P.ninstr, P.cnt)
    return nc


def a1_inputs(xT_b, pos_b, hg, w_in, q_norm, w_uq, kv_norm, w_ukv):
    f = np.float32
    kpe = w_in[:, 576:640]
    kpe_sw = np.concatenate([kpe[:, 32:], kpe[:, :32]], 1)
    wF = np.concatenate([w_in[:, 0:576], kpe, kpe_sw], 1)
    wFb = np.ascontiguousarray(wF.reshape(KC, 128, -1).transpose(1, 0, 2)).astype(f)
    wq = np.zeros((512, 512), f)
    for j in range(2):
        hd = 2 * hg + j
        nope = w_uq[:, hd * 192:hd * 192 + 128]
        pe = w_uq[:, hd * 192 + 128:hd * 192 + 192]
        pesw = np.concatenate([pe[:, 32:], pe[:, :32]], 1)
        wq[:448, j * 256:(j + 1) * 256] = np.concatenate([nope, pe, pesw], 1)
    wuq = np.ascontiguousarray(wq.reshape(4, 128, 512).transpose(1, 0, 2))
    wkv = np.concatenate([w_ukv[:, (2 * hg + j) * 256:(2 * hg + j) * 256 + 128].T for j in range(2)] +
                         [w_ukv[:, (2 * hg + j) * 256 + 128:(2 * hg + j) * 256 + 256] for j in range(2)], 1).astype(f)
    i = np.arange(128)
    pvec = np.zeros((128, 8), f)
    qn = np.zeros(512, f); qn[:448] = q_norm
    pvec[:, 0:4] = qn.reshape(4, 128).T
    pvec[:, 4] = kv_norm
    pvec[:, 5] = (10000.0 ** (-np.arange(0, 64, 2) / 64.0))[i % 32]
    pvec[:, 6] = np.where((i % 64) < 32, -1.0, 1.0)
    cst = np.concatenate([np.eye(128, dtype=f), (i[:, None] <= i[None, :]).astype(f), np.ones((128, 128), f)], 1)
    return dict(xT=xT_b, pos=pos_b, wF=wFb, wuq=wuq, wkv=np.ascontiguousarray(wkv), pvec=pvec, cst=cst)


import contextlib
import numpy as np

D = 2048
FF = 5632
KC = D // 128
HC = FF // 128
ALPHA_ = (2 * 2) ** 0.25
LN_EPS_ = 1e-5


def cast_weight(P, src, dst, dst_reg, stage, stage_bf, nrows, ncols_total, qi=[0]):
    nblk = src.shape[0]
    for b in range(nblk):
        i = qi[0] % len(stage)
        qi[0] += 1
        sa, sr = stage[i]
        ba, br = stage_bf[i]
        cols = src.shape[2]
        P.dma("sp", sa[:, :cols], src[b], writes=[sr])
        eng = ("dve", "act", "pool")[b % 3]
        if eng == "act":
            P.op("act", lambda e: e.activation(out=ba[:, :cols], in_=sa[:, :cols], func=AF.Copy), reads=[sr], writes=[br])
        else:
            P.op(eng, lambda e: e.tensor_copy(out=ba[:, :cols], in_=sa[:, :cols]), reads=[sr], writes=[br])
        P.dma("pool", dst[b], ba[:, :cols], reads=[br], writes=[dst_reg])


def build_tail(T, ntile=512):
    nc = bass.Bass("TRN2", target_bir_lowering=False)
    dt = nc.dram_tensor
    omT = dt("omT", [D, 2 + T], BF16, kind="ExternalInput").ap()
    xT = dt("xT", [D, 2 + T], F32, kind="ExternalInput").ap()
    hmask = dt("hmask", [128, 1], F32, kind="ExternalInput").ap()
    w_out = dt("w_out", [KC, 128, KC * 128], F32, kind="ExternalInput").ap()
    w_gate = dt("w_gate", [HC, 128, KC * 128], F32, kind="ExternalInput").ap()
    w_val = dt("w_val", [HC, 128, KC * 128], F32, kind="ExternalInput").ap()
    w_down = dt("w_down", [KC, 128, HC * 128], F32, kind="ExternalInput").ap()
    vecs = dt("vecs", [128, 4 * KC], F32, kind="ExternalInput").ap()
    cvec = dt("cvec", [128, 4 * HC], F32, kind="ExternalInput").ap()
    yT = dt("yT", [D, T], F32, kind="ExternalOutput").ap()
    wo_b = dt("wo_b", [KC, 128, KC * 128], BF16, kind="Internal").ap()
    wg_b = dt("wg_b", [HC, 128, KC * 128], BF16, kind="Internal").ap()
    wv_b = dt("wv_b", [HC, 128, KC * 128], BF16, kind="Internal").ap()
    wd_b = dt("wd_b", [KC, 128, HC * 128], BF16, kind="Internal").ap()

    with contextlib.ExitStack() as st:
        P = Prog(nc, st)
        NT = ntile
        om_b = P.sbuf("om_b", [128, KC, NT], BF16)
        xs = P.sbuf("xs", [128, KC, NT], F32)
        x1b = P.sbuf("x1b", [128, KC, NT], BF16)
        actb = P.sbuf("actb", [128, HC, NT], BF16)
        NWB = 3
        wgb = [P.sbuf(f"wgb{i}", [128, KC * 128], BF16) for i in range(NWB)]
        wvb = [P.sbuf(f"wvb{i}", [128, KC * 128], BF16) for i in range(NWB)]
        wdb = [P.sbuf(f"wdb{i}", [128, HC * 128], BF16) for i in range(2)]
        r_wgb, r_wvb, r_wdb = P.regions(NWB), P.regions(NWB), P.regions(2)
        hbuf = [P.sbuf(f"hbuf{i}", [128, NT + 2], F32) for i in range(2)]
        r_hbuf = P.regions(2)
        cbuf = [P.sbuf(f"cbuf{i}", [128, NT], F32) for i in range(2)]
        r_cbuf = P.regions(2)
        sbuf_ = [P.sbuf(f"sbuf{i}", [128, NT], F32) for i in range(2)]
        r_sbuf = P.regions(2)
        carry = P.sbuf("carry", [128, HC, 2], F32)
        r_carry = P.regions(HC)
        sq = [P.sbuf(f"sq{i}", [128, NT], BF16) for i in range(2)]
        r_sq = P.regions(2)
        rb = [P.sbuf(f"rb{i}", [128, NT], BF16) for i in range(2)]
        r_rb = P.regions(2)
        mean = P.sbuf("mean", [128, NT], F32)
        rstd = P.sbuf("rstd", [128, NT], F32)
        tmp = P.sbuf("tmp", [128, NT], F32)
        r_mean, r_rstd, r_tmp = P.regions(3)
        lt = [P.sbuf(f"lt{i}", [128, NT], F32) for i in range(2)]
        r_lt = P.regions(2)
        ones = P.sbuf("ones", [128, 128], BF16)
        r_ones = P.region()
        vec_s = P.sbuf("vec_s", [128, 4 * KC], F32)
        cvec_s = P.sbuf("cvec_s", [128, 4 * HC], F32)
        hm_s = P.sbuf("hm_s", [128, 1], F32)
        r_vec, r_cvec, r_hm = P.regions(3)
        stage = [(P.sbuf(f"stg{i}", [128, 2048], F32), P.region()) for i in range(2)]
        r_om, r_xs, r_x1b = P.region(), P.regions(KC), P.regions(KC)
        r_act = P.regions(HC)
        ps = [P.psum(f"ps{i}", [128, 512], F32) for i in range(8)]
        r_ps = P.regions(8)

        P.op("pool", lambda e: e.memset(ones[:], 1.0), writes=[r_ones])
        P.dma("sp", vec_s[:], vecs, writes=[r_vec])
        P.dma("sp", cvec_s[:], cvec, writes=[r_cvec])
        P.dma("sp", hm_s[:], hmask, writes=[r_hm])

        r_wo, r_wg, r_wv, r_wd = P.regions(4)

        def cast_w(src, dst, reg):
            nblk, _, cols = src.shape
            step = 2048
            k = 0
            for b in range(nblk):
                for c0 in range(0, cols, step):
                    cw = min(step, cols - c0)
                    i = k % 2
                    k += 1
                    (sa, sr), (ba, br) = stage[i], stage_bf[i]
                    P.dma("sp", sa[:, :cw], src[b, :, c0:c0 + cw], writes=[sr])
                    if k % 2 == 0:
                        P.op("act", lambda e: e.activation(out=ba[:, :cw], in_=sa[:, :cw], func=AF.Copy),
                             reads=[sr], writes=[br])
                    else:
                        P.op("dve", lambda e: e.tensor_copy(out=ba[:, :cw], in_=sa[:, :cw]), reads=[sr], writes=[br])
                    P.dma("pool", dst[b, :, c0:c0 + cw], ba[:, :cw], reads=[br], writes=[reg])

        seen_blk = set()
        stq = [0]

        def fetch(kind, idx, dst, dst_reg):
            src32, scr, reg = {"o": (w_out, wo_b, r_wo), "g": (w_gate, wg_b, r_wg), "v": (w_val, wv_b, r_wv), "d": (w_down, wd_b, r_wd)}[kind]
            if (kind, idx) in seen_blk:
                P.dma("sp", dst[:], scr[idx], reads=[reg], writes=[dst_reg])
                return
            seen_blk.add((kind, idx))
            cols = src32.shape[2]
            for c0 in range(0, cols, 2048):
                cw = min(2048, cols - c0)
                sa, sr = stage[stq[0] % 2]
                stq[0] += 1
                P.dma("sp", sa[:, :cw], src32[idx, :, c0:c0 + cw], writes=[sr])
                if stq[0] % 2 == 0:
                    P.op("act", lambda e: e.activation(out=dst[:, c0:c0 + cw], in_=sa[:, :cw], func=AF.Copy), reads=[sr], writes=[dst_reg])
                else:
                    P.op("pool", lambda e: e.tensor_copy(out=dst[:, c0:c0 + cw], in_=sa[:, :cw]), reads=[sr], writes=[dst_reg])
            P.dma("pool", scr[idx], dst[:], reads=[dst_reg], writes=[reg])

        psi = [0]

        def next_ps():
            i = psi[0] % 4
            psi[0] += 1
            return ps[i], r_ps[i]

        wq = [0]

        def layer_norm(N, gcol, bcol, out_f32, r_out_f32, out_bf, r_out_bf, src, r_src):
            s1, rs1 = ps[4], r_ps[4]
            s2, rs2 = ps[5], r_ps[5]
            for c in range(KC):
                i = c % 2
                P.op("pool", lambda e: e.tensor_copy(out=rb[i][:, :N], in_=src(c)), reads=[r_src[c]], writes=[r_rb[i]])
                P.op("act", lambda e: e.activation(out=sq[i][:, :N], in_=src(c), func=AF.Square),
                     reads=[r_src[c]], writes=[r_sq[i]])
                P.op("pe", lambda e: e.matmul(s1[:, :N], lhsT=ones[:], rhs=rb[i][:, :N], start=(c == 0), stop=(c == KC - 1)),
                     reads=[r_ones, r_rb[i]], writes=[rs1])
                P.op("pe", lambda e: e.matmul(s2[:, :N], lhsT=ones[:], rhs=sq[i][:, :N], start=(c == 0), stop=(c == KC - 1)),
                     reads=[r_ones, r_sq[i]], writes=[rs2])
            P.op("act", lambda e: e.activation(out=mean[:, :N], in_=s1[:, :N], func=AF.Copy, scale=1.0 / D),
                 reads=[rs1], writes=[r_mean])
            P.op("dve", lambda e: e.tensor_tensor(out=tmp[:, :N], in0=mean[:, :N], in1=mean[:, :N], op=ALU.mult),
                 reads=[r_mean], writes=[r_tmp])
            P.op("dve", lambda e: e.scalar_tensor_tensor(out=tmp[:, :N], in0=s2[:, :N], scalar=1.0 / D, in1=tmp[:, :N],
                                                         op0=ALU.mult, op1=ALU.subtract), reads=[rs2, r_tmp], writes=[r_tmp])
            P.op("dve", lambda e: e.tensor_scalar(out=tmp[:, :N], in0=tmp[:, :N], scalar1=LN_EPS_, scalar2=None, op0=ALU.add),
                 reads=[r_tmp], writes=[r_tmp])
            P.op("act", lambda e: e.activation(out=tmp[:, :N], in_=tmp[:, :N], func=AF.Sqrt), reads=[r_tmp], writes=[r_tmp])
            P.op("dve", lambda e: e.reciprocal(out=rstd[:, :N], in_=tmp[:, :N]), reads=[r_tmp], writes=[r_rstd])
            for c in range(KC):
                i = c % 2
                eng = "dve" if c % 2 == 0 else "pool"
                P.op(eng, lambda e: e.tensor_tensor(out=lt[i][:, :N], in0=src(c), in1=mean[:, :N], op=ALU.subtract),
                     reads=[r_src[c], r_mean], writes=[r_lt[i]])
                P.op(eng, lambda e: e.tensor_tensor(out=lt[i][:, :N], in0=lt[i][:, :N], in1=rstd[:, :N], op=ALU.mult),
                     reads=[r_lt[i], r_rstd], writes=[r_lt[i]])
                P.op(eng, lambda e: e.tensor_scalar(out=out_f32(c), in0=lt[i][:, :N],
                                                    scalar1=vec_s[:, gcol * KC + c:gcol * KC + c + 1],
                                                    scalar2=vec_s[:, bcol * KC + c:bcol * KC + c + 1],
                                                    op0=ALU.mult, op1=ALU.add),
                     reads=[r_lt[i], r_vec], writes=[r_out_f32[c]])
                if out_bf is not None:
                    P.op("act", lambda e: e.activation(out=out_bf(c), in_=out_f32(c), func=AF.Copy),
                         reads=[r_out_f32[c]], writes=[r_out_bf[c]])

        def process(col0, N, halo_only, out_col0):
            P.dma("sp", om_b[:, :, :N], omT[:, col0:col0 + N].rearrange("(c p) n -> p c n", p=128), writes=[r_om])
            for c in range(KC):
                P.dma("sp", xs[:, c, :N], xT[c * 128:(c + 1) * 128, col0:col0 + N], writes=[r_xs[c]])
            for mo in range(KC):
                i = wq[0] % NWB
                wq[0] += 1
                fetch("o", mo, wgb[i], r_wgb[i])
                pa, pr = next_ps()
                for kc in range(KC):
                    P.op("pe", lambda e: e.matmul(pa[:, :N], lhsT=wgb[i][:, kc * 128:(kc + 1) * 128], rhs=om_b[:, kc, :N],
                                                  start=(kc == 0), stop=(kc == KC - 1)),
                         reads=[r_wgb[i], r_om], writes=[pr])
                P.op("dve", lambda e: e.scalar_tensor_tensor(out=xs[:, mo, :N], in0=xs[:, mo, :N], scalar=ALPHA_, in1=pa[:, :N],
                                                             op0=ALU.mult, op1=ALU.add), reads=[pr, r_xs[mo]], writes=[r_xs[mo]])
            layer_norm(N, 0, 1, lambda c: xs[:, c, :N], r_xs, lambda c: x1b[:, c, :N], r_x1b, lambda c: xs[:, c, :N], r_xs)
            for hc in range(HC):
                i = wq[0] % NWB
                wq[0] += 1
                fetch("g", hc, wgb[i], r_wgb[i])
                pg, prg = next_ps()
                for kc in range(KC):
                    P.op("pe", lambda e: e.matmul(pg[:, :N], lhsT=wgb[i][:, kc * 128:(kc + 1) * 128], rhs=x1b[:, kc, :N],
                                                  start=(kc == 0), stop=(kc == KC - 1)),
                         reads=[r_wgb[i], r_x1b[kc]], writes=[prg])
                if halo_only:
                    P.op("dve", lambda e: e.tensor_scalar(out=carry[:, hc, :], in0=pg[:, :2], scalar1=hm_s[:, 0:1], scalar2=None,
                                                          op0=ALU.mult), reads=[prg, r_hm], writes=[r_carry[hc]])
                    continue
                fetch("v", hc, wvb[i], r_wvb[i])
                pv, prv = next_ps()
                for kc in range(KC):
                    P.op("pe", lambda e: e.matmul(pv[:, :N], lhsT=wvb[i][:, kc * 128:(kc + 1) * 128], rhs=x1b[:, kc, :N],
                                                  start=(kc == 0), stop=(kc == KC - 1)),
                         reads=[r_wvb[i], r_x1b[kc]], writes=[prv])
                j = hc % 2
                hb, rh = hbuf[j], r_hbuf[j]
                cb, rc = cbuf[j], r_cbuf[j]
                sb, rs = sbuf_[j], r_sbuf[j]
                P.op("act", lambda e: e.activation(out=hb[:, 2:2 + N], in_=pg[:, :N], func=AF.Copy), reads=[prg], writes=[rh])
                P.op("pool", lambda e: e.tensor_copy(out=hb[:, 0:2], in_=carry[:, hc, :]), reads=[r_carry[hc]], writes=[rh])
                P.op("pool", lambda e: e.tensor_copy(out=carry[:, hc, :], in_=hb[:, N:N + 2]), reads=[rh], writes=[r_carry[hc]])
                cw = lambda k: cvec_s[:, k * HC + hc:k * HC + hc + 1]
                P.op("dve", lambda e: e.tensor_scalar(out=cb[:, :N], in0=hb[:, 2:2 + N], scalar1=cw(2), scalar2=cw(3),
                                                      op0=ALU.mult, op1=ALU.add), reads=[rh, r_cvec], writes=[rc])
                P.op("dve", lambda e: e.scalar_tensor_tensor(out=cb[:, :N], in0=hb[:, 1:1 + N], scalar=cw(1), in1=cb[:, :N],
                                                             op0=ALU.mult, op1=ALU.add), reads=[rh, rc, r_cvec], writes=[rc])
                P.op("dve", lambda e: e.scalar_tensor_tensor(out=cb[:, :N], in0=hb[:, 0:N], scalar=cw(0), in1=cb[:, :N],
                                                             op0=ALU.mult, op1=ALU.add), reads=[rh, rc, r_cvec], writes=[rc])
                P.op("act", lambda e: e.activation(out=sb[:, :N], in_=cb[:, :N], func=AF.Silu), reads=[rc], writes=[rs])
                P.op("dve", lambda e: e.tensor_tensor(out=actb[:, hc, :N], in0=sb[:, :N], in1=pv[:, :N], op=ALU.mult),
                     reads=[rs, prv], writes=[r_act[hc]])
            if halo_only:
                return
            for mo in range(KC):
                i = mo % 2
                fetch("d", mo, wdb[i], r_wdb[i])
                pa, pr = next_ps()
                for hc in range(HC):
                    P.op("pe", lambda e: e.matmul(pa[:, :N], lhsT=wdb[i][:, hc * 128:(hc + 1) * 128], rhs=actb[:, hc, :N],
                                                  start=(hc == 0), stop=(hc == HC - 1)),
                         reads=[r_wdb[i], r_act[hc]], writes=[pr])
                P.op("dve", lambda e: e.scalar_tensor_tensor(out=xs[:, mo, :N], in0=xs[:, mo, :N], scalar=ALPHA_, in1=pa[:, :N],
                                                             op0=ALU.mult, op1=ALU.add), reads=[pr, r_xs[mo]], writes=[r_xs[mo]])
            layer_norm(N, 2, 3, lambda c: xs[:, c, :N], r_xs, None, None, lambda c: xs[:, c, :N], r_xs)
            for c in range(KC):
                P.dma("pool", yT[c * 128:(c + 1) * 128, out_col0:out_col0 + N], xs[:, c, :N], reads=[r_xs[c]], writes=[r_y])

        r_y = P.region()
        process(0, 2, True, 0)
        for t0 in range(0, T, NT):
            n = min(NT, T - t0)
            process(2 + t0, n, False, t0)
        P.finish([r_y], "sp")
        print("tail ninstr", P.ninstr, P.cnt)
    return nc


def blk_w(w, kc_rows=True):
    K, M = w.shape
    a = w.reshape(K // 128, 128, M // 128, 128)
    return np.ascontiguousarray(a.transpose(2, 1, 0, 3)).reshape(M // 128, 128, (K // 128) * 128)


def vec_pc(v):
    return np.ascontiguousarray(v.reshape(-1, 128).T)


import ml_dtypes
from concourse.bass_utils import run_bass_kernel_spmd

_B, _S, _NC = 2, 16384, 8
_TT = _S // 4
_PROGS = {}


def _prog(name, fn):
    if name not in _PROGS:
        _PROGS[name] = fn()
    return _PROGS[name]


def _run(nc, in_maps):
    res = run_bass_kernel_spmd(nc, in_maps, core_ids=list(range(_NC)))
    return res.results


def _tail(omix, xres, w_out, g1, b1, w_gate, w_val, conv_w, conv_b, w_down, g2, b2):
    f = np.float32
    wo, wg, wv, wd = blk_w(w_out), blk_w(w_gate), blk_w(w_val), blk_w(w_down)
    vecs = np.concatenate([vec_pc(v) for v in (g1, b1, g2, b2)], 1).astype(f)
    cvec = np.concatenate([vec_pc(v) for v in (conv_w[0], conv_w[1], conv_w[2], conv_b)], 1).astype(f)
    maps = []
    for c in range(_NC):
        b, q = divmod(c, 4)
        t0 = q * _TT
        omT = np.zeros((D, 2 + _TT), ml_dtypes.bfloat16)
        xT = np.zeros((D, 2 + _TT), f)
        lo = max(t0 - 2, 0)
        omT[:, 2 - (t0 - lo):] = omix[b, lo:t0 + _TT, :].T
        xT[:, 2 - (t0 - lo):] = xres[b][:, lo:t0 + _TT]
        maps.append(dict(omT=omT, xT=xT, hmask=np.full((128, 1), 0.0 if q == 0 else 1.0, f),
                         w_out=wo, w_gate=wg, w_val=wv, w_down=wd, vecs=vecs, cvec=cvec))
    res = _run(_prog("tail", lambda: build_tail(_TT)), maps)
    out = [np.empty((D, _S), f) for _ in range(_B)]
    for c in range(_NC):
        b, q = divmod(c, 4)
        out[b][:, q * _TT:(q + 1) * _TT] = res[c]["yT"]
    return out


def kernel(**inp):
    f = np.float32
    inp = {k: np.asarray(v) for k, v in inp.items()}
    x = inp["x"].astype(f, copy=False)
    pos = inp["positions"].astype(np.int32, copy=False)
    xT = [np.ascontiguousarray(x[b].T) for b in range(_B)]
    posb = [np.ascontiguousarray(pos[b][None, :]) for b in range(_B)]
    g = lambda k: inp[k].astype(f, copy=False)
    maps = [a1_inputs(xT[c // 4], posb[c // 4], c % 4, g("l0_w_in"), g("l0_q_norm"), g("l0_w_uq"), g("l0_kv_norm"), g("l0_w_ukv"))
            for c in range(_NC)]
    r1 = _run(_prog("a1", lambda: build_a1(_S)), maps)
    maps = [a2_inputs(xT[c // 4], c % 4, g("l0_w_in"), g("l0_rwkv_mu"), g("l0_rwkv_w0"), g("l0_rwkv_w2"), g("l0_rwkv_a0"),
                      g("l0_rwkv_a2"), g("l0_rwkv_g2"), g("l0_rwkv_k_k"), g("l0_rwkv_k_a"), g("l0_rwkv_r_k"),
                      g("l0_rwkv_gn_w"), g("l0_rwkv_gn_b")) for c in range(_NC)]
    r2 = _run(_prog("a2", lambda: build_a2(_S)), maps)
    omix = np.empty((_B, _S, D), ml_dtypes.bfloat16)
    for c in range(_NC):
        b, hg = divmod(c, 4)
        omix[b, :, hg * 256:(hg + 1) * 256] = r1[c]["om"]
        omix[b, :, 1024 + hg * 256:1024 + (hg + 1) * 256] = r2[c]["om"]
    x1T = _tail(omix, xT, g("l0_w_out"), g("l0_ln1_g"), g("l0_ln1_b"), g("l0_ffn_w_gate"), g("l0_ffn_w_val"),
                g("l0_ffn_conv_w"), g("l0_ffn_conv_b"), g("l0_ffn_w_down"), g("l0_ln2_g"), g("l0_ln2_b"))
    maps = [c_inputs(x1T[c // 4], posb[c // 4], c % 4, g("l1_w_in"), g("l1_gdn_conv_w"), g("l1_gdn_A_log"), g("l1_gdn_dt_bias"),
                     g("l1_gdn_norm"), g("l1_ret_gn_w"), g("l1_ret_gn_b")) for c in range(_NC)]
    r3 = _run(_prog("c", lambda: build_c(_S)), maps)
    for c in range(_NC):
        b, hg = divmod(c, 4)
        omix[b, :, 2 * hg * 128:(2 * hg + 2) * 128] = r3[c]["om"][:, 0:256]
        omix[b, :, 1024 + 2 * hg * 128:1024 + (2 * hg + 2) * 128] = r3[c]["om"][:, 256:512]
    x2T = _tail(omix, x1T, g("l1_w_out"), g("l1_ln1_g"), g("l1_ln1_b"), g("l1_ffn_w_gate"), g("l1_ffn_w_val"),
                g("l1_ffn_conv_w"), g("l1_ffn_conv_b"), g("l1_ffn_w_down"), g("l1_ln2_g"), g("l1_ln2_b"))
    out = np.empty((_B, _S, D), f)
    for b in range(_B):
        out[b] = x2T[b].T
    return out
```

```python
import contextlib
import numpy as np
import concourse.bass as bass
import concourse.mybir as mybir

F32 = mybir.dt.float32
BF16 = mybir.dt.bfloat16
I32 = mybir.dt.int32
AF = mybir.ActivationFunctionType
ALU = mybir.AluOpType
AX = mybir.AxisListType

EPOCH = 8000
NDSEM = 12


class Region:
    __slots__ = ("w", "r", "name", "excl")

    def __init__(self, name="", excl=False):
        self.w = None
        self.r = {}
        self.name = name
        self.excl = excl


class Prog:
    def __init__(self, nc, stack, self_sync=None):
        self.nc = nc
        self.stack = stack
        import os as _os
        self.self_sync = (not _os.environ.get("NOSELF")) if self_sync is None else self_sync
        self.engs = {"pe": nc.tensor, "dve": nc.vector, "act": nc.scalar,
                     "pool": nc.gpsimd, "sp": nc.sync}
        self.cnt = {e: 0 for e in self.engs}
        self.sems = {}
        self.seen = {e: {} for e in self.engs}
        self.dn = {}
        self.dsem = {}
        self.ninstr = 0
        self.mem = stack
        self.pre = ""

    def _sem(self, key):
        if key not in self.sems:
            nm = "s_" + "_".join(str(k) for k in key)
            self.sems[key] = self.stack.enter_context(self.nc.semaphore(nm))
        return self.sems[key]

    def region(self, name=""):
        return Region(name)

    def regions(self, n, name=""):
        return [Region(f"{name}{i}") for i in range(n)]

    def _wait(self, eng, toks):
        E = self.engs[eng]
        seen = self.seen[eng]
        best = {}
        for (key, val) in toks:
            if best.get(key, 0) < val:
                best[key] = val
        for key, val in best.items():
            if seen.get(key, 0) < val:
                E.wait_ge(self._sem(key), val)
                seen[key] = val
                self.ninstr += 1

    def _deps(self, eng, reads, writes):
        toks = []
        for r in reads:
            if r.w is not None:
                toks.append(r.w)
            if r.excl:
                toks.extend(t for k, t in r.r.items() if k != eng)
        for r in writes:
            if r.w is not None:
                toks.append(r.w)
            toks.extend(r.r.values())
        if eng == "pe" or not self.self_sync:
            toks = [t for t in toks if t[0][0] != eng]
        return toks

    def op(self, eng, fn, reads=(), writes=()):
        self._wait(eng, self._deps(eng, reads, writes))
        ins = fn(self.engs[eng])
        c = self.cnt[eng]
        ep, idx = divmod(c, EPOCH)
        key = (eng, ep)
        ins.then_inc(self._sem(key), 1)
        self.cnt[eng] = c + 1
        self.ninstr += 1
        tok = (key, idx + 1)
        for r in reads:
            r.r[eng] = tok
        for r in writes:
            r.w = tok
            r.r = {}
        return tok

    def dma(self, q, out, in_, reads=(), writes=(), **kw):
        n = self.dn.get(q, 0)
        j = n % NDSEM
        prev = 16 * (n // NDSEM)
        key = ("d" + q, j)
        toks = [t for t in self._deps("dma" + q, reads, writes)]
        if prev > 0:
            toks.append((key, prev))
        self._wait(q, toks)
        ins = self.engs[q].dma_start(out=out, in_=in_, **kw)
        ins.then_inc(self._sem(key), 16)
        self.dn[q] = n + 1
        self.ninstr += 1
        tok = (key, prev + 16)
        rk = "dma" + q + str(j)
        for r in reads:
            r.r[rk] = tok
        for r in writes:
            r.w = tok
            r.r = {}
        return tok

    def coll(self, kind, ins_ap, outs_ap, groups, reads=(), writes=()):
        q = "pool"
        n = self.dn.get(q, 0)
        j = n % NDSEM
        prev = 16 * (n // NDSEM)
        key = ("d" + q, j)
        toks = [t for t in self._deps("dma" + q, reads, writes)]
        if prev > 0:
            toks.append((key, prev))
        self._wait(q, toks)
        ins = self.engs[q].collective_compute(kind, ALU.bypass, replica_groups=groups, ins=[ins_ap], outs=[outs_ap])
        ins.then_inc(self._sem(key), 16)
        self.dn[q] = n + 1
        self.ninstr += 1
        tok = (key, prev + 16)
        rk = "dma" + q + str(j)
        for r in reads:
            r.r[rk] = tok
        for r in writes:
            r.w = tok
            r.r = {}
        return tok

    def barrier(self):
        toks = []
        for e in self.engs:
            c = self.cnt[e]
            if c:
                ep, idx = divmod(c - 1, EPOCH)
                toks.append(((e, ep), idx + 1))
        for q, n in self.dn.items():
            for j in range(NDSEM):
                cj = (n - j + NDSEM - 1) // NDSEM
                if cj > 0:
                    toks.append((("d" + q, j), 16 * cj))
        for e in self.engs:
            self._wait(e, toks)

    def finish(self, regions, eng="sp"):
        toks = [r.w for r in regions if r.w is not None]
        self._wait(eng, toks)

    def sbuf(self, name, shape, dt):
        return self.mem.enter_context(self.nc.sbuf_tensor(self.pre + name, list(shape), dt))

    def psum(self, name, shape, dt):
        return self.mem.enter_context(self.nc.psum_tensor(self.pre + name, list(shape), dt))


import contextlib
import math
import numpy as np

D = 2048
KC = 16
CH = 128
TWO_PI = 2 * math.pi
CW1 = 6.28125
CW2 = TWO_PI - CW1


class H:
    def __init__(self, P):
        self.P = P

    def mm(self, out, lhsT, rhs, rd, wr, start=True, stop=True):
        self.P.op("pe", lambda e: e.matmul(out, lhsT=lhsT, rhs=rhs, start=start, stop=stop), rd, wr)

    def tr(self, out, in_, ident, rd, wr):
        self.P.op("pe", lambda e: e.transpose(out, in_, ident), rd, wr)

    def tt(self, eng, out, a, b, op, rd, wr):
        self.P.op(eng, lambda e: e.tensor_tensor(out=out, in0=a, in1=b, op=op), rd, wr)

    def ts(self, eng, out, a, s1, op0, rd, wr, s2=None, op1=None):
        if op1 is None:
            self.P.op(eng, lambda e: e.tensor_scalar(out=out, in0=a, scalar1=s1, scalar2=None, op0=op0), rd, wr)
        else:
            self.P.op(eng, lambda e: e.tensor_scalar(out=out, in0=a, scalar1=s1, scalar2=s2, op0=op0, op1=op1), rd, wr)

    def stt(self, out, in0, scalar, in1, op0, op1, rd, wr):
        self.P.op("dve", lambda e: e.scalar_tensor_tensor(out=out, in0=in0, scalar=scalar, in1=in1, op0=op0, op1=op1), rd, wr)

    def act(self, out, in_, func, rd, wr, **kw):
        self.P.op("act", lambda e: e.activation(out=out, in_=in_, func=func, **kw), rd, wr)

    def cp(self, eng, out, in_, rd, wr):
        if eng == "act":
            self.act(out, in_, AF.Copy, rd, wr)
        else:
            self.P.op(eng, lambda e: e.tensor_copy(out=out, in_=in_), rd, wr)

    def rsqrt(self, out, in_, eps, rd, wr, scale=1.0):
        self.ts("dve", out, in_, scale, ALU.mult, rd, wr, s2=eps, op1=ALU.add)
        self.act(out, out, AF.Sqrt, wr, wr)
        self.P.op("dve", lambda e: e.reciprocal(out=out, in_=out), wr, wr)


def run_rr(gens):
    gens = list(gens)
    while gens:
        for g_ in list(gens):
            try:
                next(g_)
            except StopIteration:
                gens.remove(g_)


class TB:
    def __init__(self, P, name, shape, dt, n=2, psum=False):
        mk = P.psum if psum else P.sbuf
        self.a = [mk(f"{name}_{i}", shape, dt) for i in range(n)]
        self.r = [P.region(f"{name}_{i}") for i in range(n)]
        for r in self.r:
            r.excl = psum
        self.n = n

    def __call__(self, i):
        return self.a[i % self.n], self.r[i % self.n]


def rope_tables(P, h, ci, pos_f, r_pos, inv_s, r_inv, tb):
    ang, r_ang = tb["ang"](ci)
    u, r_u = tb["u"](ci)
    ki, r_ki = tb["ki"](ci)
    kf, r_kf = tb["kf"](ci)
    sn, r_sn = tb["sin"](ci)
    cs, r_cs = tb["cos"](ci)
    h.ts("dve", ang[:], pos_f, inv_s[:, 0:1], ALU.mult, [r_pos, r_inv], [r_ang])
    for (dst, r_dst, off, bias) in ((sn, r_sn, 0.0, 0.0), (cs, r_cs, 0.25, math.pi / 2)):
        h.ts("dve", u[:], ang[:], 1.0 / TWO_PI, ALU.mult, [r_ang], [r_u], s2=off, op1=ALU.add)
        h.cp("dve", ki[:], u[:], [r_u], [r_ki])
        h.cp("dve", kf[:], ki[:], [r_ki], [r_kf])
        h.stt(u[:], kf[:], -CW1, ang[:], ALU.mult, ALU.add, [r_kf, r_ang], [r_u])
        h.stt(u[:], kf[:], -CW2, u[:], ALU.mult, ALU.add, [r_kf, r_u], [r_u])
        sc = 1.0 - 2e-6
        if bias == 0.0:
            h.act(dst[:], u[:], AF.Sin, [r_u], [r_dst], scale=sc)
        else:
            h.ts("dve", u[:], u[:], bias, ALU.add, [r_u], [r_u])
            h.act(dst[:], u[:], AF.Sin, [r_u], [r_dst], scale=sc)
    return cs, sn, r_cs, r_sn


def build_c(S, stage=9):
    nc = bass.Bass("TRN2", target_bir_lowering=False)
    dt = nc.dram_tensor
    NCH = S // CH
    xT = dt("xT", [D, S], F32, kind="ExternalInput").ap()
    pos = dt("pos", [1, S], I32, kind="ExternalInput").ap()
    wF = dt("wF", [128, KC, 8 * 128], F32, kind="ExternalInput").ap()
    wT = dt("wT", [128, KC, 772], F32, kind="ExternalInput").ap()
    convw = dt("convw", [128, 16], F32, kind="ExternalInput").ap()
    rowt = dt("rowt", [128, 4 + 128 + 512], F32, kind="ExternalInput").ap()
    cst = dt("cst", [128, 5 * 128], F32, kind="ExternalInput").ap()
    rett = dt("rett", [128, 2 * 128 + 128 + 8], F32, kind="ExternalInput").ap()
    om = dt("om", [S, 512], BF16, kind="ExternalOutput").ap()

    with contextlib.ExitStack() as st:
        P = Prog(nc, st)
        h = H(P)
        wF_b = P.sbuf("wF_b", [128, KC, 8 * 128], BF16)
        wT_b = P.sbuf("wT_b", [128, KC, 772], BF16)
        r_wF, r_wT = P.region(), P.region()
        stg = TB(P, "stg", [128, 1024], F32)
        for kc in range(KC):
            a, r = stg(2 * kc)
            P.dma("sp", a[:, :1024], wF[:, kc, :], writes=[r])
            h.cp("dve", wF_b[:, kc, :], a[:, :1024], [r], [r_wF])
            a, r = stg(2 * kc + 1)
            P.dma("sp", a[:, :772], wT[:, kc, :], writes=[r])
            h.cp("act", wT_b[:, kc, :], a[:, :772], [r], [r_wT])
        convw_s = P.sbuf("convw_s", [128, 16], F32)
        rowt_s = P.sbuf("rowt_s", [128, 4 + 128 + 512], F32)
        cst_s = P.sbuf("cst_s", [128, 5 * 128], F32)
        rett_s = P.sbuf("rett_s", [128, 2 * 128 + 128 + 8], F32)
        r_c = P.region()
        for a, b in ((convw_s, convw), (rowt_s, rowt), (cst_s, cst), (rett_s, rett)):
            P.dma("sp", a[:], b, writes=[r_c])
        ident = cst_s[:, 0:128]
        maskL = cst_s[:, 128:256]
        maskU = cst_s[:, 256:384]
        ones_f = cst_s[:, 384:512]
        zeros_f = cst_s[:, 512:640]
        cb = P.sbuf("cb", [128, 3 * 128], BF16)
        h.cp("dve", cb[:, 0:128], ident, [r_c], [r_c])
        h.cp("dve", cb[:, 128:256], ones_f, [r_c], [r_c])
        h.cp("dve", cb[:, 256:384], maskU, [r_c], [r_c])
        ident_b, ones_b = cb[:, 0:128], cb[:, 128:256]
        DTret = [rett_s[:, 0:128], rett_s[:, 128:256]]
        QdRow = rett_s[:, 256:384]
        khcol = [rett_s[:, 384:385], rett_s[:, 385:386]]
        wcret = [rett_s[:, 386:387], rett_s[:, 387:388]]
        inv_s = rett_s[:, 388:389]
        sgn_s = rett_s[:, 389:390]
        Alog_row, dtb_row = rowt_s[:, 0:2], rowt_s[:, 2:4]
        gnorm_row = rowt_s[:, 4:132]
        retw_row = [rowt_s[:, 132:260], rowt_s[:, 260:388]]
        retb_row = [rowt_s[:, 388:516], rowt_s[:, 516:644]]
        eA = P.sbuf("eA", [128, 2], F32)
        h.act(eA[:], Alog_row, AF.Exp, [r_c], [r_c])
        h.ts("dve", eA[:], eA[:], -1.0, ALU.mult, [r_c], [r_c])

        pF = TB(P, "pF", [128, 512], F32, n=1, psum=True)
        pI = TB(P, "pI", [128, 512], F32, n=1, psum=True)
        pT = TB(P, "pT", [128, 512], F32, n=2, psum=True)
        pM = TB(P, "pM", [128, 512], F32, n=2, psum=True)
        pB = TB(P, "pB", [128, 1024], BF16, n=1, psum=True)
        pS = TB(P, "pS", [128, 512], F32, n=1, psum=True)
        pB2 = pS

        Sg = [P.sbuf(f"Sg{i}", [128, 128], F32) for i in range(2)]
        Sgb = [P.sbuf(f"Sgb{i}", [128, 128], BF16) for i in range(2)]
        Sr_ = P.sbuf("Sr", [128, 128], F32)
        Srb_ = P.sbuf("Srb", [128, 128], BF16)
        Sr = [Sr_[0:64, :], Sr_[64:128, :]]
        Srb = [Srb_[0:64, :], Srb_[64:128, :]]
        r_Sg, r_Sgb, r_Sr, r_Srb = P.regions(2), P.regions(2), P.regions(2), P.regions(2)
        for i in range(2):
            P.op("pool", lambda e: e.memset(Sg[i][:], 0.0), [], [r_Sg[i]])
            P.op("pool", lambda e: e.memset(Sgb[i][:], 0.0), [], [r_Sgb[i]])
            P.op("pool", lambda e: e.memset(Sr[i], 0.0), [], [r_Sr[i]])
            P.op("pool", lambda e: e.memset(Srb[i], 0.0), [], [r_Srb[i]])
        cvx = P.sbuf("cvx", [128, 4, 3 + CH], F32)
        r_cvx = P.regions(4)
        P.op("pool", lambda e: e.memset(cvx[:], 0.0), [], r_cvx)

        def T_(name, shape, dt_=F32, n=2):
            return TB(P, name, shape, dt_, n)

        xs_f = T_("xs_f", [128, KC, CH])
        xb = T_("xb", [128, KC, CH], BF16)
        posi = T_("posi", [128, CH], I32)
        posf = T_("posf", [128, CH])
        rt = {k: T_(k, [128, CH], I32 if k == "ki" else F32) for k in ("ang", "u", "ki", "kf", "sin", "cos")}
        cacc = T_("cacc", [128, 4, CH])
        qk_f = T_("qk_f", [128, 2, CH])
        vT_b = T_("vT_b", [128, 2, CH], BF16)
        sq_b = T_("sq_b", [128, 2, CH], BF16)
        rs = T_("rs", [128, 2, CH])
        qkn = T_("qkn", [128, 2, CH], BF16)
        kn_t = T_("kn_t", [128, 128], BF16)
        lg = T_("lg", [128, 4])
        beta = T_("beta", [128, 2])
        nbeta = T_("nbeta", [128, 2])
        gg = T_("gg", [128, 2])
        t2 = {k: T_("t2" + k, [128, 2]) for k in "abcd"}
        gcol = T_("gcol", [128, 2])
        gtot = T_("gtot", [128, 2])
        sc1 = T_("sc1", [128, 2])
        sc2 = T_("sc2", [128, 2])
        wc = T_("wc", [128, 2])
        gB = T_("gB", [128, 128], F32, 4)
        Gp = T_("Gp", [128, 128], F32, 4)
        Gm = T_("Gm", [128, 128], F32, 4)
        ER = T_("ER", [128, 128], F32, 4)
        A_ = T_("A_", [128, 128], F32, 4)
        N_ = T_("N_", [128, 128], F32, 4)
        A2 = T_("A2", [128, 128], F32, 4)
        N2 = T_("N2", [128, 128], F32, 4)
        Pm = T_("Pm", [128, 128], F32, 4)
        Pb = T_("Pb", [128, 128], BF16, 4)
        MrT = T_("MrT", [128, 128], BF16, 4)
        Vp = T_("Vp", [128, 128], BF16, 4)
        X1 = T_("X1", [128, 128], F32, 4)
        Ad = T_("Ad", [128, 128], BF16, 4)
        AdT = T_("AdT", [128, 128], BF16, 4)
        Kh = T_("Kh", [128, 128], BF16, 4)
        QdT = T_("QdT", [128, 128], BF16, 4)
        Ut = T_("Ut", [128, 128], BF16, 4)
        ssq = T_("ssq", [128, 1], F32, 4)
        junk = T_("junk", [128, 128], F32, 4)
        zs = T_("zs", [128, 128], F32, 4)
        ot = T_("ot", [128, 512], BF16, 2)
        KKQ = T_("KKQ", [128, 256], F32, 2)
        Vtok = T_("Vtok", [128, 256], BF16, 2)
        rfs = T_("rfs", [128, 512])
        rq = T_("rq", [128, CH])
        rk = T_("rk", [128, CH])
        rqb = T_("rqb", [128, CH], BF16)
        rkb = T_("rkb", [128, CH], BF16)
        rqd = T_("rqd", [128, CH], BF16)
        rk_t = T_("rk_t", [128, 128], BF16)
        rv_b = T_("rv_b", [128, 256], BF16)
        rMT = T_("rMT", [128, 128], BF16, 4)
        bst = T_("bst", [128, 6], F32, 4)
        mv = T_("mv", [128, 2], F32, 4)
        r_om = P.region()

        for ci in range(NCH):
            c0 = ci * CH
            xa, xr = xs_f(ci)
            xba, xbr = xb(ci)
            P.dma("sp", xa[:], xT[:, c0:c0 + CH].rearrange("(c p) n -> p c n", p=128), writes=[xr])
            h.cp("pool", xba[:, :KC // 2, :], xa[:, :KC // 2, :], [xr], [xbr])
            h.cp("dve", xba[:, KC // 2:, :], xa[:, KC // 2:, :], [xr], [xbr])
            pi_a, pi_r = posi(ci)
            pf_a, pf_r = posf(ci)
            P.dma("sp", pi_a[:], pos[:, c0:c0 + CH].partition_broadcast(128), writes=[pi_r])
            h.cp("dve", pf_a[:], pi_a[:], [pi_r], [pf_r])
            cs, sn, r_cs, r_sn = rope_tables(P, h, ci, pf_a[:], pf_r, inv_s, r_c, rt)
            gF, gFr = pF(0)

            def inproj_F(g0):
                for g in range(g0, g0 + 4):
                    for kc in range(KC):
                        h.mm(gF[:, (g % 4) * 128:(g % 4 + 1) * 128], wF_b[:, kc, g * 128:(g + 1) * 128], xba[:, kc, :],
                             [r_wF, xbr], [gFr], start=(kc == 0), stop=(kc == KC - 1))
            inproj_F(0)
            tA, tAr = pT(0)
            tB, tBr = pT(1)
            for kc in range(KC):
                h.mm(tA[:, :512], xba[:, kc, :], wT_b[:, kc, 0:512], [r_wT, xbr], [tAr], start=(kc == 0), stop=(kc == KC - 1))
            for kc in range(KC):
                h.mm(tB[:, :260], xba[:, kc, :], wT_b[:, kc, 512:772], [r_wT, xbr], [tBr], start=(kc == 0), stop=(kc == KC - 1))

            ca, car = cacc(ci)
            qka, qkr = qk_f(ci)
            vta, vtr = vT_b(ci)
            for g in range(4):
                h.cp("act", cvx[:, g, 3:3 + CH], gF[:, g * 128:(g + 1) * 128], [gFr], [r_cvx[g]])
                w = lambda j: convw_s[:, g * 4 + j:g * 4 + j + 1]
                h.ts("dve", ca[:, g, :], cvx[:, g, 3:3 + CH], w(3), ALU.mult, [r_cvx[g], r_c], [car])
                for j in range(3):
                    h.stt(ca[:, g, :], cvx[:, g, j:j + CH], w(j), ca[:, g, :], ALU.mult, ALU.add, [r_cvx[g], car, r_c], [car])
                h.cp("pool", cvx[:, g, 0:3], cvx[:, g, CH:CH + 3], [r_cvx[g]], [r_cvx[g]])
                if g < 2:
                    h.act(qka[:, g, :], ca[:, g, :], AF.Silu, [car], [qkr])
                else:
                    h.act(vta[:, g - 2, :], ca[:, g, :], AF.Silu, [car], [vtr])
            inproj_F(4)
            rF, rFr = rfs(ci)
            h.cp("act", rF[:], gF[:, :], [gFr], [rFr])
            sqa, sqr = sq_b(ci)
            h.act(sqa[:], qka[:], AF.Square, [qkr], [sqr])
            m0, m0r = pM(0)
            h.mm(m0[:, 0:256], ones_b, sqa[:].rearrange("p a b -> p (a b)"), [r_c, sqr], [m0r])
            rsa, rsr = rs(ci)
            h.rsqrt(rsa[:].rearrange("p a b -> p (a b)"), m0[:, 0:256], 1e-6, [m0r], [rsr])
            qna, qnr = qkn(ci)
            h.tt("dve", qna[:, 0, :], qka[:, 1, :], rsa[:, 1, :], ALU.mult, [qkr, rsr], [qnr])
            h.stt(qna[:, 1, :], qka[:, 0, :], 128.0 ** -0.5, rsa[:, 0, :], ALU.mult, ALU.mult, [qkr, rsr], [qnr])
            m1, m1r = pM(1)
            h.mm(m1[:, 0:256], qna[:, 0, :], qna[:].rearrange("p a b -> p (a b)"), [qnr], [m1r])
            pb, pbr = pB(0)
            h.tr(pb[:, 0:128], qna[:, 0, :], ident_b, [qnr, r_c], [pbr])
            h.tr(pb[:, 128:256], vta[:, 0, :], ident_b, [vtr, r_c], [pbr])
            h.tr(pb[:, 256:384], vta[:, 1, :], ident_b, [vtr, r_c], [pbr])
            kta, ktr = kn_t(ci)
            h.cp("act", kta[:], pb[:, 0:128], [pbr], [ktr])
            lga, lgr = lg(ci)
            h.cp("dve", lga[:], tB[:, 256:260], [tBr], [lgr])
            ba_, br_ = beta(ci)
            nb_, nbr_ = nbeta(ci)
            h.act(ba_[:], lga[:, 0:2], AF.Sigmoid, [lgr], [br_])
            h.ts("dve", nb_[:], ba_[:], -1.0, ALU.mult, [br_], [nbr_])
            xa2, xr2 = t2["a"](ci)
            ab2, abr2 = t2["b"](ci)
            e2, er2 = t2["c"](ci)
            mx2, mxr2 = t2["d"](ci)
            h.tt("dve", xa2[:], lga[:, 2:4], dtb_row, ALU.add, [lgr, r_c], [xr2])
            h.act(ab2[:], xa2[:], AF.Abs, [xr2], [abr2])
            h.act(e2[:], ab2[:], AF.Exp, [abr2], [er2], scale=-1.0)
            h.act(e2[:], e2[:], AF.Ln, [er2], [er2], bias=1.0)
            h.ts("dve", mx2[:], xa2[:], 0.0, ALU.max, [xr2], [mxr2])
            h.tt("dve", mx2[:], mx2[:], e2[:], ALU.add, [mxr2, er2], [mxr2])
            ga, gr = gg(ci)
            h.tt("dve", ga[:], mx2[:], eA[:], ALU.mult, [mxr2, r_c], [gr])
            m0b, m0br = pM(0)
            h.mm(m0b[:, 256:258], maskU, ga[:], [r_c, gr], [m0br])
            h.mm(m0b[:, 258:260], ones_f, ga[:], [r_c, gr], [m0br])
            gca, gcr = gcol(ci)
            gta, gtr = gtot(ci)
            h.cp("dve", gca[:], m0b[:, 256:258], [m0br], [gcr])
            h.cp("dve", gta[:], m0b[:, 258:260], [m0br], [gtr])
            s1a, s1r = sc1(ci)
            s2a, s2r = sc2(ci)
            wca, wcr = wc(ci)
            h.act(s1a[:], gca[:], AF.Exp, [gcr], [s1r])
            h.tt("dve", s1a[:], s1a[:], nb_[:], ALU.mult, [s1r, nbr_], [s1r])
            h.tt("dve", s2a[:], gta[:], gca[:], ALU.subtract, [gtr, gcr], [s2r])
            h.act(s2a[:], s2a[:], AF.Exp, [s2r], [s2r])
            h.act(wca[:], gta[:], AF.Exp, [gtr], [wcr])

            ota, otr = ot(ci)
            kkq, kkqr = KKQ(ci)
            h.cp("act", kkq[:], m1[:, 0:256], [m1r], [kkqr])
            vtk, vtkr = Vtok(ci)
            h.cp("act", vtk[:], pb[:, 128:384], [pbr], [vtkr])

            def gdn_head(hh):
                k2 = 2 * ci + hh
                mg_, mgr = pM(hh)
                mi, mir = (pI(0), pF(0))[hh]
                gBa, gBr = gB(k2)
                h.ts("dve", gBa[:], ones_f, ga[:, hh:hh + 1], ALU.mult, [gr, r_c], [gBr])
                GC = mg_[:, 0:128]
                h.mm(GC, gBa[:], maskU, [gBr, r_c], [mgr])
                yield
                Gpa, Gpr = Gp(k2)
                Gma, Gmr = Gm(k2)
                ERa, ERr = ER(k2)
                h.stt(Gpa[:], GC, gca[:, hh:hh + 1], zeros_f, ALU.subtract, ALU.max, [mgr, gcr, r_c], [Gpr])
                h.stt(Gma[:], GC, gca[:, hh:hh + 1], zeros_f, ALU.subtract, ALU.min, [mgr, gcr, r_c], [Gmr])
                yield
                h.act(ERa[:], GC, AF.Exp, [mgr], [ERr])
                h.act(Gpa[:], Gpa[:], AF.Exp, [Gpr], [Gpr], scale=-1.0)
                h.act(Gma[:], Gma[:], AF.Exp, [Gmr], [Gmr])
                yield
                Aa, Ar = A_(k2)
                h.tt("dve", Aa[:], kkq[:, 0:128], Gpa[:], ALU.mult, [kkqr, Gpr], [Ar])
                h.stt(Aa[:], Aa[:], nb_[:, hh:hh + 1], maskL, ALU.mult, ALU.mult, [Ar, nbr_, r_c], [Ar])
                yield
                Na, Nr = N_(k2)
                h.tr(mi[:, 0:128], Aa[:], ident, [Ar, r_c], [mir])
                Ma, Mr_ = MrT(k2)
                h.tt("dve", Gma[:], Gma[:], maskU, ALU.mult, [Gmr, r_c], [Gmr])
                h.tt("dve", Ma[:], kkq[:, 128:256], Gma[:], ALU.mult, [kkqr, Gmr], [Mr_])
                yield
                h.cp("act", Na[:], mi[:, 0:128], [mir], [Nr])
                yield
                Pa, Pr = Pm(k2)
                h.tt("dve", Pa[:], Na[:], ident, ALU.add, [Nr, r_c], [Pr])
                yield
                cur = (Na, Nr, Aa, Ar)
                nxt = (N2(k2)[0], N2(k2)[1], A2(k2)[0], A2(k2)[1])
                for lv in range(6):
                    curN, curNr, curA, curAr = cur
                    nxtN, nxtNr, nxtA, nxtAr = nxt
                    h.mm(mi[:, 128:256], curN[:], curA[:], [curNr, curAr], [mir])
                    if lv < 5:
                        h.mm(mi[:, 256:384], curA[:], curN[:], [curNr, curAr], [mir])
                    yield
                    h.cp("act", nxtA[:], mi[:, 128:256], [mir], [nxtAr])
                    if lv < 5:
                        h.cp("dve", nxtN[:], mi[:, 256:384], [mir], [nxtNr])
                    yield
                    h.mm(mi[:, 0:128], nxtA[:], Pa[:], [nxtAr, Pr], [mir])
                    yield
                    h.tt("dve", Pa[:], Pa[:], mi[:, 0:128], ALU.add, [Pr, mir], [Pr])
                    yield
                    cur, nxt = nxt, cur
                Pba, Pbr = Pb(k2)
                h.cp("act", Pba[:], Pa[:], [Pr], [Pbr])
                Vpa, Vpr = Vp(k2)
                h.ts("dve", Vpa[:], vtk[:, hh * 128:(hh + 1) * 128], ba_[:, hh:hh + 1], ALU.mult, [vtkr, br_], [Vpr])
                Ada, Adr = Ad(k2)
                h.ts("dve", Ada[:], kta[:], s1a[:, hh:hh + 1], ALU.mult, [ktr, s1r], [Adr])
                yield
                h.mm(mg_[:, 256:384], Pba[:], Vpa[:], [Pbr, Vpr], [mgr])
                h.mm(mg_[:, 384:512], Ada[:], Pba[:], [Adr, Pbr], [mgr])
                Kha, Khr = Kh(k2)
                h.ts("dve", Kha[:], kta[:], s2a[:, hh:hh + 1], ALU.mult, [ktr, s2r], [Khr])
                QdTa, QdTr = QdT(k2)
                h.tt("dve", QdTa[:], qna[:, 1, :], ERa[:], ALU.mult, [qnr, ERr], [QdTr])
                yield
                X1a, X1r = X1(k2)
                h.cp("act", X1a[:], mg_[:, 256:384], [mgr], [X1r])
                AdTa, AdTr = AdT(k2)
                h.cp("act", AdTa[:], mg_[:, 384:512], [mgr], [AdTr])
                yield
                h.mm(mg_[:, 0:128], AdTa[:], Sgb[hh][:], [AdTr, r_Sgb[hh]], [mgr])
                yield
                Uta, Utr = Ut(k2)
                h.tt("dve", Uta[:], mg_[:, 0:128], X1a[:], ALU.add, [mgr, X1r], [Utr])
                yield
                h.mm(mg_[:, 128:256], QdTa[:], Sgb[hh][:], [QdTr, r_Sgb[hh]], [mgr], start=True, stop=False)
                h.mm(mg_[:, 128:256], Ma[:], Uta[:], [Mr_, Utr], [mgr], start=False, stop=True)
                h.mm(mi[:, 384:512], Kha[:], Uta[:], [Khr, Utr], [mir])
                yield
                h.stt(Sg[hh][:], Sg[hh][:], wca[:, hh:hh + 1], mi[:, 384:512], ALU.mult, ALU.add, [r_Sg[hh], wcr, mir], [r_Sg[hh]])
                yield
                h.cp("act", Sgb[hh][:], Sg[hh][:], [r_Sg[hh]], [r_Sgb[hh]])
                ja, jr = junk(k2)
                ssa, ssr = ssq(k2)
                h.act(ja[:], mg_[:, 128:256], AF.Square, [mgr], [jr, ssr], accum_out=ssa[:])
                yield
                h.rsqrt(ssa[:], ssa[:], 1e-6, [ssr], [ssr], scale=1.0 / 128)
                za, zr = zs(k2)
                h.act(za[:], tA[:, hh * 128:(hh + 1) * 128], AF.Silu, [tAr], [zr])
                yield
                h.tt("dve", za[:], za[:], gnorm_row, ALU.mult, [zr, r_c], [zr])
                yield
                h.stt(ota[:, hh * 128:(hh + 1) * 128], mg_[:, 128:256], ssa[:, 0:1], za[:], ALU.mult, ALU.mult, [mgr, ssr, zr], [otr])

            def ret_gen():
                rqa, rqr = rq(ci)
                rka, rkr = rk(ci)
                sns, snsr = rt["ang"](ci)
                h.ts("dve", sns[:], sn[:], sgn_s[:, 0:1], ALU.mult, [r_sn, r_c], [snsr])
                yield
                for (dst, dr, g0) in ((rqa, rqr, 0), (rka, rkr, 2)):
                    h.tt("dve", dst[:], rF[:, g0 * 128:(g0 + 1) * 128], cs[:], ALU.mult, [rFr, r_cs], [dr])
                    ja, jr = junk(2 * ci + g0 // 2 + 2)
                    h.tt("dve", ja[:], rF[:, (g0 + 1) * 128:(g0 + 2) * 128], sns[:], ALU.mult, [rFr, snsr], [jr])
                    h.tt("pool", dst[:], dst[:], ja[:], ALU.add, [dr, jr], [dr])
                    yield
                rqba, rqbr = rqb(ci)
                rkba, rkbr = rkb(ci)
                rqda, rqdr = rqd(ci)
                h.cp("act", rqba[:], rqa[:], [rqr], [rqbr])
                h.cp("act", rkba[:], rka[:], [rkr], [rkbr])
                h.tt("dve", rqda[:], rqa[:], QdRow, ALU.mult, [rqr, r_c], [rqdr])
                yield
                h.tr(pb[:, 384:512], rkba[:], ident_b, [rkbr, r_c], [pbr])
                yield
                rvba, rvbr = rv_b(ci)
                h.cp("act", rvba[:], tA[:, 256:512], [tAr], [rvbr])
                yield
                rkta, rktr = rk_t(ci)
                for hh in range(2):
                    h.ts("dve", rkta[:, hh * 64:(hh + 1) * 64], pb[:, 384 + hh * 64:384 + (hh + 1) * 64], khcol[hh], ALU.mult,
                         [pbr, r_c], [rktr])
                for hh in range(2):
                    k2 = 2 * ci + hh
                    hs = slice(hh * 64, (hh + 1) * 64)
                    sa, sr = pS(0); mi, mir = sa[:, 256:512], sr
                    h.mm(mi[:, 0:128], rkba[hs, :], rqba[hs, :], [rkbr, rqbr], [mir])
                    yield
                    Ma, Mr_ = rMT(k2)
                    h.tt("dve", Ma[:], mi[:, 0:128], DTret[hh], ALU.mult, [mir, r_c], [Mr_])
                    yield
                    h.mm(sa[:, 0:128], rqda[hs, :], Srb[hh], [rqdr, r_Srb[hh]], [sr], start=True, stop=False)
                    h.mm(sa[:, 0:128], Ma[:], rvba[:, hh * 128:(hh + 1) * 128], [Mr_, rvbr], [sr], start=False, stop=True)
                    h.mm(sa[hs, 128:256], rkta[:, hs], rvba[:, hh * 128:(hh + 1) * 128], [rktr, rvbr], [sr])
                    yield
                    h.stt(Sr[hh], Sr[hh], wcret[hh][hs, :], sa[hs, 128:256], ALU.mult, ALU.add, [r_Sr[hh], r_c, sr], [r_Sr[hh]])
                    yield
                    h.cp("act", Srb[hh], Sr[hh], [r_Sr[hh]], [r_Srb[hh]])
                    ba2, br2 = bst(k2)
                    mva, mvr = mv(k2)
                    P.op("dve", lambda e: e.bn_stats(out=ba2[:], in_=sa[:, 0:128]), [sr], [br2])
                    P.op("dve", lambda e: e.bn_aggr(out=mva[:], in_=ba2[:]), [br2], [mvr])
                    yield
                    h.rsqrt(mva[:, 1:2], mva[:, 1:2], 1e-5, [mvr], [mvr])
                    yield
                    ja, jr = junk(k2 + 2)
                    h.ts("dve", ja[:], sa[:, 0:128], mva[:, 0:1], ALU.subtract, [sr, mvr], [jr], s2=mva[:, 1:2], op1=ALU.mult)
                    h.tt("dve", ja[:], ja[:], retw_row[hh], ALU.mult, [jr, r_c], [jr])
                    h.tt("pool", ja[:], ja[:], retb_row[hh], ALU.add, [jr, r_c], [jr])
                    yield
                    za, zr = zs(k2 + 2)
                    h.act(za[:], tB[:, hh * 128:(hh + 1) * 128], AF.Silu, [tBr], [zr])
                    yield
                    h.tt("dve", ota[:, 256 + hh * 128:256 + (hh + 1) * 128], ja[:], za[:], ALU.mult, [jr, zr], [otr])

            run_rr([gdn_head(0), gdn_head(1), ret_gen()])
            P.dma("pool", om[c0:c0 + CH, :], ota[:], reads=[otr], writes=[r_om])
        P.finish([r_om], "sp")
        print("C ninstr", P.ninstr, P.cnt)
    return nc


GDN_QK_HEADS, GDN_V_HEADS, RET_HEADS = 4, 8, 8


def blk_in(w):
    return np.ascontiguousarray(w.reshape(KC, 128, -1).transpose(1, 0, 2))


def c_inputs(xT_b, pos_b, hg, w_in, gdn_conv_w, A_log, dt_bias, gdn_norm, ret_gn_w, ret_gn_b):
    f = np.float32
    qk_w, v_w = 512, 1024
    gq = w_in[:, hg * 128:(hg + 1) * 128]
    gk = w_in[:, qk_w + hg * 128: qk_w + (hg + 1) * 128]
    gv = [w_in[:, 2 * qk_w + (2 * hg + i) * 128: 2 * qk_w + (2 * hg + i + 1) * 128] for i in range(2)]
    zoff = 2 * qk_w + v_w
    gz = [w_in[:, zoff + (2 * hg + i) * 128: zoff + (2 * hg + i + 1) * 128] for i in range(2)]
    boff = zoff + v_w
    gb = w_in[:, boff + 2 * hg: boff + 2 * hg + 2]
    ga = w_in[:, boff + 8 + 2 * hg: boff + 8 + 2 * hg + 2]
    R0 = boff + 16
    rqw = w_in[:, R0 + 2 * hg * 64: R0 + (2 * hg + 2) * 64]
    rkw = w_in[:, R0 + 512 + 2 * hg * 64: R0 + 512 + (2 * hg + 2) * 64]
    rvw = w_in[:, R0 + 1024 + 2 * hg * 128: R0 + 1024 + (2 * hg + 2) * 128]
    rgw = w_in[:, R0 + 2048 + 2 * hg * 128: R0 + 2048 + (2 * hg + 2) * 128]

    def sw(w):
        a = w.reshape(w.shape[0], -1, 2, 32)
        return a[:, :, ::-1, :].reshape(w.shape)

    wF = np.concatenate([gq, gk, gv[0], gv[1], rqw, sw(rqw), rkw, sw(rkw)], 1)
    wT = np.concatenate([gz[0], gz[1], rvw, rgw, gb, ga], 1)
    cw = gdn_conv_w
    cols = [slice(hg * 128, (hg + 1) * 128), slice(qk_w + hg * 128, qk_w + (hg + 1) * 128),
            slice(2 * qk_w + 2 * hg * 128, 2 * qk_w + (2 * hg + 1) * 128),
            slice(2 * qk_w + (2 * hg + 1) * 128, 2 * qk_w + (2 * hg + 2) * 128)]
    convw = np.stack([cw[:, c].T for c in cols], 1).reshape(128, 16)
    rowt = np.concatenate([A_log[2 * hg:2 * hg + 2], dt_bias[2 * hg:2 * hg + 2], gdn_norm,
                           ret_gn_w[2 * hg * 128:(2 * hg + 2) * 128], ret_gn_b[2 * hg * 128:(2 * hg + 2) * 128]])
    rowt = np.broadcast_to(rowt[None], (128, rowt.size)).astype(f)
    i = np.arange(128)
    ident = np.eye(128, dtype=f)
    maskL = (i[:, None] > i[None, :]).astype(f)
    maskU = (i[:, None] <= i[None, :]).astype(f)
    cst = np.concatenate([ident, maskL, maskU, np.ones((128, 128), f), np.zeros((128, 128), f)], 1)
    lgam = [math.log(1.0 - 2.0 ** (-5.0 - (2 * hg + j))) for j in range(2)]
    diff = (i[None, :] - i[:, None]).astype(np.float64)
    DT = [np.where(diff >= 0, np.exp(np.maximum(diff, 0) * lgam[j]), 0.0) * 64 ** -0.5 for j in range(2)]
    QdRow = np.concatenate([np.broadcast_to(np.exp((i + 1.0) * lgam[j])[None] * 64 ** -0.5, (64, 128)) for j in range(2)], 0)
    khcol = np.stack([np.exp((127.0 - i) * lgam[j]) for j in range(2)], 1)
    wcr = np.broadcast_to(np.array([math.exp(128 * lgam[j]) for j in range(2)])[None], (128, 2))
    inv = (10000.0 ** (-np.arange(0, 64, 2) / 64.0))[i % 32][:, None]
    sgn = np.where((i % 64) < 32, -1.0, 1.0)[:, None]
    rett = np.concatenate([DT[0], DT[1], QdRow, khcol, wcr, inv, sgn, np.zeros((128, 2))], 1).astype(f)
    return dict(xT=xT_b, pos=pos_b, wF=blk_in(wF).astype(f), wT=blk_in(wT).astype(f), convw=convw.astype(f),
                rowt=rowt, cst=cst, rett=rett)


import contextlib
import math
import numpy as np

CDEC = math.exp(-0.5)
NCOL = 1056


def run_rr(gens):
    gens = list(gens)
    while gens:
        for g_ in list(gens):
            try:
                next(g_)
            except StopIteration:
                gens.remove(g_)


def dpl_inverse(h, mi, mir, Na, Nr, Aa, Ar, N2a, N2r, A2a, A2r, Pa, Pr, ident, r_c):
    h.tt("dve", Pa[:], Na[:], ident, ALU.add, [Nr, r_c], [Pr])
    yield
    cur = (Na, Nr, Aa, Ar)
    nxt = (N2a, N2r, A2a, A2r)
    for lv in range(6):
        curN, curNr, curA, curAr = cur
        nxtN, nxtNr, nxtA, nxtAr = nxt
        h.mm(mi[:, 128:256], curN[:], curA[:], [curNr, curAr], [mir])
        if lv < 5:
            h.mm(mi[:, 256:384], curA[:], curN[:], [curNr, curAr], [mir])
        yield
        h.cp("act", nxtA[:], mi[:, 128:256], [mir], [nxtAr])
        if lv < 5:
            h.cp("dve", nxtN[:], mi[:, 256:384], [mir], [nxtNr])
        yield
        h.mm(mi[:, 0:128], nxtA[:], Pa[:], [nxtAr, Pr], [mir])
        yield
        h.tt("dve", Pa[:], Pa[:], mi[:, 0:128], ALU.add, [Pr, mir], [Pr])
        yield
        cur, nxt = nxt, cur


def build_a2(S, stage=9, nc=None, P=None, pre="", xT_ap=None):
    own = nc is None
    if own:
        nc = bass.Bass("TRN2", target_bir_lowering=False)
    dt = lambda name, *a, **k: nc.dram_tensor(pre + name, *a, **k)
    NCH = S // CH
    xT = xT_ap if xT_ap is not None else dt("xT", [D, S], F32, kind="ExternalInput").ap()
    wF = dt("wF", [128, KC, NCOL], F32, kind="ExternalInput").ap()
    lora = dt("lora", [128, 768], F32, kind="ExternalInput").ap()
    pvec = dt("pvec", [128, 32], F32, kind="ExternalInput").ap()
    rowt = dt("rowt", [128, 512], F32, kind="ExternalInput").ap()
    cst = dt("cst", [128, 6 * 128 + 2], F32, kind="ExternalInput").ap()
    om = dt("om", [S, 256], BF16, kind="ExternalOutput").ap()

    with contextlib.ExitStack() as st:
        if own:
            P = Prog(nc, st)
        else:
            P.mem = st
            P.pre = pre
        h = H(P)
        wF_b = P.sbuf("wF_b", [128, KC, NCOL], BF16)
        r_wF = P.region()
        stg = TB(P, "stg", [128, NCOL], F32)
        for kc in range(KC):
            a, r = stg(kc)
            P.dma("sp", a[:], wF[:, kc, :], writes=[r])
            h.cp("dve" if kc % 2 == 0 else "act", wF_b[:, kc, :], a[:], [r], [r_wF])
        lora_s = P.sbuf("lora_s", [128, 768], F32)
        lora_b = P.sbuf("lora_b", [128, 768], BF16)
        pvec_s = P.sbuf("pvec_s", [128, 32], F32)
        rowt_s = P.sbuf("rowt_s", [128, 512], F32)
        cst_s = P.sbuf("cst_s", [128, 6 * 128 + 2], F32)
        r_c = P.region()
        for a, b in ((lora_s, lora), (pvec_s, pvec), (rowt_s, rowt), (cst_s, cst)):
            P.dma("sp", a[:], b, writes=[r_c])
        h.cp("dve", lora_b[:], lora_s[:], [r_c], [r_c])
        h.ts("dve", pvec_s[:, 17:19], pvec_s[:, 15:17], -1.0, ALU.mult, [r_c], [r_c], s2=1.0, op1=ALU.add)
        ident, maskL, maskU, maskUs = (cst_s[:, i * 128:(i + 1) * 128] for i in range(4))
        ones_f = cst_s[:, 512:640]
        cb = P.sbuf("cb", [128, 3 * 128 + 2], BF16)
        h.cp("dve", cb[:, 0:128], ident, [r_c], [r_c])
        h.cp("dve", cb[:, 128:256], cst_s[:, 640:768], [r_c], [r_c])
        h.cp("dve", cb[:, 256:384], ones_f, [r_c], [r_c])
        h.cp("dve", cb[:, 384:386], cst_s[:, 768:770], [r_c], [r_c])
        ident_b, bones_b, ones_b, bsel_b = cb[:, 0:128], cb[:, 128:256], cb[:, 256:384], cb[:, 384:386]
        pv = lambda i: pvec_s[:, i:i + 1]
        MU, W0, A0, KK_, KA_, OMKA, RK_ = 0, 9, 11, 13, 15, 17, 19

        pFa = TB(P, "pFa", [128, 512], F32, n=2, psum=True)
        pM = TB(P, "pM", [128, 512], F32, n=2, psum=True)
        pI = TB(P, "pI", [128, 512], F32, n=1, psum=True)
        pB = TB(P, "pB", [128, 1024], BF16, n=1, psum=True)
        pS = TB(P, "pS", [128, 512], F32, n=1, psum=True)
        pG = TB(P, "pG", [128, 512], F32, n=1, psum=True)

        Sst = [P.sbuf(f"Sst{i}", [128, 64], F32) for i in range(2)]
        Sb = [P.sbuf(f"Sb{i}", [128, 64], BF16) for i in range(2)]
        r_Sst = [[P.region(), P.region()] for _ in range(2)]
        r_Sb = [[P.region(), P.region()] for _ in range(2)]
        for i in range(2):
            P.op("pool", lambda e: e.memset(Sst[i][:], 0.0), [], r_Sst[i])
            P.op("pool", lambda e: e.memset(Sb[i][:], 0.0), [], r_Sb[i])
        pbuf = P.sbuf("pbuf", [128, 9, 1 + CH], F32)
        r_pbuf = P.region()
        P.op("pool", lambda e: e.memset(pbuf[:], 0.0), [], [r_pbuf])

        def T_(name, shape, dt_=F32, n=2):
            return TB(P, name, shape, dt_, n)

        xs_f = T_("xs_f", [128, KC, CH])
        xb = T_("xb", [128, KC, CH], BF16)
        dif = T_("dif", [128, 9, CH])
        mix = T_("mix", [128, 9, CH])
        wab = T_("wab", [128, CH], BF16)
        sgb = T_("sgb", [128, 2, CH], BF16)
        Gt = T_("Gt", [128, 256])
        sig = T_("sig", [128, 2, CH])
        cs_ = T_("cs_", [128, 2, CH])
        csm = T_("csm", [128, 2, CH])
        ncc = T_("ncc", [128, 2])
        wcc = T_("wcc", [128, 2])
        E1 = T_("E1", [128, 2, CH]); E2 = T_("E2", [128, 2, CH]); E3 = T_("E3", [128, 2, CH]); E4 = T_("E4", [128, 2, CH])
        aa = T_("aa", [128, 2, CH])
        kkt = T_("kkt", [128, 2, CH])
        sqb = T_("sqb", [128, 2, CH], BF16)
        rs = T_("rs", [128, 2, CH])
        ka = T_("ka", [128, 2, CH])
        kp = T_("kp", [128, 2, CH])
        AR = T_("AR", [128, 2, 2, CH], BF16)
        Bt = T_("Bt", [128, 2, CH], BF16)
        Kt = T_("Kt", [128, 2, CH], BF16)
        Bh_ = T_("Bh_", [128, 2, CH], BF16)
        Kh_ = T_("Kh_", [128, 2, CH], BF16)
        vb = T_("vb", [128, 2, CH], BF16)
        rkr = T_("rkr", [128, 2, CH], BF16)
        bon = T_("bon", [128, 4])
        tokm = T_("tokm", [128, 2, 4, 128], BF16)
        N_ = T_("N_", [128, 128], F32, 4); A_ = T_("A_", [128, 128], F32, 4)
        N2 = T_("N2", [128, 128], F32, 4); A2 = T_("A2", [128, 128], F32, 4)
        Pm = T_("Pm", [128, 128], F32, 4); Pb = T_("Pb", [128, 128], BF16, 4)
        MrbT = T_("MrbT", [128, 128], BF16, 4); MakT = T_("MakT", [128, 128], BF16, 4); MrkT = T_("MrkT", [128, 128], BF16, 4)
        Z1 = T_("Z1", [128, 64], BF16, 4)
        X1 = T_("X1", [128, 64], F32, 4)
        AdT = T_("AdT", [128, CH], BF16, 4)
        Ut = T_("Ut", [128, 64], BF16, 4)
        bst = T_("bst", [128, 6], F32, 4)
        mv = T_("mv", [128, 2], F32, 4)
        yn = T_("yn", [128, 64], F32, 4)
        ot = T_("ot", [128, 256], BF16, 2)
        r_om = P.region()

        for ci in range(NCH):
            c0 = ci * CH
            xa_, xr = xs_f(ci)
            xba, xbr = xb(ci)
            P.dma("sp", xa_[:], xT[:, c0:c0 + CH].rearrange("(c p) n -> p c n", p=128), writes=[xr])
            h.cp("pool", xba[:, :KC // 2, :], xa_[:, :KC // 2, :], [xr], [xbr])
            h.cp("dve", xba[:, KC // 2:, :], xa_[:, KC // 2:, :], [xr], [xbr])
            fa, far = pFa(0)
            fb, fbr = pFa(1)
            gcols = [(i * 128, 128) for i in range(8)] + [(1024, 32)]
            for g, (cc, m) in enumerate(gcols):
                if g < 4:
                    dst, dr, oc = fa, far, g * 128
                elif g < 8:
                    dst, dr, oc = fb, fbr, (g - 4) * 128
                else:
                    dst, dr, oc = pG(0)[0], pG(0)[1], 256
                for kc in range(KC):
                    h.mm(dst[0:m, oc:oc + 128], wF_b[:, kc, cc:cc + m], xba[:, kc, :], [r_wF, xbr], [dr],
                         start=(kc == 0), stop=(kc == KC - 1))
            h.cp("act", pbuf[:, 0:4, 1:1 + CH], fa[:, :].rearrange("p (g n) -> p g n", g=4), [far], [r_pbuf])
            h.cp("act", pbuf[:, 4:8, 1:1 + CH], fb[:, :].rearrange("p (g n) -> p g n", g=4), [fbr], [r_pbuf])
            h.cp("act", pbuf[0:32, 8, 1:1 + CH], pG(0)[0][0:32, 256:384], [pG(0)[1]], [r_pbuf])
            da, dr_ = dif(ci)
            ma, mr = mix(ci)
            h.tt("dve", da[:], pbuf[:, :, 0:CH], pbuf[:, :, 1:1 + CH], ALU.subtract, [r_pbuf], [dr_])
            for g in range(9):
                h.stt(ma[:, g, :], da[:, g, :], pv(MU + g), pbuf[:, g, 1:1 + CH], ALU.mult, ALU.add, [dr_, r_pbuf, r_c], [mr])
            h.cp("pool", pbuf[:, :, 0:1], pbuf[:, :, CH:CH + 1], [r_pbuf], [r_pbuf])
            if stage == 1:
                ota, otr = ot(ci)
                h.cp('dve', ota[:], xba[:, 0:2, :].rearrange('p a b -> p (a b)'), [xbr, mr, r_pbuf], [otr])
                P.dma('pool', om[c0:c0 + CH, :], ota[:], reads=[otr], writes=[r_om])
                continue
            waa, war = wab(ci)
            h.act(waa[0:64, :], ma[0:64, 6, :], AF.Tanh, [mr], [war])
            h.cp("dve", waa[64:128, :], ma[64:128, 6, :], [mr], [war])
            sga, sgr = sgb(ci)
            h.act(sga[:, 0, :], ma[:, 7, :], AF.Sigmoid, [mr], [sgr])
            h.act(sga[0:32, 1, :], ma[0:32, 8, :], AF.Sigmoid, [mr], [sgr])
            if stage == 21:
                ota, otr = ot(ci)
                h.cp('dve', ota[:], xba[:, 0:2, :].rearrange('p a b -> p (a b)'), [xbr, war, sgr], [otr])
                P.dma('pool', om[c0:c0 + CH, :], ota[:], reads=[otr], writes=[r_om])
                continue
            m0, m0r = pM(0)
            m1, m1r = pM(1)
            siga, sigr = sig(ci)
            aaa, aar = aa(ci)
            for cg in range(2):
                h.mm(m0[:, cg * 128:(cg + 1) * 128], lora_b[0:64, cg * 128:(cg + 1) * 128], waa[0:64, :], [r_c, war], [m0r])
                h.mm(m1[:, 256 + cg * 128:256 + (cg + 1) * 128], lora_b[64:128, cg * 128:(cg + 1) * 128], waa[64:128, :], [r_c, war], [m1r])
            if stage == 22:
                ota, otr = ot(ci)
                h.cp('dve', ota[:], xba[:, 0:2, :].rearrange('p a b -> p (a b)'), [xbr, war, sgr, m0r, m1r], [otr])
                P.dma('pool', om[c0:c0 + CH, :], ota[:], reads=[otr], writes=[r_om])
                continue
            for cg in range(2):
                h.act(siga[:, cg, :], m0[:, cg * 128:(cg + 1) * 128], AF.Sigmoid, [m0r, r_c], [sigr], bias=pv(W0 + cg))
                h.act(aaa[:, cg, :], m1[:, 256 + cg * 128:256 + (cg + 1) * 128], AF.Sigmoid, [m1r, r_c], [aar], bias=pv(A0 + cg))
            if stage == 23:
                ota, otr = ot(ci)
                h.cp('dve', ota[:], xba[:, 0:2, :].rearrange('p a b -> p (a b)'), [xbr, sigr, aar], [otr])
                P.dma('pool', om[c0:c0 + CH, :], ota[:], reads=[otr], writes=[r_om])
                continue
            gps, gpr = pG(0)
            h.mm(gps[:, 0:256], sga[:, 0, :], lora_b[:, 256:512], [sgr, r_c], [gpr], start=True, stop=False)
            h.mm(gps[:, 0:256], sga[0:32, 1, :], lora_b[0:32, 512:768], [sgr, r_c], [gpr], start=False, stop=True)
            Gta, Gtr = Gt(ci)
            h.cp("act", Gta[:], gps[:, 0:256], [gpr], [Gtr])
            if stage == 2:
                ota, otr = ot(ci)
                h.cp('dve', ota[:], xba[:, 0:2, :].rearrange('p a b -> p (a b)'), [xbr, mr, sigr, aar, Gtr], [otr])
                P.dma('pool', om[c0:c0 + CH, :], ota[:], reads=[otr], writes=[r_om])
                continue
            csa, csr = cs_(ci)
            cma, cmr = csm(ci)
            for cg in range(2):
                P.op("dve", lambda e: e.tensor_tensor_scan(out=csa[:, cg, :], data0=ones_f, data1=siga[:, cg, :], initial=0.0,
                                                           op0=ALU.mult, op1=ALU.add), [sigr, r_c], [csr])
            h.tt("dve", cma[:], csa[:], siga[:], ALU.subtract, [csr, sigr], [cmr])
            nca, ncr = ncc(ci)
            wca, wcr = wcc(ci)
            h.ts("dve", nca[:], csa[:, :, CH - 1], -CDEC, ALU.mult, [csr], [ncr])
            h.act(wca[:], nca[:], AF.Exp, [ncr], [wcr])
            e1, e1r = E1(ci); e2, e2r = E2(ci); e3, e3r = E3(ci); e4, e4r = E4(ci)
            h.act(e1[:], csa[:], AF.Exp, [csr], [e1r], scale=-CDEC)
            h.act(e2[:], cma[:], AF.Exp, [cmr], [e2r], scale=-CDEC)
            h.act(e3[:], csa[:], AF.Exp, [csr], [e3r], scale=CDEC)
            for cg in range(2):
                h.act(e4[:, cg, :], csa[:, cg, :], AF.Exp, [csr, ncr], [e4r], scale=CDEC, bias=nca[:, cg:cg + 1])
            if stage == 3:
                ota, otr = ot(ci)
                h.cp('dve', ota[:], xba[:, 0:2, :].rearrange('p a b -> p (a b)'), [xbr, e1r, e2r, e3r, e4r, wcr], [otr])
                P.dma('pool', om[c0:c0 + CH, :], ota[:], reads=[otr], writes=[r_om])
                continue
            kka, kkr = kkt(ci)
            sqa, sqr = sqb(ci)
            rsa, rsr = rs(ci)
            kaa, kar = ka(ci)
            kpa, kpr = kp(ci)
            for cg in range(2):
                h.ts("dve", kka[:, cg, :], ma[:, 2 + cg, :], pv(KK_ + cg), ALU.mult, [mr, r_c], [kkr])
            h.act(sqa[:], kka[:], AF.Square, [kkr], [sqr])
            m1, m1r = pM(1)
            h.mm(m1[:, 0:256], bones_b, sqa[:].rearrange("p a b -> p (a b)"), [r_c, sqr], [m1r])
            h.rsqrt(rsa[:].rearrange("p a b -> p (a b)"), m1[:, 0:256], 1e-6, [m1r], [rsr])
            h.tt("dve", kka[:], kka[:], rsa[:], ALU.mult, [kkr, rsr], [kkr])
            h.tt("dve", kaa[:], kka[:], aaa[:], ALU.mult, [kkr, aar], [kar])
            for cg in range(2):
                h.ts("dve", kpa[:, cg, :], aaa[:, cg, :], pv(KA_ + cg), ALU.mult, [aar, r_c], [kpr], s2=pv(OMKA + cg), op1=ALU.add)
            h.tt("dve", kpa[:], kpa[:], ma[:, 2:4, :], ALU.mult, [kpr, mr], [kpr])
            if stage == 4:
                ota, otr = ot(ci)
                h.cp('dve', ota[:], xba[:, 0:2, :].rearrange('p a b -> p (a b)'), [xbr, kkr, kar, kpr], [otr])
                P.dma('pool', om[c0:c0 + CH, :], ota[:], reads=[otr], writes=[r_om])
                continue
            ARa, ARr = AR(ci); Bta, Btr = Bt(ci); Kta, Ktr = Kt(ci); Bha, Bhr = Bh_(ci); Kha, Khr = Kh_(ci)
            vba, vbr = vb(ci); rka, rkr_ = rkr(ci)
            for cg in range(2):
                h.stt(ARa[:, cg, 0, :], kka[:, cg, :], -1.0, e2[:, cg, :], ALU.mult, ALU.mult, [kkr, e2r], [ARr])
                h.stt(rka[:, cg, :], ma[:, cg, :], pv(RK_ + cg), kpa[:, cg, :], ALU.mult, ALU.mult, [mr, kpr, r_c], [rkr_])
            h.tt("dve", ARa[:, :, 1, :], ma[:, 0:2, :], e1[:], ALU.mult, [mr, e1r], [ARr])
            h.tt("dve", Bta[:], kaa[:], e3[:], ALU.mult, [kar, e3r], [Btr])
            h.tt("dve", Kta[:], kpa[:], e3[:], ALU.mult, [kpr, e3r], [Ktr])
            h.tt("pool", Bha[:], kaa[:], e4[:], ALU.mult, [kar, e4r], [Bhr])
            h.tt("pool", Kha[:], kpa[:], e4[:], ALU.mult, [kpr, e4r], [Khr])
            h.cp("act", vba[:], ma[:, 4:6, :], [mr], [vbr])
            for cg in range(2):
                h.mm(m1[:, 256 + 2 * cg:256 + 2 * cg + 2], rka[:, cg, :], bsel_b, [rkr_, r_c], [m1r])
            bona, bonr = bon(ci)
            h.cp("dve", bona[:], m1[:, 256:260], [m1r], [bonr])
            pb, pbr = pB(0)
            tka, tkr = tokm(ci)
            for cg in range(2):
                for j, (src, sr_) in enumerate(((ARa[:, cg, 0, :], ARr), (Bha[:, cg, :], Bhr), (Kha[:, cg, :], Khr), (vba[:, cg, :], vbr))):
                    h.tr(pb[:, (cg * 4 + j) * 128:(cg * 4 + j + 1) * 128], src, ident_b, [sr_, r_c], [pbr])
            h.cp("act", tka[:].rearrange("p a b c -> p (a b c)"), pb[:, 0:1024], [pbr], [tkr])

            if stage == 5:
                ota, otr = ot(ci)
                h.cp('dve', ota[:], xba[:, 0:2, :].rearrange('p a b -> p (a b)'), [xbr, tkr, bonr], [otr])
                P.dma('pool', om[c0:c0 + CH, :], ota[:], reads=[otr], writes=[r_om])
                continue
            ota, otr = ot(ci)
            Mbank = [pM(0), pM(1), pFa(0), pFa(1)]
            Ibank = [pI(0), pG(0), pS(0), (pB(0)[0][:, :].bitcast(F32), pB(0)[1])]

            def head_gen(hd):
                cg, j = hd // 2, hd % 2
                hs = slice(j * 64, (j + 1) * 64)
                k4 = 4 * ci + hd
                mm_, mmr = Mbank[hd]
                h.mm(mm_[:, 0:256], Bta[hs, cg, :], ARa[hs, cg, :, :].rearrange("p a b -> p (a b)"), [Btr, ARr], [mmr])
                h.mm(mm_[:, 256:512], Kta[hs, cg, :], ARa[hs, cg, :, :].rearrange("p a b -> p (a b)"), [Ktr, ARr], [mmr])
                mi, mir = Ibank[hd]
                h.mm(mi[:, 384:512], ARa[hs, cg, 0, :], Bta[hs, cg, :], [ARr, Btr], [mir])
                yield
                Na, Nr = N_(k4); Aa, Ar = A_(k4)
                h.tt("dve", Na[:], mm_[:, 0:128], maskUs, ALU.mult, [mmr, r_c], [Nr])
                h.tt("dve", Aa[:], mi[:, 384:512], maskL, ALU.mult, [mir, r_c], [Ar])
                Mrb, Mrbr = MrbT(k4); Mak, Makr = MakT(k4); Mrk, Mrkr = MrkT(k4)
                h.tt("dve", Mrb[:], mm_[:, 128:256], maskU, ALU.mult, [mmr, r_c], [Mrbr])
                h.tt("dve", Mak[:], mm_[:, 256:384], maskUs, ALU.mult, [mmr, r_c], [Makr])
                h.tt("dve", Mrk[:], mm_[:, 384:512], maskU, ALU.mult, [mmr, r_c], [Mrkr])
                yield
                Pa, Pr = Pm(k4)
                yield from dpl_inverse(h, mi, mir, Na, Nr, Aa, Ar, N2(k4)[0], N2(k4)[1], A2(k4)[0], A2(k4)[1], Pa, Pr, ident, r_c)
                Pba, Pbr = Pb(k4)
                h.cp("act", Pba[:], Pa[:], [Pr], [Pbr])
                yield
                Vh = tka[:, cg, 3, hs]
                h.mm(mi[:, 0:64], Mak[:], Vh, [Makr, tkr], [mir])
                h.mm(mi[hs, 128:256], tka[:, cg, 0, hs], Pba[:], [tkr, Pbr], [mir])
                yield
                Z1a, Z1r = Z1(k4)
                h.cp("act", Z1a[:], mi[:, 0:64], [mir], [Z1r])
                AdTa, AdTr = AdT(k4)
                h.cp("act", AdTa[hs, :], mi[hs, 128:256], [mir], [AdTr])
                yield
                h.mm(mi[:, 64:128], Pba[:], Z1a[:], [Pbr, Z1r], [mir])
                yield
                X1a, X1r = X1(k4)
                h.cp("act", X1a[:], mi[:, 64:128], [mir], [X1r])
                yield
                sa, sr = mm_, mmr
                h.mm(sa[:, 0:64], AdTa[hs, :], Sb[cg][hs, :], [AdTr, r_Sb[cg][j]], [sr])
                yield
                Uta, Utr = Ut(k4)
                h.tt("dve", Uta[:], sa[:, 0:64], X1a[:], ALU.add, [sr, X1r], [Utr])
                yield
                h.mm(sa[:, 64:128], ARa[hs, cg, 1, :], Sb[cg][hs, :], [ARr, r_Sb[cg][j]], [sr], start=True, stop=False)
                h.mm(sa[:, 64:128], Mrb[:], Uta[:], [Mrbr, Utr], [sr], start=False, stop=False)
                h.mm(sa[:, 64:128], Mrk[:], Vh, [Mrkr, tkr], [sr], start=False, stop=True)
                h.mm(sa[hs, 128:192], tka[:, cg, 1, hs], Uta[:], [tkr, Utr], [sr], start=True, stop=False)
                h.mm(sa[hs, 128:192], tka[:, cg, 2, hs], Vh, [tkr], [sr], start=False, stop=True)
                yield
                h.stt(Sst[cg][hs, :], Sst[cg][hs, :], wca[hs, cg:cg + 1], sa[hs, 128:192], ALU.mult, ALU.add,
                      [r_Sst[cg][j], wcr, sr], [r_Sst[cg][j]])
                h.cp("act", Sb[cg][hs, :], Sst[cg][hs, :], [r_Sst[cg][j]], [r_Sb[cg][j]])
                yield
                ba2, br2 = bst(k4)
                mva, mvr = mv(k4)
                P.op("dve", lambda e: e.bn_stats(out=ba2[:], in_=sa[:, 64:128]), [sr], [br2])
                P.op("dve", lambda e: e.bn_aggr(out=mva[:], in_=ba2[:]), [br2], [mvr])
                yield
                h.rsqrt(mva[:, 1:2], mva[:, 1:2], 64e-5, [mvr], [mvr])
                yield
                yna, ynr = yn(k4)
                h.ts("dve", yna[:], sa[:, 64:128], mva[:, 0:1], ALU.subtract, [sr, mvr], [ynr], s2=mva[:, 1:2], op1=ALU.mult)
                yield
                h.tt("dve", yna[:], yna[:], rowt_s[:, hd * 64:(hd + 1) * 64], ALU.mult, [ynr, r_c], [ynr])
                yield
                h.tt("pool", yna[:], yna[:], rowt_s[:, 256 + hd * 64:256 + (hd + 1) * 64], ALU.add, [ynr, r_c], [ynr])
                yield
                h.stt(yna[:], Vh, bona[:, hd:hd + 1], yna[:], ALU.mult, ALU.add, [tkr, bonr, ynr], [ynr])
                yield
                h.tt("dve", ota[:, hd * 64:(hd + 1) * 64], yna[:], Gta[:, hd * 64:(hd + 1) * 64], ALU.mult, [ynr, Gtr], [otr])
            run_rr([head_gen(0), head_gen(1), head_gen(2), head_gen(3)])
            P.dma("pool", om[c0:c0 + CH, :], ota[:], reads=[otr], writes=[r_om])
        P.finish([r_om], "sp")
        if not own:
            P.barrier()
        print("A2 ninstr", P.ninstr, P.cnt)
    return nc


def a2_inputs(xT_b, hg, w_in, mu, w0, w2, a0, a2, g2, k_k, k_a, r_k, gn_w, gn_b):
    f = np.float32
    M0 = 640
    ch = slice(hg * 256, (hg + 1) * 256)
    colsel = np.concatenate([M0 + np.arange(hg * 256, (hg + 1) * 256), M0 + 1024 + np.arange(hg * 256, (hg + 1) * 256),
                             M0 + 2048 + np.arange(hg * 256, (hg + 1) * 256), M0 + 3072 + np.arange(288)])
    wF = w_in[:, colsel]
    wFb = np.ascontiguousarray(wF.reshape(KC, 128, -1).transpose(1, 0, 2)).astype(f)
    mu_c = mu[colsel - M0]
    pvec = np.zeros((128, 32), f)
    for g in range(8):
        pvec[:, g] = mu_c[g * 128:(g + 1) * 128]
    pvec[:32, 8] = mu_c[1024:1056]
    for cg in range(2):
        sl = slice(hg * 256 + cg * 128, hg * 256 + (cg + 1) * 128)
        pvec[:, 9 + cg] = w0[sl]
        pvec[:, 11 + cg] = a0[sl]
        pvec[:, 13 + cg] = k_k[sl]
        pvec[:, 15 + cg] = k_a[sl]
        pvec[:, 19 + cg] = r_k.reshape(-1)[sl]
    lora = np.zeros((128, 768), f)
    lora[0:64, 0:256] = w2[:, ch]
    lora[64:128, 0:256] = a2[:, ch]
    lora[:, 256:512] = g2[0:128, ch]
    lora[0:32, 512:768] = g2[128:160, ch]
    rowt = np.broadcast_to(np.concatenate([gn_w[ch], gn_b[ch]])[None], (128, 512)).astype(f)
    i = np.arange(128)
    ident = np.eye(128, dtype=f)
    maskL = (i[:, None] > i[None, :]).astype(f)
    maskU = (i[:, None] <= i[None, :]).astype(f)
    maskUs = (i[:, None] < i[None, :]).astype(f)
    bones = ((i[:, None] // 64) == (i[None, :] // 64)).astype(f)
    bsel = np.stack([(i // 64 == 0), (i // 64 == 1)], 1).astype(f)
    cst = np.concatenate([ident, maskL, maskU, maskUs, np.ones((128, 128), f), bones, bsel], 1)
    return dict(xT=xT_b, wF=wFb, lora=lora, pvec=pvec, rowt=rowt, cst=cst)


import contextlib
import math
import numpy as np

ST = 512
SCALE = 192.0 ** -0.5
NIN = 704


def build_a1(S, nc=None, P=None, pre="", xT_ap=None):
    own = nc is None
    if own:
        nc = bass.Bass("TRN2", target_bir_lowering=False)
    dt = lambda name, *a, **k: nc.dram_tensor(pre + name, *a, **k)
    NST = S // ST
    NB = S // 128
    xT = xT_ap if xT_ap is not None else dt("xT", [D, S], F32, kind="ExternalInput").ap()
    pos = dt("pos", [1, S], I32, kind="ExternalInput").ap()
    wF = dt("wF", [128, KC, NIN], F32, kind="ExternalInput").ap()
    wuq = dt("wuq", [128, 4, 512], F32, kind="ExternalInput").ap()
    wkv = dt("wkv", [128, 512], F32, kind="ExternalInput").ap()
    pvec = dt("pvec", [128, 8], F32, kind="ExternalInput").ap()
    cst = dt("cst", [128, 3 * 128], F32, kind="ExternalInput").ap()
    om = dt("om", [S, 256], BF16, kind="ExternalOutput").ap()

    with contextlib.ExitStack() as st:
        if own:
            P = Prog(nc, st)
        else:
            P.mem = st
            P.pre = pre
        h = H(P)
        wF_b = P.sbuf("wF_b", [128, KC, NIN], BF16)
        r_wF = P.region()
        stg = TB(P, "stg", [128, NIN], F32)
        for kc in range(KC):
            a, r = stg(kc)
            P.dma("sp", a[:], wF[:, kc, :], writes=[r])
            h.cp("dve" if kc % 2 == 0 else "act", wF_b[:, kc, :], a[:], [r], [r_wF])
        wuq_b = P.sbuf("wuq_b", [128, 4, 512], BF16)
        wkv_b = P.sbuf("wkv_b", [128, 512], BF16)
        wuq_fl = wuq.rearrange("p g c -> p (g c)")
        wuq_bfl = wuq_b[:].rearrange("p g c -> p (g c)")
        r_wq = P.region()
        for i, c0_ in enumerate(range(0, 2048, 512)):
            a, r = stg(i)
            P.dma("sp", a[:, 0:512], wuq_fl[:, c0_:c0_ + 512], writes=[r])
            h.cp("dve", wuq_bfl[:, c0_:c0_ + 512], a[:, 0:512], [r], [r_wq])
        a, r = stg(4)
        P.dma("sp", a[:, 0:512], wkv, writes=[r])
        h.cp("dve", wkv_b[:], a[:, 0:512], [r], [r_wq])
        pvec_s = P.sbuf("pvec_s", [128, 8], F32)
        cst_s = P.sbuf("cst_s", [128, 384], F32)
        cb = P.sbuf("cb", [128, 384], BF16)
        r_c = P.region()
        for a, b in ((pvec_s, pvec), (cst_s, cst)):
            P.dma("sp", a[:], b, writes=[r_c])
        P.op("dve", lambda e: e.tensor_copy(out=pvec_s[:, 7:8], in_=pvec_s[:, 7:8]), [r_c, r_wq], [r_c])
        h.cp("dve", cb[:], cst_s[:], [r_c], [r_c])
        ident_b, maskU_b, ones_b = cb[:, 0:128], cb[:, 128:256], cb[:, 256:384]
        pv = lambda i: pvec_s[:, i:i + 1]
        inv_s, sgn_s = pv(5), pv(6)

        Kc = P.sbuf("Kc", [128, S], BF16)
        Kpe = P.sbuf("Kpe", [128, S], BF16)
        Va = P.sbuf("Va", [128, NB, 130], BF16)
        r_Kc, r_Kpe, r_Va = P.regions(NST), P.regions(NST), P.regions(NST)
        rK_all = P.region()
        P.op("pool", lambda e: e.memset(Kpe[64:65, :], 1.0), [], [rK_all])
        P.op("pool", lambda e: e.memset(Va[:, :, 128:129], 1.0), [], [rK_all])
        kmax2 = P.sbuf("kmax2", [128, 1], F32)
        r_kmax2 = P.region()
        P.op("pool", lambda e: e.memset(kmax2[:], 0.0), [], [r_kmax2])

        pF = TB(P, "pF", [128, 512], F32, n=2, psum=True)
        pSs = TB(P, "pSs", [128, 512], F32, n=2, psum=True)
        pO = TB(P, "pO", [128, 512], F32, n=2, psum=True)
        pM = TB(P, "pM", [128, 512], F32, n=1, psum=True)
        pB = TB(P, "pB", [128, 1024], BF16, n=1, psum=True)

        def T_(name, shape, dt_=F32, n=2):
            return TB(P, name, shape, dt_, n)

        xstg = T_("xstg", [128, ST], F32, 2)
        xb = T_("xb", [128, KC, ST], BF16, 1)
        posi = T_("posi", [128, ST], I32, 1)
        posf = T_("posf", [128, ST], F32, 1)
        ang = T_("ang", [128, ST], F32, 1); uu = T_("uu", [128, ST], F32, 1); ki = posi; kf = posf
        cosT = T_("cosT", [128, ST], F32, 1); sinT = T_("sinT", [128, ST], F32, 1)
        cq_f = T_("cq_f", [128, 4, ST], BF16, 1)
        sq = T_("sq", [128, ST], BF16, 2)
        rstd = T_("rstd", [128, ST], F32, 1)
        cqn = T_("cqn", [128, 4, ST], BF16, 1)
        ckv_f = T_("ckv_f", [128, ST], F32, 1)
        kpf = T_("kpf", [64, 2, ST], F32, 1)
        kpr = T_("kpr", [64, ST], F32, 1)
        tmp64 = T_("tmp64", [64, ST], F32, 1)
        kmx = T_("kmx", [128, 1], F32, 1)
        kmaxn = T_("kmaxn", [128, 1], F32, 1)
        qn_b = T_("qn_b", [128, ST], BF16, 2)
        qpf = kpf
        qpr = kpr
        Qabs = T_("Qabs", [128, ST], BF16, 2)
        Qpe = T_("Qpe", [128, ST], BF16, 2)
        qnrm = T_("qnrm", [128, ST], F32, 1)
        PTh = [T_("PT0", [128, ST], BF16, 2), T_("PT1", [128, ST], BF16, 2)]
        rec = T_("rec", [128, 1], F32, 8)
        olat = T_("olat", [128, 128], BF16, 8)
        olT = T_("olT", [128, 128], BF16, 8)
        ot = T_("ot", [128, 4, 256], BF16, 1)
        r_om = P.region()

        def rope_tab(q0):
            pia, pir = posi(0); pfa, pfr = posf(0)
            P.dma("sp", pia[:], pos[:, q0:q0 + ST].partition_broadcast(128), writes=[pir])
            h.cp("dve", pfa[:], pia[:], [pir], [pfr])
            aa, ar = ang(0); ua, ur = uu(0); kia, kir = ki(0); kfa, kfr = kf(0)
            h.ts("dve", aa[:], pfa[:], inv_s, ALU.mult, [pfr, r_c], [ar])
            for (dst, off, bias) in ((sinT(0), 0.0, 0.0), (cosT(0), 0.25, math.pi / 2)):
                da, dr = dst
                h.ts("dve", ua[:], aa[:], 1.0 / TWO_PI, ALU.mult, [ar], [ur], s2=off, op1=ALU.add)
                h.cp("dve", kia[:], ua[:], [ur], [kir])
                h.cp("dve", kfa[:], kia[:], [kir], [kfr])
                h.stt(ua[:], kfa[:], -CW1, aa[:], ALU.mult, ALU.add, [kfr, ar], [ur])
                h.stt(ua[:], kfa[:], -CW2, ua[:], ALU.mult, ALU.add, [kfr, ur], [ur])
                if bias != 0.0:
                    h.ts("dve", ua[:], ua[:], bias, ALU.add, [ur], [ur])
                h.act(da[:], ua[:], AF.Sin, [ur], [dr], scale=1.0 - 2e-6)
            sa_, sr_ = sinT(0)
            h.ts("dve", sa_[:], sa_[:], sgn_s, ALU.mult, [sr_, r_c], [sr_])

        def rope_apply(dst, dr, src2, sr2):
            ca, cr = cosT(0); sa_, sr_ = sinT(0)
            ta, tr_ = tmp64(0)
            h.tt("dve", dst, src2[:, 0, :], ca[0:64, :], ALU.mult, [sr2, cr], [dr])
            h.tt("pool", ta[:], src2[:, 1, :], sa_[0:64, :], ALU.mult, [sr2, sr_], [tr_])
            h.tt("dve", dst, dst, ta[:], ALU.add, [dr, tr_], [dr])

        psi = [0]

        def inproj(c0, m, evac):
            pa, pr = pF(psi[0]); psi[0] += 1
            xba, xbr = xb(0)
            for kc in range(KC):
                h.mm(pa[0:m, :], wF_b[:, kc, c0:c0 + m], xba[:, kc, :], [r_wF, xbr], [pr], start=(kc == 0), stop=(kc == KC - 1))
            evac(pa, pr)

        for Q in range(NST):
            q0 = Q * ST
            xba, xbr = xb(0)
            for kc in range(KC):
                sa_, sr_ = xstg(kc)
                P.dma("sp", sa_[:], xT[kc * 128:(kc + 1) * 128, q0:q0 + ST], writes=[sr_])
                h.cp(("dve", "pool", "act")[kc % 3], xba[:, kc, :], sa_[:], [sr_], [xbr])
            rope_tab(q0)
            cqa, cqr = cq_f(0)
            m0, m0r = pM(0)
            gsz = (128, 128, 128, 64)
            for g in range(4):
                def ev(pa, pr, g=g):
                    m = gsz[g]
                    h.cp("act", cqa[0:m, g, :], pa[0:m, :], [pr], [cqr])
                    sqa, sqr = sq(g)
                    h.act(sqa[0:m, :], pa[0:m, :], AF.Square, [pr], [sqr])
                    h.mm(m0[:, :], ones_b[0:m, :], sqa[0:m, :], [r_c, sqr], [m0r], start=(g == 0), stop=(g == 3))
                inproj(g * 128, gsz[g], ev)
            rsa, rsr = rstd(0)
            h.rsqrt(rsa[:], m0[:, :], 1e-6, [m0r], [rsr], scale=1.0 / 448)
            cna, cnr = cqn(0)
            for g in range(4):
                m = gsz[g]
                h.stt(cna[0:m, g, :], cqa[0:m, g, :], pvec_s[0:m, g:g + 1], rsa[0:m, :], ALU.mult, ALU.mult, [cqr, rsr, r_c], [cnr])
            cka, ckr = ckv_f(0)

            def ev_kv(pa, pr):
                h.cp("act", cka[:], pa[:, :], [pr], [ckr])
                sqa, sqr = sq(0)
                h.act(sqa[:], pa[:, :], AF.Square, [pr], [sqr])
                h.mm(m0[:, :], ones_b, sqa[:], [r_c, sqr], [m0r])
            inproj(448, 128, ev_kv)
            h.rsqrt(rsa[:], m0[:, :], 1e-6, [m0r], [rsr], scale=1.0 / 128)
            h.stt(Kc[:, q0:q0 + ST], cka[:], pv(4), rsa[:], ALU.mult, ALU.mult, [ckr, rsr, r_c, rK_all], [r_Kc[Q]])
            pb, pbr = pB(0)
            for j in range(4):
                h.tr(pb[:, j * 128:(j + 1) * 128], Kc[:, q0 + j * 128:q0 + (j + 1) * 128], ident_b, [r_Kc[Q], r_c], [pbr])
            h.cp("act", Va[:, 4 * Q:4 * Q + 4, 0:128], pb[:, 0:512].rearrange("p (j n) -> p j n", j=4), [pbr, rK_all], [r_Va[Q]])
            kpa, kpr_ = kpf(0)
            inproj(576, 64, lambda pa, pr: h.cp("act", kpa[:, 0, :], pa[0:64, :], [pr], [kpr_]))
            inproj(640, 64, lambda pa, pr: h.cp("act", kpa[:, 1, :], pa[0:64, :], [pr], [kpr_]))
            kra, krr = kpr(0)
            rope_apply(kra[:], krr, kpa, kpr_)
            h.cp("act", Kpe[0:64, q0:q0 + ST], kra[:], [krr, rK_all], [r_Kpe[Q]])
            sqa, sqr = sq(0)
            h.act(sqa[:], Kc[:, q0:q0 + ST], AF.Square, [r_Kc[Q]], [sqr])
            sqb_, sqbr = sq(1)
            h.act(sqb_[0:64, :], Kpe[0:64, q0:q0 + ST], AF.Square, [r_Kpe[Q]], [sqbr])
            h.mm(m0[:, :], ones_b, sqa[:], [r_c, sqr], [m0r], start=True, stop=False)
            h.mm(m0[:, :], ones_b[0:64, :], sqb_[0:64, :], [r_c, sqbr], [m0r], start=False, stop=True)
            kma, kmr = kmx(0)
            P.op("dve", lambda e: e.reduce_max(out=kma[:], in_=m0[:, :], axis=AX.X), [m0r], [kmr])
            h.tt("dve", kmax2[:], kmax2[:], kma[:], ALU.max, [r_kmax2, kmr], [r_kmax2])
            kna, knr = kmaxn(0)
            h.act(kna[:], kmax2[:], AF.Sqrt, [r_kmax2], [knr])
            h.ts("dve", kna[:], kna[:], -1.0, ALU.mult, [knr], [knr])

            ota, otr = ot(Q)
            for hh in range(2):
                wc0 = hh * 256
                qna, qnr = qn_b(hh)
                pa, pr = pF(psi[0]); psi[0] += 1
                for g in range(4):
                    m = gsz[g]
                    h.mm(pa[:, :], wuq_b[0:m, g, wc0:wc0 + 128], cna[0:m, g, :], [r_c, cnr], [pr], start=(g == 0), stop=(g == 3))
                h.cp("act", qna[:], pa[:, :], [pr], [qnr])
                qpa, qpr_ = qpf(0)
                for w in range(2):
                    pa2, pr2 = pF(psi[0]); psi[0] += 1
                    for g in range(4):
                        m = gsz[g]
                        h.mm(pa2[0:64, :], wuq_b[0:m, g, wc0 + 128 + w * 64:wc0 + 192 + w * 64], cna[0:m, g, :], [r_c, cnr], [pr2],
                             start=(g == 0), stop=(g == 3))
                    h.cp("act", qpa[:, w, :], pa2[0:64, :], [pr2], [qpr_])
                qra, qrr = qpr(0)
                rope_apply(qra[:], qrr, qpa, qpr_)
                Qpa, Qpr = Qpe(hh)
                h.cp("act", Qpa[0:64, :], qra[:], [qrr], [Qpr])
                pa3, pr3 = pF(psi[0]); psi[0] += 1
                h.mm(pa3[:, :], wkv_b[:, hh * 128:(hh + 1) * 128], qna[:], [r_c, qnr], [pr3])
                Qaa, Qar = Qabs(hh)
                h.cp("act", Qaa[:], pa3[:, :], [pr3], [Qar])
                sqa, sqr = sq(0)
                h.act(sqa[:], Qaa[:], AF.Square, [Qar], [sqr])
                sqb_, sqbr = sq(1)
                h.act(sqb_[0:64, :], Qpa[0:64, :], AF.Square, [Qpr], [sqbr])
                h.mm(m0[:, :], ones_b, sqa[:], [r_c, sqr], [m0r], start=True, stop=False)
                h.mm(m0[:, :], ones_b[0:64, :], sqb_[0:64, :], [r_c, sqbr], [m0r], start=False, stop=True)
                qma, qmr = qnrm(0)
                h.act(qma[:], m0[:, :], AF.Sqrt, [m0r], [qmr])
                h.ts("dve", Qpa[64:65, :], qma[64:65, :], kna[64:65, 0:1], ALU.mult, [qmr, knr], [Qpr])

            nfull = 4 * Q
            nblk = nfull + 4
            pbf = (pB(0)[0][:, :].bitcast(F32), pB(0)[1])
            Sbanks = [[pSs(0), pSs(1)], [pF(0), pF(1)]]
            Obanks = [[pO(0), pO(1)], [pM(0), pbf]]
            Oaccs = []
            for hh in range(2):
                (o0, o0r), (o1, o1r) = Obanks[hh]
                Oaccs.append([(o0, o0r, 0), (o0, o0r, 129), (o1, o1r, 0), (o1, o1r, 129)])

            def attn_gen(hh):
                Qaa, Qar = Qabs(hh)
                Qpa, Qpr = Qpe(hh)
                Oacc = Oaccs[hh]
                started = [False, False]

                def qk(jb):
                    jj = max(0, jb - nfull)
                    qc0 = jj * 128
                    s_, sr_ = Sbanks[hh][jb % 2]
                    kb = slice(jb * 128, (jb + 1) * 128)
                    Qi = jb // 4
                    h.mm(s_[:, qc0:ST], Kc[:, kb], Qaa[:, qc0:ST], [r_Kc[Qi], Qar], [sr_], start=True, stop=False)
                    h.mm(s_[:, qc0:ST], Kpe[0:65, kb], Qpa[0:65, qc0:ST], [r_Kpe[Qi], rK_all, Qpr], [sr_], start=False, stop=True)

                qk(0)
                yield
                for jb in range(nblk):
                    if jb + 1 < nblk:
                        qk(jb + 1)
                        yield
                    jj = max(0, jb - nfull)
                    qc0 = jj * 128
                    s_, sr_ = Sbanks[hh][jb % 2]
                    Qi = jb // 4
                    pta, ptr = PTh[hh](jb)
                    h.act(pta[:, qc0:ST], s_[:, qc0:ST], AF.Exp, [sr_], [ptr], scale=SCALE)
                    if jb >= nfull:
                        h.tt("dve", pta[:, qc0:qc0 + 128], pta[:, qc0:qc0 + 128], maskU_b, ALU.mult, [ptr, r_c], [ptr])
                    yield
                    for qs in range(jj, 4):
                        oa, orr, oc = Oacc[qs]
                        bank = qs // 2
                        st_ = not started[bank]
                        started[bank] = True
                        P.op("pe", lambda e: e.matmul(oa[:, oc:oc + 129], lhsT=pta[:, qs * 128:(qs + 1) * 128], rhs=Va[:, jb, 0:129],
                                                      start=st_, stop=(jb == nfull + qs), skip_group_check=True),
                             [ptr, r_Va[Qi], rK_all], [orr])
                    yield

            run_rr([attn_gen(0), attn_gen(1)])
            for hh in range(2):
                Oacc = Oaccs[hh]
                for qs in range(4):
                    oa, orr, oc = Oacc[qs]
                    k4 = 4 * hh + qs
                    ra, rr = rec(k4)
                    P.op("dve", lambda e: e.reciprocal(out=ra[:], in_=oa[:, oc + 128:oc + 129]), [orr], [rr])
                    ola, olr = olat(k4)
                    h.ts("dve", ola[:], oa[:, oc:oc + 128], ra[:, 0:1], ALU.mult, [orr, rr], [olr])
            for hh in range(2):
                for qs in range(4):
                    k4 = 4 * hh + qs
                    ola, olr = olat(k4)
                    s_, sr_ = pSs(qs)
                    sb16 = s_[:, :].bitcast(BF16)
                    h.tr(sb16[:, 0:128], ola[:], ident_b, [olr, r_c], [sr_])
                    olTa, olTr = olT(k4)
                    h.cp("act", olTa[:], sb16[:, 0:128], [sr_], [olTr])
                    h.mm(s_[:, 256:384], olTa[:], wkv_b[:, 256 + hh * 128:256 + (hh + 1) * 128], [olTr, r_c], [sr_])
                    h.cp("act", ota[:, qs, hh * 128:(hh + 1) * 128], s_[:, 256:384], [sr_], [otr])
            P.dma("pool", om[q0:q0 + ST, :].rearrange("(j p) n -> p j n", p=128), ota[:], reads=[otr], writes=[r_om])
        P.finish([r_om], "sp")
        if not own:
            P.barrier()
        print("A1 ninstr", P.ninstr, P.cnt)
    return nc


def a1_inputs(xT_b, pos_b, hg, w_in, q_norm, w_uq, kv_norm, w_ukv):
    f = np.float32
    kpe = w_in[:, 576:640]
    kpe_sw = np.concatenate([kpe[:, 32:], kpe[:, :32]], 1)
    wF = np.concatenate([w_in[:, 0:576], kpe, kpe_sw], 1)
    wFb = np.ascontiguousarray(wF.reshape(KC, 128, -1).transpose(1, 0, 2)).astype(f)
    wq = np.zeros((512, 512), f)
    for j in range(2):
        hd = 2 * hg + j
        nope = w_uq[:, hd * 192:hd * 192 + 128]
        pe = w_uq[:, hd * 192 + 128:hd * 192 + 192]
        pesw = np.concatenate([pe[:, 32:], pe[:, :32]], 1)
        wq[:448, j * 256:(j + 1) * 256] = np.concatenate([nope, pe, pesw], 1)
    wuq = np.ascontiguousarray(wq.reshape(4, 128, 512).transpose(1, 0, 2))
    wkv = np.concatenate([w_ukv[:, (2 * hg + j) * 256:(2 * hg + j) * 256 + 128].T for j in range(2)] +
                         [w_ukv[:, (2 * hg + j) * 256 + 128:(2 * hg + j) * 256 + 256] for j in range(2)], 1).astype(f)
    i = np.arange(128)
    pvec = np.zeros((128, 8), f)
    qn = np.zeros(512, f); qn[:448] = q_norm
    pvec[:, 0:4] = qn.reshape(4, 128).T
    pvec[:, 4] = kv_norm
    pvec[:, 5] = (10000.0 ** (-np.arange(0, 64, 2) / 64.0))[i % 32]
    pvec[:, 6] = np.where((i % 64) < 32, -1.0, 1.0)
    cst = np.concatenate([np.eye(128, dtype=f), (i[:, None] <= i[None, :]).astype(f), np.ones((128, 128), f)], 1)
    return dict(xT=xT_b, pos=pos_b, wF=wFb, wuq=wuq, wkv=np.ascontiguousarray(wkv), pvec=pvec, cst=cst)


import contextlib
import numpy as np

D = 2048
FF = 5632
KC = D // 128
HC = FF // 128
ALPHA_ = (2 * 2) ** 0.25
LN_EPS_ = 1e-5


def cast_weight(P, src, dst, dst_reg, stage, stage_bf, nrows, ncols_total, qi=[0]):
    nblk = src.shape[0]
    for b in range(nblk):
        i = qi[0] % len(stage)
        qi[0] += 1
        sa, sr = stage[i]
        ba, br = stage_bf[i]
        cols = src.shape[2]
        P.dma("sp", sa[:, :cols], src[b], writes=[sr])
        eng = ("dve", "act", "pool")[b % 3]
        if eng == "act":
            P.op("act", lambda e: e.activation(out=ba[:, :cols], in_=sa[:, :cols], func=AF.Copy), reads=[sr], writes=[br])
        else:
            P.op(eng, lambda e: e.tensor_copy(out=ba[:, :cols], in_=sa[:, :cols]), reads=[sr], writes=[br])
        P.dma("pool", dst[b], ba[:, :cols], reads=[br], writes=[dst_reg])


def build_tail(T, ntile=512):
    nc = bass.Bass("TRN2", target_bir_lowering=False)
    dt = nc.dram_tensor
    omT = dt("omT", [D, 2 + T], BF16, kind="ExternalInput").ap()
    xT = dt("xT", [D, 2 + T], F32, kind="ExternalInput").ap()
    hmask = dt("hmask", [128, 1], F32, kind="ExternalInput").ap()
    w_out = dt("w_out", [KC, 128, KC * 128], F32, kind="ExternalInput").ap()
    w_gate = dt("w_gate", [HC, 128, KC * 128], F32, kind="ExternalInput").ap()
    w_val = dt("w_val", [HC, 128, KC * 128], F32, kind="ExternalInput").ap()
    w_down = dt("w_down", [KC, 128, HC * 128], F32, kind="ExternalInput").ap()
    vecs = dt("vecs", [128, 4 * KC], F32, kind="ExternalInput").ap()
    cvec = dt("cvec", [128, 4 * HC], F32, kind="ExternalInput").ap()
    yT = dt("yT", [D, T], F32, kind="ExternalOutput").ap()
    wo_b = dt("wo_b", [KC, 128, KC * 128], BF16, kind="Internal").ap()
    wg_b = dt("wg_b", [HC, 128, KC * 128], BF16, kind="Internal").ap()
    wv_b = dt("wv_b", [HC, 128, KC * 128], BF16, kind="Internal").ap()
    wd_b = dt("wd_b", [KC, 128, HC * 128], BF16, kind="Internal").ap()

    with contextlib.ExitStack() as st:
        P = Prog(nc, st)
        NT = ntile
        om_b = P.sbuf("om_b", [128, KC, NT], BF16)
        xs = P.sbuf("xs", [128, KC, NT], F32)
        x1b = P.sbuf("x1b", [128, KC, NT], BF16)
        actb = P.sbuf("actb", [128, HC, NT], BF16)
        NWB = 3
        wgb = [P.sbuf(f"wgb{i}", [128, KC * 128], BF16) for i in range(NWB)]
        wvb = [P.sbuf(f"wvb{i}", [128, KC * 128], BF16) for i in range(NWB)]
        wdb = [P.sbuf(f"wdb{i}", [128, HC * 128], BF16) for i in range(2)]
        r_wgb, r_wvb, r_wdb = P.regions(NWB), P.regions(NWB), P.regions(2)
        hbuf = [P.sbuf(f"hbuf{i}", [128, NT + 2], F32) for i in range(2)]
        r_hbuf = P.regions(2)
        cbuf = [P.sbuf(f"cbuf{i}", [128, NT], F32) for i in range(2)]
        r_cbuf = P.regions(2)
        sbuf_ = [P.sbuf(f"sbuf{i}", [128, NT], F32) for i in range(2)]
        r_sbuf = P.regions(2)
        carry = P.sbuf("carry", [128, HC, 2], F32)
        r_carry = P.regions(HC)
        sq = [P.sbuf(f"sq{i}", [128, NT], BF16) for i in range(2)]
        r_sq = P.regions(2)
        rb = [P.sbuf(f"rb{i}", [128, NT], BF16) for i in range(2)]
        r_rb = P.regions(2)
        mean = P.sbuf("mean", [128, NT], F32)
        rstd = P.sbuf("rstd", [128, NT], F32)
        tmp = P.sbuf("tmp", [128, NT], F32)
        r_mean, r_rstd, r_tmp = P.regions(3)
        lt = [P.sbuf(f"lt{i}", [128, NT], F32) for i in range(2)]
        r_lt = P.regions(2)
        ones = P.sbuf("ones", [128, 128], BF16)
        r_ones = P.region()
        vec_s = P.sbuf("vec_s", [128, 4 * KC], F32)
        cvec_s = P.sbuf("cvec_s", [128, 4 * HC], F32)
        hm_s = P.sbuf("hm_s", [128, 1], F32)
        r_vec, r_cvec, r_hm = P.regions(3)
        stage = [(P.sbuf(f"stg{i}", [128, 2048], F32), P.region()) for i in range(2)]
        r_om, r_xs, r_x1b = P.region(), P.regions(KC), P.regions(KC)
        r_act = P.regions(HC)
        ps = [P.psum(f"ps{i}", [128, 512], F32) for i in range(8)]
        r_ps = P.regions(8)

        P.op("pool", lambda e: e.memset(ones[:], 1.0), writes=[r_ones])
        P.dma("sp", vec_s[:], vecs, writes=[r_vec])
        P.dma("sp", cvec_s[:], cvec, writes=[r_cvec])
        P.dma("sp", hm_s[:], hmask, writes=[r_hm])

        r_wo, r_wg, r_wv, r_wd = P.regions(4)

        def cast_w(src, dst, reg):
            nblk, _, cols = src.shape
            step = 2048
            k = 0
            for b in range(nblk):
                for c0 in range(0, cols, step):
                    cw = min(step, cols - c0)
                    i = k % 2
                    k += 1
                    (sa, sr), (ba, br) = stage[i], stage_bf[i]
                    P.dma("sp", sa[:, :cw], src[b, :, c0:c0 + cw], writes=[sr])
                    if k % 2 == 0:
                        P.op("act", lambda e: e.activation(out=ba[:, :cw], in_=sa[:, :cw], func=AF.Copy),
                             reads=[sr], writes=[br])
                    else:
                        P.op("dve", lambda e: e.tensor_copy(out=ba[:, :cw], in_=sa[:, :cw]), reads=[sr], writes=[br])
                    P.dma("pool", dst[b, :, c0:c0 + cw], ba[:, :cw], reads=[br], writes=[reg])

        seen_blk = set()
        stq = [0]

        def fetch(kind, idx, dst, dst_reg):
            src32, scr, reg = {"o": (w_out, wo_b, r_wo), "g": (w_gate, wg_b, r_wg), "v": (w_val, wv_b, r_wv), "d": (w_down, wd_b, r_wd)}[kind]
            if (kind, idx) in seen_blk:
                P.dma("sp", dst[:], scr[idx], reads=[reg], writes=[dst_reg])
                return
            seen_blk.add((kind, idx))
            cols = src32.shape[2]
            for c0 in range(0, cols, 2048):
                cw = min(2048, cols - c0)
                sa, sr = stage[stq[0] % 2]
                stq[0] += 1
                P.dma("sp", sa[:, :cw], src32[idx, :, c0:c0 + cw], writes=[sr])
                if stq[0] % 2 == 0:
                    P.op("act", lambda e: e.activation(out=dst[:, c0:c0 + cw], in_=sa[:, :cw], func=AF.Copy), reads=[sr], writes=[dst_reg])
                else:
                    P.op("pool", lambda e: e.tensor_copy(out=dst[:, c0:c0 + cw], in_=sa[:, :cw]), reads=[sr], writes=[dst_reg])
            P.dma("pool", scr[idx], dst[:], reads=[dst_reg], writes=[reg])

        psi = [0]

        def next_ps():
            i = psi[0] % 4
            psi[0] += 1
            return ps[i], r_ps[i]

        wq = [0]

        def layer_norm(N, gcol, bcol, out_f32, r_out_f32, out_bf, r_out_bf, src, r_src):
            s1, rs1 = ps[4], r_ps[4]
            s2, rs2 = ps[5], r_ps[5]
            for c in range(KC):
                i = c % 2
                P.op("pool", lambda e: e.tensor_copy(out=rb[i][:, :N], in_=src(c)), reads=[r_src[c]], writes=[r_rb[i]])
                P.op("act", lambda e: e.activation(out=sq[i][:, :N], in_=src(c), func=AF.Square),
                     reads=[r_src[c]], writes=[r_sq[i]])
                P.op("pe", lambda e: e.matmul(s1[:, :N], lhsT=ones[:], rhs=rb[i][:, :N], start=(c == 0), stop=(c == KC - 1)),
                     reads=[r_ones, r_rb[i]], writes=[rs1])
                P.op("pe", lambda e: e.matmul(s2[:, :N], lhsT=ones[:], rhs=sq[i][:, :N], start=(c == 0), stop=(c == KC - 1)),
                     reads=[r_ones, r_sq[i]], writes=[rs2])
            P.op("act", lambda e: e.activation(out=mean[:, :N], in_=s1[:, :N], func=AF.Copy, scale=1.0 / D),
                 reads=[rs1], writes=[r_mean])
            P.op("dve", lambda e: e.tensor_tensor(out=tmp[:, :N], in0=mean[:, :N], in1=mean[:, :N], op=ALU.mult),
                 reads=[r_mean], writes=[r_tmp])
            P.op("dve", lambda e: e.scalar_tensor_tensor(out=tmp[:, :N], in0=s2[:, :N], scalar=1.0 / D, in1=tmp[:, :N],
                                                         op0=ALU.mult, op1=ALU.subtract), reads=[rs2, r_tmp], writes=[r_tmp])
            P.op("dve", lambda e: e.tensor_scalar(out=tmp[:, :N], in0=tmp[:, :N], scalar1=LN_EPS_, scalar2=None, op0=ALU.add),
                 reads=[r_tmp], writes=[r_tmp])
            P.op("act", lambda e: e.activation(out=tmp[:, :N], in_=tmp[:, :N], func=AF.Sqrt), reads=[r_tmp], writes=[r_tmp])
            P.op("dve", lambda e: e.reciprocal(out=rstd[:, :N], in_=tmp[:, :N]), reads=[r_tmp], writes=[r_rstd])
            for c in range(KC):
                i = c % 2
                eng = "dve" if c % 2 == 0 else "pool"
                P.op(eng, lambda e: e.tensor_tensor(out=lt[i][:, :N], in0=src(c), in1=mean[:, :N], op=ALU.subtract),
                     reads=[r_src[c], r_mean], writes=[r_lt[i]])
                P.op(eng, lambda e: e.tensor_tensor(out=lt[i][:, :N], in0=lt[i][:, :N], in1=rstd[:, :N], op=ALU.mult),
                     reads=[r_lt[i], r_rstd], writes=[r_lt[i]])
                P.op(eng, lambda e: e.tensor_scalar(out=out_f32(c), in0=lt[i][:, :N],
                                                    scalar1=vec_s[:, gcol * KC + c:gcol * KC + c + 1],
                                                    scalar2=vec_s[:, bcol * KC + c:bcol * KC + c + 1],
                                                    op0=ALU.mult, op1=ALU.add),
                     reads=[r_lt[i], r_vec], writes=[r_out_f32[c]])
                if out_bf is not None:
                    P.op("act", lambda e: e.activation(out=out_bf(c), in_=out_f32(c), func=AF.Copy),
                         reads=[r_out_f32[c]], writes=[r_out_bf[c]])

        def process(col0, N, halo_only, out_col0):
            P.dma("sp", om_b[:, :, :N], omT[:, col0:col0 + N].rearrange("(c p) n -> p c n", p=128), writes=[r_om])
            for c in range(KC):
                P.dma("sp", xs[:, c, :N], xT[c * 128:(c + 1) * 128, col0:col0 + N], writes=[r_xs[c]])
            for mo in range(KC):
                i = wq[0] % NWB
                wq[0] += 1
                fetch("o", mo, wgb[i], r_wgb[i])
                pa, pr = next_ps()
                for kc in range(KC):
                    P.op("pe", lambda e: e.matmul(pa[:, :N], lhsT=wgb[i][:, kc * 128:(kc + 1) * 128], rhs=om_b[:, kc, :N],
                                                  start=(kc == 0), stop=(kc == KC - 1)),
                         reads=[r_wgb[i], r_om], writes=[pr])
                P.op("dve", lambda e: e.scalar_tensor_tensor(out=xs[:, mo, :N], in0=xs[:, mo, :N], scalar=ALPHA_, in1=pa[:, :N],
                                                             op0=ALU.mult, op1=ALU.add), reads=[pr, r_xs[mo]], writes=[r_xs[mo]])
            layer_norm(N, 0, 1, lambda c: xs[:, c, :N], r_xs, lambda c: x1b[:, c, :N], r_x1b, lambda c: xs[:, c, :N], r_xs)
            for hc in range(HC):
                i = wq[0] % NWB
                wq[0] += 1
                fetch("g", hc, wgb[i], r_wgb[i])
                pg, prg = next_ps()
                for kc in range(KC):
                    P.op("pe", lambda e: e.matmul(pg[:, :N], lhsT=wgb[i][:, kc * 128:(kc + 1) * 128], rhs=x1b[:, kc, :N],
                                                  start=(kc == 0), stop=(kc == KC - 1)),
                         reads=[r_wgb[i], r_x1b[kc]], writes=[prg])
                if halo_only:
                    P.op("dve", lambda e: e.tensor_scalar(out=carry[:, hc, :], in0=pg[:, :2], scalar1=hm_s[:, 0:1], scalar2=None,
                                                          op0=ALU.mult), reads=[prg, r_hm], writes=[r_carry[hc]])
                    continue
                fetch("v", hc, wvb[i], r_wvb[i])
                pv, prv = next_ps()
                for kc in range(KC):
                    P.op("pe", lambda e: e.matmul(pv[:, :N], lhsT=wvb[i][:, kc * 128:(kc + 1) * 128], rhs=x1b[:, kc, :N],
                                                  start=(kc == 0), stop=(kc == KC - 1)),
                         reads=[r_wvb[i], r_x1b[kc]], writes=[prv])
                j = hc % 2
                hb, rh = hbuf[j], r_hbuf[j]
                cb, rc = cbuf[j], r_cbuf[j]
                sb, rs = sbuf_[j], r_sbuf[j]
                P.op("act", lambda e: e.activation(out=hb[:, 2:2 + N], in_=pg[:, :N], func=AF.Copy), reads=[prg], writes=[rh])
                P.op("pool", lambda e: e.tensor_copy(out=hb[:, 0:2], in_=carry[:, hc, :]), reads=[r_carry[hc]], writes=[rh])
                P.op("pool", lambda e: e.tensor_copy(out=carry[:, hc, :], in_=hb[:, N:N + 2]), reads=[rh], writes=[r_carry[hc]])
                cw = lambda k: cvec_s[:, k * HC + hc:k * HC + hc + 1]
                P.op("dve", lambda e: e.tensor_scalar(out=cb[:, :N], in0=hb[:, 2:2 + N], scalar1=cw(2), scalar2=cw(3),
                                                      op0=ALU.mult, op1=ALU.add), reads=[rh, r_cvec], writes=[rc])
                P.op("dve", lambda e: e.scalar_tensor_tensor(out=cb[:, :N], in0=hb[:, 1:1 + N], scalar=cw(1), in1=cb[:, :N],
                                                             op0=ALU.mult, op1=ALU.add), reads=[rh, rc, r_cvec], writes=[rc])
                P.op("dve", lambda e: e.scalar_tensor_tensor(out=cb[:, :N], in0=hb[:, 0:N], scalar=cw(0), in1=cb[:, :N],
                                                             op0=ALU.mult, op1=ALU.add), reads=[rh, rc, r_cvec], writes=[rc])
                P.op("act", lambda e: e.activation(out=sb[:, :N], in_=cb[:, :N], func=AF.Silu), reads=[rc], writes=[rs])
                P.op("dve", lambda e: e.tensor_tensor(out=actb[:, hc, :N], in0=sb[:, :N], in1=pv[:, :N], op=ALU.mult),
                     reads=[rs, prv], writes=[r_act[hc]])
            if halo_only:
                return
            for mo in range(KC):
                i = mo % 2
                fetch("d", mo, wdb[i], r_wdb[i])
                pa, pr = next_ps()
                for hc in range(HC):
                    P.op("pe", lambda e: e.matmul(pa[:, :N], lhsT=wdb[i][:, hc * 128:(hc + 1) * 128], rhs=actb[:, hc, :N],
                                                  start=(hc == 0), stop=(hc == HC - 1)),
                         reads=[r_wdb[i], r_act[hc]], writes=[pr])
                P.op("dve", lambda e: e.scalar_tensor_tensor(out=xs[:, mo, :N], in0=xs[:, mo, :N], scalar=ALPHA_, in1=pa[:, :N],
                                                             op0=ALU.mult, op1=ALU.add), reads=[pr, r_xs[mo]], writes=[r_xs[mo]])
            layer_norm(N, 2, 3, lambda c: xs[:, c, :N], r_xs, None, None, lambda c: xs[:, c, :N], r_xs)
            for c in range(KC):
                P.dma("pool", yT[c * 128:(c + 1) * 128, out_col0:out_col0 + N], xs[:, c, :N], reads=[r_xs[c]], writes=[r_y])

        r_y = P.region()
        process(0, 2, True, 0)
        for t0 in range(0, T, NT):
            n = min(NT, T - t0)
            process(2 + t0, n, False, t0)
        P.finish([r_y], "sp")
        print("tail ninstr", P.ninstr, P.cnt)
    return nc


def blk_w(w, kc_rows=True):
    K, M = w.shape
    a = w.reshape(K // 128, 128, M // 128, 128)
    return np.ascontiguousarray(a.transpose(2, 1, 0, 3)).reshape(M // 128, 128, (K // 128) * 128)


def vec_pc(v):
    return np.ascontiguousarray(v.reshape(-1, 128).T)


import ml_dtypes
from concourse.bass_utils import run_bass_kernel_spmd

_B, _S, _NC = 2, 16384, 8
_TT = _S // 4
_PROGS = {}


def _prog(name, fn):
    if name not in _PROGS:
        _PROGS[name] = fn()
    return _PROGS[name]


def _run(nc, in_maps):
    res = run_bass_kernel_spmd(nc, in_maps, core_ids=list(range(_NC)))
    return res.results


def build_a12(S):
    nc = bass.Bass("TRN2", target_bir_lowering=False)
    xT = nc.dram_tensor("xT", [D, S], F32, kind="ExternalInput").ap()
    with contextlib.ExitStack() as outer:
        P = Prog(nc, outer)
        build_a1(S, nc=nc, P=P, pre="m_", xT_ap=xT)
        build_a2(S, nc=nc, P=P, pre="r_", xT_ap=xT)
    return nc


def _tail(omix, xres, w_out, g1, b1, w_gate, w_val, conv_w, conv_b, w_down, g2, b2):
    f = np.float32
    wo, wg, wv, wd = blk_w(w_out), blk_w(w_gate), blk_w(w_val), blk_w(w_down)
    vecs = np.concatenate([vec_pc(v) for v in (g1, b1, g2, b2)], 1).astype(f)
    cvec = np.concatenate([vec_pc(v) for v in (conv_w[0], conv_w[1], conv_w[2], conv_b)], 1).astype(f)
    maps = []
    for c in range(_NC):
        b, q = divmod(c, 4)
        t0 = q * _TT
        omT = np.zeros((D, 2 + _TT), ml_dtypes.bfloat16)
        xT = np.zeros((D, 2 + _TT), f)
        lo = max(t0 - 2, 0)
        omT[:, 2 - (t0 - lo):] = omix[b, lo:t0 + _TT, :].T
        xT[:, 2 - (t0 - lo):] = xres[b][:, lo:t0 + _TT]
        maps.append(dict(omT=omT, xT=xT, hmask=np.full((128, 1), 0.0 if q == 0 else 1.0, f),
                         w_out=wo, w_gate=wg, w_val=wv, w_down=wd, vecs=vecs, cvec=cvec))
    res = _run(_prog("tail", lambda: build_tail(_TT)), maps)
    out = [np.empty((D, _S), f) for _ in range(_B)]
    for c in range(_NC):
        b, q = divmod(c, 4)
        out[b][:, q * _TT:(q + 1) * _TT] = res[c]["yT"]
    return out


def kernel(**inp):
    f = np.float32
    inp = {k: np.asarray(v) for k, v in inp.items()}
    x = inp["x"].astype(f, copy=False)
    pos = inp["positions"].astype(np.int32, copy=False)
    xT = [np.ascontiguousarray(x[b].T) for b in range(_B)]
    posb = [np.ascontiguousarray(pos[b][None, :]) for b in range(_B)]
    g = lambda k: inp[k].astype(f, copy=False)
    maps = []
    for c in range(_NC):
        m1 = a1_inputs(xT[c // 4], posb[c // 4], c % 4, g("l0_w_in"), g("l0_q_norm"), g("l0_w_uq"), g("l0_kv_norm"), g("l0_w_ukv"))
        m2 = a2_inputs(xT[c // 4], c % 4, g("l0_w_in"), g("l0_rwkv_mu"), g("l0_rwkv_w0"), g("l0_rwkv_w2"), g("l0_rwkv_a0"),
                       g("l0_rwkv_a2"), g("l0_rwkv_g2"), g("l0_rwkv_k_k"), g("l0_rwkv_k_a"), g("l0_rwkv_r_k"),
                       g("l0_rwkv_gn_w"), g("l0_rwkv_gn_b"))
        d = {"xT": m1.pop("xT")}
        m2.pop("xT")
        d.update({"m_" + kk: vv for kk, vv in m1.items()})
        d.update({"r_" + kk: vv for kk, vv in m2.items()})
        maps.append(d)
    r12 = _run(_prog("a12", lambda: build_a12(_S)), maps)
    r1 = [{"om": r["m_om"]} for r in r12]
    r2 = [{"om": r["r_om"]} for r in r12]
    omix = np.empty((_B, _S, D), ml_dtypes.bfloat16)
    for c in range(_NC):
        b, hg = divmod(c, 4)
        omix[b, :, hg * 256:(hg + 1) * 256] = r1[c]["om"]
        omix[b, :, 1024 + hg * 256:1024 + (hg + 1) * 256] = r2[c]["om"]
    x1T = _tail(omix, xT, g("l0_w_out"), g("l0_ln1_g"), g("l0_ln1_b"), g("l0_ffn_w_gate"), g("l0_ffn_w_val"),
                g("l0_ffn_conv_w"), g("l0_ffn_conv_b"), g("l0_ffn_w_down"), g("l0_ln2_g"), g("l0_ln2_b"))
    maps = [c_inputs(x1T[c // 4], posb[c // 4], c % 4, g("l1_w_in"), g("l1_gdn_conv_w"), g("l1_gdn_A_log"), g("l1_gdn_dt_bias"),
                     g("l1_gdn_norm"), g("l1_ret_gn_w"), g("l1_ret_gn_b")) for c in range(_NC)]
    r3 = _run(_prog("c", lambda: build_c(_S)), maps)
    for c in range(_NC):
        b, hg = divmod(c, 4)
        omix[b, :, 2 * hg * 128:(2 * hg + 2) * 128] = r3[c]["om"][:, 0:256]
        omix[b, :, 1024 + 2 * hg * 128:1024 + (2 * hg + 2) * 128] = r3[c]["om"][:, 256:512]
    x2T = _tail(omix, x1T, g("l1_w_out"), g("l1_ln1_g"), g("l1_ln1_b"), g("l1_ffn_w_gate"), g("l1_ffn_w_val"),
                g("l1_ffn_conv_w"), g("l1_ffn_conv_b"), g("l1_ffn_w_down"), g("l1_ln2_g"), g("l1_ln2_b"))
    out = np.empty((_B, _S, D), f)
    for b in range(_B):
        out[b] = x2T[b].T
    return out
```
